# Optimizing a Trainium2 kernel written in Bass

```python
import math
import jax, jax.numpy as jnp
from jax import lax
import numpy as np

D_MODEL = 1024
BATCH = 8
SEQ = 2048
DEPTH = 2

BRANCH_WIDTH = D_MODEL
N_BRANCH = 3
EPS = 1e-6
NEG_INF = -1e30
FORCED = 1e4

LRU_WIDTH = BRANCH_WIDTH
LRU_BLOCKS = 8
LRU_BLOCK = LRU_WIDTH // LRU_BLOCKS
CONV_WIDTH = 4
LRU_C = 8.0

NSA_HEADS = 16
NSA_KV_HEADS = 4
NSA_GROUP = NSA_HEADS // NSA_KV_HEADS
NSA_HEAD_DIM = BRANCH_WIDTH // NSA_HEADS
NSA_WIDTH = NSA_HEADS * NSA_HEAD_DIM
CMP_BLOCK = 32
CMP_STRIDE = 16
CMP_HIDDEN = 256
SLC_BLOCK = 64
SLC_TOP_N = 8
SLC_Q_BLOCK = 64
WINDOW = 256
Q_BLOCK = 128

GLA_HEADS = 4
GLA_KEY_WIDTH = D_MODEL // 2
GLA_VALUE_WIDTH = BRANCH_WIDTH
GLA_DK = GLA_KEY_WIDTH // GLA_HEADS
GLA_DV = GLA_VALUE_WIDTH // GLA_HEADS
GLA_GATE_RANK = 16
GLA_TAU = 16.0
GLA_CHUNK = 32

REL_BUCKETS = 32
REL_MAX_EXACT = 16
REL_MAX_DIST = 128

D_FF = -(-8 * D_MODEL // (3 * 256)) * 256

IN_SIZES = (
    LRU_WIDTH,
    LRU_WIDTH,
    NSA_WIDTH,
    6 * NSA_KV_HEADS * NSA_HEAD_DIM,
    3 * NSA_HEADS,
    GLA_KEY_WIDTH,
    GLA_KEY_WIDTH,
    GLA_VALUE_WIDTH,
    GLA_VALUE_WIDTH,
    GLA_GATE_RANK,
    N_BRANCH * D_MODEL,
)
IN_WIDTH = sum(IN_SIZES)

kernel_name = "hybrid_rglru_nsa_gla_block"


def rms_norm(x, g):
    xf = x.astype(jnp.float32)
    y = xf * lax.rsqrt(jnp.mean(xf * xf, axis=-1, keepdims=True) + EPS)
    return (y * g.astype(jnp.float32)).astype(x.dtype)


def rel_bucket(dist):
    n = jnp.maximum(dist, 0)
    nf = jnp.maximum(n, REL_MAX_EXACT).astype(jnp.float32)
    large = REL_MAX_EXACT + (jnp.log(nf / REL_MAX_EXACT) / math.log(REL_MAX_DIST / REL_MAX_EXACT)
                             * (REL_BUCKETS - REL_MAX_EXACT)).astype(jnp.int32)
    large = jnp.minimum(large, REL_BUCKETS - 1)
    return jnp.where(n < REL_MAX_EXACT, n, large)


def rglru_mixer(xa, ga, conv_w, conv_b, w_gates, b_gates, lam):
    B_, S_, _ = xa.shape
    f32 = jnp.float32
    xc = lax.conv_general_dilated(
        xa, conv_w[:, None, :].astype(xa.dtype), window_strides=(1,),
        padding=[(CONV_WIDTH - 1, 0)], dimension_numbers=("NWC", "WIO", "NWC"),
        feature_group_count=LRU_WIDTH) + conv_b
    xb = xc.reshape(B_, S_, LRU_BLOCKS, LRU_BLOCK)
    gates = jnp.einsum("bsnc,knce->kbsne", xb, w_gates).reshape(2, B_, S_, LRU_WIDTH) + b_gates[:, None, None, :]
    r = jax.nn.sigmoid(gates[0].astype(f32))
    i = jax.nn.sigmoid(gates[1].astype(f32))
    log_a = -LRU_C * r * jax.nn.softplus(-lam.astype(f32))
    a = jnp.exp(log_a)
    u = jnp.sqrt(-jnp.expm1(2.0 * log_a)) * (i * xc.astype(f32))

    def combine(left, right):
        a1, b1 = left
        a2, b2 = right
        return a1 * a2, a2 * b1 + b2

    _, h = lax.associative_scan(combine, (a, u), axis=1)
    return h.astype(xa.dtype) * jax.nn.gelu(ga)


def compress_blocks(t, pos_emb, w1, w2):
    B_, G_, S_, hd = t.shape
    r = CMP_BLOCK // CMP_STRIDE
    n_cmp = S_ // CMP_STRIDE - r + 1
    ch = t.reshape(B_, G_, S_ // CMP_STRIDE, CMP_STRIDE, hd)
    blocks = jnp.concatenate([ch[:, :, j:j + n_cmp] for j in range(r)], axis=3) + pos_emb
    return jax.nn.gelu(blocks.reshape(B_, G_, n_cmp, CMP_BLOCK * hd) @ w1) @ w2


def nsa_mixer(q, kv, gate, rel_table, cmp_pos, cmp_w1, cmp_w2):
    B_, S_, _ = q.shape
    G, R, hd = NSA_KV_HEADS, NSA_GROUP, NSA_HEAD_DIM
    f32 = jnp.float32
    q = q.reshape(B_, S_, G, R, hd).transpose(0, 2, 3, 1, 4) * (hd ** -0.5)
    kv = kv.reshape(B_, S_, 6, G, hd).transpose(2, 0, 3, 1, 4)
    k_cmp, v_cmp, k_slc, v_slc, k_win, v_win = kv[0], kv[1], kv[2], kv[3], kv[4], kv[5]
    tbl = rel_table.reshape(REL_BUCKETS, G, R)
    pos = jnp.arange(S_)

    kc = compress_blocks(k_cmp, cmp_pos[0], cmp_w1[0], cmp_w2[0])
    vc = compress_blocks(v_cmp, cmp_pos[1], cmp_w1[1], cmp_w2[1])
    n_cmp = kc.shape[2]
    cmp_end = jnp.arange(n_cmp) * CMP_STRIDE + CMP_BLOCK - 1
    valid_c = cmp_end[None, :] <= pos[:, None]
    bias_c = tbl[rel_bucket(pos[:, None] - cmp_end[None, :])].transpose(2, 3, 0, 1)
    logit_c = jnp.einsum("bgrsd,bgcd->bgrsc", q, kc).astype(f32) + bias_c
    p_c = jax.nn.softmax(jnp.where(valid_c, logit_c, NEG_INF), axis=-1) * valid_c
    o_cmp = jnp.einsum("bgrsc,bgcd->bgrsd", p_c.astype(q.dtype), vc)

    n_slc = S_ // SLC_BLOCK
    top_n = min(SLC_TOP_N, n_slc)
    slc_start = jnp.arange(n_slc) * SLC_BLOCK
    cmp_start = cmp_end - (CMP_BLOCK - 1)
    overlap = jnp.clip(jnp.minimum(cmp_start[:, None] + CMP_BLOCK, slc_start[None, :] + SLC_BLOCK)
                       - jnp.maximum(cmp_start[:, None], slc_start[None, :]), 0).astype(f32) / CMP_BLOCK
    imp = jnp.einsum("bgrsc,cj->bgsj", p_c, overlap)
    blk = jnp.arange(n_slc)[None, :]
    cur = (pos // SLC_BLOCK)[:, None]
    forced = (blk == 0) | (blk == cur) | (blk == cur - 1)
    score = jnp.where(forced, FORCED, jnp.where(blk <= cur, imp, -FORCED))
    _, sel = lax.top_k(score, top_n)

    kb = k_slc.reshape(B_, G, n_slc, SLC_BLOCK, hd)
    vb = v_slc.reshape(B_, G, n_slc, SLC_BLOCK, hd)
    nqs = S_ // SLC_Q_BLOCK
    q_blk = q.reshape(B_, G, R, nqs, SLC_Q_BLOCK, hd).transpose(3, 0, 1, 2, 4, 5)
    sel_blk = sel.reshape(B_, G, nqs, SLC_Q_BLOCK, top_n).transpose(2, 0, 1, 3, 4)
    bi = jnp.arange(B_)[:, None, None, None]
    gi = jnp.arange(G)[None, :, None, None]
    gi5 = jnp.arange(G)[None, :, None, None, None]
    tbl_g = tbl.transpose(1, 0, 2)

    def slc_block(args):
        qb, sb, start = args
        ks = kb[bi, gi, sb]
        vs = vb[bi, gi, sb]
        tq = start + jnp.arange(SLC_Q_BLOCK)
        tk = sb[..., None] * SLC_BLOCK + jnp.arange(SLC_BLOCK)
        dist = tq[:, None, None] - tk
        bias = tbl_g[gi5, rel_bucket(dist)].transpose(0, 1, 5, 2, 3, 4)
        logit = jnp.einsum("bgrqd,bgqnld->bgrqnl", qb, ks).astype(f32) + bias
        logit = jnp.where((dist >= 0)[:, :, None], logit, NEG_INF)
        p = jax.nn.softmax(logit.reshape(B_, G, R, SLC_Q_BLOCK, top_n * SLC_BLOCK), axis=-1)
        p = p.reshape(B_, G, R, SLC_Q_BLOCK, top_n, SLC_BLOCK)
        return jnp.einsum("bgrqnl,bgqnld->bgrqd", p.astype(qb.dtype), vs)

    o_slc = lax.map(slc_block, (q_blk, sel_blk, jnp.arange(nqs) * SLC_Q_BLOCK))
    o_slc = o_slc.transpose(1, 2, 3, 0, 4, 5).reshape(B_, G, R, S_, hd)

    nqb = S_ // Q_BLOCK
    nband = WINDOW // Q_BLOCK + 1

    def band(t):
        tp = jnp.pad(t, ((0, 0), (0, 0), ((nband - 1) * Q_BLOCK, 0), (0, 0)))
        tb = tp.reshape(B_, G, nqb + nband - 1, Q_BLOCK, hd)
        return jnp.concatenate([tb[:, :, j:j + nqb] for j in range(nband)], axis=3)

    kw, vw = band(k_win), band(v_win)
    qi = jnp.arange(Q_BLOCK)
    kj = jnp.arange(nband * Q_BLOCK)
    dist_w = (nband - 1) * Q_BLOCK + qi[:, None] - kj[None, :]
    key_pos = jnp.arange(nqb)[:, None] * Q_BLOCK - (nband - 1) * Q_BLOCK + kj[None, :]
    mask_w = ((dist_w >= 0) & (dist_w < WINDOW))[None] & (key_pos >= 0)[:, None, :]
    bias_w = tbl[rel_bucket(dist_w)].transpose(2, 3, 0, 1)[:, :, None]
    qw = q.reshape(B_, G, R, nqb, Q_BLOCK, hd)
    logit_w = jnp.einsum("bgrnqd,bgnkd->bgrnqk", qw, kw).astype(f32) + bias_w
    p_w = jax.nn.softmax(jnp.where(mask_w, logit_w, NEG_INF), axis=-1)
    o_win = jnp.einsum("bgrnqk,bgnkd->bgrnqd", p_w.astype(q.dtype), vw).reshape(B_, G, R, S_, hd)

    g = jax.nn.sigmoid(gate).reshape(B_, S_, 3, G, R).transpose(2, 0, 3, 4, 1)[..., None]
    o = g[0] * o_cmp + g[1] * o_slc + g[2] * o_win
    return o.transpose(0, 3, 1, 2, 4).reshape(B_, S_, NSA_WIDTH)


def gla_mixer(q, k, v, og, lr, wa2, ba, norm_g):
    B_, S_, _ = q.shape
    f32 = jnp.float32
    nc = S_ // GLA_CHUNK
    log_alpha = jax.nn.log_sigmoid((lr @ wa2 + ba).astype(f32)) / GLA_TAU

    def heads(t, d):
        return t.reshape(B_, nc, GLA_CHUNK, GLA_HEADS, d).transpose(1, 0, 3, 2, 4).astype(f32)

    qh = heads(q, GLA_DK) * (GLA_DK ** -0.5)
    kh = heads(k, GLA_DK)
    vh = heads(v, GLA_DV)
    b = jnp.cumsum(heads(log_alpha, GLA_DK), axis=3)
    q_t = qh * jnp.exp(b)
    k_t = kh * jnp.exp(-b)
    b_last = b[:, :, :, -1:]
    k_end = kh * jnp.exp(b_last - b)
    decay = jnp.exp(b_last[:, :, :, 0])
    causal = jnp.tril(jnp.ones((GLA_CHUNK, GLA_CHUNK), bool))
    att = jnp.where(causal, jnp.einsum("nbhcd,nbhsd->nbhcs", q_t, k_t), 0.0)
    o_intra = jnp.einsum("nbhcs,nbhse->nbhce", att, vh)

    def step(state, xs):
        q_c, ke_c, v_c, dec_c = xs
        o = jnp.einsum("bhcd,bhde->bhce", q_c, state)
        state = state * dec_c[..., None] + jnp.einsum("bhcd,bhce->bhde", ke_c, v_c)
        return state, o

    s0 = jnp.zeros((B_, GLA_HEADS, GLA_DK, GLA_DV), f32)
    _, o_inter = lax.scan(step, s0, (q_t, k_end, vh, decay))
    o = rms_norm(o_intra + o_inter, norm_g)
    o = o.transpose(1, 0, 3, 2, 4).reshape(B_, S_, GLA_VALUE_WIDTH)
    return o.astype(og.dtype) * jax.nn.silu(og)


def hybrid_layer(x, rel_table, norm_g, w_in, conv_w, conv_b, lru_w_gates, lru_b_gates, lru_lambda,
                 cmp_pos, cmp_w1, cmp_w2, gla_wa2, gla_ba, gla_norm, w_branch, w_out, w_ffn_in, w_ffn_out):
    B_, S_, D_ = x.shape
    h = rms_norm(x, norm_g[0])
    proj = h @ w_in
    cuts, acc = [], 0
    for size in IN_SIZES[:-1]:
        acc += size
        cuts.append(acc)
    (lru_x, lru_g, nsa_q, nsa_kv, nsa_gate, gla_q, gla_k, gla_v, gla_og, gla_lr,
     merge_logit) = jnp.split(proj, cuts, axis=-1)
    y_a = rglru_mixer(lru_x, lru_g, conv_w, conv_b, lru_w_gates, lru_b_gates, lru_lambda)
    y_b = nsa_mixer(nsa_q, nsa_kv, nsa_gate, rel_table, cmp_pos, cmp_w1, cmp_w2)
    y_c = gla_mixer(gla_q, gla_k, gla_v, gla_og, gla_lr, gla_wa2, gla_ba, gla_norm)
    gates = jax.nn.sigmoid(merge_logit.reshape(B_, S_, N_BRANCH, D_))
    merged = (gates[:, :, 0] * (y_a @ w_branch[0])
              + gates[:, :, 1] * (y_b @ w_branch[1])
              + gates[:, :, 2] * (y_c @ w_branch[2]))
    x = x + rms_norm(merged @ w_out, norm_g[1])
    h = rms_norm(x, norm_g[2])
    gt, up = jnp.split(h @ w_ffn_in, 2, axis=-1)
    x = x + rms_norm((jax.nn.silu(gt) * up) @ w_ffn_out, norm_g[3])
    return x


def setup_inputs(seed: int = 0) -> dict:
    key = jax.random.key(seed)
    ks = jax.random.split(key, 20)
    f32 = jnp.float32

    def nrm(k, shape, scale):
        return jax.random.normal(k, shape, f32) * scale

    u = jax.random.uniform(ks[8], (DEPTH, LRU_WIDTH), f32, 0.9, 0.999)
    s = u ** (1.0 / LRU_C)
    return {
        "x": nrm(ks[0], (BATCH, SEQ, D_MODEL), 1.0),
        "rel_table": nrm(ks[1], (REL_BUCKETS, NSA_HEADS), 0.5),
        "norm_g": 1.0 + nrm(ks[2], (DEPTH, 4, D_MODEL), 0.05),
        "w_in": nrm(ks[3], (DEPTH, D_MODEL, IN_WIDTH), D_MODEL ** -0.5),
        "conv_w": nrm(ks[4], (DEPTH, CONV_WIDTH, LRU_WIDTH), CONV_WIDTH ** -0.5),
        "conv_b": nrm(ks[5], (DEPTH, LRU_WIDTH), 0.01),
        "lru_w_gates": nrm(ks[6], (DEPTH, 2, LRU_BLOCKS, LRU_BLOCK, LRU_BLOCK), LRU_BLOCK ** -0.5),
        "lru_b_gates": nrm(ks[7], (DEPTH, 2, LRU_WIDTH), 0.01),
        "lru_lambda": jnp.log(s) - jnp.log1p(-s),
        "cmp_pos": nrm(ks[9], (DEPTH, 2, CMP_BLOCK, NSA_HEAD_DIM), 0.1),
        "cmp_w1": nrm(ks[10], (DEPTH, 2, CMP_BLOCK * NSA_HEAD_DIM, CMP_HIDDEN), (CMP_BLOCK * NSA_HEAD_DIM) ** -0.5),
        "cmp_w2": nrm(ks[11], (DEPTH, 2, CMP_HIDDEN, NSA_HEAD_DIM), CMP_HIDDEN ** -0.5),
        "gla_wa2": nrm(ks[12], (DEPTH, GLA_GATE_RANK, GLA_KEY_WIDTH), GLA_GATE_RANK ** -0.5),
        "gla_ba": 1.0 + nrm(ks[13], (DEPTH, GLA_KEY_WIDTH), 0.5),
        "gla_norm": 1.0 + nrm(ks[14], (DEPTH, GLA_DV), 0.05),
        "w_branch": nrm(ks[15], (DEPTH, N_BRANCH, BRANCH_WIDTH, D_MODEL), BRANCH_WIDTH ** -0.5),
        "w_out": nrm(ks[16], (DEPTH, D_MODEL, D_MODEL), D_MODEL ** -0.5),
        "w_ffn_in": nrm(ks[17], (DEPTH, D_MODEL, 2 * D_FF), D_MODEL ** -0.5),
        "w_ffn_out": nrm(ks[18], (DEPTH, D_FF, D_MODEL), D_FF ** -0.5),
    }


def reference(x, rel_table, norm_g, w_in, conv_w, conv_b, lru_w_gates, lru_b_gates, lru_lambda,
              cmp_pos, cmp_w1, cmp_w2, gla_wa2, gla_ba, gla_norm, w_branch, w_out, w_ffn_in, w_ffn_out):
    for l in range(DEPTH):
        x = hybrid_layer(x, rel_table, norm_g[l], w_in[l], conv_w[l], conv_b[l], lru_w_gates[l],
                         lru_b_gates[l], lru_lambda[l], cmp_pos[l], cmp_w1[l], cmp_w2[l],
                         gla_wa2[l], gla_ba[l], gla_norm[l], w_branch[l], w_out[l],
                         w_ffn_in[l], w_ffn_out[l])
    return x
```

```python
import numpy as np
from contextlib import ExitStack
import concourse.bass as bass
import concourse.mybir as mybir
from concourse.bass_utils import run_bass_kernel_spmd

F32 = mybir.dt.float32
BF16 = mybir.dt.bfloat16
ALU = mybir.AluOpType
AF = mybir.ActivationFunctionType
AX = mybir.AxisListType

SEQ = 2048
D = 1024
NT = 16
DEPTH = 2
EPS = 1e-6
IN_W = 10816
C_LRUX, C_LRUG, C_Q, C_KV, C_GATE, C_GQ, C_GK, C_GV, C_GOG, C_GLR, C_MG = 0, 1024, 2048, 3072, 4608, 4656, 5168, 5680, 6704, 7728, 7744
DFF = 2816
NEG = -30000.0


class Buf:
    __slots__ = ("name", "w", "r", "excl")

    def __init__(self, name="", excl=False):
        self.name = name
        self.w = None
        self.r = []
        self.excl = excl


class Sched:
    ENG = ("sync", "scalar", "vector", "gpsimd", "tensor")
    DMAQ = ("sync", "gpsimd", "scalar")

    def __init__(self, nc, es, n_dma_sems=12):
        self.nc = nc
        self.q = {e: [] for e in self.ENG}
        self.cnt = {e: 0 for e in self.ENG}
        self.sems = []
        self.esem = {}
        for e in self.ENG:
            self.esem[e] = len(self.sems)
            self.sems.append(es.enter_context(nc.semaphore("s_" + e)))
        self.known = {e: {} for e in self.ENG}
        self.dpool = {}
        self.dcnt = {}
        self.dlast = {}
        for qn in self.DMAQ:
            self.dpool[qn] = []
            for i in range(n_dma_sems):
                self.dpool[qn].append(len(self.sems))
                self.sems.append(es.enter_context(nc.semaphore(f"d_{qn}_{i}")))
            self.dcnt[qn] = 0
        self.K = n_dma_sems

    def _waits(self, eng, r, w):
        waits = {}
        kn = self.known[eng]
        own_pe = self.esem["tensor"] if eng == "tensor" else -1

        def need(kv):
            k, v = kv
            if k == own_pe:
                return
            if kn.get(k, 0) < v and waits.get(k, 0) < v:
                waits[k] = v

        own = self.esem.get(eng, -2)
        for b in r:
            if b.w is not None:
                need(b.w)
            if b.excl:
                for x in b.r:
                    if x[0] != own:
                        need(x)
        for b in w:
            if b.w is not None:
                need(b.w)
            for x in b.r:
                need(x)
        for k, v in waits.items():
            kn[k] = v
        return list(waits.items())

    def op(self, eng, fn, r=(), w=()):
        waits = self._waits(eng, r, w)
        self.cnt[eng] += 1
        seq = self.cnt[eng]
        k = self.esem[eng]
        self.q[eng].append((waits, fn, (k, 1)))
        for b in w:
            b.w = (k, seq)
            b.r = []
        for b in r:
            if b not in w:
                b.r.append((k, seq))
                if len(b.r) > 24:
                    b.r = b.r[-24:] if False else self._compact(b.r)

    @staticmethod
    def _compact(lst):
        d = {}
        for k, v in lst:
            if d.get(k, 0) < v:
                d[k] = v
        return list(d.items())

    def dma(self, qn, out, in_, r=(), w=(), **kw):
        waits = self._waits(qn, r, w)
        i = self.dcnt[qn]
        self.dcnt[qn] += 1
        k = self.dpool[qn][i % self.K]
        val = 16 * (i // self.K + 1)
        if val > 16 and self.known[qn].get(k, 0) < val - 16:
            waits.append((k, val - 16))
            self.known[qn][k] = val - 16
        self.dlast[k] = val
        self.q[qn].append((waits, lambda e: e.dma_start(out=out, in_=in_, **kw), (k, 16)))
        for b in w:
            b.w = (k, val)
            b.r = []
        for b in r:
            if b not in w:
                b.r.append((k, val))
                if len(b.r) > 24:
                    b.r = self._compact(b.r)

    def pe_drain(self):
        k = self.esem["tensor"]
        if self.cnt["tensor"] > 0:
            self.q["tensor"].append(([(k, self.cnt["tensor"])], None, None))

    def barrier(self):
        tgt = [(self.esem[e], self.cnt[e]) for e in self.ENG if self.cnt[e] > 0]
        tgt += list(self.dlast.items())
        for e in self.ENG:
            waits = []
            for k, v in tgt:
                if e == "tensor" and k == self.esem["tensor"]:
                    continue
                if self.known[e].get(k, 0) < v:
                    waits.append((k, v))
                    self.known[e][k] = v
            if waits:
                self.q[e].append((waits, None, None))

    def emit(self):
        nc = self.nc
        with nc.Block() as block:
            for e in self.ENG:
                def body(eng, _e=e):
                    for waits, fn, inc in self.q[_e]:
                        for k, v in waits:
                            eng.wait_ge(self.sems[k], v)
                        if fn is not None:
                            ins = fn(eng)
                            ins.then_inc(self.sems[inc[0]], inc[1])
                getattr(block, e)(body)


def fap(a, dims):
    return bass.AP(a.tensor, a.offset, [list(a.ap[0])] + [list(d) for d in dims])


def _rel_bucket(d):
    d = np.asarray(d)
    n = np.maximum(d, 0)
    nf = np.maximum(n, 16).astype(np.float32)
    large = 16 + (np.log(nf / np.float32(16)) / np.float32(np.log(128 / 16)) * np.float32(16)).astype(np.int32)
    large = np.minimum(large, 31)
    return np.where(n < 16, n, large)


def host_consts():
    c = {}
    c["ident"] = np.eye(128, dtype=np.float32)
    c["antiid"] = np.eye(128, dtype=np.float32)[::-1].copy()
    aid127 = np.zeros((128, 128), np.float32)
    for i in range(127):
        aid127[i, 126 - i] = 1.0
    aid127[127, 127] = 1.0
    c["antiid127"] = aid127
    s = np.arange(128)
    c["triu"] = (s[:, None] <= s[None, :]).astype(np.float32)
    c["tril"] = (s[:, None] > s[None, :]).astype(np.float32)
    def oh(deltas, valid):
        m = np.zeros((33, len(deltas)), np.float32)
        b = _rel_bucket(deltas)
        for i, (dd, v) in enumerate(zip(deltas, valid)):
            if v:
                m[b[i], i] = 1.0
            else:
                m[32, i] = 1.0
        return m
    dc = np.arange(-2048, 2048)
    c["oh_c"] = oh(dc, dc >= 0)
    ds = np.arange(-512, 512)
    c["oh_s"] = oh(ds, ds >= 0)
    c["oh_w"] = oh(ds, (ds >= 0) & (ds < 256))
    cs = np.arange(127) * 16
    js = np.arange(32) * 64
    ov = np.clip(np.minimum(cs[:, None] + 32, js[None, :] + 64) - np.maximum(cs[:, None], js[None, :]), 0, None).astype(np.float32) / 32.0
    ovx = np.zeros((128, 33), np.float32)
    ovx[:127, :32] = ov
    ovx[:127, 32] = 1.0
    c["ovx"] = ovx
    pos = np.arange(SEQ)
    cur = pos // 64
    blk = np.arange(32)[None, :]
    cand = (blk >= 1) & (blk <= cur[:, None] - 2)
    forced = (blk == 0) | (blk == cur[:, None]) | (blk == cur[:, None] - 1)
    c["cand"] = cand.astype(np.float32).reshape(NT, 128, 32).transpose(1, 0, 2).copy()
    c["negc"] = ((cand.astype(np.float32) - 1.0) * 1e4).reshape(NT, 128, 32).transpose(1, 0, 2).copy()
    c["forced"] = forced.astype(np.float32).reshape(NT, 128, 32).transpose(1, 0, 2).copy()
    ex = np.zeros((128, NT, 128), np.float32)
    for kt in range(NT):
        for key in range(128):
            ex[2 * kt + key // 64, kt, key] = 1.0
    c["expand_near"] = ex.copy()
    ex[32:34] = 1.0
    c["expand"] = ex
    return c


CONST_SHAPES = None


def build(debug=False, n_layers=DEPTH, stop=None):
    nc = bass.Bass("TRN2", target_bir_lowering=False)
    consts = host_consts()
    din = {}

    def inp(name, shape, dt=F32):
        din[name] = nc.dram_tensor(name, list(shape), dt, kind="ExternalInput").ap()
        return din[name]

    x_in = inp("x", [SEQ, D])
    rel_table = inp("rel_table", [32, 16])
    norm_g = inp("norm_g", [DEPTH, 4, D])
    w_in = inp("w_in", [DEPTH, D, IN_W])
    conv_w = inp("conv_w", [DEPTH, 4, D])
    conv_b = inp("conv_b", [DEPTH, D])
    lru_wg = inp("lru_w_gates", [DEPTH, 2, 8, 128, 128])
    lru_bg = inp("lru_b_gates", [DEPTH, 2, D])
    lru_lam = inp("lru_lambda", [DEPTH, D])
    cmp_pos = inp("cmp_pos", [DEPTH, 2, 32, 64])
    cmp_w1 = inp("cmp_w1", [DEPTH, 2, 2048, 256])
    cmp_w2 = inp("cmp_w2", [DEPTH, 2, 256, 64])
    gla_wa2 = inp("gla_wa2", [DEPTH, 16, 512])
    gla_ba = inp("gla_ba", [DEPTH, 512])
    gla_norm = inp("gla_norm", [DEPTH, 256])
    w_branch = inp("w_branch", [DEPTH, 3, D, D])
    w_out = inp("w_out", [DEPTH, D, D])
    w_ffn_in = inp("w_ffn_in", [DEPTH, D, 2 * DFF])
    w_ffn_out = inp("w_ffn_out", [DEPTH, DFF, D])
    cin = {k: inp("c_" + k, v.shape) for k, v in consts.items()}

    okind = "ExternalOutput"
    y_out = nc.dram_tensor("out", [SEQ, D], F32, kind=okind).ap()
    skind = "ExternalOutput"
    xres = nc.dram_tensor("xres", [SEQ, D], F32, kind=skind).ap()
    xmid = nc.dram_tensor("xmid", [SEQ, D], F32, kind=skind).ap()
    ysc = nc.dram_tensor("ysc", [3, D, SEQ], BF16, kind=skind).ap()
    tc_d = nc.dram_tensor("tc_d", [2, 16, 4096], BF16, kind="Internal").ap()
    ts_d = nc.dram_tensor("ts_d", [2, 16, 1024], BF16, kind="Internal").ap()
    tw_d = nc.dram_tensor("tw_d", [2, 16, 1024], BF16, kind="Internal").ap()
    b_xres, b_xmid, b_ysc, b_tabs = Buf(), Buf(), [Buf(), Buf(), Buf()], Buf()

    with ExitStack() as es:
        S = Sched(nc, es)
        es.enter_context(nc.allow_non_contiguous_dma(reason="small param loads"))

        def mm(out, lhsT, rhs, start, stop, r, w):
            S.op("tensor", lambda e: e.matmul(out, lhsT=lhsT, rhs=rhs, start=start, stop=stop), r, w)

        def trp(out, in_, ident, r, w):
            S.op("tensor", lambda e: e.transpose(out, in_, ident), r, w)

        def act(out, in_, func, r, w, **kw):
            S.op("scalar", lambda e: e.activation(out=out, in_=in_, func=func, **kw), r, w)

        def tt(eng, out, in0, in1, op, r, w):
            S.op(eng, lambda e: e.tensor_tensor(out=out, in0=in0, in1=in1, op=op), r, w)

        def tsc(eng, out, in0, s1, op0, r, w, s2=None, op1=None):
            if op1 is None:
                S.op(eng, lambda e: e.tensor_scalar(out=out, in0=in0, scalar1=s1, scalar2=None, op0=op0), r, w)
            else:
                S.op(eng, lambda e: e.tensor_scalar(out=out, in0=in0, scalar1=s1, scalar2=s2, op0=op0, op1=op1), r, w)

        def stt(out, in0, scalar, in1, op0, op1, r, w):
            S.op("vector", lambda e: e.scalar_tensor_tensor(out=out, in0=in0, scalar=scalar, in1=in1, op0=op0, op1=op1), r, w)

        def cp(eng, out, in_, r, w):
            if eng == "scalar":
                S.op("scalar", lambda e: e.copy(out=out, in_=in_), r, w)
            else:
                S.op(eng, lambda e: e.tensor_copy(out=out, in_=in_), r, w)

        def recip(out, in_, r, w):
            S.op("vector", lambda e: e.reciprocal(out=out, in_=in_), r, w)

        def memset(eng, ap, val, w):
            S.op(eng, lambda e: e.memset(ap, val), (), w)

        def dma(q, out, in_, r=(), w=()):
            S.dma(q, out, in_, r, w)

        class T:
            _n = [0]

            def __init__(self, stack, name, shape, dt, psum=False):
                T._n[0] += 1
                name = f"{name}_{T._n[0]}"
                self.t = stack.enter_context((nc.psum_tensor if psum else nc.sbuf_tensor)(name, list(shape), dt))
                self.b = Buf(name)

            def __getitem__(self, idx):
                return self.t[idx]

        PA = T(es, "PA", [128, 2048], F32, psum=True)
        PB = T(es, "PB", [128, 2048], F32, psum=True)
        pbank = []
        for i in range(8):
            src = PA if i < 4 else PB
            pbank.append((src.t[:, (i % 4) * 512:(i % 4 + 1) * 512], Buf(f"bank{i}", excl=True)))
        PAb = PA.t.bitcast(BF16)
        PBb = PB.t.bitcast(BF16)

        def bank_bf(i):
            src = PAb if i < 4 else PBb
            return src[:, (i % 4) * 1024:(i % 4 + 1) * 1024]

        ident_f = T(es, "ident_f", [128, 128], F32)
        ident_b = T(es, "ident_b", [128, 128], BF16)
        dma("sync", ident_f[:], cin["ident"], w=[ident_f.b])
        cp("vector", ident_b[:], ident_f[:], [ident_f.b], [ident_b.b])

        hT = T(es, "hT", [128, 8, SEQ], BF16)
        hT_b = [Buf(f"hT{t}") for t in range(NT)]

        def load_gain(ph, l, i):
            gt = T(ph, f"gain{i}", [128, D], F32)
            src = norm_g[l, i:i + 1, :]
            dma("sync", gt[:], bass.AP(src.tensor, src.offset, [[0, 128], [1, D]]), w=[gt.b])
            return gt

        def norm_transpose_tile(ph, t, xt_ap, xt_buf, gt, ss, rs, hb, junk, pbi):
            act(junk[:], xt_ap, AF.Square, [xt_buf], [junk.b, ss.b], accum_out=ss[:, t:t + 1])
            act(rs[:, t:t + 1], ss[:, t:t + 1], AF.Sqrt, [ss.b], [rs.b], scale=1.0 / D, bias=EPS)
            recip(rs[:, t:t + 1], rs[:, t:t + 1], [rs.b], [rs.b])
            stt(hb[:], xt_ap, rs[:, t:t + 1], gt[:], ALU.mult, ALU.mult, [xt_buf, rs.b, gt.b], [hb.b])
            pv, pbuf = bank_bf(pbi), pbank[pbi][1]
            for kc in range(8):
                trp(pv[:, kc * 128:(kc + 1) * 128], hb[:, kc * 128:(kc + 1) * 128], ident_b[:], [hb.b, ident_b.b], [pbuf])
            cp("scalar", hT[:, :, t * 128:(t + 1) * 128], pv.rearrange("p (k s) -> p k s", k=8), [pbuf], [hT_b[t]])

        def load_slab(dst_ap, w2d, c0, ncols, wbuf, nk=8):
            src = w2d[:, c0:c0 + ncols].rearrange("(kc p) n -> p kc n", p=128)
            dma("gpsimd", dst_ap, src, w=[wbuf])

        def proj_fm(wslab, wbuf, col_off, M, rhs_tile, rhs_bufs, out_banks, nk=8, sc_list=(0, 1, 2, 3)):
            for i, sc in enumerate(sc_list):
                pa, pb_ = out_banks[i]
                for kc in range(nk):
                    mm(pa[0:M, :], wslab[:, kc, col_off:col_off + M], rhs_tile[:, kc, sc * 512:(sc + 1) * 512],
                       kc == 0, kc == nk - 1, [wbuf] + rhs_bufs[sc * 4:(sc + 1) * 4], [pb_])

        def phase_A(l, x_src):
            with ExitStack() as ph:
                xt = [T(ph, f"xtA{i}", [128, D], F32) for i in range(2)]
                hb = [T(ph, f"hbA{i}", [128, D], BF16) for i in range(2)]
                junk = T(ph, "junkA", [128, D], BF16)
                ss = T(ph, "ssA", [128, NT], F32)
                rs = T(ph, "rsA", [128, NT], F32)
                g0 = load_gain(ph, l, 0)
                for t in range(NT):
                    dma("sync", xt[t % 2][:], x_src[t * 128:(t + 1) * 128, :], r=[b_xres], w=[xt[t % 2].b])
                    norm_transpose_tile(ph, t, xt[t % 2][:], xt[t % 2].b, g0, ss, rs, hb[t % 2], junk, t % 2)
                S.barrier()

        def phase_lru(l):
            with ExitStack() as ph:
                prow = T(ph, "prow", [8, D], F32)
                lpT = T(ph, "lpT", [128, 8, 8], F32)
                sp = T(ph, "lru_sp", [128, 8, 6], F32)
                wg = T(ph, "lru_wg", [128, 2, 8, 128], BF16)
                slab = [T(ph, f"lslab{i}", [128, 8, 2, 128], BF16) for i in range(2)]
                XA = T(ph, "XA", [128, SEQ + 4], F32)
                XC = T(ph, "XC", [128, SEQ], F32)
                XCB = T(ph, "XCB", [128, SEQ], BF16)
                R = T(ph, "R", [128, SEQ], F32)
                A = T(ph, "A", [128, SEQ], F32)
                I = T(ph, "I", [128, SEQ], F32)
                H = T(ph, "H", [128, SEQ], F32)
                YA = [T(ph, f"YA{i}", [128, SEQ], BF16) for i in range(2)]
                for k in range(4):
                    dma("sync", prow[k:k + 1, :], conv_w[l, k:k + 1, :], w=[prow.b])
                dma("sync", prow[4:5, :], conv_b[l:l + 1, :], w=[prow.b])
                dma("sync", prow[5:7, :], lru_bg[l], w=[prow.b])
                dma("sync", prow[7:8, :], lru_lam[l:l + 1, :], w=[prow.b])
                pv, pbuf = pbank[7]
                for c in range(8):
                    trp(pv[:, c * 8:(c + 1) * 8], prow[0:8, c * 128:(c + 1) * 128], ident_f[0:8, 0:8], [prow.b, ident_f.b], [pbuf])
                cp("vector", lpT[:], pv[:, 0:64].rearrange("p (c k) -> p c k", c=8), [pbuf], [lpT.b])
                xs, ln1, ser, msk, nsp8, nsp16 = (sp[:, :, i] for i in range(6))
                act(xs, lpT[:, :, 7], AF.Exp, [lpT.b], [sp.b], scale=-1.0)
                act(ln1, xs, AF.Ln, [sp.b], [sp.b], bias=1.0)
                tsc("vector", ser, xs, -0.25, ALU.mult, [sp.b], [sp.b], 1.0 / 3.0, ALU.add)
                tt("vector", ser, ser, xs, ALU.mult, [sp.b], [sp.b])
                tsc("vector", ser, ser, -1.0, ALU.mult, [sp.b], [sp.b], 0.5, ALU.add)
                tt("vector", ser, ser, xs, ALU.mult, [sp.b], [sp.b])
                tsc("vector", ser, ser, -1.0, ALU.mult, [sp.b], [sp.b], 1.0, ALU.add)
                tt("vector", ser, ser, xs, ALU.mult, [sp.b], [sp.b])
                tsc("vector", msk, xs, 0.03, ALU.is_lt, [sp.b], [sp.b])
                tt("vector", ser, ser, ln1, ALU.subtract, [sp.b], [sp.b])
                tt("vector", ser, ser, msk, ALU.mult, [sp.b], [sp.b])
                tt("vector", ser, ser, ln1, ALU.add, [sp.b], [sp.b])
                tsc("vector", nsp8, ser, -8.0, ALU.mult, [sp.b], [sp.b])
                tsc("vector", nsp16, ser, -16.0, ALU.mult, [sp.b], [sp.b])
                dma("gpsimd", wg[:], lru_wg[l].rearrange("k n c e -> c k n e"), w=[wg.b])
                memset("vector", XA[:, 0:3], 0.0, [XA.b])
                w2d = w_in[l]
                for c in range(8):
                    sl = slab[c % 2]
                    dma("gpsimd", sl[:, :, 0, :], w2d[:, C_LRUX + c * 128:C_LRUX + (c + 1) * 128].rearrange("(kc p) n -> p kc n", p=128), w=[sl.b])
                    dma("gpsimd", sl[:, :, 1, :], w2d[:, C_LRUG + c * 128:C_LRUG + (c + 1) * 128].rearrange("(kc p) n -> p kc n", p=128), w=[sl.b])
                    slv = sl.t.rearrange("p k a n -> p k (a n)")
                    proj_fm(slv, sl.b, 0, 128, hT.t, hT_b, pbank[0:4])
                    cp("scalar", XA[:, 3:3 + SEQ], PA[:, :], [pbank[i][1] for i in range(4)], [XA.b])
                    cw = lambda k: lpT[:, c, k:k + 1]
                    tsc("vector", XC[:], XA[:, 3:3 + SEQ], cw(3), ALU.mult, [XA.b, lpT.b], [XC.b], cw(4), ALU.add)
                    for k in range(3):
                        stt(XC[:], XA[:, k:k + SEQ], cw(k), XC[:], ALU.mult, ALU.add, [XA.b, lpT.b, XC.b], [XC.b])
                    cp("gpsimd", XCB[:], XC[:], [XC.b], [XCB.b])
                    for gk in range(2):
                        banks = pbank[4:8] if gk == 0 else pbank[0:4]
                        for sc in range(4):
                            mm(banks[sc][0], wg[:, gk, c, :], XCB[:, sc * 512:(sc + 1) * 512], True, True, [wg.b, XCB.b], [banks[sc][1]])
                    act(R[:], PB[:, :], AF.Sigmoid, [pbank[i][1] for i in range(4, 8)], [R.b], bias=lpT[:, c, 5:6])
                    act(I[:], PA[:, :], AF.Sigmoid, [pbank[i][1] for i in range(4)], [I.b], bias=lpT[:, c, 6:7])
                    proj_fm(slv, sl.b, 128, 128, hT.t, hT_b, pbank[4:8])
                    act(A[:], R[:], AF.Exp, [R.b, sp.b], [A.b], scale=sp[:, c, 4:5])
                    act(R[:], R[:], AF.Exp, [R.b, sp.b], [R.b], scale=sp[:, c, 5:6])
                    tsc("vector", R[:], R[:], -1.0, ALU.mult, [R.b], [R.b], 1.0, ALU.add)
                    act(R[:], R[:], AF.Sqrt, [R.b], [R.b])
                    tt("gpsimd", I[:], I[:], XC[:], ALU.mult, [I.b, XC.b], [I.b])
                    tt("gpsimd", I[:], I[:], R[:], ALU.mult, [I.b, R.b], [I.b])
                    S.op("vector", lambda e: e.tensor_tensor_scan(out=H[:], data0=A[:], data1=I[:], initial=0.0, op0=ALU.mult, op1=ALU.add),
                         [A.b, I.b], [H.b])
                    gab = [pbank[i][1] for i in range(4, 8)]
                    act(R[:], PB[:, :], AF.Square, gab, [R.b])
                    tsc("vector", R[:], R[:], 0.044715, ALU.mult, [R.b], [R.b], 1.0, ALU.add)
                    tt("vector", R[:], R[:], PB[:, :], ALU.mult, [R.b] + gab, [R.b])
                    act(R[:], R[:], AF.Sigmoid, [R.b], [R.b], scale=1.5957691216057308)
                    tt("vector", R[:], R[:], PB[:, :], ALU.mult, [R.b] + gab, [R.b])
                    ya = YA[c % 2]
                    tt("vector", ya[:], R[:], H[:], ALU.mult, [R.b, H.b], [ya.b])
                    dma("sync", ysc[0, c * 128:(c + 1) * 128, :], ya[:], r=[ya.b], w=[b_ysc[0]])
                S.barrier()

        def phase_gla(l):
            w2d = w_in[l]
            with ExitStack() as ph:
                qT = T(ph, "gqT", [128, 4, SEQ], F32)
                kT = T(ph, "gkT", [128, 4, SEQ], F32)
                lrT = T(ph, "lrT", [32, SEQ], F32)
                wa2x = T(ph, "wa2x", [32, 512], F32)
                wres = T(ph, "gwres", [128, 8, 2560], BF16)
                gnb = T(ph, "gnb", [128, 4, 256], F32)
                st_f = T(ph, "st_f", [128, 4, 256], F32)
                st_b = T(ph, "st_b", [128, 4, 256], BF16)
                cm4 = T(ph, "cm4", [128, 4, 128], F32)
                triu = T(ph, "triu", [128, 128], F32)
                tril = T(ph, "tril", [128, 128], F32)
                dma("sync", triu[:], cin["triu"], w=[triu.b])
                dma("sync", tril[:], cin["tril"], w=[tril.b])
                for hh in range(4):
                    dma("sync", cm4[:, hh, :], cin["triu"], w=[cm4.b])
                    src = gla_norm[l:l + 1, :]
                    dma("sync", gnb[:, hh, :], bass.AP(src.tensor, src.offset, [[0, 128], [1, 256]]), w=[gnb.b])
                memset("vector", wa2x[:], 0.0, [wa2x.b])
                memset("vector", lrT[:], 1.0, [lrT.b])
                dma("sync", wa2x[0:16, :], gla_wa2[l], w=[wa2x.b])
                dma("sync", wa2x[16:17, :], gla_ba[l:l + 1, :], w=[wa2x.b])
                for i, c0 in enumerate((C_GK, C_GV, C_GV + 512, C_GOG, C_GOG + 512)):
                    load_slab(wres[:, :, i * 512:(i + 1) * 512], w2d, c0, 512, wres.b)
                with ExitStack() as ph2:
                    slab = [T(ph2, f"gslab{i}", [128, 8, 512], BF16) for i in range(2)]
                    lslab = T(ph2, "glslab", [128, 8, 16], BF16)
                    load_slab(slab[0][:], w2d, C_GQ, 512, slab[0].b)
                    load_slab(slab[1][:], w2d, C_GK, 512, slab[1].b)
                    load_slab(lslab[:], w2d, C_GLR, 16, lslab.b)
                    for i in range(8):
                        banks = pbank[0:4] if i % 2 == 0 else pbank[4:8]
                        src = PA if i % 2 == 0 else PB
                        proj_fm(slab[i // 4].t, slab[i // 4].b, (i % 4) * 128, 128, hT.t, hT_b, banks)
                        dst = qT if i < 4 else kT
                        act(dst[:, i % 4, :], src[:, :], AF.Copy, [b for _, b in banks], [dst.b], scale=(128 ** -0.5 if i < 4 else 1.0))
                    proj_fm(lslab.t, lslab.b, 0, 16, hT.t, hT_b, pbank[0:4])
                    cp("vector", lrT[0:16, :], PA[0:16, :], [b for _, b in pbank[0:4]], [lrT.b])
                    S.barrier()
                sp_t = T(ph, "g_sp", [128, 512], F32)
                E1 = T(ph, "g_E1", [128, 512], F32)
                E2 = T(ph, "g_E2", [128, 512], F32)
                Erb = T(ph, "g_Erb", [128, 512], F32)
                qtb = T(ph, "g_qtb", [128, 4, 128], BF16)
                ktb = T(ph, "g_ktb", [128, 4, 128], BF16)
                kend = T(ph, "g_kend", [128, 512], BF16)
                v_bf = T(ph, "g_vbf", [128, 1024], BF16)
                sg = T(ph, "g_sg", [128, 1024], F32)
                attm = T(ph, "g_attm", [128, 4, 128], BF16)
                on = T(ph, "g_on", [128, 1024], F32)
                yc = [T(ph, f"g_yc{i}", [128, 1024], BF16) for i in range(2)]
                ycT = [T(ph, f"g_ycT{i}", [128, 8, 128], BF16) for i in range(2)]
                junk = T(ph, "g_junk", [128, 256], BF16)
                ssq = T(ph, "g_ssq", [128, 4], F32)
                rst = T(ph, "g_rst", [128, 4], F32)
                bk = lambda i: pbank[i][0]
                bb = lambda i: pbank[i][1]
                for t in range(NT):
                    tsl = slice(t * 128, (t + 1) * 128)
                    mm(bk(0), lrT[0:17, tsl], wa2x[0:17, :], True, True, [lrT.b, wa2x.b], [bb(0)])
                    act(sp_t[:], bk(0), AF.Exp, [bb(0)], [sp_t.b], scale=-1.0)
                    act(sp_t[:], sp_t[:], AF.Ln, [sp_t.b], [sp_t.b], bias=1.0)
                    for hh in range(4):
                        mm(bk(1)[:, hh * 128:(hh + 1) * 128], sp_t[:, hh * 128:(hh + 1) * 128], triu[:], True, True, [sp_t.b, triu.b], [bb(1)])
                    mm(bk(2), tril[:], sp_t[:], True, True, [tril.b, sp_t.b], [bb(2)])
                    act(E1[:], bk(1), AF.Exp, [bb(1)], [E1.b], scale=-1.0 / 16.0)
                    act(E2[:], bk(1), AF.Exp, [bb(1)], [E2.b], scale=1.0 / 16.0)
                    act(Erb[:], bk(2), AF.Exp, [bb(2)], [Erb.b], scale=-1.0 / 16.0)
                    tt("vector", qtb[:], qT[:, :, tsl], E1.t.rearrange("p (h s) -> p h s", h=4), ALU.mult, [qT.b, E1.b], [qtb.b])
                    tt("gpsimd", ktb[:], kT[:, :, tsl], E2.t.rearrange("p (h s) -> p h s", h=4), ALU.mult, [kT.b, E2.b], [ktb.b])
                    for kc in range(8):
                        mm(bk(3), hT[:, kc, tsl], wres[:, kc, 0:512], kc == 0, kc == 7, [hT_b[t], wres.b], [bb(3)])
                    tt("vector", kend[:], bk(3), Erb[:], ALU.mult, [bb(3), Erb.b], [kend.b])
                    for half in range(2):
                        for kc in range(8):
                            mm(bk(4 + half), hT[:, kc, tsl], wres[:, kc, 512 + half * 512:1024 + half * 512], kc == 0, kc == 7, [hT_b[t], wres.b], [bb(4 + half)])
                    cp("scalar", v_bf[:], PB[:, 0:1024], [bb(4), bb(5)], [v_bf.b])
                    for half in range(2):
                        for kc in range(8):
                            mm(bk(6 + half), hT[:, kc, tsl], wres[:, kc, 1536 + half * 512:2048 + half * 512], kc == 0, kc == 7, [hT_b[t], wres.b], [bb(6 + half)])
                    act(sg[:], PB[:, 1024:2048], AF.Silu, [bb(6), bb(7)], [sg.b])
                    tt("gpsimd", sg[:], sg[:], gnb.t.rearrange("p h e -> p (h e)"), ALU.mult, [sg.b, gnb.b], [sg.b])
                    for hh in range(4):
                        mm(bk(0)[:, hh * 128:(hh + 1) * 128], ktb[:, hh, :], qtb[:, hh, :], True, True, [ktb.b, qtb.b], [bb(0)])
                    tt("vector", attm[:], bk(0).rearrange("p (h s) -> p h s", h=4), cm4[:], ALU.mult, [bb(0), cm4.b], [attm.b])
                    for hh in range(4):
                        ob = 4 + hh // 2
                        oap = bk(ob)[:, (hh % 2) * 256:(hh % 2 + 1) * 256]
                        mm(oap, attm[:, hh, :], v_bf[:, hh * 256:(hh + 1) * 256], hh % 2 == 0, t == 0 and hh % 2 == 1, [attm.b, v_bf.b], [bb(ob)])
                        if t > 0:
                            mm(oap, qtb[:, hh, :], st_b[:, hh, :], False, hh % 2 == 1, [qtb.b, st_b.b], [bb(ob)])
                    for hh in range(4):
                        kb_ = 6 + hh // 2
                        mm(bk(kb_)[:, (hh % 2) * 256:(hh % 2 + 1) * 256], kend[:, hh * 128:(hh + 1) * 128], v_bf[:, hh * 256:(hh + 1) * 256],
                           hh % 2 == 0, hh % 2 == 1, [kend.b, v_bf.b], [bb(kb_)])
                    for hh in range(4):
                        kvp = bk(6 + hh // 2)[:, (hh % 2) * 256:(hh % 2 + 1) * 256]
                        if t == 0:
                            cp("vector", st_f[:, hh, :], kvp, [bb(6 + hh // 2)], [st_f.b])
                        else:
                            dec = E1[:, hh * 128 + 127:hh * 128 + 128]
                            stt(st_f[:, hh, :], st_f[:, hh, :], dec, kvp, ALU.mult, ALU.add, [st_f.b, E1.b, bb(6 + hh // 2)], [st_f.b])
                    cp("gpsimd", st_b[:], st_f[:], [st_f.b], [st_b.b])
                    for hh in range(4):
                        oap = bk(4 + hh // 2)[:, (hh % 2) * 256:(hh % 2 + 1) * 256]
                        act(junk[:], oap, AF.Square, [bb(4 + hh // 2)], [junk.b, ssq.b], accum_out=ssq[:, hh:hh + 1])
                    act(rst[:], ssq[:], AF.Sqrt, [ssq.b], [rst.b], scale=1.0 / 256.0, bias=EPS)
                    recip(rst[:], rst[:], [rst.b], [rst.b])
                    tt("vector", on.t.rearrange("p (h e) -> p h e", h=4), PB[:, 0:1024].rearrange("p (h e) -> p h e", h=4),
                       fap(rst[:], [[1, 4], [0, 256]]), ALU.mult, [bb(4), bb(5), rst.b], [on.b])
                    y = yc[t % 2]
                    tt("gpsimd", y[:], on[:], sg[:], ALU.mult, [on.b, sg.b], [y.b])
                    pv = bank_bf(1)
                    for c in range(8):
                        trp(pv[:, c * 128:(c + 1) * 128], y[:, c * 128:(c + 1) * 128], ident_b[:], [y.b, ident_b.b], [bb(1)])
                    yT = ycT[t % 2]
                    cp("scalar", yT[:], pv.rearrange("p (k s) -> p k s", k=8), [bb(1)], [yT.b])
                    dma("sync", ysc[2, :, tsl].rearrange("(c p) s -> p c s", p=128), yT[:], r=[yT.b], w=[b_ysc[2]])
                S.barrier()

        def setup_tables():
            with ExitStack() as ph:
                tblx = T(ph, "tblx", [33, 16], F32)
                memset("vector", tblx[:], NEG, [tblx.b])
                dma("sync", tblx[0:32, :], rel_table, w=[tblx.b])
                for name, dst, n in (("oh_c", tc_d, 4096), ("oh_s", ts_d, 1024), ("oh_w", tw_d, 1024)):
                    oh = T(ph, "t_" + name, [33, n], F32)
                    thi = T(ph, "thi_" + name, [16, n], BF16)
                    tlo = T(ph, "tlo_" + name, [16, n], BF16)
                    dma("sync", oh[:], cin[name], w=[oh.b])
                    for ch in range(n // 512):
                        pa, pbuf = pbank[ch % 8]
                        mm(pa[0:16, :], tblx[0:33, 0:16], oh[0:33, ch * 512:(ch + 1) * 512], True, True, [tblx.b, oh.b], [pbuf])
                        cp("vector", thi[:, ch * 512:(ch + 1) * 512], pa[0:16, :], [pbuf], [thi.b])
                        tt("vector", tlo[:, ch * 512:(ch + 1) * 512], pa[0:16, :], thi[:, ch * 512:(ch + 1) * 512], ALU.subtract, [pbuf, thi.b], [tlo.b])
                    dma("sync", dst[0], thi[:], r=[thi.b], w=[b_tabs])
                    dma("sync", dst[1], tlo[:], r=[tlo.b], w=[b_tabs])
                S.barrier()

        def phase_nsa(l):
            w2d = w_in[l]
            bk = lambda i: pbank[i][0]
            bb = lambda i: pbank[i][1]
            with ExitStack() as ph:
                qT = T(ph, "nqT", [128, 8, SEQ], BF16)
                kS = T(ph, "nkS", [128, 4, SEQ], BF16)
                kW = T(ph, "nkW", [128, 4, SEQ], BF16)
                vS = T(ph, "nvS", [128, NT, 4, 66], BF16)
                vW = T(ph, "nvW", [128, NT, 4, 66], BF16)
                sgate = T(ph, "nsg", [128, NT, 48], F32)
                kcP = T(ph, "nkcP", [128, 2, 4, 128], BF16)
                vcx = T(ph, "nvcx", [128, 4, 98], BF16)
                hbt = T(ph, "nhbt", [128, 3, 4, 2, 512], BF16)
                NM = T(ph, "nNM", [128, 4, 2, 512], BF16)
                Jb = T(ph, "nJb", [128, 2, 128], BF16)
                expd = T(ph, "nexpd", [128, 2, NT, 128], BF16)
                cand = T(ph, "ncand", [128, NT, 32], F32)
                negc = T(ph, "nnegc", [128, NT, 32], F32)
                forced = T(ph, "nforced", [128, NT, 32], F32)
                dma("gpsimd", Jb[:, 0, :], cin["antiid"], w=[Jb.b])
                dma("gpsimd", Jb[:, 1, :], cin["antiid127"], w=[Jb.b])
                dma("gpsimd", expd[:, 0, :, :], cin["expand"], w=[expd.b])
                dma("gpsimd", expd[:, 1, :, :], cin["expand_near"], w=[expd.b])
                memset("vector", NM[:], 0.0, [NM.b])
                dma("sync", cand[:], cin["cand"], w=[cand.b])
                dma("sync", negc[:], cin["negc"], w=[negc.b])
                dma("sync", forced[:], cin["forced"], w=[forced.b])
                memset("vector", vcx[:], 0.0, [vcx.b])
                memset("vector", kcP[:], 0.0, [kcP.b])
                for g in range(4):
                    dma("gpsimd", vcx[:, g, 64:97], cin["ovx"], w=[vcx.b])
                for dl in range(3):
                    tsrc = tw_d if dl == 2 else ts_d
                    for g in range(4):
                        for hl in range(2):
                            for par in range(2):
                                for rp in range(2):
                                    h = 4 * g + 2 * rp + par
                                    a0 = tsrc[hl, h, 512 + dl * 128 - 127:512 + dl * 128 - 127 + 1]
                                    src = bass.AP(a0.tensor, a0.offset, [[1, 128], [1, 128]])
                                    dma("sync", hbt[:, dl, g, hl, par * 256 + rp * 128:par * 256 + rp * 128 + 128], src, r=[b_tabs], w=[hbt.b])
                for g in range(4):
                    for hl in range(2):
                        for par in range(2):
                            for rp in range(2):
                                h = 4 * g + 2 * rp + par
                                a0 = ts_d[hl, h, 640:641]
                                src = bass.AP(a0.tensor, a0.offset, [[0, 1], [0, 2], [1, 128]])
                                c0 = par * 256 + rp * 128
                                dma("sync", NM[32 + hl:33 + hl, g, :, c0:c0 + 128], src, r=[b_tabs], w=[NM.b])
                memset("vector", vS[:, :, :, 64:66], 1.0, [vS.b])
                memset("vector", vW[:, :, :, 64:66], 1.0, [vW.b])
                if stop == "nsa0":
                    S.barrier()
                    return
                with ExitStack() as ph2:
                    slab = [T(ph2, f"nslab{i}", [128, 8, 512], BF16) for i in range(2)]
                    wv = T(ph2, "nwv", [128, 8, 560], BF16)
                    for half in range(2):
                        sl = slab[half]
                        load_slab(sl[:], w2d, C_Q + half * 512, 512, sl.b)
                        for i in range(4):
                            c = half * 4 + i
                            banks = pbank[0:4] if c % 2 == 0 else pbank[4:8]
                            src = PA if c % 2 == 0 else PB
                            proj_fm(sl.t, sl.b, i * 128, 128, hT.t, hT_b, banks)
                            act(qT[:, c, :], src[:, :], AF.Copy, [b for _, b in banks], [qT.b], scale=0.125)
                    if stop == "nsa1a":
                        S.barrier()
                        return
                    n = 0
                    for idx, dst in ((2, kS), (4, kW)):
                        sl = slab[n % 2]
                        n += 1
                        for g in range(4):
                            c0 = C_KV + idx * 256 + g * 64
                            for dup in range(2):
                                dma("gpsimd", sl[:, :, g * 128 + dup * 64:g * 128 + dup * 64 + 64],
                                    w2d[:, c0:c0 + 64].rearrange("(kc p) n -> p kc n", p=128), w=[sl.b])
                        for g in range(4):
                            banks = pbank[0:4] if g % 2 == 0 else pbank[4:8]
                            src = PA if g % 2 == 0 else PB
                            proj_fm(sl.t, sl.b, g * 128, 128, hT.t, hT_b, banks)
                            cp("scalar" if g % 2 == 0 else "vector", dst[:, g, :], src[:, :], [b for _, b in banks], [dst.b])
                    if stop == "nsa1b":
                        S.barrier()
                        return
                    load_slab(wv[:, :, 0:256], w2d, C_KV + 3 * 256, 256, wv.b)
                    load_slab(wv[:, :, 256:512], w2d, C_KV + 5 * 256, 256, wv.b)
                    load_slab(wv[:, :, 512:560], w2d, C_GATE, 48, wv.b)
                    for t in range(NT):
                        tsl = slice(t * 128, (t + 1) * 128)
                        b0, b1 = (0, 1) if t % 2 == 0 else (2, 3)
                        for kc in range(8):
                            mm(bk(b0), hT[:, kc, tsl], wv[:, kc, 0:512], kc == 0, kc == 7, [hT_b[t], wv.b], [bb(b0)])
                        import os
                        SK = os.environ.get("NSA_SKIP", "")
                        if "g" not in SK:
                            for kc in range(8):
                                mm(bk(b1)[:, 0:48], hT[:, kc, tsl], wv[:, kc, 512:560], kc == 0, kc == 7, [hT_b[t], wv.b], [bb(b1)])
                        if "v" not in SK:
                            cp("vector", vS[:, t, :, 0:64], bk(b0)[:, 0:256].rearrange("p (g d) -> p g d", g=4), [bb(b0)], [vS.b])
                        if "w" not in SK:
                            cp("scalar", vW[:, t, :, 0:64], bk(b0)[:, 256:512].rearrange("p (g d) -> p g d", g=4), [bb(b0)], [vW.b])
                        if "g" not in SK:
                            act(sgate[:, t, :], bk(b1)[:, 0:48], AF.Sigmoid, [bb(b1)], [sgate.b])
                    S.barrier()
                if stop == "nsa1":
                    return
                with ExitStack() as ph2:
                    slab = [T(ph2, f"ncslab{i}", [128, 8, 256], BF16) for i in range(2)]
                    w1sb = T(ph2, "nw1", [128, 32, 256], BF16)
                    w2sb = T(ph2, "nw2", [128, 2, 128], BF16)
                    prow2 = T(ph2, "nprow2", [32, 128], F32)
                    posT = T(ph2, "nposT", [128, 32], F32)
                    XAB = [T(ph2, f"nXAB{i}", [128, SEQ], BF16) for i in range(2)]
                    gtmp = T(ph2, "ngtmp", [128, 2, 128], F32)
                    geluT = T(ph2, "ngeluT", [128, 2, 128], BF16)
                    for kv in range(2):
                        for dup in range(2):
                            dma("gpsimd", w1sb[dup * 64:(dup + 1) * 64, :, :], cmp_w1[l, kv].rearrange("(p d) j -> d p j", d=64), w=[w1sb.b])
                            dma("gpsimd", w2sb[:, :, dup * 64:(dup + 1) * 64], cmp_w2[l, kv].rearrange("(jc p) d -> p jc d", p=128), w=[w2sb.b])
                            dma("sync", prow2[:, dup * 64:(dup + 1) * 64], cmp_pos[l, kv], w=[prow2.b])
                        trp(bk(6)[:, 0:32], prow2[:, :], ident_f[0:32, 0:32], [prow2.b, ident_f.b], [bb(6)])
                        cp("vector", posT[:], bk(6)[:, 0:32], [bb(6)], [posT.b])
                        sl = slab[kv]
                        load_slab(sl[:, :, 0:256], w2d, C_KV + kv * 256, 256, sl.b)
                        for cc in range(2):
                            banks = pbank[0:4]
                            proj_fm(sl.t, sl.b, cc * 128, 128, hT.t, hT_b, banks)
                            for ab in range(2):
                                tt("vector" if ab == 0 else "gpsimd" if False else "vector", XAB[ab].t.rearrange("p (i q) -> p i q", q=16), PA.t.rearrange("p (i q) -> p i q", q=16),
                                   fap(posT[:, ab * 16:ab * 16 + 1], [[0, 128], [1, 16]]), ALU.add, [b for _, b in banks] + [posT.b], [XAB[ab].b])
                            for gg in range(2):
                                g = cc * 2 + gg
                                rows = slice(gg * 64, gg * 64 + 64)
                                hb_, hbb = bk(4 + 2 * gg), bb(4 + 2 * gg)
                                for jc in range(2):
                                    for p in range(32):
                                        srcT = XAB[0] if p < 16 else XAB[1]
                                        rhs = fap(srcT[rows, p:p + 1], [[16, 127]])
                                        mm(hb_[:, jc * 128:jc * 128 + 127], w1sb[rows, p, jc * 128:(jc + 1) * 128], rhs, p == 0, p == 31,
                                           [w1sb.b, srcT.b], [hbb])
                                hv = hb_[:, 0:256].rearrange("p (j i) -> p j i", j=2)[:, :, 0:127]
                                gv = gtmp[:, :, 0:127]
                                act(gv, hv, AF.Square, [hbb], [gtmp.b])
                                tsc("vector", gv, gv, 0.044715, ALU.mult, [gtmp.b], [gtmp.b], 1.0, ALU.add)
                                tt("vector", gv, gv, hv, ALU.mult, [gtmp.b, hbb], [gtmp.b])
                                act(gv, gv, AF.Sigmoid, [gtmp.b], [gtmp.b], scale=1.5957691216057308)
                                tt("vector", geluT[:, :, 0:127], gv, hv, ALU.mult, [gtmp.b, hbb], [geluT.b])
                                if kv == 0:
                                    for jc in range(2):
                                        mm(bk(5)[:, 0:127], w2sb[:, jc, :], geluT[:, jc, 0:127], jc == 0, jc == 1, [w2sb.b, geluT.b], [bb(5)])
                                    cp("scalar", kcP[0:64, 0, g, 0:127], bk(5)[0:64, 0:127], [bb(5)], [kcP.b])
                                    cp("scalar", kcP[64:128, 1, g, 0:127], bk(5)[64:128, 0:127], [bb(5)], [kcP.b])
                                else:
                                    for jc in range(2):
                                        mm(bk(5)[0:127, 0:64], geluT[:, jc, 0:127], w2sb[:, jc, 0:64], jc == 0, jc == 1, [w2sb.b, geluT.b], [bb(5)])
                                    cp("scalar", vcx[0:127, g, 0:64], bk(5)[0:127, 0:64], [bb(5)], [vcx.b])
                    S.barrier()
                if stop == "nsa2":
                    return
                E = [T(ph, f"nE{i}", [128, 512], BF16) for i in range(3)]
                cb = [T(ph, f"ncb{i}", [128, 2, 512], BF16) for i in range(2)]
                for cbx in cb:
                    memset("vector", cbx[:], NEG, [cbx.b])
                ybacc = T(ph, "nybacc", [128, 4, 64], F32)
                tmp1 = T(ph, "ntmp1", [128, 4, 64], F32)
                tmp2 = T(ph, "ntmp2", [128, 4, 64], F32)
                impr = T(ph, "nimpr", [128, 4, 32], F32)
                imp = T(ph, "nimp", [128, 32], F32)
                m8 = T(ph, "nm8", [128, 8], F32)
                sm = T(ph, "nsm", [128, 3, 4], F32)
                ybt = [T(ph, f"nybt{i}", [128, 1024], BF16) for i in range(2)]
                ybT = [T(ph, f"nybT{i}", [128, 8, 128], BF16) for i in range(2)]
                KP = [T(ph, f"nKP{i}", [128, 2, 128], BF16) for i in range(3)]
                for kpx in KP:
                    memset("vector", kpx[:], 0.0, [kpx.b])
                kpn = 0
                lrot = [0, 1, 7]
                ln = 0
                en = 0
                cbn = 0
                colb = lambda r: (r % 2) * 256 + (r // 2) * 128
                for qt in range(1 if stop == 'nsa3' else NT):
                    qsl = slice(qt * 128, (qt + 1) * 128)
                    ybq = ybt[qt % 2]
                    for g in range(4):
                        buf = (qt * 4 + g) % 2
                        cbt = cb[cbn % 2]
                        cbn += 1
                        for hl in range(2):
                            for par in range(2):
                                for rp in range(2):
                                    h = 4 * g + 2 * rp + par
                                    a0 = tc_d[hl, h, qt * 128 + 1:qt * 128 + 2]
                                    src = bass.AP(a0.tensor, a0.offset, [[16, 127], [1, 128]])
                                    c0 = par * 256 + rp * 128
                                    dma("sync", cbt[0:127, hl, c0:c0 + 128], src, r=[b_tabs], w=[cbt.b])
                        import os
                        CUT = int(os.environ.get("NSA_CUT", "99"))
                        if CUT <= 1:
                            continue
                        L, Lb = pbank[lrot[ln % 3]]
                        ln += 1
                        for par in range(2):
                            mm(L[:, par * 256:(par + 1) * 256], kcP[:, par, g, :], qT[:, 2 * g:2 * g + 2, qsl], par == 0, CUT == 2 and par == 1, [kcP.b, qT.b], [Lb])
                        if CUT <= 2:
                            continue
                        for hl in range(2):
                            mm(L, Jb[:, 1, :], cbt[:, hl, :], False, hl == 1, [Jb.b, cbt.b], [Lb])
                        if CUT <= 3:
                            continue
                        Ec = E[en % 3]
                        en += 1
                        act(Ec[:], L, AF.Exp, [Lb], [Ec.b])
                        if CUT == -5:
                            for r in range(4):
                                mm(L[0:127, :], Jb[0:127, 1, 0:127], cbt[0:127, 0, :], True, True, [Jb.b, cbt.b], [Lb])
                        if CUT == -6:
                            for r in range(4):
                                mm(bk(2)[0:127, :], Jb[0:127, 1, 0:127], cbt[0:127, 0, :], True, True, [Jb.b, cbt.b], [bb(2)])
                        if CUT == -7:
                            for r in range(4):
                                mm(L[:, r * 128:(r + 1) * 128], hbt[0:127, 0, g, 0, colb(r):colb(r) + 128], hbt[0:127, 1, g, 0, 0:128], True, True, [hbt.b], [Lb])
                        if CUT == -8:
                            for r in range(4):
                                mm(bk(2)[:, r * 128:(r + 1) * 128], hbt[0:127, 0, g, 0, colb(r):colb(r) + 128], hbt[0:127, 1, g, 0, 0:128], True, True, [hbt.b], [bb(2)])
                        if CUT <= 4:
                            continue
                        UCB = int(os.environ.get('UCB', '2'))
                        Uc = bk(UCB)
                        for r in range(4):
                            PVK = int(os.environ.get("PVK", "128")); PVN = int(os.environ.get("PVN", "98"))
                            PVS = os.environ.get("PVS", "E")
                            lh = Ec[0:PVK, colb(r):colb(r) + 128] if PVS in ("E", "EV") else hbt[0:PVK, 0, g, 0, colb(r):colb(r) + 128]
                            rh = vcx[0:PVK, g, 0:PVN] if PVS in ("E", "HV") else hbt[0:PVK, 1, g, 0, 0:PVN]
                            PVO = int(os.environ.get('PVO', '98'))
                            mm(Uc[:, r * PVO:r * PVO + PVN], lh, rh, (r == 0) or os.environ.get('PVF') == '1', (r == 3) or os.environ.get('PVF') == '1', [] if os.environ.get('PVD') == '0' else [Ec.b, vcx.b], [bb(UCB)])
                        import os
                        CUT = int(os.environ.get("NSA_CUT", "99"))
                        if CUT <= 5:
                            continue
                        ucv = lambda a, b_: fap(Uc[:, a:a + 1], [[98, 4], [1, b_]])
                        rs4, wc = sm[:, 0, :], sm[:, 1, :]
                        tsc("vector", rs4, fap(Uc[:, 96:97], [[98, 4]]), 1e-30, ALU.max, [bb(2)], [sm.b])
                        recip(rs4, rs4, [sm.b], [sm.b])
                        tt("vector", wc, rs4, sgate[:, qt, 4 * g:4 * g + 4], ALU.mult, [sm.b, sgate.b], [sm.b])
                        tt("vector", ybacc[:], ucv(0, 64), fap(wc, [[1, 4], [0, 64]]), ALU.mult, [bb(2), sm.b], [ybacc.b])
                        tt("vector", impr[:], ucv(64, 32), fap(rs4, [[1, 4], [0, 32]]), ALU.mult, [bb(2), sm.b], [impr.b])
                        S.op("vector", lambda e: e.tensor_reduce(out=imp[:], in_=fap(impr[:, 0, 0:1], [[1, 32], [32, 4]]), axis=AX.X, op=ALU.add),
                             [impr.b], [imp.b])
                        tt("vector", imp[:], imp[:], cand[:, qt, :], ALU.mult, [imp.b, cand.b], [imp.b])
                        tt("vector", imp[:], imp[:], negc[:, qt, :], ALU.add, [imp.b, negc.b], [imp.b])
                        S.op("vector", lambda e: e.max(out=m8[:], in_=imp[:]), [imp.b], [m8.b])
                        tsc("vector", imp[:], imp[:], m8[:, 4:5], ALU.is_ge, [imp.b, m8.b], [imp.b])
                        tt("vector", imp[:], imp[:], forced[:, qt, :], ALU.max, [imp.b, forced.b], [imp.b])
                        tsc("vector", imp[:], imp[:], -1.0, ALU.add, [imp.b], [imp.b], -NEG, ALU.mult)
                        if CUT <= 6:
                            continue
                        trp(bk(5)[0:32, 0:128], imp[:, :], ident_f[:, :], [imp.b, ident_f.b], [bb(5)])
                        cp("vector", NM[0:32, g, buf, :].rearrange("p (a s) -> p a s", a=4), fap(bk(5)[0:32, 0:1], [[0, 4], [1, 128]]), [bb(5)], [NM.b])
                        if CUT <= 7:
                            continue
                        for br in range(2):
                            kts = range(0, qt + 1) if br == 0 else range(max(0, qt - 2), qt + 1)
                            KT = kS if br == 0 else kW
                            VT = vS if br == 0 else vW
                            Ub = 3 + br
                            for kt in kts:
                                ksl = slice(kt * 128, (kt + 1) * 128)
                                dl = qt - kt
                                L, Lb = pbank[lrot[ln % 3]]
                                ln += 1
                                kp = KP[kpn % 3]
                                kpn += 1
                                cp("gpsimd", kp[0:64, 0, :], KT[0:64, g, ksl], [KT.b], [kp.b])
                                cp("gpsimd", kp[64:128, 1, :], KT[64:128, g, ksl], [KT.b], [kp.b])
                                for par in range(2):
                                    mm(L[:, par * 256:(par + 1) * 256], kp[:, par, :], qT[:, 2 * g:2 * g + 2, qsl], par == 0, False, [kp.b, qT.b], [Lb])
                                near = dl < (2 if br == 0 else 3)
                                if br == 0:
                                    mm(L, expd[:, 1 if near else 0, kt, :], NM[:, g, buf, :], False, not near, [expd.b, NM.b], [Lb])
                                if near:
                                    for hl in range(2):
                                        mm(L, Jb[:, 0, :], hbt[:, dl, g, hl, :], False, hl == 1, [Jb.b, hbt.b], [Lb])
                                Et = E[en % 3]
                                en += 1
                                act(Et[:], L, AF.Exp, [Lb], [Et.b])
                                for r in range(4):
                                    mm(bk(Ub)[:, r * 66:(r + 1) * 66], Et[:, colb(r):colb(r) + 128], VT[:, kt, g, 0:66],
                                       kt == kts[0] and r == 0, kt == qt and r == 3, [Et.b, VT.b], [bb(Ub)])
                        if CUT <= 8:
                            continue
                        for br in range(2):
                            U = bk(3 + br)
                            rsb, wb_ = sm[:, 0, :], sm[:, 1 + br, :]
                            S.op("vector", lambda e, U=U, rsb=rsb: e.reciprocal(out=rsb, in_=fap(U[:, 64:65], [[66, 4]])), [bb(3 + br)], [sm.b])
                            tt("vector", wb_, rsb, sgate[:, qt, 16 * (br + 1) + 4 * g:16 * (br + 1) + 4 * g + 4], ALU.mult, [sm.b, sgate.b], [sm.b])
                            tgt = tmp1 if br == 0 else tmp2
                            tt("vector", tgt[:], fap(U[:, 0:1], [[66, 4], [1, 64]]), fap(wb_, [[1, 4], [0, 64]]), ALU.mult, [bb(3 + br), sm.b], [tgt.b])
                        tt("gpsimd", tmp1[:], tmp1[:], ybacc[:], ALU.add, [tmp1.b, ybacc.b], [tmp1.b])
                        tt("gpsimd", ybq[:, g * 256:(g + 1) * 256].rearrange("p (r d) -> p r d", r=4), tmp1[:], tmp2[:], ALU.add, [tmp1.b, tmp2.b], [ybq.b])
                    pv = bank_bf(6)
                    for c in range(8):
                        trp(pv[:, c * 128:(c + 1) * 128], ybq[:, c * 128:(c + 1) * 128], ident_b[:], [ybq.b, ident_b.b], [bb(6)])
                    yT = ybT[qt % 2]
                    cp("scalar", yT[:], pv.rearrange("p (k s) -> p k s", k=8), [bb(6)], [yT.b])
                    dma("sync", ysc[1, :, qsl].rearrange("(c p) s -> p c s", p=128), yT[:], r=[yT.b], w=[b_ysc[1]])
                S.barrier()

        def phase_tail(l, x_src, x_dst, b_xsrc, b_xdst):
            bk = lambda i: pbank[i][0]
            bb = lambda i: pbank[i][1]
            w2d = w_in[l]
            with ExitStack() as ph:
                mrgb = T(ph, "mrgb", [128, 8, SEQ], BF16)
                with ExitStack() as ph2:
                    mrg = T(ph2, "mrg", [128, 8, SEQ], F32)
                    yT = T(ph2, "m_yT", [128, 8, SEQ], BF16)
                    yb_ = [Buf(f"m_yT{c}") for c in range(8)]
                    wbr = [T(ph2, f"m_wbr{i}", [128, 8, 256], BF16) for i in range(2)]
                    wmg = [T(ph2, f"m_wmg{i}", [128, 8, 256], BF16) for i in range(2)]
                    sig = [T(ph2, f"m_sig{i}", [128, 512], F32) for i in range(2)]
                    prod = [T(ph2, f"m_prod{i}", [128, 512], F32) for i in range(2)]
                    n = 0
                    bn = 0
                    for br in range(3):
                        for c in range(8):
                            dma("sync", yT[:, c, :], ysc[br, c * 128:(c + 1) * 128, :], r=[b_ysc[br]], w=[yb_[c]])
                        for oc2 in range(4):
                            wb, wm = wbr[oc2 % 2], wmg[oc2 % 2]
                            load_slab(wb[:], w_branch[l, br], oc2 * 256, 256, wb.b)
                            load_slab(wm[:], w2d, C_MG + br * 1024 + oc2 * 256, 256, wm.b)
                            for o in range(2):
                                oc = oc2 * 2 + o
                                for sc in range(4):
                                    ssl = slice(sc * 512, (sc + 1) * 512)
                                    bB, bG = (bn % 4) * 2, (bn % 4) * 2 + 1
                                    bn += 1
                                    for c in range(8):
                                        mm(bk(bB), wb[:, c, o * 128:(o + 1) * 128], yT[:, c, ssl], c == 0, c == 7, [wb.b, yb_[c]], [bb(bB)])
                                    for kc in range(8):
                                        mm(bk(bG), wm[:, kc, o * 128:(o + 1) * 128], hT[:, kc, ssl], kc == 0, kc == 7, [wm.b] + hT_b[sc * 4:(sc + 1) * 4], [bb(bG)])
                                    sg_, pr_ = sig[n % 2], prod[n % 2]
                                    n += 1
                                    act(sg_[:], bk(bG), AF.Sigmoid, [bb(bG)], [sg_.b])
                                    if br == 0:
                                        tt("vector", mrg[:, oc, ssl], sg_[:], bk(bB), ALU.mult, [sg_.b, bb(bB)], [mrg.b])
                                    elif br == 1:
                                        tt("vector", pr_[:], sg_[:], bk(bB), ALU.mult, [sg_.b, bb(bB)], [pr_.b])
                                        tt("gpsimd", mrg[:, oc, ssl], mrg[:, oc, ssl], pr_[:], ALU.add, [mrg.b, pr_.b], [mrg.b])
                                    else:
                                        tt("vector", pr_[:], sg_[:], bk(bB), ALU.mult, [sg_.b, bb(bB)], [pr_.b])
                                        tt("gpsimd", mrgb[:, oc, ssl], mrg[:, oc, ssl], pr_[:], ALU.add, [mrg.b, pr_.b], [mrgb.b])
                    S.barrier()
                with ExitStack() as ph2:
                    wout = T(ph2, "p_wout", [128, 8, D], BF16)
                    g1 = load_gain(ph2, l, 1)
                    g2 = load_gain(ph2, l, 2)
                    xt = [T(ph2, f"p_xt{i}", [128, D], F32) for i in range(2)]
                    on = [T(ph2, f"p_on{i}", [128, D], F32) for i in range(2)]
                    hb = [T(ph2, f"p_hb{i}", [128, D], BF16) for i in range(2)]
                    junk = T(ph2, "p_junk", [128, D], BF16)
                    ss = T(ph2, "p_ss", [128, NT], F32)
                    rs = T(ph2, "p_rs", [128, NT], F32)
                    ss2 = T(ph2, "p_ss2", [128, NT], F32)
                    rs2 = T(ph2, "p_rs2", [128, NT], F32)
                    for half in range(2):
                        load_slab(wout[:, :, half * 512:(half + 1) * 512], w_out[l], half * 512, 512, wout.b)
                    for t in range(NT):
                        tsl = slice(t * 128, (t + 1) * 128)
                        b0 = (t % 2) * 2
                        ov = PA[:, b0 * 512:(b0 + 2) * 512]
                        obufs = [bb(b0), bb(b0 + 1)]
                        for half in range(2):
                            for c in range(8):
                                mm(bk(b0 + half), mrgb[:, c, tsl], wout[:, c, half * 512:(half + 1) * 512], c == 0, c == 7, [mrgb.b, wout.b], [bb(b0 + half)])
                        x_, o_ = xt[t % 2], on[t % 2]
                        dma("sync", x_[:], x_src[tsl, :], r=[b_xsrc], w=[x_.b])
                        act(junk[:], ov, AF.Square, obufs, [junk.b, ss.b], accum_out=ss[:, t:t + 1])
                        act(rs[:, t:t + 1], ss[:, t:t + 1], AF.Sqrt, [ss.b], [rs.b], scale=1.0 / D, bias=EPS)
                        recip(rs[:, t:t + 1], rs[:, t:t + 1], [rs.b], [rs.b])
                        stt(o_[:], ov, rs[:, t:t + 1], g1[:], ALU.mult, ALU.mult, obufs + [rs.b, g1.b], [o_.b])
                        tt("gpsimd", o_[:], o_[:], x_[:], ALU.add, [o_.b, x_.b], [o_.b])
                        dma("sync", xmid[tsl, :], o_[:], r=[o_.b], w=[b_xmid])
                        norm_transpose_tile(ph2, t, o_[:], o_.b, g2, ss2, rs2, hb[t % 2], junk, 4 + t % 2)
                    S.barrier()
            with ExitStack() as ph:
                hid = T(ph, "f_hid", [128, 22, SEQ], BF16)
                hid_b = [Buf(f"hid{j}") for j in range(22)]
                wfo = T(ph, "f_wfo", [128, 22, D], BF16)
                wfo_b = [Buf(f"wfo{j}") for j in range(11)]
                wfi = [T(ph, f"f_wfi{i}", [128, 8, 2, 128], BF16) for i in range(2)]
                sgt = [T(ph, f"f_sg{i}", [128, 512], F32) for i in range(2)]
                g3 = load_gain(ph, l, 3)
                xt = [T(ph, f"f_xt{i}", [128, D], F32) for i in range(2)]
                on = [T(ph, f"f_on{i}", [128, D], F32) for i in range(2)]
                junk = T(ph, "f_junk", [128, D], BF16)
                ss = T(ph, "f_ss", [128, NT], F32)
                rs = T(ph, "f_rs", [128, NT], F32)
                n = 0
                bn = 0
                for j in range(22):
                    wf = wfi[j % 2]
                    dma("gpsimd", wf[:, :, 0, :], w_ffn_in[l][:, j * 128:(j + 1) * 128].rearrange("(kc p) n -> p kc n", p=128), w=[wf.b])
                    dma("gpsimd", wf[:, :, 1, :], w_ffn_in[l][:, DFF + j * 128:DFF + (j + 1) * 128].rearrange("(kc p) n -> p kc n", p=128), w=[wf.b])
                    if j % 2 == 0:
                        jj = j // 2
                        dma("gpsimd", wfo[:, 2 * jj:2 * jj + 2, :], w_ffn_out[l][jj * 256:(jj + 1) * 256, :].rearrange("(j p) n -> p j n", p=128), w=[wfo_b[jj]])
                    for sc in range(4):
                        ssl = slice(sc * 512, (sc + 1) * 512)
                        bG, bU = (bn % 4) * 2, (bn % 4) * 2 + 1
                        bn += 1
                        for kc in range(8):
                            mm(bk(bG), wf[:, kc, 0, :], hT[:, kc, ssl], kc == 0, kc == 7, [wf.b] + hT_b[sc * 4:(sc + 1) * 4], [bb(bG)])
                        for kc in range(8):
                            mm(bk(bU), wf[:, kc, 1, :], hT[:, kc, ssl], kc == 0, kc == 7, [wf.b] + hT_b[sc * 4:(sc + 1) * 4], [bb(bU)])
                        sg_ = sgt[n % 2]
                        n += 1
                        act(sg_[:], bk(bG), AF.Silu, [bb(bG)], [sg_.b])
                        tt("vector", hid[:, j, ssl], sg_[:], bk(bU), ALU.mult, [sg_.b, bb(bU)], [hid_b[j]])
                for t in range(NT):
                    tsl = slice(t * 128, (t + 1) * 128)
                    b0 = (t % 2) * 2
                    ov = PA[:, b0 * 512:(b0 + 2) * 512]
                    obufs = [bb(b0), bb(b0 + 1)]
                    for half in range(2):
                        for j in range(22):
                            mm(bk(b0 + half), hid[:, j, tsl], wfo[:, j, half * 512:(half + 1) * 512], j == 0, j == 21, [hid_b[j], wfo_b[j // 2]], [bb(b0 + half)])
                    x_, o_ = xt[t % 2], on[t % 2]
                    dma("sync", x_[:], xmid[tsl, :], r=[b_xmid], w=[x_.b])
                    act(junk[:], ov, AF.Square, obufs, [junk.b, ss.b], accum_out=ss[:, t:t + 1])
                    act(rs[:, t:t + 1], ss[:, t:t + 1], AF.Sqrt, [ss.b], [rs.b], scale=1.0 / D, bias=EPS)
                    recip(rs[:, t:t + 1], rs[:, t:t + 1], [rs.b], [rs.b])
                    stt(o_[:], ov, rs[:, t:t + 1], g3[:], ALU.mult, ALU.mult, obufs + [rs.b, g3.b], [o_.b])
                    tt("gpsimd", o_[:], o_[:], x_[:], ALU.add, [o_.b, x_.b], [o_.b])
                    dma("sync", x_dst[tsl, :], o_[:], r=[o_.b], w=[b_xdst])
                S.barrier()

        setup_tables()
        for l in range(n_layers):
            phase_A(l, x_in if l == 0 else xres)
            import os
            if not os.environ.get("SKIP_LG"):
                phase_lru(l)
                if stop == "lru":
                    break
                phase_gla(l)
                if stop == "gla":
                    break
            phase_nsa(l)
            if stop is not None and stop.startswith("nsa"):
                break
            last = (l == n_layers - 1)
            phase_tail(l, x_in if l == 0 else xres, y_out if last else xres, Buf() if l == 0 else b_xres, Buf() if last else b_xres)
        S.barrier()
        S.emit()
    return nc, consts


_CACHE = {}


def kernel(**inputs):
    if "nc" not in _CACHE:
        _CACHE["nc"] = build()
    nc, consts = _CACHE["nc"]
    x = np.ascontiguousarray(np.asarray(inputs["x"], dtype=np.float32))
    shared = {k: np.ascontiguousarray(np.asarray(v, dtype=np.float32)) for k, v in inputs.items() if k != "x"}
    for k, v in consts.items():
        shared["c_" + k] = v
    in_maps = [dict(shared, x=x[i]) for i in range(8)]
    res = run_bass_kernel_spmd(nc, in_maps, core_ids=list(range(8)))
    return np.stack([np.asarray(r["out"], dtype=np.float32) for r in res.results], axis=0)
```

```python
import numpy as np
from contextlib import ExitStack
import concourse.bass as bass
import concourse.mybir as mybir
from concourse.bass_utils import run_bass_kernel_spmd

F32 = mybir.dt.float32
BF16 = mybir.dt.bfloat16
ALU = mybir.AluOpType
AF = mybir.ActivationFunctionType
AX = mybir.AxisListType

SEQ = 2048
D = 1024
NT = 16
DEPTH = 2
EPS = 1e-6
IN_W = 10816
C_LRUX, C_LRUG, C_Q, C_KV, C_GATE, C_GQ, C_GK, C_GV, C_GOG, C_GLR, C_MG = 0, 1024, 2048, 3072, 4608, 4656, 5168, 5680, 6704, 7728, 7744
DFF = 2816
NEG = -30000.0


class Buf:
    __slots__ = ("name", "w", "r", "excl")

    def __init__(self, name="", excl=False):
        self.name = name
        self.w = None
        self.r = []
        self.excl = excl


class Sched:
    ENG = ("sync", "scalar", "vector", "gpsimd", "tensor")
    DMAQ = ("sync", "gpsimd", "scalar")

    def __init__(self, nc, es, n_dma_sems=12):
        self.nc = nc
        self.q = {e: [] for e in self.ENG}
        self.cnt = {e: 0 for e in self.ENG}
        self.sems = []
        self.esem = {}
        for e in self.ENG:
            self.esem[e] = len(self.sems)
            self.sems.append(es.enter_context(nc.semaphore("s_" + e)))
        self.known = {e: {} for e in self.ENG}
        self.dpool = {}
        self.dcnt = {}
        self.dlast = {}
        for qn in self.DMAQ:
            self.dpool[qn] = []
            for i in range(n_dma_sems):
                self.dpool[qn].append(len(self.sems))
                self.sems.append(es.enter_context(nc.semaphore(f"d_{qn}_{i}")))
            self.dcnt[qn] = 0
        self.K = n_dma_sems

    def _waits(self, eng, r, w):
        waits = {}
        kn = self.known[eng]
        own_pe = self.esem["tensor"] if eng == "tensor" else -1

        def need(kv):
            k, v = kv
            if k == own_pe:
                return
            if kn.get(k, 0) < v and waits.get(k, 0) < v:
                waits[k] = v

        own = self.esem.get(eng, -2)
        for b in r:
            if b.w is not None:
                need(b.w)
            if b.excl:
                for x in b.r:
                    if x[0] != own:
                        need(x)
        for b in w:
            if b.w is not None:
                need(b.w)
            for x in b.r:
                need(x)
        for k, v in waits.items():
            kn[k] = v
        return list(waits.items())

    def op(self, eng, fn, r=(), w=()):
        waits = self._waits(eng, r, w)
        self.cnt[eng] += 1
        seq = self.cnt[eng]
        k = self.esem[eng]
        self.q[eng].append((waits, fn, (k, 1)))
        for b in w:
            b.w = (k, seq)
            b.r = []
        for b in r:
            if b not in w:
                b.r.append((k, seq))
                if len(b.r) > 24:
                    b.r = b.r[-24:] if False else self._compact(b.r)

    @staticmethod
    def _compact(lst):
        d = {}
        for k, v in lst:
            if d.get(k, 0) < v:
                d[k] = v
        return list(d.items())

    def dma(self, qn, out, in_, r=(), w=(), **kw):
        waits = self._waits(qn, r, w)
        i = self.dcnt[qn]
        self.dcnt[qn] += 1
        k = self.dpool[qn][i % self.K]
        val = 16 * (i // self.K + 1)
        if val > 16 and self.known[qn].get(k, 0) < val - 16:
            waits.append((k, val - 16))
            self.known[qn][k] = val - 16
        self.dlast[k] = val
        self.q[qn].append((waits, lambda e: e.dma_start(out=out, in_=in_, **kw), (k, 16)))
        for b in w:
            b.w = (k, val)
            b.r = []
        for b in r:
            if b not in w:
                b.r.append((k, val))
                if len(b.r) > 24:
                    b.r = self._compact(b.r)

    def pe_drain(self):
        k = self.esem["tensor"]
        if self.cnt["tensor"] > 0:
            self.q["tensor"].append(([(k, self.cnt["tensor"])], None, None))

    def barrier(self):
        tgt = [(self.esem[e], self.cnt[e]) for e in self.ENG if self.cnt[e] > 0]
        tgt += list(self.dlast.items())
        for e in self.ENG:
            waits = []
            for k, v in tgt:
                if e == "tensor" and k == self.esem["tensor"]:
                    continue
                if self.known[e].get(k, 0) < v:
                    waits.append((k, v))
                    self.known[e][k] = v
            if waits:
                self.q[e].append((waits, None, None))

    def emit(self):
        nc = self.nc
        with nc.Block() as block:
            for e in self.ENG:
                def body(eng, _e=e):
                    for waits, fn, inc in self.q[_e]:
                        for k, v in waits:
                            eng.wait_ge(self.sems[k], v)
                        if fn is not None:
                            ins = fn(eng)
                            ins.then_inc(self.sems[inc[0]], inc[1])
                getattr(block, e)(body)


def fap(a, dims):
    return bass.AP(a.tensor, a.offset, [list(a.ap[0])] + [list(d) for d in dims])


def _rel_bucket(d):
    d = np.asarray(d)
    n = np.maximum(d, 0)
    nf = np.maximum(n, 16).astype(np.float32)
    large = 16 + (np.log(nf / np.float32(16)) / np.float32(np.log(128 / 16)) * np.float32(16)).astype(np.int32)
    large = np.minimum(large, 31)
    return np.where(n < 16, n, large)


def host_consts():
    c = {}
    c["ident"] = np.eye(128, dtype=np.float32)
    c["antiid"] = np.eye(128, dtype=np.float32)[::-1].copy()
    aid127 = np.zeros((128, 128), np.float32)
    for i in range(127):
        aid127[i, 126 - i] = 1.0
    aid127[127, 127] = 1.0
    c["antiid127"] = aid127
    s = np.arange(128)
    c["triu"] = (s[:, None] <= s[None, :]).astype(np.float32)
    c["tril"] = (s[:, None] > s[None, :]).astype(np.float32)
    def oh(deltas, valid):
        m = np.zeros((33, len(deltas)), np.float32)
        b = _rel_bucket(deltas)
        for i, (dd, v) in enumerate(zip(deltas, valid)):
            if v:
                m[b[i], i] = 1.0
            else:
                m[32, i] = 1.0
        return m
    dc = np.arange(-2048, 2048)
    c["oh_c"] = oh(dc, dc >= 0)
    ds = np.arange(-512, 512)
    c["oh_s"] = oh(ds, ds >= 0)
    c["oh_w"] = oh(ds, (ds >= 0) & (ds < 256))
    cs = np.arange(127) * 16
    js = np.arange(32) * 64
    ov = np.clip(np.minimum(cs[:, None] + 32, js[None, :] + 64) - np.maximum(cs[:, None], js[None, :]), 0, None).astype(np.float32) / 32.0
    ovx = np.zeros((128, 33), np.float32)
    ovx[:127, :32] = ov
    ovx[:127, 32] = 1.0
    c["ovx"] = ovx
    pos = np.arange(SEQ)
    cur = pos // 64
    blk = np.arange(32)[None, :]
    cand = (blk >= 1) & (blk <= cur[:, None] - 2)
    forced = (blk == 0) | (blk == cur[:, None]) | (blk == cur[:, None] - 1)
    c["cand"] = cand.astype(np.float32).reshape(NT, 128, 32).transpose(1, 0, 2).copy()
    c["negc"] = ((cand.astype(np.float32) - 1.0) * 1e4).reshape(NT, 128, 32).transpose(1, 0, 2).copy()
    c["forced"] = forced.astype(np.float32).reshape(NT, 128, 32).transpose(1, 0, 2).copy()
    ex = np.zeros((128, NT, 128), np.float32)
    for kt in range(NT):
        for key in range(128):
            ex[2 * kt + key // 64, kt, key] = 1.0
    c["expand_near"] = ex.copy()
    ex[32:34] = 1.0
    c["expand"] = ex
    return c


CONST_SHAPES = None


def build(debug=False, n_layers=DEPTH, stop=None):
    nc = bass.Bass("TRN2", target_bir_lowering=False)
    consts = host_consts()
    din = {}

    def inp(name, shape, dt=F32):
        din[name] = nc.dram_tensor(name, list(shape), dt, kind="ExternalInput").ap()
        return din[name]

    x_in = inp("x", [SEQ, D])
    rel_table = inp("rel_table", [32, 16])
    norm_g = inp("norm_g", [DEPTH, 4, D])
    w_in = inp("w_in", [DEPTH, D, IN_W])
    conv_w = inp("conv_w", [DEPTH, 4, D])
    conv_b = inp("conv_b", [DEPTH, D])
    lru_wg = inp("lru_w_gates", [DEPTH, 2, 8, 128, 128])
    lru_bg = inp("lru_b_gates", [DEPTH, 2, D])
    lru_lam = inp("lru_lambda", [DEPTH, D])
    cmp_pos = inp("cmp_pos", [DEPTH, 2, 32, 64])
    cmp_w1 = inp("cmp_w1", [DEPTH, 2, 2048, 256])
    cmp_w2 = inp("cmp_w2", [DEPTH, 2, 256, 64])
    gla_wa2 = inp("gla_wa2", [DEPTH, 16, 512])
    gla_ba = inp("gla_ba", [DEPTH, 512])
    gla_norm = inp("gla_norm", [DEPTH, 256])
    w_branch = inp("w_branch", [DEPTH, 3, D, D])
    w_out = inp("w_out", [DEPTH, D, D])
    w_ffn_in = inp("w_ffn_in", [DEPTH, D, 2 * DFF])
    w_ffn_out = inp("w_ffn_out", [DEPTH, DFF, D])
    cin = {k: inp("c_" + k, v.shape) for k, v in consts.items()}

    okind = "ExternalOutput"
    y_out = nc.dram_tensor("out", [SEQ, D], F32, kind=okind).ap()
    skind = "ExternalOutput"
    xres = nc.dram_tensor("xres", [SEQ, D], F32, kind=skind).ap()
    xmid = nc.dram_tensor("xmid", [SEQ, D], F32, kind=skind).ap()
    ysc = nc.dram_tensor("ysc", [3, D, SEQ], BF16, kind=skind).ap()
    tc_d = nc.dram_tensor("tc_d", [2, 16, 4096], BF16, kind="Internal").ap()
    ts_d = nc.dram_tensor("ts_d", [2, 16, 1024], BF16, kind="Internal").ap()
    tw_d = nc.dram_tensor("tw_d", [2, 16, 1024], BF16, kind="Internal").ap()
    hsc = nc.dram_tensor("hsc", [128, 8 * SEQ], BF16, kind=skind).ap()
    b_hsc = Buf()
    b_xres, b_xmid, b_ysc, b_tabs = Buf(), Buf(), [Buf(), Buf(), Buf()], Buf()

    with ExitStack() as es:
        S = Sched(nc, es)
        es.enter_context(nc.allow_non_contiguous_dma(reason="small param loads"))

        def mm(out, lhsT, rhs, start, stop, r, w):
            S.op("tensor", lambda e: e.matmul(out, lhsT=lhsT, rhs=rhs, start=start, stop=stop), r, w)

        def trp(out, in_, ident, r, w):
            S.op("tensor", lambda e: e.transpose(out, in_, ident), r, w)

        def act(out, in_, func, r, w, **kw):
            S.op("scalar", lambda e: e.activation(out=out, in_=in_, func=func, **kw), r, w)

        def tt(eng, out, in0, in1, op, r, w):
            S.op(eng, lambda e: e.tensor_tensor(out=out, in0=in0, in1=in1, op=op), r, w)

        def tsc(eng, out, in0, s1, op0, r, w, s2=None, op1=None):
            if op1 is None:
                S.op(eng, lambda e: e.tensor_scalar(out=out, in0=in0, scalar1=s1, scalar2=None, op0=op0), r, w)
            else:
                S.op(eng, lambda e: e.tensor_scalar(out=out, in0=in0, scalar1=s1, scalar2=s2, op0=op0, op1=op1), r, w)

        def stt(out, in0, scalar, in1, op0, op1, r, w):
            S.op("vector", lambda e: e.scalar_tensor_tensor(out=out, in0=in0, scalar=scalar, in1=in1, op0=op0, op1=op1), r, w)

        def cp(eng, out, in_, r, w):
            if eng == "scalar":
                S.op("scalar", lambda e: e.copy(out=out, in_=in_), r, w)
            else:
                S.op(eng, lambda e: e.tensor_copy(out=out, in_=in_), r, w)

        def recip(out, in_, r, w):
            S.op("vector", lambda e: e.reciprocal(out=out, in_=in_), r, w)

        def memset(eng, ap, val, w):
            S.op(eng, lambda e: e.memset(ap, val), (), w)

        def dma(q, out, in_, r=(), w=()):
            S.dma(q, out, in_, r, w)

        class T:
            _n = [0]

            def __init__(self, stack, name, shape, dt, psum=False):
                T._n[0] += 1
                name = f"{name}_{T._n[0]}"
                self.t = stack.enter_context((nc.psum_tensor if psum else nc.sbuf_tensor)(name, list(shape), dt))
                self.b = Buf(name)

            def __getitem__(self, idx):
                return self.t[idx]

        PA = T(es, "PA", [128, 2048], F32, psum=True)
        PB = T(es, "PB", [128, 2048], F32, psum=True)
        pbank = []
        for i in range(8):
            src = PA if i < 4 else PB
            pbank.append((src.t[:, (i % 4) * 512:(i % 4 + 1) * 512], Buf(f"bank{i}", excl=True)))
        PAb = PA.t.bitcast(BF16)
        PBb = PB.t.bitcast(BF16)

        def bank_bf(i):
            src = PAb if i < 4 else PBb
            return src[:, (i % 4) * 1024:(i % 4 + 1) * 1024]

        ident_f = T(es, "ident_f", [128, 128], F32)
        ident_b = T(es, "ident_b", [128, 128], BF16)
        dma("sync", ident_f[:], cin["ident"], w=[ident_f.b])
        cp("vector", ident_b[:], ident_f[:], [ident_f.b], [ident_b.b])

        hT = T(es, "hT", [128, 8, SEQ], BF16)
        hT_b = [Buf(f"hT{t}") for t in range(NT)]

        def load_gain(ph, l, i):
            gt = T(ph, f"gain{i}", [128, D], F32)
            src = norm_g[l, i:i + 1, :]
            dma("sync", gt[:], bass.AP(src.tensor, src.offset, [[0, 128], [1, D]]), w=[gt.b])
            return gt

        def norm_transpose_tile(ph, t, xt_ap, xt_buf, gt, ss, rs, hb, junk, pbi):
            act(junk[:], xt_ap, AF.Square, [xt_buf], [junk.b, ss.b], accum_out=ss[:, t:t + 1])
            act(rs[:, t:t + 1], ss[:, t:t + 1], AF.Sqrt, [ss.b], [rs.b], scale=1.0 / D, bias=EPS)
            recip(rs[:, t:t + 1], rs[:, t:t + 1], [rs.b], [rs.b])
            stt(hb[:], xt_ap, rs[:, t:t + 1], gt[:], ALU.mult, ALU.mult, [xt_buf, rs.b, gt.b], [hb.b])
            pv, pbuf = bank_bf(pbi), pbank[pbi][1]
            for kc in range(8):
                trp(pv[:, kc * 128:(kc + 1) * 128], hb[:, kc * 128:(kc + 1) * 128], ident_b[:], [hb.b, ident_b.b], [pbuf])
            cp("scalar", hT[:, :, t * 128:(t + 1) * 128], pv.rearrange("p (k s) -> p k s", k=8), [pbuf], [hT_b[t]])

        def load_slab(dst_ap, w2d, c0, ncols, wbuf, nk=8):
            src = w2d[:, c0:c0 + ncols].rearrange("(kc p) n -> p kc n", p=128)
            dma("gpsimd", dst_ap, src, w=[wbuf])

        def proj_fm(wslab, wbuf, col_off, M, rhs_tile, rhs_bufs, out_banks, nk=8, sc_list=(0, 1, 2, 3)):
            for i, sc in enumerate(sc_list):
                pa, pb_ = out_banks[i]
                for kc in range(nk):
                    mm(pa[0:M, :], wslab[:, kc, col_off:col_off + M], rhs_tile[:, kc, sc * 512:(sc + 1) * 512],
                       kc == 0, kc == nk - 1, [wbuf] + rhs_bufs[sc * 4:(sc + 1) * 4], [pb_])

        def phase_A(l, x_src):
            with ExitStack() as ph:
                xt = [T(ph, f"xtA{i}", [128, D], F32) for i in range(2)]
                hb = [T(ph, f"hbA{i}", [128, D], BF16) for i in range(2)]
                junk = T(ph, "junkA", [128, D], BF16)
                ss = T(ph, "ssA", [128, NT], F32)
                rs = T(ph, "rsA", [128, NT], F32)
                g0 = load_gain(ph, l, 0)
                for t in range(NT):
                    dma("sync", xt[t % 2][:], x_src[t * 128:(t + 1) * 128, :], r=[b_xres], w=[xt[t % 2].b])
                    norm_transpose_tile(ph, t, xt[t % 2][:], xt[t % 2].b, g0, ss, rs, hb[t % 2], junk, t % 2)
                S.barrier()

        def phase_lru(l):
            with ExitStack() as ph:
                prow = T(ph, "prow", [8, D], F32)
                lpT = T(ph, "lpT", [128, 8, 8], F32)
                sp = T(ph, "lru_sp", [128, 8, 6], F32)
                wg = T(ph, "lru_wg", [128, 2, 8, 128], BF16)
                slab = [T(ph, f"lslab{i}", [128, 8, 2, 128], BF16) for i in range(2)]
                XA = T(ph, "XA", [128, SEQ + 4], F32)
                XC = T(ph, "XC", [128, SEQ], F32)
                XCB = T(ph, "XCB", [128, SEQ], BF16)
                R = T(ph, "R", [128, SEQ], F32)
                A = T(ph, "A", [128, SEQ], F32)
                I = T(ph, "I", [128, SEQ], F32)
                H = T(ph, "H", [128, SEQ], F32)
                YA = [T(ph, f"YA{i}", [128, SEQ], BF16) for i in range(2)]
                for k in range(4):
                    dma("sync", prow[k:k + 1, :], conv_w[l, k:k + 1, :], w=[prow.b])
                dma("sync", prow[4:5, :], conv_b[l:l + 1, :], w=[prow.b])
                dma("sync", prow[5:7, :], lru_bg[l], w=[prow.b])
                dma("sync", prow[7:8, :], lru_lam[l:l + 1, :], w=[prow.b])
                pv, pbuf = pbank[7]
                for c in range(8):
                    trp(pv[:, c * 8:(c + 1) * 8], prow[0:8, c * 128:(c + 1) * 128], ident_f[0:8, 0:8], [prow.b, ident_f.b], [pbuf])
                cp("vector", lpT[:], pv[:, 0:64].rearrange("p (c k) -> p c k", c=8), [pbuf], [lpT.b])
                xs, ln1, ser, msk, nsp8, nsp16 = (sp[:, :, i] for i in range(6))
                act(xs, lpT[:, :, 7], AF.Exp, [lpT.b], [sp.b], scale=-1.0)
                act(ln1, xs, AF.Ln, [sp.b], [sp.b], bias=1.0)
                tsc("vector", ser, xs, -0.25, ALU.mult, [sp.b], [sp.b], 1.0 / 3.0, ALU.add)
                tt("vector", ser, ser, xs, ALU.mult, [sp.b], [sp.b])
                tsc("vector", ser, ser, -1.0, ALU.mult, [sp.b], [sp.b], 0.5, ALU.add)
                tt("vector", ser, ser, xs, ALU.mult, [sp.b], [sp.b])
                tsc("vector", ser, ser, -1.0, ALU.mult, [sp.b], [sp.b], 1.0, ALU.add)
                tt("vector", ser, ser, xs, ALU.mult, [sp.b], [sp.b])
                tsc("vector", msk, xs, 0.03, ALU.is_lt, [sp.b], [sp.b])
                tt("vector", ser, ser, ln1, ALU.subtract, [sp.b], [sp.b])
                tt("vector", ser, ser, msk, ALU.mult, [sp.b], [sp.b])
                tt("vector", ser, ser, ln1, ALU.add, [sp.b], [sp.b])
                tsc("vector", nsp8, ser, -8.0, ALU.mult, [sp.b], [sp.b])
                tsc("vector", nsp16, ser, -16.0, ALU.mult, [sp.b], [sp.b])
                dma("gpsimd", wg[:], lru_wg[l].rearrange("k n c e -> c k n e"), w=[wg.b])
                memset("vector", XA[:, 0:3], 0.0, [XA.b])
                w2d = w_in[l]
                for c in range(8):
                    sl = slab[c % 2]
                    dma("gpsimd", sl[:, :, 0, :], w2d[:, C_LRUX + c * 128:C_LRUX + (c + 1) * 128].rearrange("(kc p) n -> p kc n", p=128), w=[sl.b])
                    dma("gpsimd", sl[:, :, 1, :], w2d[:, C_LRUG + c * 128:C_LRUG + (c + 1) * 128].rearrange("(kc p) n -> p kc n", p=128), w=[sl.b])
                    slv = sl.t.rearrange("p k a n -> p k (a n)")
                    proj_fm(slv, sl.b, 0, 128, hT.t, hT_b, pbank[0:4])
                    cp("scalar", XA[:, 3:3 + SEQ], PA[:, :], [pbank[i][1] for i in range(4)], [XA.b])
                    cw = lambda k: lpT[:, c, k:k + 1]
                    tsc("vector", XC[:], XA[:, 3:3 + SEQ], cw(3), ALU.mult, [XA.b, lpT.b], [XC.b], cw(4), ALU.add)
                    for k in range(3):
                        stt(XC[:], XA[:, k:k + SEQ], cw(k), XC[:], ALU.mult, ALU.add, [XA.b, lpT.b, XC.b], [XC.b])
                    cp("gpsimd", XCB[:], XC[:], [XC.b], [XCB.b])
                    for gk in range(2):
                        banks = pbank[4:8] if gk == 0 else pbank[0:4]
                        for sc in range(4):
                            mm(banks[sc][0], wg[:, gk, c, :], XCB[:, sc * 512:(sc + 1) * 512], True, True, [wg.b, XCB.b], [banks[sc][1]])
                    act(R[:], PB[:, :], AF.Sigmoid, [pbank[i][1] for i in range(4, 8)], [R.b], bias=lpT[:, c, 5:6])
                    act(I[:], PA[:, :], AF.Sigmoid, [pbank[i][1] for i in range(4)], [I.b], bias=lpT[:, c, 6:7])
                    proj_fm(slv, sl.b, 128, 128, hT.t, hT_b, pbank[4:8])
                    act(A[:], R[:], AF.Exp, [R.b, sp.b], [A.b], scale=sp[:, c, 4:5])
                    act(R[:], R[:], AF.Exp, [R.b, sp.b], [R.b], scale=sp[:, c, 5:6])
                    tsc("vector", R[:], R[:], -1.0, ALU.mult, [R.b], [R.b], 1.0, ALU.add)
                    act(R[:], R[:], AF.Sqrt, [R.b], [R.b])
                    tt("gpsimd", I[:], I[:], XC[:], ALU.mult, [I.b, XC.b], [I.b])
                    tt("gpsimd", I[:], I[:], R[:], ALU.mult, [I.b, R.b], [I.b])
                    S.op("vector", lambda e: e.tensor_tensor_scan(out=H[:], data0=A[:], data1=I[:], initial=0.0, op0=ALU.mult, op1=ALU.add),
                         [A.b, I.b], [H.b])
                    gab = [pbank[i][1] for i in range(4, 8)]
                    act(R[:], PB[:, :], AF.Square, gab, [R.b])
                    tsc("vector", R[:], R[:], 0.044715, ALU.mult, [R.b], [R.b], 1.0, ALU.add)
                    tt("vector", R[:], R[:], PB[:, :], ALU.mult, [R.b] + gab, [R.b])
                    act(R[:], R[:], AF.Sigmoid, [R.b], [R.b], scale=1.5957691216057308)
                    tt("vector", R[:], R[:], PB[:, :], ALU.mult, [R.b] + gab, [R.b])
                    ya = YA[c % 2]
                    tt("vector", ya[:], R[:], H[:], ALU.mult, [R.b, H.b], [ya.b])
                    dma("sync", ysc[0, c * 128:(c + 1) * 128, :], ya[:], r=[ya.b], w=[b_ysc[0]])
                S.barrier()

        def phase_gla(l):
            w2d = w_in[l]
            with ExitStack() as ph:
                qT = T(ph, "gqT", [128, 4, SEQ], F32)
                kT = T(ph, "gkT", [128, 4, SEQ], F32)
                lrT = T(ph, "lrT", [32, SEQ], F32)
                wa2x = T(ph, "wa2x", [32, 512], F32)
                wres = T(ph, "gwres", [128, 8, 2560], BF16)
                gnb = T(ph, "gnb", [128, 4, 256], F32)
                st_f = T(ph, "st_f", [128, 4, 256], F32)
                st_b = T(ph, "st_b", [128, 4, 256], BF16)
                cm4 = T(ph, "cm4", [128, 4, 128], F32)
                triu = T(ph, "triu", [128, 128], F32)
                tril = T(ph, "tril", [128, 128], F32)
                dma("sync", triu[:], cin["triu"], w=[triu.b])
                dma("sync", tril[:], cin["tril"], w=[tril.b])
                for hh in range(4):
                    dma("sync", cm4[:, hh, :], cin["triu"], w=[cm4.b])
                    src = gla_norm[l:l + 1, :]
                    dma("sync", gnb[:, hh, :], bass.AP(src.tensor, src.offset, [[0, 128], [1, 256]]), w=[gnb.b])
                memset("vector", wa2x[:], 0.0, [wa2x.b])
                memset("vector", lrT[:], 1.0, [lrT.b])
                dma("sync", wa2x[0:16, :], gla_wa2[l], w=[wa2x.b])
                dma("sync", wa2x[16:17, :], gla_ba[l:l + 1, :], w=[wa2x.b])
                for i, c0 in enumerate((C_GK, C_GV, C_GV + 512, C_GOG, C_GOG + 512)):
                    load_slab(wres[:, :, i * 512:(i + 1) * 512], w2d, c0, 512, wres.b)
                with ExitStack() as ph2:
                    slab = [T(ph2, f"gslab{i}", [128, 8, 512], BF16) for i in range(2)]
                    lslab = T(ph2, "glslab", [128, 8, 16], BF16)
                    load_slab(slab[0][:], w2d, C_GQ, 512, slab[0].b)
                    load_slab(slab[1][:], w2d, C_GK, 512, slab[1].b)
                    load_slab(lslab[:], w2d, C_GLR, 16, lslab.b)
                    for i in range(8):
                        banks = pbank[0:4] if i % 2 == 0 else pbank[4:8]
                        src = PA if i % 2 == 0 else PB
                        proj_fm(slab[i // 4].t, slab[i // 4].b, (i % 4) * 128, 128, hT.t, hT_b, banks)
                        dst = qT if i < 4 else kT
                        act(dst[:, i % 4, :], src[:, :], AF.Copy, [b for _, b in banks], [dst.b], scale=(128 ** -0.5 if i < 4 else 1.0))
                    proj_fm(lslab.t, lslab.b, 0, 16, hT.t, hT_b, pbank[0:4])
                    cp("vector", lrT[0:16, :], PA[0:16, :], [b for _, b in pbank[0:4]], [lrT.b])
                    S.barrier()
                sp_t = T(ph, "g_sp", [128, 512], F32)
                E1 = T(ph, "g_E1", [128, 512], F32)
                E2 = T(ph, "g_E2", [128, 512], F32)
                Erb = T(ph, "g_Erb", [128, 512], F32)
                qtb = T(ph, "g_qtb", [128, 4, 128], BF16)
                ktb = T(ph, "g_ktb", [128, 4, 128], BF16)
                kend = T(ph, "g_kend", [128, 512], BF16)
                v_bf = T(ph, "g_vbf", [128, 1024], BF16)
                sg = T(ph, "g_sg", [128, 1024], F32)
                attm = T(ph, "g_attm", [128, 4, 128], BF16)
                on = T(ph, "g_on", [128, 1024], F32)
                yc = [T(ph, f"g_yc{i}", [128, 1024], BF16) for i in range(2)]
                ycT = [T(ph, f"g_ycT{i}", [128, 8, 128], BF16) for i in range(2)]
                junk = T(ph, "g_junk", [128, 256], BF16)
                ssq = T(ph, "g_ssq", [128, 4], F32)
                rst = T(ph, "g_rst", [128, 4], F32)
                bk = lambda i: pbank[i][0]
                bb = lambda i: pbank[i][1]
                for t in range(NT):
                    tsl = slice(t * 128, (t + 1) * 128)
                    mm(bk(0), lrT[0:17, tsl], wa2x[0:17, :], True, True, [lrT.b, wa2x.b], [bb(0)])
                    act(sp_t[:], bk(0), AF.Exp, [bb(0)], [sp_t.b], scale=-1.0)
                    act(sp_t[:], sp_t[:], AF.Ln, [sp_t.b], [sp_t.b], bias=1.0)
                    for hh in range(4):
                        mm(bk(1)[:, hh * 128:(hh + 1) * 128], sp_t[:, hh * 128:(hh + 1) * 128], triu[:], True, True, [sp_t.b, triu.b], [bb(1)])
                    mm(bk(2), tril[:], sp_t[:], True, True, [tril.b, sp_t.b], [bb(2)])
                    act(E1[:], bk(1), AF.Exp, [bb(1)], [E1.b], scale=-1.0 / 16.0)
                    act(E2[:], bk(1), AF.Exp, [bb(1)], [E2.b], scale=1.0 / 16.0)
                    act(Erb[:], bk(2), AF.Exp, [bb(2)], [Erb.b], scale=-1.0 / 16.0)
                    tt("vector", qtb[:], qT[:, :, tsl], E1.t.rearrange("p (h s) -> p h s", h=4), ALU.mult, [qT.b, E1.b], [qtb.b])
                    tt("gpsimd", ktb[:], kT[:, :, tsl], E2.t.rearrange("p (h s) -> p h s", h=4), ALU.mult, [kT.b, E2.b], [ktb.b])
                    for kc in range(8):
                        mm(bk(3), hT[:, kc, tsl], wres[:, kc, 0:512], kc == 0, kc == 7, [hT_b[t], wres.b], [bb(3)])
                    tt("vector", kend[:], bk(3), Erb[:], ALU.mult, [bb(3), Erb.b], [kend.b])
                    for half in range(2):
                        for kc in range(8):
                            mm(bk(4 + half), hT[:, kc, tsl], wres[:, kc, 512 + half * 512:1024 + half * 512], kc == 0, kc == 7, [hT_b[t], wres.b], [bb(4 + half)])
                    cp("scalar", v_bf[:], PB[:, 0:1024], [bb(4), bb(5)], [v_bf.b])
                    for half in range(2):
                        for kc in range(8):
                            mm(bk(6 + half), hT[:, kc, tsl], wres[:, kc, 1536 + half * 512:2048 + half * 512], kc == 0, kc == 7, [hT_b[t], wres.b], [bb(6 + half)])
                    act(sg[:], PB[:, 1024:2048], AF.Silu, [bb(6), bb(7)], [sg.b])
                    tt("gpsimd", sg[:], sg[:], gnb.t.rearrange("p h e -> p (h e)"), ALU.mult, [sg.b, gnb.b], [sg.b])
                    for hh in range(4):
                        mm(bk(0)[:, hh * 128:(hh + 1) * 128], ktb[:, hh, :], qtb[:, hh, :], True, True, [ktb.b, qtb.b], [bb(0)])
                    tt("vector", attm[:], bk(0).rearrange("p (h s) -> p h s", h=4), cm4[:], ALU.mult, [bb(0), cm4.b], [attm.b])
                    for hh in range(4):
                        ob = 4 + hh // 2
                        oap = bk(ob)[:, (hh % 2) * 256:(hh % 2 + 1) * 256]
                        mm(oap, attm[:, hh, :], v_bf[:, hh * 256:(hh + 1) * 256], hh % 2 == 0, t == 0 and hh % 2 == 1, [attm.b, v_bf.b], [bb(ob)])
                        if t > 0:
                            mm(oap, qtb[:, hh, :], st_b[:, hh, :], False, hh % 2 == 1, [qtb.b, st_b.b], [bb(ob)])
                    for hh in range(4):
                        kb_ = 6 + hh // 2
                        mm(bk(kb_)[:, (hh % 2) * 256:(hh % 2 + 1) * 256], kend[:, hh * 128:(hh + 1) * 128], v_bf[:, hh * 256:(hh + 1) * 256],
                           hh % 2 == 0, hh % 2 == 1, [kend.b, v_bf.b], [bb(kb_)])
                    for hh in range(4):
                        kvp = bk(6 + hh // 2)[:, (hh % 2) * 256:(hh % 2 + 1) * 256]
                        if t == 0:
                            cp("vector", st_f[:, hh, :], kvp, [bb(6 + hh // 2)], [st_f.b])
                        else:
                            dec = E1[:, hh * 128 + 127:hh * 128 + 128]
                            stt(st_f[:, hh, :], st_f[:, hh, :], dec, kvp, ALU.mult, ALU.add, [st_f.b, E1.b, bb(6 + hh // 2)], [st_f.b])
                    cp("gpsimd", st_b[:], st_f[:], [st_f.b], [st_b.b])
                    for hh in range(4):
                        oap = bk(4 + hh // 2)[:, (hh % 2) * 256:(hh % 2 + 1) * 256]
                        act(junk[:], oap, AF.Square, [bb(4 + hh // 2)], [junk.b, ssq.b], accum_out=ssq[:, hh:hh + 1])
                    act(rst[:], ssq[:], AF.Sqrt, [ssq.b], [rst.b], scale=1.0 / 256.0, bias=EPS)
                    recip(rst[:], rst[:], [rst.b], [rst.b])
                    tt("vector", on.t.rearrange("p (h e) -> p h e", h=4), PB[:, 0:1024].rearrange("p (h e) -> p h e", h=4),
                       fap(rst[:], [[1, 4], [0, 256]]), ALU.mult, [bb(4), bb(5), rst.b], [on.b])
                    y = yc[t % 2]
                    tt("gpsimd", y[:], on[:], sg[:], ALU.mult, [on.b, sg.b], [y.b])
                    pv = bank_bf(1)
                    for c in range(8):
                        trp(pv[:, c * 128:(c + 1) * 128], y[:, c * 128:(c + 1) * 128], ident_b[:], [y.b, ident_b.b], [bb(1)])
                    yT = ycT[t % 2]
                    cp("scalar", yT[:], pv.rearrange("p (k s) -> p k s", k=8), [bb(1)], [yT.b])
                    dma("sync", ysc[2, :, tsl].rearrange("(c p) s -> p c s", p=128), yT[:], r=[yT.b], w=[b_ysc[2]])
                S.barrier()

        def setup_tables():
            with ExitStack() as ph:
                tblx = T(ph, "tblx", [33, 16], F32)
                memset("vector", tblx[:], NEG, [tblx.b])
                dma("sync", tblx[0:32, :], rel_table, w=[tblx.b])
                for name, dst, n in (("oh_c", tc_d, 4096), ("oh_s", ts_d, 1024), ("oh_w", tw_d, 1024)):
                    oh = T(ph, "t_" + name, [33, n], F32)
                    thi = T(ph, "thi_" + name, [16, n], BF16)
                    tlo = T(ph, "tlo_" + name, [16, n], BF16)
                    dma("sync", oh[:], cin[name], w=[oh.b])
                    for ch in range(n // 512):
                        pa, pbuf = pbank[ch % 8]
                        mm(pa[0:16, :], tblx[0:33, 0:16], oh[0:33, ch * 512:(ch + 1) * 512], True, True, [tblx.b, oh.b], [pbuf])
                        cp("vector", thi[:, ch * 512:(ch + 1) * 512], pa[0:16, :], [pbuf], [thi.b])
                        tt("vector", tlo[:, ch * 512:(ch + 1) * 512], pa[0:16, :], thi[:, ch * 512:(ch + 1) * 512], ALU.subtract, [pbuf, thi.b], [tlo.b])
                    dma("sync", dst[0], thi[:], r=[thi.b], w=[b_tabs])
                    dma("sync", dst[1], tlo[:], r=[tlo.b], w=[b_tabs])
                S.barrier()

        def phase_nsa(l):
            w2d = w_in[l]
            bk = lambda i: pbank[i][0]
            bb = lambda i: pbank[i][1]
            with ExitStack() as ph:
                qT = T(ph, "nqT", [128, 8, SEQ], BF16)
                kS = T(ph, "nkS", [128, 4, SEQ], BF16)
                kW = T(ph, "nkW", [128, 4, SEQ], BF16)
                vS = T(ph, "nvS", [128, NT, 4, 66], BF16)
                vW = T(ph, "nvW", [128, NT, 4, 66], BF16)
                sgate = T(ph, "nsg", [128, NT, 48], F32)
                kcP = T(ph, "nkcP", [128, 2, 4, 128], BF16)
                vcx = T(ph, "nvcx", [128, 4, 98], BF16)
                hbt = T(ph, "nhbt", [128, 3, 4, 2, 512], BF16)
                NM = T(ph, "nNM", [128, 4, 2, 512], BF16)
                Jb = T(ph, "nJb", [128, 2, 128], BF16)
                expd = T(ph, "nexpd", [128, 2, NT, 128], BF16)
                cand = T(ph, "ncand", [128, NT, 32], F32)
                negc = T(ph, "nnegc", [128, NT, 32], F32)
                forced = T(ph, "nforced", [128, NT, 32], F32)
                dma("gpsimd", Jb[:, 0, :], cin["antiid"], w=[Jb.b])
                dma("gpsimd", Jb[:, 1, :], cin["antiid127"], w=[Jb.b])
                dma("gpsimd", expd[:, 0, :, :], cin["expand"], w=[expd.b])
                dma("gpsimd", expd[:, 1, :, :], cin["expand_near"], w=[expd.b])
                memset("vector", NM[:], 0.0, [NM.b])
                dma("sync", cand[:], cin["cand"], w=[cand.b])
                dma("sync", negc[:], cin["negc"], w=[negc.b])
                dma("sync", forced[:], cin["forced"], w=[forced.b])
                memset("vector", vcx[:], 0.0, [vcx.b])
                memset("vector", kcP[:], 0.0, [kcP.b])
                for g in range(4):
                    dma("gpsimd", vcx[:, g, 64:97], cin["ovx"], w=[vcx.b])
                for dl in range(3):
                    tsrc = tw_d if dl == 2 else ts_d
                    for g in range(4):
                        for hl in range(2):
                            for par in range(2):
                                for rp in range(2):
                                    h = 4 * g + 2 * rp + par
                                    a0 = tsrc[hl, h, 512 + dl * 128 - 127:512 + dl * 128 - 127 + 1]
                                    src = bass.AP(a0.tensor, a0.offset, [[1, 128], [1, 128]])
                                    dma("sync", hbt[:, dl, g, hl, par * 256 + rp * 128:par * 256 + rp * 128 + 128], src, r=[b_tabs], w=[hbt.b])
                for g in range(4):
                    for hl in range(2):
                        for par in range(2):
                            for rp in range(2):
                                h = 4 * g + 2 * rp + par
                                a0 = ts_d[hl, h, 640:641]
                                src = bass.AP(a0.tensor, a0.offset, [[0, 1], [0, 2], [1, 128]])
                                c0 = par * 256 + rp * 128
                                dma("sync", NM[32 + hl:33 + hl, g, :, c0:c0 + 128], src, r=[b_tabs], w=[NM.b])
                memset("vector", vS[:, :, :, 64:66], 1.0, [vS.b])
                memset("vector", vW[:, :, :, 64:66], 1.0, [vW.b])
                if stop == "nsa0":
                    S.barrier()
                    return
                with ExitStack() as ph2:
                    slab = [T(ph2, f"nslab{i}", [128, 8, 512], BF16) for i in range(2)]
                    wv = T(ph2, "nwv", [128, 8, 560], BF16)
                    for half in range(2):
                        sl = slab[half]
                        load_slab(sl[:], w2d, C_Q + half * 512, 512, sl.b)
                        for i in range(4):
                            c = half * 4 + i
                            banks = pbank[0:4] if c % 2 == 0 else pbank[4:8]
                            src = PA if c % 2 == 0 else PB
                            proj_fm(sl.t, sl.b, i * 128, 128, hT.t, hT_b, banks)
                            act(qT[:, c, :], src[:, :], AF.Copy, [b for _, b in banks], [qT.b], scale=0.125)
                    if stop == "nsa1a":
                        S.barrier()
                        return
                    n = 0
                    for idx, dst in ((2, kS), (4, kW)):
                        sl = slab[n % 2]
                        n += 1
                        for g in range(4):
                            c0 = C_KV + idx * 256 + g * 64
                            for dup in range(2):
                                dma("gpsimd", sl[:, :, g * 128 + dup * 64:g * 128 + dup * 64 + 64],
                                    w2d[:, c0:c0 + 64].rearrange("(kc p) n -> p kc n", p=128), w=[sl.b])
                        for g in range(4):
                            banks = pbank[0:4] if g % 2 == 0 else pbank[4:8]
                            src = PA if g % 2 == 0 else PB
                            proj_fm(sl.t, sl.b, g * 128, 128, hT.t, hT_b, banks)
                            cp("scalar" if g % 2 == 0 else "vector", dst[:, g, :], src[:, :], [b for _, b in banks], [dst.b])
                    if stop == "nsa1b":
                        S.barrier()
                        return
                    load_slab(wv[:, :, 0:256], w2d, C_KV + 3 * 256, 256, wv.b)
                    load_slab(wv[:, :, 256:512], w2d, C_KV + 5 * 256, 256, wv.b)
                    load_slab(wv[:, :, 512:560], w2d, C_GATE, 48, wv.b)
                    for t in range(NT):
                        tsl = slice(t * 128, (t + 1) * 128)
                        b0, b1 = (0, 1) if t % 2 == 0 else (2, 3)
                        for kc in range(8):
                            mm(bk(b0), hT[:, kc, tsl], wv[:, kc, 0:512], kc == 0, kc == 7, [hT_b[t], wv.b], [bb(b0)])
                        import os
                        SK = os.environ.get("NSA_SKIP", "")
                        if "g" not in SK:
                            for kc in range(8):
                                mm(bk(b1)[:, 0:48], hT[:, kc, tsl], wv[:, kc, 512:560], kc == 0, kc == 7, [hT_b[t], wv.b], [bb(b1)])
                        if "v" not in SK:
                            cp("vector", vS[:, t, :, 0:64], bk(b0)[:, 0:256].rearrange("p (g d) -> p g d", g=4), [bb(b0)], [vS.b])
                        if "w" not in SK:
                            cp("scalar", vW[:, t, :, 0:64], bk(b0)[:, 256:512].rearrange("p (g d) -> p g d", g=4), [bb(b0)], [vW.b])
                        if "g" not in SK:
                            act(sgate[:, t, :], bk(b1)[:, 0:48], AF.Sigmoid, [bb(b1)], [sgate.b])
                    S.barrier()
                if stop == "nsa1":
                    return
                with ExitStack() as ph2:
                    slab = [T(ph2, f"ncslab{i}", [128, 8, 256], BF16) for i in range(2)]
                    w1sb = T(ph2, "nw1", [128, 32, 256], BF16)
                    w2sb = T(ph2, "nw2", [128, 2, 128], BF16)
                    prow2 = T(ph2, "nprow2", [32, 128], F32)
                    posT = T(ph2, "nposT", [128, 32], F32)
                    XAB = [T(ph2, f"nXAB{i}", [128, SEQ], BF16) for i in range(2)]
                    gtmp = T(ph2, "ngtmp", [128, 2, 128], F32)
                    geluT = T(ph2, "ngeluT", [128, 2, 128], BF16)
                    for kv in range(2):
                        for dup in range(2):
                            dma("gpsimd", w1sb[dup * 64:(dup + 1) * 64, :, :], cmp_w1[l, kv].rearrange("(p d) j -> d p j", d=64), w=[w1sb.b])
                            dma("gpsimd", w2sb[:, :, dup * 64:(dup + 1) * 64], cmp_w2[l, kv].rearrange("(jc p) d -> p jc d", p=128), w=[w2sb.b])
                            dma("sync", prow2[:, dup * 64:(dup + 1) * 64], cmp_pos[l, kv], w=[prow2.b])
                        trp(bk(6)[:, 0:32], prow2[:, :], ident_f[0:32, 0:32], [prow2.b, ident_f.b], [bb(6)])
                        cp("vector", posT[:], bk(6)[:, 0:32], [bb(6)], [posT.b])
                        sl = slab[kv]
                        load_slab(sl[:, :, 0:256], w2d, C_KV + kv * 256, 256, sl.b)
                        for cc in range(2):
                            banks = pbank[0:4]
                            proj_fm(sl.t, sl.b, cc * 128, 128, hT.t, hT_b, banks)
                            for ab in range(2):
                                tt("vector" if ab == 0 else "gpsimd" if False else "vector", XAB[ab].t.rearrange("p (i q) -> p i q", q=16), PA.t.rearrange("p (i q) -> p i q", q=16),
                                   fap(posT[:, ab * 16:ab * 16 + 1], [[0, 128], [1, 16]]), ALU.add, [b for _, b in banks] + [posT.b], [XAB[ab].b])
                            for gg in range(2):
                                g = cc * 2 + gg
                                rows = slice(gg * 64, gg * 64 + 64)
                                hb_, hbb = bk(4 + 2 * gg), bb(4 + 2 * gg)
                                for jc in range(2):
                                    for p in range(32):
                                        srcT = XAB[0] if p < 16 else XAB[1]
                                        rhs = fap(srcT[rows, p:p + 1], [[16, 127]])
                                        mm(hb_[:, jc * 128:jc * 128 + 127], w1sb[rows, p, jc * 128:(jc + 1) * 128], rhs, p == 0, p == 31,
                                           [w1sb.b, srcT.b], [hbb])
                                hv = hb_[:, 0:256].rearrange("p (j i) -> p j i", j=2)[:, :, 0:127]
                                gv = gtmp[:, :, 0:127]
                                act(gv, hv, AF.Square, [hbb], [gtmp.b])
                                tsc("vector", gv, gv, 0.044715, ALU.mult, [gtmp.b], [gtmp.b], 1.0, ALU.add)
                                tt("vector", gv, gv, hv, ALU.mult, [gtmp.b, hbb], [gtmp.b])
                                act(gv, gv, AF.Sigmoid, [gtmp.b], [gtmp.b], scale=1.5957691216057308)
                                tt("vector", geluT[:, :, 0:127], gv, hv, ALU.mult, [gtmp.b, hbb], [geluT.b])
                                if kv == 0:
                                    for jc in range(2):
                                        mm(bk(5)[:, 0:127], w2sb[:, jc, :], geluT[:, jc, 0:127], jc == 0, jc == 1, [w2sb.b, geluT.b], [bb(5)])
                                    cp("scalar", kcP[0:64, 0, g, 0:127], bk(5)[0:64, 0:127], [bb(5)], [kcP.b])
                                    cp("scalar", kcP[64:128, 1, g, 0:127], bk(5)[64:128, 0:127], [bb(5)], [kcP.b])
                                else:
                                    for jc in range(2):
                                        mm(bk(5)[0:127, 0:64], geluT[:, jc, 0:127], w2sb[:, jc, 0:64], jc == 0, jc == 1, [w2sb.b, geluT.b], [bb(5)])
                                    cp("scalar", vcx[0:127, g, 0:64], bk(5)[0:127, 0:64], [bb(5)], [vcx.b])
                    S.barrier()
                if stop == "nsa2":
                    return
                dma("sync", hsc, hT.t.rearrange("p k s -> p (k s)"), r=hT_b, w=[b_hsc])
                kpad1 = Buf("kpad1")
                for base, KT_ in ((0, kS), (4, kW)):
                    cp("gpsimd", hT[64:128, base:base + 4, :], KT_[64:128, :, :], [KT_.b], hT_b + [kpad1])
                    memset("vector", hT[0:64, base:base + 4, :], 0.0, hT_b + [kpad1])
                    memset("vector", KT_[64:128, :, :], 0.0, [KT_.b])
                E = [T(ph, f"nE{i}", [128, 512], BF16) for i in range(4)]
                cb = [T(ph, f"ncb{i}", [128, 2, 512], BF16) for i in range(3)]
                for cbx in cb:
                    memset("vector", cbx[:], NEG, [cbx.b])
                ybt = [T(ph, f"nybt{i}", [128, 1024], BF16) for i in range(2)]
                ybT = [T(ph, f"nybT{i}", [128, 8, 128], BF16) for i in range(2)]
                sets = []
                for i in range(2):
                    sets.append(dict(
                        ybacc=T(ph, f"nybacc{i}", [128, 4, 64], F32), tmp1=T(ph, f"ntmp1{i}", [128, 4, 64], F32),
                        tmp2=T(ph, f"ntmp2{i}", [128, 4, 64], F32), impr=T(ph, f"nimpr{i}", [128, 4, 32], F32),
                        imp=T(ph, f"nimp{i}", [128, 32], F32), m8=T(ph, f"nm8{i}", [128, 8], F32),
                        sm=T(ph, f"nsm{i}", [128, 3, 4], F32), Us=(3, 6)[i], Uw=(4, 7)[i]))
                colb = lambda r: (r % 2) * 256 + (r // 2) * 128
                cnt_ = dict(l=0, e=0, kp=0, cb=0)
                tasks = []

                inflight = set()

                def next_Li(hold=False):
                    while True:
                        i = (0, 1, 5)[cnt_["l"] % 3]
                        cnt_["l"] += 1
                        if i not in inflight:
                            break
                    if hold:
                        inflight.add(i)
                    return i

                def next_L(hold=False):
                    return pbank[next_Li(hold)]

                def release_L(Lb):
                    for i in (0, 1, 5):
                        if pbank[i][1] is Lb:
                            inflight.discard(i)

                def next_E():
                    e_ = E[cnt_["e"] % 4]
                    cnt_["e"] += 1
                    return e_

                def mk_cmp(qt, g, st):
                    qsl = slice(qt * 128, (qt + 1) * 128)
                    cbt = cb[cnt_["cb"] % 3]
                    cnt_["cb"] += 1
                    box = {}

                    def pre():
                        for hl in range(2):
                            for par in range(2):
                                for rp in range(2):
                                    h = 4 * g + 2 * rp + par
                                    a0 = tc_d[hl, h, qt * 128 + 1:qt * 128 + 2]
                                    src = bass.AP(a0.tensor, a0.offset, [[16, 127], [1, 128]])
                                    c0 = par * 256 + rp * 128
                                    dma("sync", cbt[0:127, hl, c0:c0 + 128], src, r=[b_tabs], w=[cbt.b])

                    def s1():
                        L, Lb = next_L(hold=True)
                        box["L"] = (L, Lb)
                        for par in range(2):
                            mm(L[:, par * 256:(par + 1) * 256], kcP[:, par, g, :], qT[:, 2 * g:2 * g + 2, qsl], par == 0, False, [kcP.b, qT.b], [Lb])
                        for hl in range(2):
                            mm(L, Jb[:, 1, :], cbt[:, hl, :], False, hl == 1, [Jb.b, cbt.b], [Lb])

                    def s2():
                        L, Lb = box["L"]
                        release_L(Lb)
                        Ec = next_E()
                        act(Ec[:], L, AF.Exp, [Lb], [Ec.b])
                        Uc = bk(2)
                        for r in range(4):
                            mm(Uc[:, r * 98:(r + 1) * 98], Ec[:, colb(r):colb(r) + 128], vcx[:, g, 0:98], r == 0, r == 3, [Ec.b, vcx.b], [bb(2)])

                    def post():
                        Uc = bk(2)
                        sm, ybacc, impr, imp, m8 = st["sm"], st["ybacc"], st["impr"], st["imp"], st["m8"]
                        ucv = lambda a, b_: fap(Uc[:, a:a + 1], [[98, 4], [1, b_]])
                        rs4, wc = sm[:, 0, :], sm[:, 1, :]
                        tsc("vector", rs4, fap(Uc[:, 96:97], [[98, 4]]), 1e-30, ALU.max, [bb(2)], [sm.b])
                        recip(rs4, rs4, [sm.b], [sm.b])
                        tt("vector", wc, rs4, sgate[:, qt, 4 * g:4 * g + 4], ALU.mult, [sm.b, sgate.b], [sm.b])
                        tt("vector", ybacc[:], ucv(0, 64), fap(wc, [[1, 4], [0, 64]]), ALU.mult, [bb(2), sm.b], [ybacc.b])
                        tt("vector", impr[:], ucv(64, 32), fap(rs4, [[1, 4], [0, 32]]), ALU.mult, [bb(2), sm.b], [impr.b])
                        S.op("vector", lambda e: e.tensor_reduce(out=imp[:], in_=fap(impr[:, 0, 0:1], [[1, 32], [32, 4]]), axis=AX.X, op=ALU.add),
                             [impr.b], [imp.b])
                        tt("vector", imp[:], imp[:], cand[:, qt, :], ALU.mult, [imp.b, cand.b], [imp.b])
                        tt("vector", imp[:], imp[:], negc[:, qt, :], ALU.add, [imp.b, negc.b], [imp.b])
                        S.op("vector", lambda e: e.max(out=m8[:], in_=imp[:]), [imp.b], [m8.b])
                        tsc("vector", imp[:], imp[:], m8[:, 4:5], ALU.is_ge, [imp.b, m8.b], [imp.b])
                        tt("vector", imp[:], imp[:], forced[:, qt, :], ALU.max, [imp.b, forced.b], [imp.b])
                        tsc("vector", imp[:], imp[:], -1.0, ALU.add, [imp.b], [imp.b], -NEG, ALU.mult)

                    return dict(pre=pre, s1=s1, s2=s2, post=post, defer=None)

                def mk_tile(qt, g, st, br, kt, first, last, buf, hooks_pre, hooks_post, defer):
                    qsl = slice(qt * 128, (qt + 1) * 128)
                    ksl = slice(kt * 128, (kt + 1) * 128)
                    KT = kS if br == 0 else kW
                    VT = vS if br == 0 else vW
                    Ub = st["Us"] if br == 0 else st["Uw"]
                    dl = qt - kt
                    near = dl < (2 if br == 0 else 3)
                    box = {}

                    def pre():
                        for h_ in hooks_pre:
                            h_()

                    def s1():
                        L, Lb = next_L(hold=True)
                        box["L"] = (L, Lb)
                        mm(L[:, 0:256], KT[:, g, ksl], qT[:, 2 * g:2 * g + 2, qsl], True, False, [KT.b, qT.b], [Lb])
                        mm(L[:, 256:512], hT[:, (0 if br == 0 else 4) + g, ksl], qT[:, 2 * g:2 * g + 2, qsl], False, False, [kpad1, qT.b], [Lb])
                        if br == 0:
                            mm(L, expd[:, 1 if near else 0, kt, :], NM[:, g, buf, :], False, not near, [expd.b, NM.b], [Lb])
                        if near:
                            for hl in range(2):
                                mm(L, Jb[:, 0, :], hbt[:, dl, g, hl, :], False, hl == 1, [Jb.b, hbt.b], [Lb])

                    def s2():
                        L, Lb = box["L"]
                        release_L(Lb)
                        Et = next_E()
                        act(Et[:], L, AF.Exp, [Lb], [Et.b])
                        for r in range(4):
                            mm(bk(Ub)[:, r * 66:(r + 1) * 66], Et[:, colb(r):colb(r) + 128], VT[:, kt, g, 0:66],
                               first and r == 0, last and r == 3, [Et.b, VT.b], [bb(Ub)])

                    def post():
                        for h_ in hooks_post:
                            h_()

                    return dict(pre=pre, s1=s1, s2=s2, post=post, defer=defer)

                def mk_nm_hook(g, st, buf):
                    def hook():
                        imp = st["imp"]
                        M_, Mb = next_L()
                        trp(M_[0:32, 0:128], imp[:, :], ident_f[:, :], [imp.b, ident_f.b], [Mb])
                        cp("vector", NM[0:32, g, buf, :].rearrange("p (a s) -> p a s", a=4), fap(M_[0:32, 0:1], [[0, 4], [1, 128]]), [Mb], [NM.b])
                    return hook

                def mk_combine(qt, g, st, ybq):
                    def hook():
                        sm, ybacc, tmp1, tmp2 = st["sm"], st["ybacc"], st["tmp1"], st["tmp2"]
                        for br in range(2):
                            ub = st["Us"] if br == 0 else st["Uw"]
                            U = bk(ub)
                            rsb, wb_ = sm[:, 0, :], sm[:, 1 + br, :]
                            S.op("vector", lambda e, U=U, rsb=rsb: e.reciprocal(out=rsb, in_=fap(U[:, 64:65], [[66, 4]])), [bb(ub)], [sm.b])
                            tt("vector", wb_, rsb, sgate[:, qt, 16 * (br + 1) + 4 * g:16 * (br + 1) + 4 * g + 4], ALU.mult, [sm.b, sgate.b], [sm.b])
                            tgt = tmp1 if br == 0 else tmp2
                            tt("vector", tgt[:], fap(U[:, 0:1], [[66, 4], [1, 64]]), fap(wb_, [[1, 4], [0, 64]]), ALU.mult, [bb(ub), sm.b], [tgt.b])
                        tt("gpsimd", tmp1[:], tmp1[:], ybacc[:], ALU.add, [tmp1.b, ybacc.b], [tmp1.b])
                        tt("gpsimd", ybq[:, g * 256:(g + 1) * 256].rearrange("p (r d) -> p r d", r=4), tmp1[:], tmp2[:], ALU.add, [tmp1.b, tmp2.b], [ybq.b])
                    return hook

                def mk_ybout(qt, ybq):
                    def hook():
                        qsl = slice(qt * 128, (qt + 1) * 128)
                        li = next_Li()
                        pv = bank_bf(li)
                        for c in range(8):
                            trp(pv[:, c * 128:(c + 1) * 128], ybq[:, c * 128:(c + 1) * 128], ident_b[:], [ybq.b, ident_b.b], [bb(li)])
                        yT = ybT[qt % 2]
                        cp("scalar", yT[:], pv.rearrange("p (k s) -> p k s", k=8), [bb(li)], [yT.b])
                        dma("sync", ysc[1, :, qsl].rearrange("(c p) s -> p c s", p=128), yT[:], r=[yT.b], w=[b_ysc[1]])
                    return hook

                un = 0
                for qt in range(1 if stop == 'nsa3' else NT):
                    ybq = ybt[qt % 2]
                    for g in range(4):
                        st = sets[un % 2]
                        buf = un % 2
                        un += 1
                        tasks.append(mk_cmp(qt, g, st))
                        wk = list(range(max(0, qt - 2), qt + 1))
                        for kt in wk:
                            tasks.append(mk_tile(qt, g, st, 1, kt, kt == wk[0], kt == qt, buf, [], [], None))
                        for kt in range(qt + 1):
                            hp = [mk_nm_hook(g, st, buf)] if kt == 0 else []
                            hq = [mk_combine(qt, g, st, ybq)] if kt == qt else []
                            df = mk_ybout(qt, ybq) if (kt == qt and g == 3) else None
                            tasks.append(mk_tile(qt, g, st, 0, kt, kt == 0, kt == qt, buf, hp, hq, df))
                deferred = {}
                ntk = len(tasks)
                LA = 2
                for j in range(min(LA, ntk)):
                    tasks[j]["pre"]()
                    tasks[j]["s1"]()
                for i, tk in enumerate(tasks):
                    tk["s2"]()
                    tk["post"]()
                    if tk["defer"] is not None:
                        deferred.setdefault(i + 3, []).append(tk["defer"])
                    for fn in deferred.pop(i, []):
                        fn()
                    if i + LA < ntk:
                        tasks[i + LA]["pre"]()
                        tasks[i + LA]["s1"]()
                for k_ in sorted(deferred):
                    for fn in deferred[k_]:
                        fn()
                dma("sync", hT.t.rearrange("p k s -> p (k s)"), hsc, r=[b_hsc], w=hT_b + [kpad1])
                S.barrier()

        def phase_tail(l, x_src, x_dst, b_xsrc, b_xdst):
            bk = lambda i: pbank[i][0]
            bb = lambda i: pbank[i][1]
            w2d = w_in[l]
            with ExitStack() as ph:
                mrgb = T(ph, "mrgb", [128, 8, SEQ], BF16)
                with ExitStack() as ph2:
                    mrg = T(ph2, "mrg", [128, 8, SEQ], F32)
                    yT = T(ph2, "m_yT", [128, 8, SEQ], BF16)
                    yb_ = [Buf(f"m_yT{c}") for c in range(8)]
                    wbr = [T(ph2, f"m_wbr{i}", [128, 8, 256], BF16) for i in range(2)]
                    wmg = [T(ph2, f"m_wmg{i}", [128, 8, 256], BF16) for i in range(2)]
                    sig = [T(ph2, f"m_sig{i}", [128, 512], F32) for i in range(2)]
                    prod = [T(ph2, f"m_prod{i}", [128, 512], F32) for i in range(2)]
                    n = 0
                    bn = 0
                    for br in range(3):
                        for c in range(8):
                            dma("sync", yT[:, c, :], ysc[br, c * 128:(c + 1) * 128, :], r=[b_ysc[br]], w=[yb_[c]])
                        for oc2 in range(4):
                            wb, wm = wbr[oc2 % 2], wmg[oc2 % 2]
                            load_slab(wb[:], w_branch[l, br], oc2 * 256, 256, wb.b)
                            load_slab(wm[:], w2d, C_MG + br * 1024 + oc2 * 256, 256, wm.b)
                            for o in range(2):
                                oc = oc2 * 2 + o
                                for sc in range(4):
                                    ssl = slice(sc * 512, (sc + 1) * 512)
                                    bB, bG = (bn % 4) * 2, (bn % 4) * 2 + 1
                                    bn += 1
                                    for c in range(8):
                                        mm(bk(bB), wb[:, c, o * 128:(o + 1) * 128], yT[:, c, ssl], c == 0, c == 7, [wb.b, yb_[c]], [bb(bB)])
                                    for kc in range(8):
                                        mm(bk(bG), wm[:, kc, o * 128:(o + 1) * 128], hT[:, kc, ssl], kc == 0, kc == 7, [wm.b] + hT_b[sc * 4:(sc + 1) * 4], [bb(bG)])
                                    sg_, pr_ = sig[n % 2], prod[n % 2]
                                    n += 1
                                    act(sg_[:], bk(bG), AF.Sigmoid, [bb(bG)], [sg_.b])
                                    if br == 0:
                                        tt("vector", mrg[:, oc, ssl], sg_[:], bk(bB), ALU.mult, [sg_.b, bb(bB)], [mrg.b])
                                    elif br == 1:
                                        tt("vector", pr_[:], sg_[:], bk(bB), ALU.mult, [sg_.b, bb(bB)], [pr_.b])
                                        tt("gpsimd", mrg[:, oc, ssl], mrg[:, oc, ssl], pr_[:], ALU.add, [mrg.b, pr_.b], [mrg.b])
                                    else:
                                        tt("vector", pr_[:], sg_[:], bk(bB), ALU.mult, [sg_.b, bb(bB)], [pr_.b])
                                        tt("gpsimd", mrgb[:, oc, ssl], mrg[:, oc, ssl], pr_[:], ALU.add, [mrg.b, pr_.b], [mrgb.b])
                    S.barrier()
                with ExitStack() as ph2:
                    wout = T(ph2, "p_wout", [128, 8, D], BF16)
                    g1 = load_gain(ph2, l, 1)
                    g2 = load_gain(ph2, l, 2)
                    xt = [T(ph2, f"p_xt{i}", [128, D], F32) for i in range(2)]
                    on = [T(ph2, f"p_on{i}", [128, D], F32) for i in range(2)]
                    hb = [T(ph2, f"p_hb{i}", [128, D], BF16) for i in range(2)]
                    junk = T(ph2, "p_junk", [128, D], BF16)
                    ss = T(ph2, "p_ss", [128, NT], F32)
                    rs = T(ph2, "p_rs", [128, NT], F32)
                    ss2 = T(ph2, "p_ss2", [128, NT], F32)
                    rs2 = T(ph2, "p_rs2", [128, NT], F32)
                    for half in range(2):
                        load_slab(wout[:, :, half * 512:(half + 1) * 512], w_out[l], half * 512, 512, wout.b)
                    for t in range(NT):
                        tsl = slice(t * 128, (t + 1) * 128)
                        b0 = (t % 2) * 2
                        ov = PA[:, b0 * 512:(b0 + 2) * 512]
                        obufs = [bb(b0), bb(b0 + 1)]
                        for half in range(2):
                            for c in range(8):
                                mm(bk(b0 + half), mrgb[:, c, tsl], wout[:, c, half * 512:(half + 1) * 512], c == 0, c == 7, [mrgb.b, wout.b], [bb(b0 + half)])
                        x_, o_ = xt[t % 2], on[t % 2]
                        dma("sync", x_[:], x_src[tsl, :], r=[b_xsrc], w=[x_.b])
                        act(junk[:], ov, AF.Square, obufs, [junk.b, ss.b], accum_out=ss[:, t:t + 1])
                        act(rs[:, t:t + 1], ss[:, t:t + 1], AF.Sqrt, [ss.b], [rs.b], scale=1.0 / D, bias=EPS)
                        recip(rs[:, t:t + 1], rs[:, t:t + 1], [rs.b], [rs.b])
                        stt(o_[:], ov, rs[:, t:t + 1], g1[:], ALU.mult, ALU.mult, obufs + [rs.b, g1.b], [o_.b])
                        tt("gpsimd", o_[:], o_[:], x_[:], ALU.add, [o_.b, x_.b], [o_.b])
                        dma("sync", xmid[tsl, :], o_[:], r=[o_.b], w=[b_xmid])
                        norm_transpose_tile(ph2, t, o_[:], o_.b, g2, ss2, rs2, hb[t % 2], junk, 4 + t % 2)
                    S.barrier()
            with ExitStack() as ph:
                hid = T(ph, "f_hid", [128, 22, SEQ], BF16)
                hid_b = [Buf(f"hid{j}") for j in range(22)]
                wfo = T(ph, "f_wfo", [128, 22, D], BF16)
                wfo_b = [Buf(f"wfo{j}") for j in range(11)]
                wfi = [T(ph, f"f_wfi{i}", [128, 8, 2, 128], BF16) for i in range(2)]
                sgt = [T(ph, f"f_sg{i}", [128, 512], F32) for i in range(2)]
                g3 = load_gain(ph, l, 3)
                xt = [T(ph, f"f_xt{i}", [128, D], F32) for i in range(2)]
                on = [T(ph, f"f_on{i}", [128, D], F32) for i in range(2)]
                junk = T(ph, "f_junk", [128, D], BF16)
                ss = T(ph, "f_ss", [128, NT], F32)
                rs = T(ph, "f_rs", [128, NT], F32)
                n = 0
                bn = 0
                for j in range(22):
                    wf = wfi[j % 2]
                    dma("gpsimd", wf[:, :, 0, :], w_ffn_in[l][:, j * 128:(j + 1) * 128].rearrange("(kc p) n -> p kc n", p=128), w=[wf.b])
                    dma("gpsimd", wf[:, :, 1, :], w_ffn_in[l][:, DFF + j * 128:DFF + (j + 1) * 128].rearrange("(kc p) n -> p kc n", p=128), w=[wf.b])
                    if j % 2 == 0:
                        jj = j // 2
                        dma("gpsimd", wfo[:, 2 * jj:2 * jj + 2, :], w_ffn_out[l][jj * 256:(jj + 1) * 256, :].rearrange("(j p) n -> p j n", p=128), w=[wfo_b[jj]])
                    for sc in range(4):
                        ssl = slice(sc * 512, (sc + 1) * 512)
                        bG, bU = (bn % 4) * 2, (bn % 4) * 2 + 1
                        bn += 1
                        for kc in range(8):
                            mm(bk(bG), wf[:, kc, 0, :], hT[:, kc, ssl], kc == 0, kc == 7, [wf.b] + hT_b[sc * 4:(sc + 1) * 4], [bb(bG)])
                        for kc in range(8):
                            mm(bk(bU), wf[:, kc, 1, :], hT[:, kc, ssl], kc == 0, kc == 7, [wf.b] + hT_b[sc * 4:(sc + 1) * 4], [bb(bU)])
                        sg_ = sgt[n % 2]
                        n += 1
                        act(sg_[:], bk(bG), AF.Silu, [bb(bG)], [sg_.b])
                        tt("vector", hid[:, j, ssl], sg_[:], bk(bU), ALU.mult, [sg_.b, bb(bU)], [hid_b[j]])
                for t in range(NT):
                    tsl = slice(t * 128, (t + 1) * 128)
                    b0 = (t % 2) * 2
                    ov = PA[:, b0 * 512:(b0 + 2) * 512]
                    obufs = [bb(b0), bb(b0 + 1)]
                    for half in range(2):
                        for j in range(22):
                            mm(bk(b0 + half), hid[:, j, tsl], wfo[:, j, half * 512:(half + 1) * 512], j == 0, j == 21, [hid_b[j], wfo_b[j // 2]], [bb(b0 + half)])
                    x_, o_ = xt[t % 2], on[t % 2]
                    dma("sync", x_[:], xmid[tsl, :], r=[b_xmid], w=[x_.b])
                    act(junk[:], ov, AF.Square, obufs, [junk.b, ss.b], accum_out=ss[:, t:t + 1])
                    act(rs[:, t:t + 1], ss[:, t:t + 1], AF.Sqrt, [ss.b], [rs.b], scale=1.0 / D, bias=EPS)
                    recip(rs[:, t:t + 1], rs[:, t:t + 1], [rs.b], [rs.b])
                    stt(o_[:], ov, rs[:, t:t + 1], g3[:], ALU.mult, ALU.mult, obufs + [rs.b, g3.b], [o_.b])
                    tt("gpsimd", o_[:], o_[:], x_[:], ALU.add, [o_.b, x_.b], [o_.b])
                    dma("sync", x_dst[tsl, :], o_[:], r=[o_.b], w=[b_xdst])
                S.barrier()

        setup_tables()
        for l in range(n_layers):
            phase_A(l, x_in if l == 0 else xres)
            import os
            if not os.environ.get("SKIP_LG"):
                phase_lru(l)
                if stop == "lru":
                    break
                phase_gla(l)
                if stop == "gla":
                    break
            phase_nsa(l)
            if stop is not None and stop.startswith("nsa"):
                break
            last = (l == n_layers - 1)
            phase_tail(l, x_in if l == 0 else xres, y_out if last else xres, Buf() if l == 0 else b_xres, Buf() if last else b_xres)
        S.barrier()
        S.emit()
    return nc, consts


_CACHE = {}


def kernel(**inputs):
    if "nc" not in _CACHE:
        _CACHE["nc"] = build()
    nc, consts = _CACHE["nc"]
    x = np.ascontiguousarray(np.asarray(inputs["x"], dtype=np.float32))
    shared = {k: np.ascontiguousarray(np.asarray(v, dtype=np.float32)) for k, v in inputs.items() if k != "x"}
    for k, v in consts.items():
        shared["c_" + k] = v
    in_maps = [dict(shared, x=x[i]) for i in range(8)]
    res = run_bass_kernel_spmd(nc, in_maps, core_ids=list(range(8)))
    return np.stack([np.asarray(r["out"], dtype=np.float32) for r in res.results], axis=0)
```

```python
import os
import numpy as np
from contextlib import ExitStack
import concourse.bass as bass
import concourse.mybir as mybir
from concourse.bass_utils import run_bass_kernel_spmd

F32 = mybir.dt.float32
BF16 = mybir.dt.bfloat16
ALU = mybir.AluOpType
AF = mybir.ActivationFunctionType
AX = mybir.AxisListType

SEQ = 2048
D = 1024
NT = 16
DEPTH = 2
EPS = 1e-6
IN_W = 10816
C_LRUX, C_LRUG, C_Q, C_KV, C_GATE, C_GQ, C_GK, C_GV, C_GOG, C_GLR, C_MG = 0, 1024, 2048, 3072, 4608, 4656, 5168, 5680, 6704, 7728, 7744
DFF = 2816
NEG = -30000.0


class Buf:
    __slots__ = ("name", "w", "r", "excl")

    def __init__(self, name="", excl=False):
        self.name = name
        self.w = None
        self.r = []
        self.excl = excl


class Sched:
    ENG = ("sync", "scalar", "vector", "gpsimd", "tensor")
    DMAQ = ("sync", "gpsimd", "scalar")

    def __init__(self, nc, es, n_dma_sems=12):
        self.nc = nc
        self.q = {e: [] for e in self.ENG}
        self.cnt = {e: 0 for e in self.ENG}
        self.sems = []
        self.esem = {}
        for e in self.ENG:
            self.esem[e] = len(self.sems)
            self.sems.append(es.enter_context(nc.semaphore("s_" + e)))
        self.known = {e: {} for e in self.ENG}
        self.dpool = {}
        self.dcnt = {}
        self.dlast = {}
        for qn in self.DMAQ:
            self.dpool[qn] = []
            for i in range(n_dma_sems):
                self.dpool[qn].append(len(self.sems))
                self.sems.append(es.enter_context(nc.semaphore(f"d_{qn}_{i}")))
            self.dcnt[qn] = 0
        self.K = n_dma_sems

    def _waits(self, eng, r, w):
        waits = {}
        kn = self.known[eng]
        own_pe = self.esem["tensor"] if eng == "tensor" else -1

        def need(kv):
            k, v = kv
            if k == own_pe:
                return
            if kn.get(k, 0) < v and waits.get(k, 0) < v:
                waits[k] = v

        own = self.esem.get(eng, -2)
        for b in r:
            if b.w is not None:
                need(b.w)
            if b.excl:
                for x in b.r:
                    if x[0] != own:
                        need(x)
        for b in w:
            if b.w is not None:
                need(b.w)
            for x in b.r:
                need(x)
        for k, v in waits.items():
            kn[k] = v
        return list(waits.items())

    _rec = None

    def record(self, fn):
        self._rec = []
        fn()
        r, self._rec = self._rec, None
        return r

    def replay(self, lists):
        idx = [0] * len(lists)
        live = True
        while live:
            live = False
            for j, lst in enumerate(lists):
                if idx[j] < len(lst):
                    kind, args, kw = lst[idx[j]]
                    idx[j] += 1
                    live = True
                    if kind == "op":
                        self.op(*args)
                    else:
                        self.dma(*args, **kw)

    def op(self, eng, fn, r=(), w=()):
        if self._rec is not None:
            self._rec.append(("op", (eng, fn, list(r), list(w)), {}))
            return
        waits = self._waits(eng, r, w)
        self.cnt[eng] += 1
        seq = self.cnt[eng]
        k = self.esem[eng]
        self.q[eng].append((waits, fn, (k, 1)))
        for b in w:
            b.w = (k, seq)
            b.r = []
        for b in r:
            if b not in w:
                b.r.append((k, seq))
                if len(b.r) > 24:
                    b.r = b.r[-24:] if False else self._compact(b.r)

    @staticmethod
    def _compact(lst):
        d = {}
        for k, v in lst:
            if d.get(k, 0) < v:
                d[k] = v
        return list(d.items())

    def dma(self, qn, out, in_, r=(), w=(), **kw):
        if self._rec is not None:
            self._rec.append(("dma", (qn, out, in_, list(r), list(w)), kw))
            return
        waits = self._waits(qn, r, w)
        i = self.dcnt[qn]
        self.dcnt[qn] += 1
        k = self.dpool[qn][i % self.K]
        val = 16 * (i // self.K + 1)
        if val > 16 and self.known[qn].get(k, 0) < val - 16:
            waits.append((k, val - 16))
            self.known[qn][k] = val - 16
        self.dlast[k] = val
        self.q[qn].append((waits, lambda e: e.dma_start(out=out, in_=in_, **kw), (k, 16)))
        for b in w:
            b.w = (k, val)
            b.r = []
        for b in r:
            if b not in w:
                b.r.append((k, val))
                if len(b.r) > 24:
                    b.r = self._compact(b.r)

    def pe_drain(self):
        k = self.esem["tensor"]
        if self.cnt["tensor"] > 0:
            self.q["tensor"].append(([(k, self.cnt["tensor"])], None, None))

    def barrier(self):
        tgt = [(self.esem[e], self.cnt[e]) for e in self.ENG if self.cnt[e] > 0]
        tgt += list(self.dlast.items())
        for e in self.ENG:
            waits = []
            for k, v in tgt:
                if e == "tensor" and k == self.esem["tensor"]:
                    continue
                if self.known[e].get(k, 0) < v:
                    waits.append((k, v))
                    self.known[e][k] = v
            if waits:
                self.q[e].append((waits, None, None))

    def emit(self):
        nc = self.nc
        with nc.Block() as block:
            for e in self.ENG:
                def body(eng, _e=e):
                    for waits, fn, inc in self.q[_e]:
                        for k, v in waits:
                            eng.wait_ge(self.sems[k], v)
                        if fn is not None:
                            ins = fn(eng)
                            ins.then_inc(self.sems[inc[0]], inc[1])
                getattr(block, e)(body)


def fap(a, dims):
    return bass.AP(a.tensor, a.offset, [list(a.ap[0])] + [list(d) for d in dims])


def _rel_bucket(d):
    d = np.asarray(d)
    n = np.maximum(d, 0)
    nf = np.maximum(n, 16).astype(np.float32)
    large = 16 + (np.log(nf / np.float32(16)) / np.float32(np.log(128 / 16)) * np.float32(16)).astype(np.int32)
    large = np.minimum(large, 31)
    return np.where(n < 16, n, large)


def host_consts():
    c = {}
    c["ident"] = np.eye(128, dtype=np.float32)
    c["antiid"] = np.eye(128, dtype=np.float32)[::-1].copy()
    aid127 = np.zeros((128, 128), np.float32)
    for i in range(127):
        aid127[i, 126 - i] = 1.0
    aid127[127, 127] = 1.0
    c["antiid127"] = aid127
    s = np.arange(128)
    c["triu"] = (s[:, None] <= s[None, :]).astype(np.float32)
    c["tril"] = (s[:, None] > s[None, :]).astype(np.float32)
    def oh(deltas, valid):
        m = np.zeros((33, len(deltas)), np.float32)
        b = _rel_bucket(deltas)
        for i, (dd, v) in enumerate(zip(deltas, valid)):
            if v:
                m[b[i], i] = 1.0
            else:
                m[32, i] = 1.0
        return m
    dc = np.arange(-2048, 2048)
    c["oh_c"] = oh(dc, dc >= 0)
    ds = np.arange(-512, 512)
    c["oh_s"] = oh(ds, ds >= 0)
    c["oh_w"] = oh(ds, (ds >= 0) & (ds < 256))
    cs = np.arange(127) * 16
    js = np.arange(32) * 64
    ov = np.clip(np.minimum(cs[:, None] + 32, js[None, :] + 64) - np.maximum(cs[:, None], js[None, :]), 0, None).astype(np.float32) / 32.0
    ovx = np.zeros((128, 33), np.float32)
    ovx[:127, :32] = ov
    ovx[:127, 32] = 1.0
    c["ovx"] = ovx
    pos = np.arange(SEQ)
    cur = pos // 64
    blk = np.arange(32)[None, :]
    cand = (blk >= 1) & (blk <= cur[:, None] - 2)
    forced = (blk == 0) | (blk == cur[:, None]) | (blk == cur[:, None] - 1)
    c["cand"] = cand.astype(np.float32).reshape(NT, 128, 32).transpose(1, 0, 2).copy()
    c["negc"] = ((cand.astype(np.float32) - 1.0) * 1e4).reshape(NT, 128, 32).transpose(1, 0, 2).copy()
    c["forced"] = forced.astype(np.float32).reshape(NT, 128, 32).transpose(1, 0, 2).copy()
    ex = np.zeros((128, NT, 128), np.float32)
    for kt in range(NT):
        for key in range(128):
            ex[2 * kt + key // 64, kt, key] = 1.0
    c["expand_near"] = ex.copy()
    ex[32:34] = 1.0
    c["expand"] = ex
    return c


CONST_SHAPES = None


def build(debug=False, n_layers=DEPTH, stop=None):
    nc = bass.Bass("TRN2", target_bir_lowering=False)
    consts = host_consts()
    din = {}

    def inp(name, shape, dt=F32):
        din[name] = nc.dram_tensor(name, list(shape), dt, kind="ExternalInput").ap()
        return din[name]

    x_in = inp("x", [SEQ, D])
    rel_table = inp("rel_table", [32, 16])
    norm_g = inp("norm_g", [DEPTH, 4, D])
    w_in = inp("w_in", [DEPTH, D, IN_W])
    conv_w = inp("conv_w", [DEPTH, 4, D])
    conv_b = inp("conv_b", [DEPTH, D])
    lru_wg = inp("lru_w_gates", [DEPTH, 2, 8, 128, 128])
    lru_bg = inp("lru_b_gates", [DEPTH, 2, D])
    lru_lam = inp("lru_lambda", [DEPTH, D])
    cmp_pos = inp("cmp_pos", [DEPTH, 2, 32, 64])
    cmp_w1 = inp("cmp_w1", [DEPTH, 2, 2048, 256])
    cmp_w2 = inp("cmp_w2", [DEPTH, 2, 256, 64])
    gla_wa2 = inp("gla_wa2", [DEPTH, 16, 512])
    gla_ba = inp("gla_ba", [DEPTH, 512])
    gla_norm = inp("gla_norm", [DEPTH, 256])
    w_branch = inp("w_branch", [DEPTH, 3, D, D])
    w_out = inp("w_out", [DEPTH, D, D])
    w_ffn_in = inp("w_ffn_in", [DEPTH, D, 2 * DFF])
    w_ffn_out = inp("w_ffn_out", [DEPTH, DFF, D])
    cin = {k: inp("c_" + k, v.shape) for k, v in consts.items()}

    okind = "ExternalOutput"
    y_out = nc.dram_tensor("out", [SEQ, D], F32, kind=okind).ap()
    skind = "ExternalOutput"
    xres = nc.dram_tensor("xres", [SEQ, D], F32, kind=skind).ap()
    xmid = nc.dram_tensor("xmid", [SEQ, D], F32, kind=skind).ap()
    ysc = nc.dram_tensor("ysc", [3, D, SEQ], BF16, kind=skind).ap()
    tc_d = nc.dram_tensor("tc_d", [2, 16, 4096], BF16, kind="Internal").ap()
    ts_d = nc.dram_tensor("ts_d", [2, 16, 1024], BF16, kind="Internal").ap()
    tw_d = nc.dram_tensor("tw_d", [2, 16, 1024], BF16, kind="Internal").ap()
    hsc = nc.dram_tensor("hsc", [128, 8 * SEQ], BF16, kind=skind).ap()
    b_hsc = Buf()
    b_xres, b_xmid, b_ysc, b_tabs = Buf(), Buf(), [Buf(), Buf(), Buf()], Buf()

    with ExitStack() as es:
        S = Sched(nc, es)
        es.enter_context(nc.allow_non_contiguous_dma(reason="small param loads"))

        def mm(out, lhsT, rhs, start, stop, r, w):
            S.op("tensor", lambda e: e.matmul(out, lhsT=lhsT, rhs=rhs, start=start, stop=stop), r, w)

        def trp(out, in_, ident, r, w):
            S.op("tensor", lambda e: e.transpose(out, in_, ident), r, w)

        def act(out, in_, func, r, w, **kw):
            S.op("scalar", lambda e: e.activation(out=out, in_=in_, func=func, **kw), r, w)

        def tt(eng, out, in0, in1, op, r, w):
            S.op(eng, lambda e: e.tensor_tensor(out=out, in0=in0, in1=in1, op=op), r, w)

        def tsc(eng, out, in0, s1, op0, r, w, s2=None, op1=None):
            if op1 is None:
                S.op(eng, lambda e: e.tensor_scalar(out=out, in0=in0, scalar1=s1, scalar2=None, op0=op0), r, w)
            else:
                S.op(eng, lambda e: e.tensor_scalar(out=out, in0=in0, scalar1=s1, scalar2=s2, op0=op0, op1=op1), r, w)

        def stt(out, in0, scalar, in1, op0, op1, r, w):
            S.op("vector", lambda e: e.scalar_tensor_tensor(out=out, in0=in0, scalar=scalar, in1=in1, op0=op0, op1=op1), r, w)

        def cp(eng, out, in_, r, w):
            if eng == "scalar":
                S.op("scalar", lambda e: e.copy(out=out, in_=in_), r, w)
            else:
                S.op(eng, lambda e: e.tensor_copy(out=out, in_=in_), r, w)

        def recip(out, in_, r, w):
            S.op("vector", lambda e: e.reciprocal(out=out, in_=in_), r, w)

        def memset(eng, ap, val, w):
            S.op(eng, lambda e: e.memset(ap, val), (), w)

        def dma(q, out, in_, r=(), w=()):
            S.dma(q, out, in_, r, w)

        class T:
            _n = [0]

            def __init__(self, stack, name, shape, dt, psum=False):
                T._n[0] += 1
                name = f"{name}_{T._n[0]}"
                self.t = stack.enter_context((nc.psum_tensor if psum else nc.sbuf_tensor)(name, list(shape), dt))
                self.b = Buf(name)

            def __getitem__(self, idx):
                return self.t[idx]

        PA = T(es, "PA", [128, 2048], F32, psum=True)
        PB = T(es, "PB", [128, 2048], F32, psum=True)
        pbank = []
        for i in range(8):
            src = PA if i < 4 else PB
            pbank.append((src.t[:, (i % 4) * 512:(i % 4 + 1) * 512], Buf(f"bank{i}", excl=True)))
        PAb = PA.t.bitcast(BF16)
        PBb = PB.t.bitcast(BF16)

        def bank_bf(i):
            src = PAb if i < 4 else PBb
            return src[:, (i % 4) * 1024:(i % 4 + 1) * 1024]

        ident_f = T(es, "ident_f", [128, 128], F32)
        ident_b = T(es, "ident_b", [128, 128], BF16)
        dma("sync", ident_f[:], cin["ident"], w=[ident_f.b])
        cp("vector", ident_b[:], ident_f[:], [ident_f.b], [ident_b.b])

        hT = T(es, "hT", [128, 8, SEQ], BF16)
        hT_b = [Buf(f"hT{t}") for t in range(NT)]

        def load_gain(ph, l, i):
            gt = T(ph, f"gain{i}", [128, D], F32)
            src = norm_g[l, i:i + 1, :]
            dma("sync", gt[:], bass.AP(src.tensor, src.offset, [[0, 128], [1, D]]), w=[gt.b])
            return gt

        def norm_transpose_tile(ph, t, xt_ap, xt_buf, gt, ss, rs, hb, junk, pbi):
            act(junk[:], xt_ap, AF.Square, [xt_buf], [junk.b, ss.b], accum_out=ss[:, t:t + 1])
            act(rs[:, t:t + 1], ss[:, t:t + 1], AF.Sqrt, [ss.b], [rs.b], scale=1.0 / D, bias=EPS)
            recip(rs[:, t:t + 1], rs[:, t:t + 1], [rs.b], [rs.b])
            stt(hb[:], xt_ap, rs[:, t:t + 1], gt[:], ALU.mult, ALU.mult, [xt_buf, rs.b, gt.b], [hb.b])
            pv, pbuf = bank_bf(pbi), pbank[pbi][1]
            for kc in range(8):
                trp(pv[:, kc * 128:(kc + 1) * 128], hb[:, kc * 128:(kc + 1) * 128], ident_b[:], [hb.b, ident_b.b], [pbuf])
            cp("scalar", hT[:, :, t * 128:(t + 1) * 128], pv.rearrange("p (k s) -> p k s", k=8), [pbuf], [hT_b[t]])

        def load_slab(dst_ap, w2d, c0, ncols, wbuf, nk=8):
            src = w2d[:, c0:c0 + ncols].rearrange("(kc p) n -> p kc n", p=128)
            dma("gpsimd", dst_ap, src, w=[wbuf])

        def proj_fm(wslab, wbuf, col_off, M, rhs_tile, rhs_bufs, out_banks, nk=8, sc_list=(0, 1, 2, 3)):
            for i, sc in enumerate(sc_list):
                pa, pb_ = out_banks[i]
                for kc in range(nk):
                    mm(pa[0:M, :], wslab[:, kc, col_off:col_off + M], rhs_tile[:, kc, sc * 512:(sc + 1) * 512],
                       kc == 0, kc == nk - 1, [wbuf] + rhs_bufs[sc * 4:(sc + 1) * 4], [pb_])

        def phase_A(l, x_src):
            with ExitStack() as ph:
                xt = [T(ph, f"xtA{i}", [128, D], F32) for i in range(2)]
                hb = [T(ph, f"hbA{i}", [128, D], BF16) for i in range(2)]
                junk = T(ph, "junkA", [128, D], BF16)
                ss = T(ph, "ssA", [128, NT], F32)
                rs = T(ph, "rsA", [128, NT], F32)
                g0 = load_gain(ph, l, 0)
                for t in range(NT):
                    dma("sync", xt[t % 2][:], x_src[t * 128:(t + 1) * 128, :], r=[b_xres], w=[xt[t % 2].b])
                    norm_transpose_tile(ph, t, xt[t % 2][:], xt[t % 2].b, g0, ss, rs, hb[t % 2], junk, t % 2)
                S.barrier()

        def phase_lru(l):
            with ExitStack() as ph:
                prow = T(ph, "prow", [8, D], F32)
                lpT = T(ph, "lpT", [128, 8, 8], F32)
                sp = T(ph, "lru_sp", [128, 8, 6], F32)
                wg = T(ph, "lru_wg", [128, 2, 8, 128], BF16)
                slab = [T(ph, f"lslab{i}", [128, 8, 2, 128], BF16) for i in range(2)]
                XA = [T(ph, f"XA{i}", [128, SEQ + 4], F32) for i in range(2)]
                XC = [T(ph, f"XC{i}", [128, SEQ], F32) for i in range(2)]
                XCB = [T(ph, f"XCB{i}", [128, SEQ], BF16) for i in range(2)]
                R = [T(ph, f"R{i}", [128, SEQ], F32) for i in range(2)]
                A = [T(ph, f"A{i}", [128, SEQ], F32) for i in range(2)]
                I = [T(ph, f"I{i}", [128, SEQ], F32) for i in range(2)]
                H = [T(ph, f"H{i}", [128, SEQ], F32) for i in range(2)]
                GA = [T(ph, f"GA{i}", [128, SEQ], F32) for i in range(2)]
                G = [T(ph, f"G{i}", [128, SEQ], F32) for i in range(2)]
                YA = [T(ph, f"YA{i}", [128, SEQ], BF16) for i in range(2)]
                if os.environ.get("SBUF_DBG"):
                    print("LRU sbuf remaining", nc.sbuf_bytes_remaining)
                for k in range(4):
                    dma("sync", prow[k:k + 1, :], conv_w[l, k:k + 1, :], w=[prow.b])
                dma("sync", prow[4:5, :], conv_b[l:l + 1, :], w=[prow.b])
                dma("sync", prow[5:7, :], lru_bg[l], w=[prow.b])
                dma("sync", prow[7:8, :], lru_lam[l:l + 1, :], w=[prow.b])
                pv, pbuf = pbank[7]
                for c in range(8):
                    trp(pv[:, c * 8:(c + 1) * 8], prow[0:8, c * 128:(c + 1) * 128], ident_f[0:8, 0:8], [prow.b, ident_f.b], [pbuf])
                cp("vector", lpT[:], pv[:, 0:64].rearrange("p (c k) -> p c k", c=8), [pbuf], [lpT.b])
                xs, ln1, ser, msk, nsp8, nsp16 = (sp[:, :, i] for i in range(6))
                act(xs, lpT[:, :, 7], AF.Exp, [lpT.b], [sp.b], scale=-1.0)
                act(ln1, xs, AF.Ln, [sp.b], [sp.b], bias=1.0)
                tsc("vector", ser, xs, -0.25, ALU.mult, [sp.b], [sp.b], 1.0 / 3.0, ALU.add)
                tt("vector", ser, ser, xs, ALU.mult, [sp.b], [sp.b])
                tsc("vector", ser, ser, -1.0, ALU.mult, [sp.b], [sp.b], 0.5, ALU.add)
                tt("vector", ser, ser, xs, ALU.mult, [sp.b], [sp.b])
                tsc("vector", ser, ser, -1.0, ALU.mult, [sp.b], [sp.b], 1.0, ALU.add)
                tt("vector", ser, ser, xs, ALU.mult, [sp.b], [sp.b])
                tsc("vector", msk, xs, 0.03, ALU.is_lt, [sp.b], [sp.b])
                tt("vector", ser, ser, ln1, ALU.subtract, [sp.b], [sp.b])
                tt("vector", ser, ser, msk, ALU.mult, [sp.b], [sp.b])
                tt("vector", ser, ser, ln1, ALU.add, [sp.b], [sp.b])
                tsc("vector", nsp8, ser, -8.0, ALU.mult, [sp.b], [sp.b])
                tsc("vector", nsp16, ser, -16.0, ALU.mult, [sp.b], [sp.b])
                dma("gpsimd", wg[:], lru_wg[l].rearrange("k n c e -> c k n e"), w=[wg.b])
                for p_ in range(2):
                    memset("vector", XA[p_][:, 0:3], 0.0, [XA[p_].b])
                w2d = w_in[l]

                def lru_A(c):
                    p = c % 2
                    xa, xc, xcb, r_, i_, ga = XA[p], XC[p], XCB[p], R[p], I[p], GA[p]
                    sl = slab[p]
                    dma("gpsimd", sl[:, :, 0, :], w2d[:, C_LRUX + c * 128:C_LRUX + (c + 1) * 128].rearrange("(kc p) n -> p kc n", p=128), w=[sl.b])
                    dma("gpsimd", sl[:, :, 1, :], w2d[:, C_LRUG + c * 128:C_LRUG + (c + 1) * 128].rearrange("(kc p) n -> p kc n", p=128), w=[sl.b])
                    slv = sl.t.rearrange("p k a n -> p k (a n)")
                    proj_fm(slv, sl.b, 0, 128, hT.t, hT_b, pbank[0:4])
                    cp("scalar", xa[:, 3:3 + SEQ], PA[:, :], [pbank[i][1] for i in range(4)], [xa.b])
                    cw = lambda k: lpT[:, c, k:k + 1]
                    act(xc[:], xa[:, 3:3 + SEQ], AF.Identity, [xa.b, lpT.b], [xc.b], scale=cw(3), bias=cw(4))
                    for k in range(3):
                        stt(xc[:], xa[:, k:k + SEQ], cw(k), xc[:], ALU.mult, ALU.add, [xa.b, lpT.b, xc.b], [xc.b])
                    cp("gpsimd", xcb[:], xc[:], [xc.b], [xcb.b])
                    for gk in range(2):
                        banks = pbank[4:8] if gk == 0 else pbank[0:4]
                        for sc in range(4):
                            mm(banks[sc][0], wg[:, gk, c, :], xcb[:, sc * 512:(sc + 1) * 512], True, True, [wg.b, xcb.b], [banks[sc][1]])
                    act(r_[:], PB[:, :], AF.Sigmoid, [pbank[i][1] for i in range(4, 8)], [r_.b], bias=lpT[:, c, 5:6])
                    act(i_[:], PA[:, :], AF.Sigmoid, [pbank[i][1] for i in range(4)], [i_.b], bias=lpT[:, c, 6:7])
                    proj_fm(slv, sl.b, 128, 128, hT.t, hT_b, pbank[4:8])
                    cp("scalar", ga[:], PB[:, :], [pbank[i][1] for i in range(4, 8)], [ga.b])

                def lru_B(c):
                    p = c % 2
                    xc, r_, a_, i_, h_, ga, g_ = XC[p], R[p], A[p], I[p], H[p], GA[p], G[p]
                    act(a_[:], r_[:], AF.Exp, [r_.b, sp.b], [a_.b], scale=sp[:, c, 4:5])
                    act(r_[:], r_[:], AF.Exp, [r_.b, sp.b], [r_.b], scale=sp[:, c, 5:6])
                    act(r_[:], r_[:], AF.Identity, [r_.b], [r_.b], scale=-1.0, bias=1.0)
                    act(r_[:], r_[:], AF.Sqrt, [r_.b], [r_.b])
                    tt("gpsimd", i_[:], i_[:], xc[:], ALU.mult, [i_.b, xc.b], [i_.b])
                    tt("gpsimd", i_[:], i_[:], r_[:], ALU.mult, [i_.b, r_.b], [i_.b])
                    S.op("vector", lambda e, h_=h_, a_=a_, i_=i_: e.tensor_tensor_scan(out=h_[:], data0=a_[:], data1=i_[:], initial=0.0, op0=ALU.mult, op1=ALU.add),
                         [a_.b, i_.b], [h_.b])
                    act(g_[:], ga[:], AF.Square, [ga.b], [g_.b])
                    tsc("vector", g_[:], g_[:], 0.044715, ALU.mult, [g_.b], [g_.b], 1.0, ALU.add)
                    tt("gpsimd", g_[:], g_[:], ga[:], ALU.mult, [g_.b, ga.b], [g_.b])
                    act(g_[:], g_[:], AF.Sigmoid, [g_.b], [g_.b], scale=1.5957691216057308)
                    tt("gpsimd", g_[:], g_[:], ga[:], ALU.mult, [g_.b, ga.b], [g_.b])
                    ya = YA[p]
                    tt("vector", ya[:], g_[:], h_[:], ALU.mult, [g_.b, h_.b], [ya.b])
                    dma("sync", ysc[0, c * 128:(c + 1) * 128, :], ya[:], r=[ya.b], w=[b_ysc[0]])

                lru_A(0)
                for c in range(8):
                    lists = [S.record(lambda: lru_B(c))]
                    if c + 1 < 8:
                        lists.append(S.record(lambda: lru_A(c + 1)))
                    S.replay(lists)
                S.barrier()

        def phase_gla(l):
            w2d = w_in[l]
            with ExitStack() as ph:
                qT = T(ph, "gqT", [128, 4, SEQ], F32)
                kT = T(ph, "gkT", [128, 4, SEQ], F32)
                lrT = T(ph, "lrT", [32, SEQ], F32)
                wa2x = T(ph, "wa2x", [32, 512], F32)
                wres = T(ph, "gwres", [128, 8, 2560], BF16)
                gnb = T(ph, "gnb", [128, 4, 256], F32)
                st_f = T(ph, "st_f", [128, 4, 256], F32)
                st_b = T(ph, "st_b", [128, 4, 256], BF16)
                cm4 = T(ph, "cm4", [128, 4, 128], F32)
                triu = T(ph, "triu", [128, 128], F32)
                tril = T(ph, "tril", [128, 128], F32)
                dma("sync", triu[:], cin["triu"], w=[triu.b])
                dma("sync", tril[:], cin["tril"], w=[tril.b])
                for hh in range(4):
                    dma("sync", cm4[:, hh, :], cin["triu"], w=[cm4.b])
                    src = gla_norm[l:l + 1, :]
                    dma("sync", gnb[:, hh, :], bass.AP(src.tensor, src.offset, [[0, 128], [1, 256]]), w=[gnb.b])
                memset("vector", wa2x[:], 0.0, [wa2x.b])
                memset("vector", lrT[:], 1.0, [lrT.b])
                dma("sync", wa2x[0:16, :], gla_wa2[l], w=[wa2x.b])
                dma("sync", wa2x[16:17, :], gla_ba[l:l + 1, :], w=[wa2x.b])
                for i, c0 in enumerate((C_GK, C_GV, C_GV + 512, C_GOG, C_GOG + 512)):
                    load_slab(wres[:, :, i * 512:(i + 1) * 512], w2d, c0, 512, wres.b)
                with ExitStack() as ph2:
                    slab = [T(ph2, f"gslab{i}", [128, 8, 512], BF16) for i in range(2)]
                    lslab = T(ph2, "glslab", [128, 8, 16], BF16)
                    load_slab(slab[0][:], w2d, C_GQ, 512, slab[0].b)
                    load_slab(slab[1][:], w2d, C_GK, 512, slab[1].b)
                    load_slab(lslab[:], w2d, C_GLR, 16, lslab.b)
                    for i in range(8):
                        banks = pbank[0:4] if i % 2 == 0 else pbank[4:8]
                        src = PA if i % 2 == 0 else PB
                        proj_fm(slab[i // 4].t, slab[i // 4].b, (i % 4) * 128, 128, hT.t, hT_b, banks)
                        dst = qT if i < 4 else kT
                        act(dst[:, i % 4, :], src[:, :], AF.Copy, [b for _, b in banks], [dst.b], scale=(128 ** -0.5 if i < 4 else 1.0))
                    proj_fm(lslab.t, lslab.b, 0, 16, hT.t, hT_b, pbank[0:4])
                    cp("vector", lrT[0:16, :], PA[0:16, :], [b for _, b in pbank[0:4]], [lrT.b])
                    S.barrier()
                sp_t = T(ph, "g_sp", [128, 512], F32)
                E1 = [T(ph, f"g_E1{i}", [128, 512], F32) for i in range(2)]
                E2 = T(ph, "g_E2", [128, 512], F32)
                Erb = T(ph, "g_Erb", [128, 512], F32)
                qtb = [T(ph, f"g_qtb{i}", [128, 4, 128], BF16) for i in range(2)]
                ktb = T(ph, "g_ktb", [128, 4, 128], BF16)
                kend = [T(ph, f"g_kend{i}", [128, 512], BF16) for i in range(2)]
                v_bf = [T(ph, f"g_vbf{i}", [128, 1024], BF16) for i in range(2)]
                sg = [T(ph, f"g_sg{i}", [128, 1024], F32) for i in range(2)]
                attm = [T(ph, f"g_attm{i}", [128, 4, 128], BF16) for i in range(2)]
                on = T(ph, "g_on", [128, 1024], F32)
                yc = [T(ph, f"g_yc{i}", [128, 1024], BF16) for i in range(2)]
                ycT = [T(ph, f"g_ycT{i}", [128, 8, 128], BF16) for i in range(2)]
                junk = T(ph, "g_junk", [128, 256], BF16)
                ssq = T(ph, "g_ssq", [128, 4], F32)
                rst = T(ph, "g_rst", [128, 4], F32)
                if os.environ.get("SBUF_DBG"):
                    print("GLA sbuf remaining", nc.sbuf_bytes_remaining)
                bk = lambda i: pbank[i][0]
                bb = lambda i: pbank[i][1]

                def gla_A(t):
                    p = t % 2
                    tsl = slice(t * 128, (t + 1) * 128)
                    e1, qb, ke, vb, sg_, am = E1[p], qtb[p], kend[p], v_bf[p], sg[p], attm[p]
                    mm(bk(0), lrT[0:17, tsl], wa2x[0:17, :], True, True, [lrT.b, wa2x.b], [bb(0)])
                    act(sp_t[:], bk(0), AF.Exp, [bb(0)], [sp_t.b], scale=-1.0)
                    act(sp_t[:], sp_t[:], AF.Ln, [sp_t.b], [sp_t.b], bias=1.0)
                    for hh in range(4):
                        mm(bk(1)[:, hh * 128:(hh + 1) * 128], sp_t[:, hh * 128:(hh + 1) * 128], triu[:], True, True, [sp_t.b, triu.b], [bb(1)])
                    mm(bk(2), tril[:], sp_t[:], True, True, [tril.b, sp_t.b], [bb(2)])
                    act(e1[:], bk(1), AF.Exp, [bb(1)], [e1.b], scale=-1.0 / 16.0)
                    act(E2[:], bk(1), AF.Exp, [bb(1)], [E2.b], scale=1.0 / 16.0)
                    act(Erb[:], bk(2), AF.Exp, [bb(2)], [Erb.b], scale=-1.0 / 16.0)
                    tt("vector", qb[:], qT[:, :, tsl], e1.t.rearrange("p (h s) -> p h s", h=4), ALU.mult, [qT.b, e1.b], [qb.b])
                    tt("gpsimd", ktb[:], kT[:, :, tsl], E2.t.rearrange("p (h s) -> p h s", h=4), ALU.mult, [kT.b, E2.b], [ktb.b])
                    for kc in range(8):
                        mm(bk(3), hT[:, kc, tsl], wres[:, kc, 0:512], kc == 0, kc == 7, [hT_b[t], wres.b], [bb(3)])
                    tt("vector", ke[:], bk(3), Erb[:], ALU.mult, [bb(3), Erb.b], [ke.b])
                    for half in range(2):
                        for kc in range(8):
                            mm(bk(half), hT[:, kc, tsl], wres[:, kc, 512 + half * 512:1024 + half * 512], kc == 0, kc == 7, [hT_b[t], wres.b], [bb(half)])
                    cp("scalar", vb[:], PA[:, 0:1024], [bb(0), bb(1)], [vb.b])
                    for half in range(2):
                        for kc in range(8):
                            mm(bk(2 + half), hT[:, kc, tsl], wres[:, kc, 1536 + half * 512:2048 + half * 512], kc == 0, kc == 7, [hT_b[t], wres.b], [bb(2 + half)])
                    act(sg_[:], PA[:, 1024:2048], AF.Silu, [bb(2), bb(3)], [sg_.b])
                    tt("gpsimd", sg_[:], sg_[:], gnb.t.rearrange("p h e -> p (h e)"), ALU.mult, [sg_.b, gnb.b], [sg_.b])
                    for hh in range(4):
                        mm(bk(0)[:, hh * 128:(hh + 1) * 128], ktb[:, hh, :], qb[:, hh, :], True, True, [ktb.b, qb.b], [bb(0)])
                    tt("vector", am[:], bk(0).rearrange("p (h s) -> p h s", h=4), cm4[:], ALU.mult, [bb(0), cm4.b], [am.b])

                def gla_B(t):
                    p = t % 2
                    tsl = slice(t * 128, (t + 1) * 128)
                    e1, qb, ke, vb, sg_, am = E1[p], qtb[p], kend[p], v_bf[p], sg[p], attm[p]
                    for hh in range(4):
                        ob = 4 + hh // 2
                        oap = bk(ob)[:, (hh % 2) * 256:(hh % 2 + 1) * 256]
                        mm(oap, am[:, hh, :], vb[:, hh * 256:(hh + 1) * 256], hh % 2 == 0, t == 0 and hh % 2 == 1, [am.b, vb.b], [bb(ob)])
                        if t > 0:
                            mm(oap, qb[:, hh, :], st_b[:, hh, :], False, hh % 2 == 1, [qb.b, st_b.b], [bb(ob)])
                    for hh in range(4):
                        kb_ = 6 + hh // 2
                        mm(bk(kb_)[:, (hh % 2) * 256:(hh % 2 + 1) * 256], ke[:, hh * 128:(hh + 1) * 128], vb[:, hh * 256:(hh + 1) * 256],
                           hh % 2 == 0, hh % 2 == 1, [ke.b, vb.b], [bb(kb_)])
                    for hh in range(4):
                        kvp = bk(6 + hh // 2)[:, (hh % 2) * 256:(hh % 2 + 1) * 256]
                        if t == 0:
                            cp("vector", st_f[:, hh, :], kvp, [bb(6 + hh // 2)], [st_f.b])
                        else:
                            dec = e1[:, hh * 128 + 127:hh * 128 + 128]
                            stt(st_f[:, hh, :], st_f[:, hh, :], dec, kvp, ALU.mult, ALU.add, [st_f.b, e1.b, bb(6 + hh // 2)], [st_f.b])
                    cp("gpsimd", st_b[:], st_f[:], [st_f.b], [st_b.b])
                    for hh in range(4):
                        oap = bk(4 + hh // 2)[:, (hh % 2) * 256:(hh % 2 + 1) * 256]
                        act(junk[:], oap, AF.Square, [bb(4 + hh // 2)], [junk.b, ssq.b], accum_out=ssq[:, hh:hh + 1])
                    act(rst[:], ssq[:], AF.Sqrt, [ssq.b], [rst.b], scale=1.0 / 256.0, bias=EPS)
                    recip(rst[:], rst[:], [rst.b], [rst.b])
                    tt("vector", on.t.rearrange("p (h e) -> p h e", h=4), PB[:, 0:1024].rearrange("p (h e) -> p h e", h=4),
                       fap(rst[:], [[1, 4], [0, 256]]), ALU.mult, [bb(4), bb(5), rst.b], [on.b])
                    y = yc[p]
                    tt("gpsimd", y[:], on[:], sg_[:], ALU.mult, [on.b, sg_.b], [y.b])
                    pv = bank_bf(7)
                    for c in range(8):
                        trp(pv[:, c * 128:(c + 1) * 128], y[:, c * 128:(c + 1) * 128], ident_b[:], [y.b, ident_b.b], [bb(7)])
                    yT = ycT[p]
                    cp("scalar", yT[:], pv.rearrange("p (k s) -> p k s", k=8), [bb(7)], [yT.b])
                    dma("sync", ysc[2, :, tsl].rearrange("(c p) s -> p c s", p=128), yT[:], r=[yT.b], w=[b_ysc[2]])

                gla_A(0)
                for t in range(NT):
                    lists = [S.record(lambda: gla_B(t))]
                    if t + 1 < NT:
                        lists.append(S.record(lambda: gla_A(t + 1)))
                    S.replay(lists)
                S.barrier()

        def setup_tables():
            with ExitStack() as ph:
                tblx = T(ph, "tblx", [33, 16], F32)
                memset("vector", tblx[:], NEG, [tblx.b])
                dma("sync", tblx[0:32, :], rel_table, w=[tblx.b])
                for name, dst, n in (("oh_c", tc_d, 4096), ("oh_s", ts_d, 1024), ("oh_w", tw_d, 1024)):
                    oh = T(ph, "t_" + name, [33, n], F32)
                    thi = T(ph, "thi_" + name, [16, n], BF16)
                    tlo = T(ph, "tlo_" + name, [16, n], BF16)
                    dma("sync", oh[:], cin[name], w=[oh.b])
                    for ch in range(n // 512):
                        pa, pbuf = pbank[ch % 8]
                        mm(pa[0:16, :], tblx[0:33, 0:16], oh[0:33, ch * 512:(ch + 1) * 512], True, True, [tblx.b, oh.b], [pbuf])
                        cp("vector", thi[:, ch * 512:(ch + 1) * 512], pa[0:16, :], [pbuf], [thi.b])
                        tt("vector", tlo[:, ch * 512:(ch + 1) * 512], pa[0:16, :], thi[:, ch * 512:(ch + 1) * 512], ALU.subtract, [pbuf, thi.b], [tlo.b])
                    dma("sync", dst[0], thi[:], r=[thi.b], w=[b_tabs])
                    dma("sync", dst[1], tlo[:], r=[tlo.b], w=[b_tabs])
                S.barrier()

        def phase_nsa(l):
            w2d = w_in[l]
            bk = lambda i: pbank[i][0]
            bb = lambda i: pbank[i][1]
            with ExitStack() as ph:
                qT = T(ph, "nqT", [128, 8, SEQ], BF16)
                kS = T(ph, "nkS", [128, 4, SEQ], BF16)
                kW = T(ph, "nkW", [128, 4, SEQ], BF16)
                vS = T(ph, "nvS", [128, NT, 4, 66], BF16)
                vW = T(ph, "nvW", [128, NT, 4, 66], BF16)
                sgate = T(ph, "nsg", [128, NT, 48], F32)
                kcP = T(ph, "nkcP", [128, 2, 4, 128], BF16)
                vcx = T(ph, "nvcx", [128, 4, 98], BF16)
                hbt = T(ph, "nhbt", [128, 3, 4, 2, 512], BF16)
                NM = T(ph, "nNM", [128, 4, 2, 512], BF16)
                Jb = T(ph, "nJb", [128, 2, 128], BF16)
                expd = T(ph, "nexpd", [128, 2, NT, 128], BF16)
                cand = T(ph, "ncand", [128, NT, 32], F32)
                negc = T(ph, "nnegc", [128, NT, 32], F32)
                forced = T(ph, "nforced", [128, NT, 32], F32)
                dma("gpsimd", Jb[:, 0, :], cin["antiid"], w=[Jb.b])
                dma("gpsimd", Jb[:, 1, :], cin["antiid127"], w=[Jb.b])
                dma("gpsimd", expd[:, 0, :, :], cin["expand"], w=[expd.b])
                dma("gpsimd", expd[:, 1, :, :], cin["expand_near"], w=[expd.b])
                memset("vector", NM[:], 0.0, [NM.b])
                dma("sync", cand[:], cin["cand"], w=[cand.b])
                dma("sync", negc[:], cin["negc"], w=[negc.b])
                dma("sync", forced[:], cin["forced"], w=[forced.b])
                memset("vector", vcx[:], 0.0, [vcx.b])
                memset("vector", kcP[:], 0.0, [kcP.b])
                for g in range(4):
                    dma("gpsimd", vcx[:, g, 64:97], cin["ovx"], w=[vcx.b])
                for dl in range(3):
                    tsrc = tw_d if dl == 2 else ts_d
                    for g in range(4):
                        for hl in range(2):
                            for par in range(2):
                                for rp in range(2):
                                    h = 4 * g + 2 * rp + par
                                    a0 = tsrc[hl, h, 512 + dl * 128 - 127:512 + dl * 128 - 127 + 1]
                                    src = bass.AP(a0.tensor, a0.offset, [[1, 128], [1, 128]])
                                    dma("sync", hbt[:, dl, g, hl, par * 256 + rp * 128:par * 256 + rp * 128 + 128], src, r=[b_tabs], w=[hbt.b])
                for g in range(4):
                    for hl in range(2):
                        for par in range(2):
                            for rp in range(2):
                                h = 4 * g + 2 * rp + par
                                a0 = ts_d[hl, h, 640:641]
                                src = bass.AP(a0.tensor, a0.offset, [[0, 1], [0, 2], [1, 128]])
                                c0 = par * 256 + rp * 128
                                dma("sync", NM[32 + hl:33 + hl, g, :, c0:c0 + 128], src, r=[b_tabs], w=[NM.b])
                memset("vector", vS[:, :, :, 64:66], 1.0, [vS.b])
                memset("vector", vW[:, :, :, 64:66], 1.0, [vW.b])
                if stop == "nsa0":
                    S.barrier()
                    return
                with ExitStack() as ph2:
                    slab = [T(ph2, f"nslab{i}", [128, 8, 512], BF16) for i in range(2)]
                    wv = T(ph2, "nwv", [128, 8, 560], BF16)
                    for half in range(2):
                        sl = slab[half]
                        load_slab(sl[:], w2d, C_Q + half * 512, 512, sl.b)
                        for i in range(4):
                            c = half * 4 + i
                            banks = pbank[0:4] if c % 2 == 0 else pbank[4:8]
                            src = PA if c % 2 == 0 else PB
                            proj_fm(sl.t, sl.b, i * 128, 128, hT.t, hT_b, banks)
                            act(qT[:, c, :], src[:, :], AF.Copy, [b for _, b in banks], [qT.b], scale=0.125)
                    if stop == "nsa1a":
                        S.barrier()
                        return
                    n = 0
                    for idx, dst in ((2, kS), (4, kW)):
                        sl = slab[n % 2]
                        n += 1
                        for g in range(4):
                            c0 = C_KV + idx * 256 + g * 64
                            for dup in range(2):
                                dma("gpsimd", sl[:, :, g * 128 + dup * 64:g * 128 + dup * 64 + 64],
                                    w2d[:, c0:c0 + 64].rearrange("(kc p) n -> p kc n", p=128), w=[sl.b])
                        for g in range(4):
                            banks = pbank[0:4] if g % 2 == 0 else pbank[4:8]
                            src = PA if g % 2 == 0 else PB
                            proj_fm(sl.t, sl.b, g * 128, 128, hT.t, hT_b, banks)
                            cp("scalar" if g % 2 == 0 else "vector", dst[:, g, :], src[:, :], [b for _, b in banks], [dst.b])
                    if stop == "nsa1b":
                        S.barrier()
                        return
                    load_slab(wv[:, :, 0:256], w2d, C_KV + 3 * 256, 256, wv.b)
                    load_slab(wv[:, :, 256:512], w2d, C_KV + 5 * 256, 256, wv.b)
                    load_slab(wv[:, :, 512:560], w2d, C_GATE, 48, wv.b)
                    for t in range(NT):
                        tsl = slice(t * 128, (t + 1) * 128)
                        b0, b1 = (0, 1) if t % 2 == 0 else (2, 3)
                        for kc in range(8):
                            mm(bk(b0), hT[:, kc, tsl], wv[:, kc, 0:512], kc == 0, kc == 7, [hT_b[t], wv.b], [bb(b0)])
                        import os
                        SK = os.environ.get("NSA_SKIP", "")
                        if "g" not in SK:
                            for kc in range(8):
                                mm(bk(b1)[:, 0:48], hT[:, kc, tsl], wv[:, kc, 512:560], kc == 0, kc == 7, [hT_b[t], wv.b], [bb(b1)])
                        if "v" not in SK:
                            cp("vector", vS[:, t, :, 0:64], bk(b0)[:, 0:256].rearrange("p (g d) -> p g d", g=4), [bb(b0)], [vS.b])
                        if "w" not in SK:
                            cp("scalar", vW[:, t, :, 0:64], bk(b0)[:, 256:512].rearrange("p (g d) -> p g d", g=4), [bb(b0)], [vW.b])
                        if "g" not in SK:
                            act(sgate[:, t, :], bk(b1)[:, 0:48], AF.Sigmoid, [bb(b1)], [sgate.b])
                    S.barrier()
                if stop == "nsa1":
                    return
                with ExitStack() as ph2:
                    slab = [T(ph2, f"ncslab{i}", [128, 8, 256], BF16) for i in range(2)]
                    w1sb = T(ph2, "nw1", [128, 32, 256], BF16)
                    w2sb = T(ph2, "nw2", [128, 2, 128], BF16)
                    prow2 = T(ph2, "nprow2", [32, 128], F32)
                    posT = T(ph2, "nposT", [128, 32], F32)
                    XAB = [T(ph2, f"nXAB{i}", [128, SEQ], BF16) for i in range(2)]
                    gtmp = T(ph2, "ngtmp", [128, 2, 128], F32)
                    geluT = T(ph2, "ngeluT", [128, 2, 128], BF16)
                    for kv in range(2):
                        for dup in range(2):
                            dma("gpsimd", w1sb[dup * 64:(dup + 1) * 64, :, :], cmp_w1[l, kv].rearrange("(p d) j -> d p j", d=64), w=[w1sb.b])
                            dma("gpsimd", w2sb[:, :, dup * 64:(dup + 1) * 64], cmp_w2[l, kv].rearrange("(jc p) d -> p jc d", p=128), w=[w2sb.b])
                            dma("sync", prow2[:, dup * 64:(dup + 1) * 64], cmp_pos[l, kv], w=[prow2.b])
                        trp(bk(6)[:, 0:32], prow2[:, :], ident_f[0:32, 0:32], [prow2.b, ident_f.b], [bb(6)])
                        cp("vector", posT[:], bk(6)[:, 0:32], [bb(6)], [posT.b])
                        sl = slab[kv]
                        load_slab(sl[:, :, 0:256], w2d, C_KV + kv * 256, 256, sl.b)
                        for cc in range(2):
                            banks = pbank[0:4]
                            proj_fm(sl.t, sl.b, cc * 128, 128, hT.t, hT_b, banks)
                            for ab in range(2):
                                tt("vector" if ab == 0 else "gpsimd" if False else "vector", XAB[ab].t.rearrange("p (i q) -> p i q", q=16), PA.t.rearrange("p (i q) -> p i q", q=16),
                                   fap(posT[:, ab * 16:ab * 16 + 1], [[0, 128], [1, 16]]), ALU.add, [b for _, b in banks] + [posT.b], [XAB[ab].b])
                            for gg in range(2):
                                g = cc * 2 + gg
                                rows = slice(gg * 64, gg * 64 + 64)
                                hb_, hbb = bk(4 + 2 * gg), bb(4 + 2 * gg)
                                for jc in range(2):
                                    for p in range(32):
                                        srcT = XAB[0] if p < 16 else XAB[1]
                                        rhs = fap(srcT[rows, p:p + 1], [[16, 127]])
                                        mm(hb_[:, jc * 128:jc * 128 + 127], w1sb[rows, p, jc * 128:(jc + 1) * 128], rhs, p == 0, p == 31,
                                           [w1sb.b, srcT.b], [hbb])
                                hv = hb_[:, 0:256].rearrange("p (j i) -> p j i", j=2)[:, :, 0:127]
                                gv = gtmp[:, :, 0:127]
                                act(gv, hv, AF.Square, [hbb], [gtmp.b])
                                tsc("vector", gv, gv, 0.044715, ALU.mult, [gtmp.b], [gtmp.b], 1.0, ALU.add)
                                tt("vector", gv, gv, hv, ALU.mult, [gtmp.b, hbb], [gtmp.b])
                                act(gv, gv, AF.Sigmoid, [gtmp.b], [gtmp.b], scale=1.5957691216057308)
                                tt("vector", geluT[:, :, 0:127], gv, hv, ALU.mult, [gtmp.b, hbb], [geluT.b])
                                if kv == 0:
                                    for jc in range(2):
                                        mm(bk(5)[:, 0:127], w2sb[:, jc, :], geluT[:, jc, 0:127], jc == 0, jc == 1, [w2sb.b, geluT.b], [bb(5)])
                                    cp("scalar", kcP[0:64, 0, g, 0:127], bk(5)[0:64, 0:127], [bb(5)], [kcP.b])
                                    cp("scalar", kcP[64:128, 1, g, 0:127], bk(5)[64:128, 0:127], [bb(5)], [kcP.b])
                                else:
                                    for jc in range(2):
                                        mm(bk(5)[0:127, 0:64], geluT[:, jc, 0:127], w2sb[:, jc, 0:64], jc == 0, jc == 1, [w2sb.b, geluT.b], [bb(5)])
                                    cp("scalar", vcx[0:127, g, 0:64], bk(5)[0:127, 0:64], [bb(5)], [vcx.b])
                    S.barrier()
                if stop == "nsa2":
                    return
                dma("sync", hsc, hT.t.rearrange("p k s -> p (k s)"), r=hT_b, w=[b_hsc])
                kpad1 = Buf("kpad1")
                for base, KT_ in ((0, kS), (4, kW)):
                    cp("gpsimd", hT[64:128, base:base + 4, :], KT_[64:128, :, :], [KT_.b], hT_b + [kpad1])
                    memset("vector", hT[0:64, base:base + 4, :], 0.0, hT_b + [kpad1])
                    memset("vector", KT_[64:128, :, :], 0.0, [KT_.b])
                E = [T(ph, f"nE{i}", [128, 512], BF16) for i in range(4)]
                cb = [T(ph, f"ncb{i}", [128, 2, 512], BF16) for i in range(4)]
                for cbx in cb:
                    memset("vector", cbx[:], NEG, [cbx.b])
                ybt = [T(ph, f"nybt{i}", [128, 1024], BF16) for i in range(2)]
                ybT = [T(ph, f"nybT{i}", [128, 8, 128], BF16) for i in range(2)]
                sets = []
                for i in range(2):
                    sets.append(dict(
                        ybacc=T(ph, f"nybacc{i}", [128, 4, 64], F32), tmp1=T(ph, f"ntmp1{i}", [128, 4, 64], F32),
                        tmp2=T(ph, f"ntmp2{i}", [128, 4, 64], F32), impr=T(ph, f"nimpr{i}", [128, 4, 32], F32),
                        imp=T(ph, f"nimp{i}", [128, 32], F32), m8=T(ph, f"nm8{i}", [128, 8], F32),
                        sm=T(ph, f"nsm{i}", [128, 3, 4], F32), Us=(3, 6)[i], Uw=(4, 7)[i]))
                colb = lambda r: (r % 2) * 256 + (r // 2) * 128
                cnt_ = dict(l=0, e=0, kp=0, cb=0)
                tasks = []

                inflight = set()

                def next_Li(hold=False):
                    while True:
                        i = (0, 1, 5)[cnt_["l"] % 3]
                        cnt_["l"] += 1
                        if i not in inflight:
                            break
                    if hold:
                        inflight.add(i)
                    return i

                def next_L(hold=False):
                    return pbank[next_Li(hold)]

                def release_L(Lb):
                    for i in (0, 1, 5):
                        if pbank[i][1] is Lb:
                            inflight.discard(i)

                def next_E():
                    e_ = E[cnt_["e"] % 4]
                    cnt_["e"] += 1
                    return e_

                cb_dma = []
                PF = 3

                def mk_cmp(qt, g, st):
                    qsl = slice(qt * 128, (qt + 1) * 128)
                    ui = len(cb_dma)
                    cbt = cb[ui % 4]
                    box = {}

                    def issue():
                        for hl in range(2):
                            for rp in range(2):
                                a0 = tc_d[hl, 4 * g + 2 * rp, qt * 128 + 1:qt * 128 + 2]
                                src = bass.AP(a0.tensor, a0.offset, [[16, 127], [4096, 2], [1, 128]])
                                dst = cbt[0:127, hl, :].rearrange("p (a b s) -> p a b s", a=2, b=2)[:, :, rp, :]
                                dma("sync", dst, src, r=[b_tabs], w=[cbt.b])

                    cb_dma.append(issue)

                    def pre():
                        if ui + PF < len(cb_dma):
                            cb_dma[ui + PF]()

                    def s1():
                        L, Lb = next_L(hold=True)
                        box["L"] = (L, Lb)
                        for par in range(2):
                            mm(L[:, par * 256:(par + 1) * 256], kcP[:, par, g, :], qT[:, 2 * g:2 * g + 2, qsl], par == 0, False, [kcP.b, qT.b], [Lb])
                        for hl in range(2):
                            mm(L, Jb[:, 1, :], cbt[:, hl, :], False, hl == 1, [Jb.b, cbt.b], [Lb])

                    def s2():
                        L, Lb = box["L"]
                        release_L(Lb)
                        Ec = next_E()
                        act(Ec[:], L, AF.Exp, [Lb], [Ec.b])
                        Uc = bk(2)
                        for r in range(4):
                            mm(Uc[:, r * 98:(r + 1) * 98], Ec[:, colb(r):colb(r) + 128], vcx[:, g, 0:98], r == 0, r == 3, [Ec.b, vcx.b], [bb(2)])

                    def post():
                        Uc = bk(2)
                        sm, ybacc, impr, imp, m8 = st["sm"], st["ybacc"], st["impr"], st["imp"], st["m8"]
                        ucv = lambda a, b_: fap(Uc[:, a:a + 1], [[98, 4], [1, b_]])
                        rs4, wc = sm[:, 0, :], sm[:, 1, :]
                        tsc("vector", rs4, fap(Uc[:, 96:97], [[98, 4]]), 1e-30, ALU.max, [bb(2)], [sm.b])
                        recip(rs4, rs4, [sm.b], [sm.b])
                        tt("vector", wc, rs4, sgate[:, qt, 4 * g:4 * g + 4], ALU.mult, [sm.b, sgate.b], [sm.b])
                        tt("vector", ybacc[:], ucv(0, 64), fap(wc, [[1, 4], [0, 64]]), ALU.mult, [bb(2), sm.b], [ybacc.b])
                        tt("vector", impr[:], ucv(64, 32), fap(rs4, [[1, 4], [0, 32]]), ALU.mult, [bb(2), sm.b], [impr.b])
                        S.op("vector", lambda e: e.tensor_reduce(out=imp[:], in_=fap(impr[:, 0, 0:1], [[1, 32], [32, 4]]), axis=AX.X, op=ALU.add),
                             [impr.b], [imp.b])
                        tt("vector", imp[:], imp[:], cand[:, qt, :], ALU.mult, [imp.b, cand.b], [imp.b])
                        tt("vector", imp[:], imp[:], negc[:, qt, :], ALU.add, [imp.b, negc.b], [imp.b])
                        S.op("vector", lambda e: e.max(out=m8[:], in_=imp[:]), [imp.b], [m8.b])
                        tsc("vector", imp[:], imp[:], m8[:, 4:5], ALU.is_ge, [imp.b, m8.b], [imp.b])
                        tt("vector", imp[:], imp[:], forced[:, qt, :], ALU.max, [imp.b, forced.b], [imp.b])
                        tsc("vector", imp[:], imp[:], -1.0, ALU.add, [imp.b], [imp.b], -NEG, ALU.mult)

                    return dict(pre=pre, s1=s1, s2=s2, post=post, defer=None)

                def mk_tile(qt, g, st, br, kt, first, last, buf, hooks_pre, hooks_post, defer):
                    qsl = slice(qt * 128, (qt + 1) * 128)
                    ksl = slice(kt * 128, (kt + 1) * 128)
                    KT = kS if br == 0 else kW
                    VT = vS if br == 0 else vW
                    Ub = st["Us"] if br == 0 else st["Uw"]
                    dl = qt - kt
                    near = dl < (2 if br == 0 else 3)
                    box = {}

                    def pre():
                        for h_ in hooks_pre:
                            h_()

                    def s1():
                        L, Lb = next_L(hold=True)
                        box["L"] = (L, Lb)
                        mm(L[:, 0:256], KT[:, g, ksl], qT[:, 2 * g:2 * g + 2, qsl], True, False, [KT.b, qT.b], [Lb])
                        mm(L[:, 256:512], hT[:, (0 if br == 0 else 4) + g, ksl], qT[:, 2 * g:2 * g + 2, qsl], False, False, [kpad1, qT.b], [Lb])
                        if br == 0:
                            mm(L, expd[:, 1 if near else 0, kt, :], NM[:, g, buf, :], False, not near, [expd.b, NM.b], [Lb])
                        if near:
                            for hl in range(2):
                                mm(L, Jb[:, 0, :], hbt[:, dl, g, hl, :], False, hl == 1, [Jb.b, hbt.b], [Lb])

                    def s2():
                        L, Lb = box["L"]
                        release_L(Lb)
                        Et = next_E()
                        act(Et[:], L, AF.Exp, [Lb], [Et.b])
                        for r in range(4):
                            mm(bk(Ub)[:, r * 66:(r + 1) * 66], Et[:, colb(r):colb(r) + 128], VT[:, kt, g, 0:66],
                               first and r == 0, last and r == 3, [Et.b, VT.b], [bb(Ub)])

                    def post():
                        for h_ in hooks_post:
                            h_()

                    return dict(pre=pre, s1=s1, s2=s2, post=post, defer=defer)

                def mk_nm_hook(g, st, buf):
                    def hook():
                        imp = st["imp"]
                        M_, Mb = next_L()
                        trp(M_[0:32, 0:128], imp[:, :], ident_f[:, :], [imp.b, ident_f.b], [Mb])
                        cp("vector", NM[0:32, g, buf, :].rearrange("p (a s) -> p a s", a=4), fap(M_[0:32, 0:1], [[0, 4], [1, 128]]), [Mb], [NM.b])
                    return hook

                def mk_combine(qt, g, st, ybq):
                    def hook():
                        sm, ybacc, tmp1, tmp2 = st["sm"], st["ybacc"], st["tmp1"], st["tmp2"]
                        for br in range(2):
                            ub = st["Us"] if br == 0 else st["Uw"]
                            U = bk(ub)
                            rsb, wb_ = sm[:, 0, :], sm[:, 1 + br, :]
                            S.op("vector", lambda e, U=U, rsb=rsb: e.reciprocal(out=rsb, in_=fap(U[:, 64:65], [[66, 4]])), [bb(ub)], [sm.b])
                            tt("vector", wb_, rsb, sgate[:, qt, 16 * (br + 1) + 4 * g:16 * (br + 1) + 4 * g + 4], ALU.mult, [sm.b, sgate.b], [sm.b])
                            tgt = tmp1 if br == 0 else tmp2
                            tt("vector", tgt[:], fap(U[:, 0:1], [[66, 4], [1, 64]]), fap(wb_, [[1, 4], [0, 64]]), ALU.mult, [bb(ub), sm.b], [tgt.b])
                        tt("gpsimd", tmp1[:], tmp1[:], ybacc[:], ALU.add, [tmp1.b, ybacc.b], [tmp1.b])
                        tt("gpsimd", ybq[:, g * 256:(g + 1) * 256].rearrange("p (r d) -> p r d", r=4), tmp1[:], tmp2[:], ALU.add, [tmp1.b, tmp2.b], [ybq.b])
                    return hook

                def mk_ybout(qt, ybq):
                    def hook():
                        qsl = slice(qt * 128, (qt + 1) * 128)
                        li = next_Li()
                        pv = bank_bf(li)
                        for c in range(8):
                            trp(pv[:, c * 128:(c + 1) * 128], ybq[:, c * 128:(c + 1) * 128], ident_b[:], [ybq.b, ident_b.b], [bb(li)])
                        yT = ybT[qt % 2]
                        cp("scalar", yT[:], pv.rearrange("p (k s) -> p k s", k=8), [bb(li)], [yT.b])
                        dma("sync", ysc[1, :, qsl].rearrange("(c p) s -> p c s", p=128), yT[:], r=[yT.b], w=[b_ysc[1]])
                    return hook

                un = 0
                units = []
                for qt in range(1 if stop == 'nsa3' else NT):
                    ybq = ybt[qt % 2]
                    for g in range(4):
                        st = sets[un % 2]
                        buf = un % 2
                        un += 1
                        uc_ = [mk_cmp(qt, g, st)]
                        wk = list(range(max(0, qt - 2), qt + 1))
                        uw_ = [mk_tile(qt, g, st, 1, kt, kt == wk[0], kt == qt, buf, [], [], None) for kt in wk]
                        us_ = []
                        for kt in range(qt + 1):
                            hp = [mk_nm_hook(g, st, buf)] if kt == 0 else []
                            hq = [mk_combine(qt, g, st, ybq)] if kt == qt else []
                            df = mk_ybout(qt, ybq) if (kt == qt and g == 3) else None
                            us_.append(mk_tile(qt, g, st, 0, kt, kt == 0, kt == qt, buf, hp, hq, df))
                        units.append((uc_, uw_, us_))
                tasks += units[0][0] + units[0][1]
                for ui in range(len(units)):
                    if ui + 1 < len(units):
                        tasks += units[ui + 1][0]
                    tasks += units[ui][2]
                    if ui + 1 < len(units):
                        tasks += units[ui + 1][1]
                for ui_ in range(min(PF, len(cb_dma))):
                    cb_dma[ui_]()
                deferred = {}
                ntk = len(tasks)
                LA = 2
                for j in range(min(LA, ntk)):
                    tasks[j]["pre"]()
                    tasks[j]["s1"]()
                for i, tk in enumerate(tasks):
                    tk["s2"]()
                    tk["post"]()
                    if tk["defer"] is not None:
                        deferred.setdefault(i + 3, []).append(tk["defer"])
                    for fn in deferred.pop(i, []):
                        fn()
                    if i + LA < ntk:
                        tasks[i + LA]["pre"]()
                        tasks[i + LA]["s1"]()
                for k_ in sorted(deferred):
                    for fn in deferred[k_]:
                        fn()
                dma("sync", hT.t.rearrange("p k s -> p (k s)"), hsc, r=[b_hsc], w=hT_b + [kpad1])
                S.barrier()

        def phase_tail(l, x_src, x_dst, b_xsrc, b_xdst):
            bk = lambda i: pbank[i][0]
            bb = lambda i: pbank[i][1]
            w2d = w_in[l]
            with ExitStack() as ph:
                mrgb = T(ph, "mrgb", [128, 8, SEQ], BF16)
                with ExitStack() as ph2:
                    mrg = T(ph2, "mrg", [128, 8, SEQ], F32)
                    yT = T(ph2, "m_yT", [128, 8, SEQ], BF16)
                    yb_ = [Buf(f"m_yT{c}") for c in range(8)]
                    wbr = [T(ph2, f"m_wbr{i}", [128, 8, 256], BF16) for i in range(2)]
                    wmg = [T(ph2, f"m_wmg{i}", [128, 8, 256], BF16) for i in range(2)]
                    sig = [T(ph2, f"m_sig{i}", [128, 512], F32) for i in range(2)]
                    prod = [T(ph2, f"m_prod{i}", [128, 512], F32) for i in range(2)]
                    n = 0
                    bn = 0
                    for br in range(3):
                        for c in range(8):
                            dma("sync", yT[:, c, :], ysc[br, c * 128:(c + 1) * 128, :], r=[b_ysc[br]], w=[yb_[c]])
                        for oc2 in range(4):
                            wb, wm = wbr[oc2 % 2], wmg[oc2 % 2]
                            load_slab(wb[:], w_branch[l, br], oc2 * 256, 256, wb.b)
                            load_slab(wm[:], w2d, C_MG + br * 1024 + oc2 * 256, 256, wm.b)
                            for o in range(2):
                                oc = oc2 * 2 + o
                                for sc in range(4):
                                    ssl = slice(sc * 512, (sc + 1) * 512)
                                    bB, bG = (bn % 4) * 2, (bn % 4) * 2 + 1
                                    bn += 1
                                    for c in range(8):
                                        mm(bk(bB), wb[:, c, o * 128:(o + 1) * 128], yT[:, c, ssl], c == 0, c == 7, [wb.b, yb_[c]], [bb(bB)])
                                    for kc in range(8):
                                        mm(bk(bG), wm[:, kc, o * 128:(o + 1) * 128], hT[:, kc, ssl], kc == 0, kc == 7, [wm.b] + hT_b[sc * 4:(sc + 1) * 4], [bb(bG)])
                                    sg_, pr_ = sig[n % 2], prod[n % 2]
                                    n += 1
                                    act(sg_[:], bk(bG), AF.Sigmoid, [bb(bG)], [sg_.b])
                                    if br == 0:
                                        tt("vector", mrg[:, oc, ssl], sg_[:], bk(bB), ALU.mult, [sg_.b, bb(bB)], [mrg.b])
                                    elif br == 1:
                                        tt("vector", pr_[:], sg_[:], bk(bB), ALU.mult, [sg_.b, bb(bB)], [pr_.b])
                                        tt("gpsimd", mrg[:, oc, ssl], mrg[:, oc, ssl], pr_[:], ALU.add, [mrg.b, pr_.b], [mrg.b])
                                    else:
                                        tt("vector", pr_[:], sg_[:], bk(bB), ALU.mult, [sg_.b, bb(bB)], [pr_.b])
                                        tt("gpsimd", mrgb[:, oc, ssl], mrg[:, oc, ssl], pr_[:], ALU.add, [mrg.b, pr_.b], [mrgb.b])
                    S.barrier()
                with ExitStack() as ph2:
                    wout = T(ph2, "p_wout", [128, 8, D], BF16)
                    g1 = load_gain(ph2, l, 1)
                    g2 = load_gain(ph2, l, 2)
                    xt = [T(ph2, f"p_xt{i}", [128, D], F32) for i in range(2)]
                    on = [T(ph2, f"p_on{i}", [128, D], F32) for i in range(2)]
                    hb = [T(ph2, f"p_hb{i}", [128, D], BF16) for i in range(2)]
                    junk = T(ph2, "p_junk", [128, D], BF16)
                    ss = T(ph2, "p_ss", [128, NT], F32)
                    rs = T(ph2, "p_rs", [128, NT], F32)
                    ss2 = T(ph2, "p_ss2", [128, NT], F32)
                    rs2 = T(ph2, "p_rs2", [128, NT], F32)
                    for half in range(2):
                        load_slab(wout[:, :, half * 512:(half + 1) * 512], w_out[l], half * 512, 512, wout.b)
                    for t in range(NT):
                        tsl = slice(t * 128, (t + 1) * 128)
                        b0 = (t % 2) * 2
                        ov = PA[:, b0 * 512:(b0 + 2) * 512]
                        obufs = [bb(b0), bb(b0 + 1)]
                        for half in range(2):
                            for c in range(8):
                                mm(bk(b0 + half), mrgb[:, c, tsl], wout[:, c, half * 512:(half + 1) * 512], c == 0, c == 7, [mrgb.b, wout.b], [bb(b0 + half)])
                        x_, o_ = xt[t % 2], on[t % 2]
                        dma("sync", x_[:], x_src[tsl, :], r=[b_xsrc], w=[x_.b])
                        act(junk[:], ov, AF.Square, obufs, [junk.b, ss.b], accum_out=ss[:, t:t + 1])
                        act(rs[:, t:t + 1], ss[:, t:t + 1], AF.Sqrt, [ss.b], [rs.b], scale=1.0 / D, bias=EPS)
                        recip(rs[:, t:t + 1], rs[:, t:t + 1], [rs.b], [rs.b])
                        stt(o_[:], ov, rs[:, t:t + 1], g1[:], ALU.mult, ALU.mult, obufs + [rs.b, g1.b], [o_.b])
                        tt("gpsimd", o_[:], o_[:], x_[:], ALU.add, [o_.b, x_.b], [o_.b])
                        dma("sync", xmid[tsl, :], o_[:], r=[o_.b], w=[b_xmid])
                        norm_transpose_tile(ph2, t, o_[:], o_.b, g2, ss2, rs2, hb[t % 2], junk, 4 + t % 2)
                    S.barrier()
            with ExitStack() as ph:
                hid = T(ph, "f_hid", [128, 22, SEQ], BF16)
                hid_b = [Buf(f"hid{j}") for j in range(22)]
                wfo = T(ph, "f_wfo", [128, 22, D], BF16)
                wfo_b = [Buf(f"wfo{j}") for j in range(11)]
                wfi = [T(ph, f"f_wfi{i}", [128, 8, 2, 128], BF16) for i in range(2)]
                sgt = [T(ph, f"f_sg{i}", [128, 512], F32) for i in range(2)]
                g3 = load_gain(ph, l, 3)
                xt = [T(ph, f"f_xt{i}", [128, D], F32) for i in range(2)]
                on = [T(ph, f"f_on{i}", [128, D], F32) for i in range(2)]
                junk = T(ph, "f_junk", [128, D], BF16)
                ss = T(ph, "f_ss", [128, NT], F32)
                rs = T(ph, "f_rs", [128, NT], F32)
                n = 0
                bn = 0
                for j in range(22):
                    wf = wfi[j % 2]
                    dma("gpsimd", wf[:, :, 0, :], w_ffn_in[l][:, j * 128:(j + 1) * 128].rearrange("(kc p) n -> p kc n", p=128), w=[wf.b])
                    dma("gpsimd", wf[:, :, 1, :], w_ffn_in[l][:, DFF + j * 128:DFF + (j + 1) * 128].rearrange("(kc p) n -> p kc n", p=128), w=[wf.b])
                    if j % 2 == 0:
                        jj = j // 2
                        dma("gpsimd", wfo[:, 2 * jj:2 * jj + 2, :], w_ffn_out[l][jj * 256:(jj + 1) * 256, :].rearrange("(j p) n -> p j n", p=128), w=[wfo_b[jj]])
                    for sc in range(4):
                        ssl = slice(sc * 512, (sc + 1) * 512)
                        bG, bU = (bn % 4) * 2, (bn % 4) * 2 + 1
                        bn += 1
                        for kc in range(8):
                            mm(bk(bG), wf[:, kc, 0, :], hT[:, kc, ssl], kc == 0, kc == 7, [wf.b] + hT_b[sc * 4:(sc + 1) * 4], [bb(bG)])
                        for kc in range(8):
                            mm(bk(bU), wf[:, kc, 1, :], hT[:, kc, ssl], kc == 0, kc == 7, [wf.b] + hT_b[sc * 4:(sc + 1) * 4], [bb(bU)])
                        sg_ = sgt[n % 2]
                        n += 1
                        act(sg_[:], bk(bG), AF.Silu, [bb(bG)], [sg_.b])
                        tt("vector", hid[:, j, ssl], sg_[:], bk(bU), ALU.mult, [sg_.b, bb(bU)], [hid_b[j]])
                for t in range(NT):
                    tsl = slice(t * 128, (t + 1) * 128)
                    b0 = (t % 2) * 2
                    ov = PA[:, b0 * 512:(b0 + 2) * 512]
                    obufs = [bb(b0), bb(b0 + 1)]
                    for half in range(2):
                        for j in range(22):
                            mm(bk(b0 + half), hid[:, j, tsl], wfo[:, j, half * 512:(half + 1) * 512], j == 0, j == 21, [hid_b[j], wfo_b[j // 2]], [bb(b0 + half)])
                    x_, o_ = xt[t % 2], on[t % 2]
                    dma("sync", x_[:], xmid[tsl, :], r=[b_xmid], w=[x_.b])
                    act(junk[:], ov, AF.Square, obufs, [junk.b, ss.b], accum_out=ss[:, t:t + 1])
                    act(rs[:, t:t + 1], ss[:, t:t + 1], AF.Sqrt, [ss.b], [rs.b], scale=1.0 / D, bias=EPS)
                    recip(rs[:, t:t + 1], rs[:, t:t + 1], [rs.b], [rs.b])
                    stt(o_[:], ov, rs[:, t:t + 1], g3[:], ALU.mult, ALU.mult, obufs + [rs.b, g3.b], [o_.b])
                    tt("gpsimd", o_[:], o_[:], x_[:], ALU.add, [o_.b, x_.b], [o_.b])
                    dma("sync", x_dst[tsl, :], o_[:], r=[o_.b], w=[b_xdst])
                S.barrier()

        setup_tables()
        for l in range(n_layers):
            phase_A(l, x_in if l == 0 else xres)
            import os
            if not os.environ.get("SKIP_LG"):
                phase_lru(l)
                if stop == "lru":
                    break
                phase_gla(l)
                if stop == "gla":
                    break
            phase_nsa(l)
            if stop is not None and stop.startswith("nsa"):
                break
            last = (l == n_layers - 1)
            phase_tail(l, x_in if l == 0 else xres, y_out if last else xres, Buf() if l == 0 else b_xres, Buf() if last else b_xres)
        S.barrier()
        S.emit()
    return nc, consts


_CACHE = {}


def kernel(**inputs):
    if "nc" not in _CACHE:
        _CACHE["nc"] = build()
    nc, consts = _CACHE["nc"]
    x = np.ascontiguousarray(np.asarray(inputs["x"], dtype=np.float32))
    shared = {k: np.ascontiguousarray(np.asarray(v, dtype=np.float32)) for k, v in inputs.items() if k != "x"}
    for k, v in consts.items():
        shared["c_" + k] = v
    in_maps = [dict(shared, x=x[i]) for i in range(8)]
    res = run_bass_kernel_spmd(nc, in_maps, core_ids=list(range(8)))
    return np.stack([np.asarray(r["out"], dtype=np.float32) for r in res.results], axis=0)
```

```python
import os
import numpy as np
from contextlib import ExitStack
import concourse.bass as bass
import concourse.mybir as mybir
from concourse.bass_utils import run_bass_kernel_spmd

F32 = mybir.dt.float32
BF16 = mybir.dt.bfloat16
ALU = mybir.AluOpType
AF = mybir.ActivationFunctionType
AX = mybir.AxisListType

SEQ = 2048
D = 1024
NT = 16
DEPTH = 2
EPS = 1e-6
IN_W = 10816
C_LRUX, C_LRUG, C_Q, C_KV, C_GATE, C_GQ, C_GK, C_GV, C_GOG, C_GLR, C_MG = 0, 1024, 2048, 3072, 4608, 4656, 5168, 5680, 6704, 7728, 7744
DFF = 2816
NEG = -30000.0


class Buf:
    __slots__ = ("name", "w", "r", "excl")

    def __init__(self, name="", excl=False):
        self.name = name
        self.w = None
        self.r = []
        self.excl = excl


class Sched:
    ENG = ("sync", "scalar", "vector", "gpsimd", "tensor")
    DMAQ = ("sync", "gpsimd", "scalar")

    def __init__(self, nc, es, n_dma_sems=12):
        self.nc = nc
        self.q = {e: [] for e in self.ENG}
        self.cnt = {e: 0 for e in self.ENG}
        self.sems = []
        self.esem = {}
        for e in self.ENG:
            self.esem[e] = len(self.sems)
            self.sems.append(es.enter_context(nc.semaphore("s_" + e)))
        self.known = {e: {} for e in self.ENG}
        self.dpool = {}
        self.dcnt = {}
        self.dlast = {}
        for qn in self.DMAQ:
            self.dpool[qn] = []
            for i in range(n_dma_sems):
                self.dpool[qn].append(len(self.sems))
                self.sems.append(es.enter_context(nc.semaphore(f"d_{qn}_{i}")))
            self.dcnt[qn] = 0
        self.K = n_dma_sems

    def _waits(self, eng, r, w):
        waits = {}
        kn = self.known[eng]
        own_pe = self.esem["tensor"] if eng == "tensor" else -1

        def need(kv):
            k, v = kv
            if k == own_pe:
                return
            if kn.get(k, 0) < v and waits.get(k, 0) < v:
                waits[k] = v

        own = self.esem.get(eng, -2)
        for b in r:
            if b.w is not None:
                need(b.w)
            if b.excl:
                for x in b.r:
                    if x[0] != own:
                        need(x)
        for b in w:
            if b.w is not None:
                need(b.w)
            for x in b.r:
                need(x)
        for k, v in waits.items():
            kn[k] = v
        return list(waits.items())

    _rec = None

    def record(self, fn):
        self._rec = []
        fn()
        r, self._rec = self._rec, None
        return r

    def replay(self, lists):
        idx = [0] * len(lists)
        live = True
        while live:
            live = False
            for j, lst in enumerate(lists):
                if idx[j] < len(lst):
                    kind, args, kw = lst[idx[j]]
                    idx[j] += 1
                    live = True
                    if kind == "op":
                        self.op(*args)
                    else:
                        self.dma(*args, **kw)

    def op(self, eng, fn, r=(), w=()):
        if self._rec is not None:
            self._rec.append(("op", (eng, fn, list(r), list(w)), {}))
            return
        waits = self._waits(eng, r, w)
        self.cnt[eng] += 1
        seq = self.cnt[eng]
        k = self.esem[eng]
        self.q[eng].append((waits, fn, (k, 1)))
        for b in w:
            b.w = (k, seq)
            b.r = []
        for b in r:
            if b not in w:
                b.r.append((k, seq))
                if len(b.r) > 24:
                    b.r = b.r[-24:] if False else self._compact(b.r)

    @staticmethod
    def _compact(lst):
        d = {}
        for k, v in lst:
            if d.get(k, 0) < v:
                d[k] = v
        return list(d.items())

    def dma(self, qn, out, in_, r=(), w=(), **kw):
        if self._rec is not None:
            self._rec.append(("dma", (qn, out, in_, list(r), list(w)), kw))
            return
        waits = self._waits(qn, r, w)
        i = self.dcnt[qn]
        self.dcnt[qn] += 1
        k = self.dpool[qn][i % self.K]
        val = 16 * (i // self.K + 1)
        if val > 16 and self.known[qn].get(k, 0) < val - 16:
            waits.append((k, val - 16))
            self.known[qn][k] = val - 16
        self.dlast[k] = val
        self.q[qn].append((waits, lambda e: e.dma_start(out=out, in_=in_, **kw), (k, 16)))
        for b in w:
            b.w = (k, val)
            b.r = []
        for b in r:
            if b not in w:
                b.r.append((k, val))
                if len(b.r) > 24:
                    b.r = self._compact(b.r)

    def pe_drain(self):
        k = self.esem["tensor"]
        if self.cnt["tensor"] > 0:
            self.q["tensor"].append(([(k, self.cnt["tensor"])], None, None))

    def barrier(self):
        tgt = [(self.esem[e], self.cnt[e]) for e in self.ENG if self.cnt[e] > 0]
        tgt += list(self.dlast.items())
        for e in self.ENG:
            waits = []
            for k, v in tgt:
                if e == "tensor" and k == self.esem["tensor"]:
                    continue
                if self.known[e].get(k, 0) < v:
                    waits.append((k, v))
                    self.known[e][k] = v
            if waits:
                self.q[e].append((waits, None, None))

    def emit(self):
        nc = self.nc
        with nc.Block() as block:
            for e in self.ENG:
                def body(eng, _e=e):
                    for waits, fn, inc in self.q[_e]:
                        for k, v in waits:
                            eng.wait_ge(self.sems[k], v)
                        if fn is not None:
                            ins = fn(eng)
                            ins.then_inc(self.sems[inc[0]], inc[1])
                getattr(block, e)(body)


def fap(a, dims):
    return bass.AP(a.tensor, a.offset, [list(a.ap[0])] + [list(d) for d in dims])


def _rel_bucket(d):
    d = np.asarray(d)
    n = np.maximum(d, 0)
    nf = np.maximum(n, 16).astype(np.float32)
    large = 16 + (np.log(nf / np.float32(16)) / np.float32(np.log(128 / 16)) * np.float32(16)).astype(np.int32)
    large = np.minimum(large, 31)
    return np.where(n < 16, n, large)


def host_consts():
    c = {}
    c["ident"] = np.eye(128, dtype=np.float32)
    c["antiid"] = np.eye(128, dtype=np.float32)[::-1].copy()
    aid127 = np.zeros((128, 128), np.float32)
    for i in range(127):
        aid127[i, 126 - i] = 1.0
    aid127[127, 127] = 1.0
    c["antiid127"] = aid127
    s = np.arange(128)
    c["triu"] = (s[:, None] <= s[None, :]).astype(np.float32)
    c["tril"] = (s[:, None] > s[None, :]).astype(np.float32)
    def oh(deltas, valid):
        m = np.zeros((33, len(deltas)), np.float32)
        b = _rel_bucket(deltas)
        for i, (dd, v) in enumerate(zip(deltas, valid)):
            if v:
                m[b[i], i] = 1.0
            else:
                m[32, i] = 1.0
        return m
    dc = np.arange(-2048, 2048)
    c["oh_c"] = oh(dc, dc >= 0)
    ds = np.arange(-512, 512)
    c["oh_s"] = oh(ds, ds >= 0)
    c["oh_w"] = oh(ds, (ds >= 0) & (ds < 256))
    cs = np.arange(127) * 16
    js = np.arange(32) * 64
    ov = np.clip(np.minimum(cs[:, None] + 32, js[None, :] + 64) - np.maximum(cs[:, None], js[None, :]), 0, None).astype(np.float32) / 32.0
    ovx = np.zeros((128, 33), np.float32)
    ovx[:127, :32] = ov
    ovx[:127, 32] = 1.0
    c["ovx"] = ovx
    pos = np.arange(SEQ)
    cur = pos // 64
    blk = np.arange(32)[None, :]
    cand = (blk >= 1) & (blk <= cur[:, None] - 2)
    forced = (blk == 0) | (blk == cur[:, None]) | (blk == cur[:, None] - 1)
    c["cand"] = cand.astype(np.float32).reshape(NT, 128, 32).transpose(1, 0, 2).copy()
    c["negc"] = ((cand.astype(np.float32) - 1.0) * 1e4).reshape(NT, 128, 32).transpose(1, 0, 2).copy()
    c["forced"] = forced.astype(np.float32).reshape(NT, 128, 32).transpose(1, 0, 2).copy()
    ex = np.zeros((128, NT, 128), np.float32)
    for kt in range(NT):
        for key in range(128):
            ex[2 * kt + key // 64, kt, key] = 1.0
    c["expand_near"] = ex.copy()
    ex[32:34] = 1.0
    c["expand"] = ex
    return c


CONST_SHAPES = None


def build(debug=False, n_layers=DEPTH, stop=None):
    nc = bass.Bass("TRN2", target_bir_lowering=False)
    consts = host_consts()
    din = {}

    def inp(name, shape, dt=F32):
        din[name] = nc.dram_tensor(name, list(shape), dt, kind="ExternalInput").ap()
        return din[name]

    x_in = inp("x", [SEQ, D])
    rel_table = inp("rel_table", [32, 16])
    norm_g = inp("norm_g", [DEPTH, 4, D])
    w_in = inp("w_in", [DEPTH, D, IN_W])
    conv_w = inp("conv_w", [DEPTH, 4, D])
    conv_b = inp("conv_b", [DEPTH, D])
    lru_wg = inp("lru_w_gates", [DEPTH, 2, 8, 128, 128])
    lru_bg = inp("lru_b_gates", [DEPTH, 2, D])
    lru_lam = inp("lru_lambda", [DEPTH, D])
    cmp_pos = inp("cmp_pos", [DEPTH, 2, 32, 64])
    cmp_w1 = inp("cmp_w1", [DEPTH, 2, 2048, 256])
    cmp_w2 = inp("cmp_w2", [DEPTH, 2, 256, 64])
    gla_wa2 = inp("gla_wa2", [DEPTH, 16, 512])
    gla_ba = inp("gla_ba", [DEPTH, 512])
    gla_norm = inp("gla_norm", [DEPTH, 256])
    w_branch = inp("w_branch", [DEPTH, 3, D, D])
    w_out = inp("w_out", [DEPTH, D, D])
    w_ffn_in = inp("w_ffn_in", [DEPTH, D, 2 * DFF])
    w_ffn_out = inp("w_ffn_out", [DEPTH, DFF, D])
    cin = {k: inp("c_" + k, v.shape) for k, v in consts.items()}

    okind = "ExternalOutput"
    y_out = nc.dram_tensor("out", [SEQ, D], F32, kind=okind).ap()
    skind = "ExternalOutput"
    xres = nc.dram_tensor("xres", [SEQ, D], F32, kind=skind).ap()
    xmid = nc.dram_tensor("xmid", [SEQ, D], F32, kind=skind).ap()
    ysc = nc.dram_tensor("ysc", [3, D, SEQ], BF16, kind=skind).ap()
    tc_d = nc.dram_tensor("tc_d", [2, 16, 4096], BF16, kind="Internal").ap()
    ts_d = nc.dram_tensor("ts_d", [2, 16, 1024], BF16, kind="Internal").ap()
    tw_d = nc.dram_tensor("tw_d", [2, 16, 1024], BF16, kind="Internal").ap()
    hsc = nc.dram_tensor("hsc", [128, 8 * SEQ], BF16, kind=skind).ap()
    b_hsc = Buf()
    b_xres, b_xmid, b_ysc, b_tabs = Buf(), Buf(), [Buf(), Buf(), Buf()], Buf()

    with ExitStack() as es:
        S = Sched(nc, es)
        es.enter_context(nc.allow_non_contiguous_dma(reason="small param loads"))

        def mm(out, lhsT, rhs, start, stop, r, w):
            S.op("tensor", lambda e: e.matmul(out, lhsT=lhsT, rhs=rhs, start=start, stop=stop), r, w)

        def trp(out, in_, ident, r, w):
            S.op("tensor", lambda e: e.transpose(out, in_, ident), r, w)

        def act(out, in_, func, r, w, **kw):
            S.op("scalar", lambda e: e.activation(out=out, in_=in_, func=func, **kw), r, w)

        def tt(eng, out, in0, in1, op, r, w):
            S.op(eng, lambda e: e.tensor_tensor(out=out, in0=in0, in1=in1, op=op), r, w)

        def tsc(eng, out, in0, s1, op0, r, w, s2=None, op1=None):
            if op1 is None:
                S.op(eng, lambda e: e.tensor_scalar(out=out, in0=in0, scalar1=s1, scalar2=None, op0=op0), r, w)
            else:
                S.op(eng, lambda e: e.tensor_scalar(out=out, in0=in0, scalar1=s1, scalar2=s2, op0=op0, op1=op1), r, w)

        def stt(out, in0, scalar, in1, op0, op1, r, w):
            S.op("vector", lambda e: e.scalar_tensor_tensor(out=out, in0=in0, scalar=scalar, in1=in1, op0=op0, op1=op1), r, w)

        def cp(eng, out, in_, r, w):
            if eng == "scalar":
                S.op("scalar", lambda e: e.copy(out=out, in_=in_), r, w)
            else:
                S.op(eng, lambda e: e.tensor_copy(out=out, in_=in_), r, w)

        def recip(out, in_, r, w):
            S.op("vector", lambda e: e.reciprocal(out=out, in_=in_), r, w)

        def memset(eng, ap, val, w):
            S.op(eng, lambda e: e.memset(ap, val), (), w)

        def dma(q, out, in_, r=(), w=()):
            S.dma(q, out, in_, r, w)

        class T:
            _n = [0]

            def __init__(self, stack, name, shape, dt, psum=False):
                T._n[0] += 1
                name = f"{name}_{T._n[0]}"
                self.t = stack.enter_context((nc.psum_tensor if psum else nc.sbuf_tensor)(name, list(shape), dt))
                self.b = Buf(name)

            def __getitem__(self, idx):
                return self.t[idx]

        PA = T(es, "PA", [128, 2048], F32, psum=True)
        PB = T(es, "PB", [128, 2048], F32, psum=True)
        pbank = []
        for i in range(8):
            src = PA if i < 4 else PB
            pbank.append((src.t[:, (i % 4) * 512:(i % 4 + 1) * 512], Buf(f"bank{i}", excl=True)))
        PAb = PA.t.bitcast(BF16)
        PBb = PB.t.bitcast(BF16)

        def bank_bf(i):
            src = PAb if i < 4 else PBb
            return src[:, (i % 4) * 1024:(i % 4 + 1) * 1024]

        ident_f = T(es, "ident_f", [128, 128], F32)
        ident_b = T(es, "ident_b", [128, 128], BF16)
        dma("sync", ident_f[:], cin["ident"], w=[ident_f.b])
        cp("vector", ident_b[:], ident_f[:], [ident_f.b], [ident_b.b])

        hT = T(es, "hT", [128, 8, SEQ], BF16)
        hT_b = [Buf(f"hT{t}") for t in range(NT)]

        def load_gain(ph, l, i):
            gt = T(ph, f"gain{i}", [128, D], F32)
            src = norm_g[l, i:i + 1, :]
            dma("sync", gt[:], bass.AP(src.tensor, src.offset, [[0, 128], [1, D]]), w=[gt.b])
            return gt

        def norm_transpose_tile(ph, t, xt_ap, xt_buf, gt, ss, rs, hb, junk, pbi):
            act(junk[:], xt_ap, AF.Square, [xt_buf], [junk.b, ss.b], accum_out=ss[:, t:t + 1])
            act(rs[:, t:t + 1], ss[:, t:t + 1], AF.Sqrt, [ss.b], [rs.b], scale=1.0 / D, bias=EPS)
            recip(rs[:, t:t + 1], rs[:, t:t + 1], [rs.b], [rs.b])
            stt(hb[:], xt_ap, rs[:, t:t + 1], gt[:], ALU.mult, ALU.mult, [xt_buf, rs.b, gt.b], [hb.b])
            pv, pbuf = bank_bf(pbi), pbank[pbi][1]
            for kc in range(8):
                trp(pv[:, kc * 128:(kc + 1) * 128], hb[:, kc * 128:(kc + 1) * 128], ident_b[:], [hb.b, ident_b.b], [pbuf])
            cp("scalar", hT[:, :, t * 128:(t + 1) * 128], pv.rearrange("p (k s) -> p k s", k=8), [pbuf], [hT_b[t]])

        def load_slab(dst_ap, w2d, c0, ncols, wbuf, nk=8):
            src = w2d[:, c0:c0 + ncols].rearrange("(kc p) n -> p kc n", p=128)
            dma("gpsimd", dst_ap, src, w=[wbuf])

        def proj_fm(wslab, wbuf, col_off, M, rhs_tile, rhs_bufs, out_banks, nk=8, sc_list=(0, 1, 2, 3)):
            for i, sc in enumerate(sc_list):
                pa, pb_ = out_banks[i]
                for kc in range(nk):
                    mm(pa[0:M, :], wslab[:, kc, col_off:col_off + M], rhs_tile[:, kc, sc * 512:(sc + 1) * 512],
                       kc == 0, kc == nk - 1, [wbuf] + rhs_bufs[sc * 4:(sc + 1) * 4], [pb_])

        def phase_A(l, x_src):
            with ExitStack() as ph:
                xt = [T(ph, f"xtA{i}", [128, D], F32) for i in range(2)]
                hb = [T(ph, f"hbA{i}", [128, D], BF16) for i in range(2)]
                junk = T(ph, "junkA", [128, D], BF16)
                ss = T(ph, "ssA", [128, NT], F32)
                rs = T(ph, "rsA", [128, NT], F32)
                g0 = load_gain(ph, l, 0)
                for t in range(NT):
                    dma("sync", xt[t % 2][:], x_src[t * 128:(t + 1) * 128, :], r=[b_xres], w=[xt[t % 2].b])
                    norm_transpose_tile(ph, t, xt[t % 2][:], xt[t % 2].b, g0, ss, rs, hb[t % 2], junk, t % 2)
                S.barrier()

        def phase_lru(l):
            with ExitStack() as ph:
                prow = T(ph, "prow", [8, D], F32)
                lpT = T(ph, "lpT", [128, 8, 8], F32)
                sp = T(ph, "lru_sp", [128, 8, 6], F32)
                wg = T(ph, "lru_wg", [128, 2, 8, 128], BF16)
                slab = [T(ph, f"lslab{i}", [128, 8, 2, 128], BF16) for i in range(2)]
                XA = [T(ph, f"XA{i}", [128, SEQ + 4], F32) for i in range(2)]
                XC = [T(ph, f"XC{i}", [128, SEQ], F32) for i in range(2)]
                XCB = [T(ph, f"XCB{i}", [128, SEQ], BF16) for i in range(2)]
                R = [T(ph, f"R{i}", [128, SEQ], F32) for i in range(2)]
                A = [T(ph, f"A{i}", [128, SEQ], F32) for i in range(2)]
                I = [T(ph, f"I{i}", [128, SEQ], F32) for i in range(2)]
                H = [T(ph, f"H{i}", [128, SEQ], F32) for i in range(2)]
                GA = [T(ph, f"GA{i}", [128, SEQ], F32) for i in range(2)]
                G = [T(ph, f"G{i}", [128, SEQ], F32) for i in range(2)]
                YA = [T(ph, f"YA{i}", [128, SEQ], BF16) for i in range(2)]
                if os.environ.get("SBUF_DBG"):
                    print("LRU sbuf remaining", nc.sbuf_bytes_remaining)
                for k in range(4):
                    dma("sync", prow[k:k + 1, :], conv_w[l, k:k + 1, :], w=[prow.b])
                dma("sync", prow[4:5, :], conv_b[l:l + 1, :], w=[prow.b])
                dma("sync", prow[5:7, :], lru_bg[l], w=[prow.b])
                dma("sync", prow[7:8, :], lru_lam[l:l + 1, :], w=[prow.b])
                pv, pbuf = pbank[7]
                for c in range(8):
                    trp(pv[:, c * 8:(c + 1) * 8], prow[0:8, c * 128:(c + 1) * 128], ident_f[0:8, 0:8], [prow.b, ident_f.b], [pbuf])
                cp("vector", lpT[:], pv[:, 0:64].rearrange("p (c k) -> p c k", c=8), [pbuf], [lpT.b])
                xs, ln1, ser, msk, nsp8, nsp16 = (sp[:, :, i] for i in range(6))
                act(xs, lpT[:, :, 7], AF.Exp, [lpT.b], [sp.b], scale=-1.0)
                act(ln1, xs, AF.Ln, [sp.b], [sp.b], bias=1.0)
                tsc("vector", ser, xs, -0.25, ALU.mult, [sp.b], [sp.b], 1.0 / 3.0, ALU.add)
                tt("vector", ser, ser, xs, ALU.mult, [sp.b], [sp.b])
                tsc("vector", ser, ser, -1.0, ALU.mult, [sp.b], [sp.b], 0.5, ALU.add)
                tt("vector", ser, ser, xs, ALU.mult, [sp.b], [sp.b])
                tsc("vector", ser, ser, -1.0, ALU.mult, [sp.b], [sp.b], 1.0, ALU.add)
                tt("vector", ser, ser, xs, ALU.mult, [sp.b], [sp.b])
                tsc("vector", msk, xs, 0.03, ALU.is_lt, [sp.b], [sp.b])
                tt("vector", ser, ser, ln1, ALU.subtract, [sp.b], [sp.b])
                tt("vector", ser, ser, msk, ALU.mult, [sp.b], [sp.b])
                tt("vector", ser, ser, ln1, ALU.add, [sp.b], [sp.b])
                tsc("vector", nsp8, ser, -8.0, ALU.mult, [sp.b], [sp.b])
                tsc("vector", nsp16, ser, -16.0, ALU.mult, [sp.b], [sp.b])
                dma("gpsimd", wg[:], lru_wg[l].rearrange("k n c e -> c k n e"), w=[wg.b])
                for p_ in range(2):
                    memset("vector", XA[p_][:, 0:3], 0.0, [XA[p_].b])
                w2d = w_in[l]

                def lru_A(c):
                    p = c % 2
                    xa, xc, xcb, r_, i_, ga = XA[p], XC[p], XCB[p], R[p], I[p], GA[p]
                    sl = slab[p]
                    dma("gpsimd", sl[:, :, 0, :], w2d[:, C_LRUX + c * 128:C_LRUX + (c + 1) * 128].rearrange("(kc p) n -> p kc n", p=128), w=[sl.b])
                    dma("gpsimd", sl[:, :, 1, :], w2d[:, C_LRUG + c * 128:C_LRUG + (c + 1) * 128].rearrange("(kc p) n -> p kc n", p=128), w=[sl.b])
                    slv = sl.t.rearrange("p k a n -> p k (a n)")
                    proj_fm(slv, sl.b, 0, 128, hT.t, hT_b, pbank[0:4])
                    cp("scalar", xa[:, 3:3 + SEQ], PA[:, :], [pbank[i][1] for i in range(4)], [xa.b])
                    cw = lambda k: lpT[:, c, k:k + 1]
                    act(xc[:], xa[:, 3:3 + SEQ], AF.Identity, [xa.b, lpT.b], [xc.b], scale=cw(3), bias=cw(4))
                    for k in range(3):
                        stt(xc[:], xa[:, k:k + SEQ], cw(k), xc[:], ALU.mult, ALU.add, [xa.b, lpT.b, xc.b], [xc.b])
                    cp("gpsimd", xcb[:], xc[:], [xc.b], [xcb.b])
                    for gk in range(2):
                        banks = pbank[4:8] if gk == 0 else pbank[0:4]
                        for sc in range(4):
                            mm(banks[sc][0], wg[:, gk, c, :], xcb[:, sc * 512:(sc + 1) * 512], True, True, [wg.b, xcb.b], [banks[sc][1]])
                    act(r_[:], PB[:, :], AF.Sigmoid, [pbank[i][1] for i in range(4, 8)], [r_.b], bias=lpT[:, c, 5:6])
                    act(i_[:], PA[:, :], AF.Sigmoid, [pbank[i][1] for i in range(4)], [i_.b], bias=lpT[:, c, 6:7])
                    proj_fm(slv, sl.b, 128, 128, hT.t, hT_b, pbank[4:8])
                    cp("scalar", ga[:], PB[:, :], [pbank[i][1] for i in range(4, 8)], [ga.b])

                def lru_B(c):
                    p = c % 2
                    xc, r_, a_, i_, h_, ga, g_ = XC[p], R[p], A[p], I[p], H[p], GA[p], G[p]
                    act(a_[:], r_[:], AF.Exp, [r_.b, sp.b], [a_.b], scale=sp[:, c, 4:5])
                    act(r_[:], r_[:], AF.Exp, [r_.b, sp.b], [r_.b], scale=sp[:, c, 5:6])
                    act(r_[:], r_[:], AF.Identity, [r_.b], [r_.b], scale=-1.0, bias=1.0)
                    act(r_[:], r_[:], AF.Sqrt, [r_.b], [r_.b])
                    tt("gpsimd", i_[:], i_[:], xc[:], ALU.mult, [i_.b, xc.b], [i_.b])
                    tt("gpsimd", i_[:], i_[:], r_[:], ALU.mult, [i_.b, r_.b], [i_.b])
                    S.op("vector", lambda e, h_=h_, a_=a_, i_=i_: e.tensor_tensor_scan(out=h_[:], data0=a_[:], data1=i_[:], initial=0.0, op0=ALU.mult, op1=ALU.add),
                         [a_.b, i_.b], [h_.b])
                    act(g_[:], ga[:], AF.Square, [ga.b], [g_.b])
                    tsc("vector", g_[:], g_[:], 0.044715, ALU.mult, [g_.b], [g_.b], 1.0, ALU.add)
                    tt("gpsimd", g_[:], g_[:], ga[:], ALU.mult, [g_.b, ga.b], [g_.b])
                    act(g_[:], g_[:], AF.Sigmoid, [g_.b], [g_.b], scale=1.5957691216057308)
                    tt("gpsimd", g_[:], g_[:], ga[:], ALU.mult, [g_.b, ga.b], [g_.b])
                    ya = YA[p]
                    tt("vector", ya[:], g_[:], h_[:], ALU.mult, [g_.b, h_.b], [ya.b])
                    dma("sync", ysc[0, c * 128:(c + 1) * 128, :], ya[:], r=[ya.b], w=[b_ysc[0]])

                lru_A(0)
                for c in range(8):
                    lists = [S.record(lambda: lru_B(c))]
                    if c + 1 < 8:
                        lists.append(S.record(lambda: lru_A(c + 1)))
                    S.replay(lists)
                S.barrier()

        def phase_gla(l):
            w2d = w_in[l]
            with ExitStack() as ph:
                qT = T(ph, "gqT", [128, 4, SEQ], F32)
                kT = T(ph, "gkT", [128, 4, SEQ], F32)
                lrT = T(ph, "lrT", [32, SEQ], F32)
                wa2x = T(ph, "wa2x", [32, 512], F32)
                wres = T(ph, "gwres", [128, 8, 2560], BF16)
                gnb = T(ph, "gnb", [128, 4, 256], F32)
                st_f = T(ph, "st_f", [128, 4, 256], F32)
                st_b = T(ph, "st_b", [128, 4, 256], BF16)
                cm4 = T(ph, "cm4", [128, 4, 128], F32)
                triu = T(ph, "triu", [128, 128], F32)
                tril = T(ph, "tril", [128, 128], F32)
                dma("sync", triu[:], cin["triu"], w=[triu.b])
                dma("sync", tril[:], cin["tril"], w=[tril.b])
                for hh in range(4):
                    dma("sync", cm4[:, hh, :], cin["triu"], w=[cm4.b])
                    src = gla_norm[l:l + 1, :]
                    dma("sync", gnb[:, hh, :], bass.AP(src.tensor, src.offset, [[0, 128], [1, 256]]), w=[gnb.b])
                memset("vector", wa2x[:], 0.0, [wa2x.b])
                memset("vector", lrT[:], 1.0, [lrT.b])
                dma("sync", wa2x[0:16, :], gla_wa2[l], w=[wa2x.b])
                dma("sync", wa2x[16:17, :], gla_ba[l:l + 1, :], w=[wa2x.b])
                for i, c0 in enumerate((C_GK, C_GV, C_GV + 512, C_GOG, C_GOG + 512)):
                    load_slab(wres[:, :, i * 512:(i + 1) * 512], w2d, c0, 512, wres.b)
                with ExitStack() as ph2:
                    slab = [T(ph2, f"gslab{i}", [128, 8, 512], BF16) for i in range(2)]
                    lslab = T(ph2, "glslab", [128, 8, 16], BF16)
                    load_slab(slab[0][:], w2d, C_GQ, 512, slab[0].b)
                    load_slab(slab[1][:], w2d, C_GK, 512, slab[1].b)
                    load_slab(lslab[:], w2d, C_GLR, 16, lslab.b)
                    for i in range(8):
                        banks = pbank[0:4] if i % 2 == 0 else pbank[4:8]
                        src = PA if i % 2 == 0 else PB
                        proj_fm(slab[i // 4].t, slab[i // 4].b, (i % 4) * 128, 128, hT.t, hT_b, banks)
                        dst = qT if i < 4 else kT
                        act(dst[:, i % 4, :], src[:, :], AF.Copy, [b for _, b in banks], [dst.b], scale=(128 ** -0.5 if i < 4 else 1.0))
                    proj_fm(lslab.t, lslab.b, 0, 16, hT.t, hT_b, pbank[0:4])
                    cp("vector", lrT[0:16, :], PA[0:16, :], [b for _, b in pbank[0:4]], [lrT.b])
                    S.barrier()
                sp_t = T(ph, "g_sp", [128, 512], F32)
                E1 = [T(ph, f"g_E1{i}", [128, 512], F32) for i in range(2)]
                E2 = T(ph, "g_E2", [128, 512], F32)
                Erb = T(ph, "g_Erb", [128, 512], F32)
                qtb = [T(ph, f"g_qtb{i}", [128, 4, 128], BF16) for i in range(2)]
                ktb = T(ph, "g_ktb", [128, 4, 128], BF16)
                kend = [T(ph, f"g_kend{i}", [128, 512], BF16) for i in range(2)]
                v_bf = [T(ph, f"g_vbf{i}", [128, 1024], BF16) for i in range(2)]
                sg = [T(ph, f"g_sg{i}", [128, 1024], F32) for i in range(2)]
                attm = [T(ph, f"g_attm{i}", [128, 4, 128], BF16) for i in range(2)]
                on = T(ph, "g_on", [128, 1024], F32)
                yc = [T(ph, f"g_yc{i}", [128, 1024], BF16) for i in range(2)]
                ycT = [T(ph, f"g_ycT{i}", [128, 8, 128], BF16) for i in range(2)]
                junk = T(ph, "g_junk", [128, 256], BF16)
                ssq = T(ph, "g_ssq", [128, 4], F32)
                rst = T(ph, "g_rst", [128, 4], F32)
                if os.environ.get("SBUF_DBG"):
                    print("GLA sbuf remaining", nc.sbuf_bytes_remaining)
                bk = lambda i: pbank[i][0]
                bb = lambda i: pbank[i][1]

                def gla_A(t):
                    p = t % 2
                    tsl = slice(t * 128, (t + 1) * 128)
                    e1, qb, ke, vb, sg_, am = E1[p], qtb[p], kend[p], v_bf[p], sg[p], attm[p]
                    mm(bk(0), lrT[0:17, tsl], wa2x[0:17, :], True, True, [lrT.b, wa2x.b], [bb(0)])
                    act(sp_t[:], bk(0), AF.Exp, [bb(0)], [sp_t.b], scale=-1.0)
                    act(sp_t[:], sp_t[:], AF.Ln, [sp_t.b], [sp_t.b], bias=1.0)
                    for hh in range(4):
                        mm(bk(1)[:, hh * 128:(hh + 1) * 128], sp_t[:, hh * 128:(hh + 1) * 128], triu[:], True, True, [sp_t.b, triu.b], [bb(1)])
                    mm(bk(2), tril[:], sp_t[:], True, True, [tril.b, sp_t.b], [bb(2)])
                    act(e1[:], bk(1), AF.Exp, [bb(1)], [e1.b], scale=-1.0 / 16.0)
                    act(E2[:], bk(1), AF.Exp, [bb(1)], [E2.b], scale=1.0 / 16.0)
                    act(Erb[:], bk(2), AF.Exp, [bb(2)], [Erb.b], scale=-1.0 / 16.0)
                    tt("vector", qb[:], qT[:, :, tsl], e1.t.rearrange("p (h s) -> p h s", h=4), ALU.mult, [qT.b, e1.b], [qb.b])
                    tt("gpsimd", ktb[:], kT[:, :, tsl], E2.t.rearrange("p (h s) -> p h s", h=4), ALU.mult, [kT.b, E2.b], [ktb.b])
                    for kc in range(8):
                        mm(bk(3), hT[:, kc, tsl], wres[:, kc, 0:512], kc == 0, kc == 7, [hT_b[t], wres.b], [bb(3)])
                    tt("vector", ke[:], bk(3), Erb[:], ALU.mult, [bb(3), Erb.b], [ke.b])
                    for half in range(2):
                        for kc in range(8):
                            mm(bk(half), hT[:, kc, tsl], wres[:, kc, 512 + half * 512:1024 + half * 512], kc == 0, kc == 7, [hT_b[t], wres.b], [bb(half)])
                    cp("scalar", vb[:], PA[:, 0:1024], [bb(0), bb(1)], [vb.b])
                    for half in range(2):
                        for kc in range(8):
                            mm(bk(2 + half), hT[:, kc, tsl], wres[:, kc, 1536 + half * 512:2048 + half * 512], kc == 0, kc == 7, [hT_b[t], wres.b], [bb(2 + half)])
                    act(sg_[:], PA[:, 1024:2048], AF.Silu, [bb(2), bb(3)], [sg_.b])
                    tt("gpsimd", sg_[:], sg_[:], gnb.t.rearrange("p h e -> p (h e)"), ALU.mult, [sg_.b, gnb.b], [sg_.b])
                    for hh in range(4):
                        mm(bk(0)[:, hh * 128:(hh + 1) * 128], ktb[:, hh, :], qb[:, hh, :], True, True, [ktb.b, qb.b], [bb(0)])
                    tt("vector", am[:], bk(0).rearrange("p (h s) -> p h s", h=4), cm4[:], ALU.mult, [bb(0), cm4.b], [am.b])

                def gla_B(t):
                    p = t % 2
                    tsl = slice(t * 128, (t + 1) * 128)
                    e1, qb, ke, vb, sg_, am = E1[p], qtb[p], kend[p], v_bf[p], sg[p], attm[p]
                    for hh in range(4):
                        ob = 4 + hh // 2
                        oap = bk(ob)[:, (hh % 2) * 256:(hh % 2 + 1) * 256]
                        mm(oap, am[:, hh, :], vb[:, hh * 256:(hh + 1) * 256], hh % 2 == 0, t == 0 and hh % 2 == 1, [am.b, vb.b], [bb(ob)])
                        if t > 0:
                            mm(oap, qb[:, hh, :], st_b[:, hh, :], False, hh % 2 == 1, [qb.b, st_b.b], [bb(ob)])
                    for hh in range(4):
                        kb_ = 6 + hh // 2
                        mm(bk(kb_)[:, (hh % 2) * 256:(hh % 2 + 1) * 256], ke[:, hh * 128:(hh + 1) * 128], vb[:, hh * 256:(hh + 1) * 256],
                           hh % 2 == 0, hh % 2 == 1, [ke.b, vb.b], [bb(kb_)])
                    for hh in range(4):
                        kvp = bk(6 + hh // 2)[:, (hh % 2) * 256:(hh % 2 + 1) * 256]
                        if t == 0:
                            cp("vector", st_f[:, hh, :], kvp, [bb(6 + hh // 2)], [st_f.b])
                        else:
                            dec = e1[:, hh * 128 + 127:hh * 128 + 128]
                            stt(st_f[:, hh, :], st_f[:, hh, :], dec, kvp, ALU.mult, ALU.add, [st_f.b, e1.b, bb(6 + hh // 2)], [st_f.b])
                    cp("gpsimd", st_b[:], st_f[:], [st_f.b], [st_b.b])
                    for hh in range(4):
                        oap = bk(4 + hh // 2)[:, (hh % 2) * 256:(hh % 2 + 1) * 256]
                        act(junk[:], oap, AF.Square, [bb(4 + hh // 2)], [junk.b, ssq.b], accum_out=ssq[:, hh:hh + 1])
                    act(rst[:], ssq[:], AF.Sqrt, [ssq.b], [rst.b], scale=1.0 / 256.0, bias=EPS)
                    recip(rst[:], rst[:], [rst.b], [rst.b])
                    tt("vector", on.t.rearrange("p (h e) -> p h e", h=4), PB[:, 0:1024].rearrange("p (h e) -> p h e", h=4),
                       fap(rst[:], [[1, 4], [0, 256]]), ALU.mult, [bb(4), bb(5), rst.b], [on.b])
                    y = yc[p]
                    tt("gpsimd", y[:], on[:], sg_[:], ALU.mult, [on.b, sg_.b], [y.b])
                    pv = bank_bf(7)
                    for c in range(8):
                        trp(pv[:, c * 128:(c + 1) * 128], y[:, c * 128:(c + 1) * 128], ident_b[:], [y.b, ident_b.b], [bb(7)])
                    yT = ycT[p]
                    cp("scalar", yT[:], pv.rearrange("p (k s) -> p k s", k=8), [bb(7)], [yT.b])
                    dma("sync", ysc[2, :, tsl].rearrange("(c p) s -> p c s", p=128), yT[:], r=[yT.b], w=[b_ysc[2]])

                gla_A(0)
                for t in range(NT):
                    lists = [S.record(lambda: gla_B(t))]
                    if t + 1 < NT:
                        lists.append(S.record(lambda: gla_A(t + 1)))
                    S.replay(lists)
                S.barrier()

        def setup_tables():
            with ExitStack() as ph:
                tblx = T(ph, "tblx", [33, 16], F32)
                memset("vector", tblx[:], NEG, [tblx.b])
                dma("sync", tblx[0:32, :], rel_table, w=[tblx.b])
                for name, dst, n in (("oh_c", tc_d, 4096), ("oh_s", ts_d, 1024), ("oh_w", tw_d, 1024)):
                    oh = T(ph, "t_" + name, [33, n], F32)
                    thi = T(ph, "thi_" + name, [16, n], BF16)
                    tlo = T(ph, "tlo_" + name, [16, n], BF16)
                    dma("sync", oh[:], cin[name], w=[oh.b])
                    for ch in range(n // 512):
                        pa, pbuf = pbank[ch % 8]
                        mm(pa[0:16, :], tblx[0:33, 0:16], oh[0:33, ch * 512:(ch + 1) * 512], True, True, [tblx.b, oh.b], [pbuf])
                        cp("vector", thi[:, ch * 512:(ch + 1) * 512], pa[0:16, :], [pbuf], [thi.b])
                        tt("vector", tlo[:, ch * 512:(ch + 1) * 512], pa[0:16, :], thi[:, ch * 512:(ch + 1) * 512], ALU.subtract, [pbuf, thi.b], [tlo.b])
                    dma("sync", dst[0], thi[:], r=[thi.b], w=[b_tabs])
                    dma("sync", dst[1], tlo[:], r=[tlo.b], w=[b_tabs])
                S.barrier()

        def phase_nsa(l):
            w2d = w_in[l]
            bk = lambda i: pbank[i][0]
            bb = lambda i: pbank[i][1]
            with ExitStack() as ph:
                qT = T(ph, "nqT", [128, 8, SEQ], BF16)
                kS = T(ph, "nkS", [128, 4, SEQ], BF16)
                kW = T(ph, "nkW", [128, 4, SEQ], BF16)
                vS = T(ph, "nvS", [128, NT, 4, 66], BF16)
                vW = T(ph, "nvW", [128, NT, 4, 66], BF16)
                sgate = T(ph, "nsg", [128, NT, 48], F32)
                kcP = T(ph, "nkcP", [128, 2, 4, 128], BF16)
                vcx = T(ph, "nvcx", [128, 4, 98], BF16)
                hbt = T(ph, "nhbt", [128, 3, 4, 2, 512], BF16)
                NM = T(ph, "nNM", [128, 4, 2, 512], BF16)
                Jb = T(ph, "nJb", [128, 2, 128], BF16)
                expd = T(ph, "nexpd", [128, 2, NT, 128], BF16)
                cand = T(ph, "ncand", [128, NT, 32], F32)
                negc = T(ph, "nnegc", [128, NT, 32], F32)
                forced = T(ph, "nforced", [128, NT, 32], F32)
                dma("gpsimd", Jb[:, 0, :], cin["antiid"], w=[Jb.b])
                dma("gpsimd", Jb[:, 1, :], cin["antiid127"], w=[Jb.b])
                dma("gpsimd", expd[:, 0, :, :], cin["expand"], w=[expd.b])
                dma("gpsimd", expd[:, 1, :, :], cin["expand_near"], w=[expd.b])
                memset("vector", NM[:], 0.0, [NM.b])
                dma("sync", cand[:], cin["cand"], w=[cand.b])
                dma("sync", negc[:], cin["negc"], w=[negc.b])
                dma("sync", forced[:], cin["forced"], w=[forced.b])
                memset("vector", vcx[:], 0.0, [vcx.b])
                memset("vector", kcP[:], 0.0, [kcP.b])
                for g in range(4):
                    dma("gpsimd", vcx[:, g, 64:97], cin["ovx"], w=[vcx.b])
                for dl in range(3):
                    tsrc = tw_d if dl == 2 else ts_d
                    for g in range(4):
                        for hl in range(2):
                            for rp in range(2):
                                a0 = tsrc[hl, 4 * g + 2 * rp, 512 + dl * 128 - 127:512 + dl * 128 - 127 + 1]
                                src = bass.AP(a0.tensor, a0.offset, [[1, 128], [1024, 2], [1, 128]])
                                dst = hbt[:, dl, g, hl, :].rearrange("p (a b s) -> p a b s", a=2, b=2)[:, :, rp, :]
                                dma("sync", dst, src, r=[b_tabs], w=[hbt.b])
                for g in range(4):
                    for hl in range(2):
                        for par in range(2):
                            for rp in range(2):
                                h = 4 * g + 2 * rp + par
                                a0 = ts_d[hl, h, 640:641]
                                src = bass.AP(a0.tensor, a0.offset, [[0, 1], [0, 2], [1, 128]])
                                c0 = par * 256 + rp * 128
                                dma("sync", NM[32 + hl:33 + hl, g, :, c0:c0 + 128], src, r=[b_tabs], w=[NM.b])
                memset("vector", vS[:, :, :, 64:66], 1.0, [vS.b])
                memset("vector", vW[:, :, :, 64:66], 1.0, [vW.b])
                if stop == "nsa0":
                    S.barrier()
                    return
                with ExitStack() as ph2:
                    slab = [T(ph2, f"nslab{i}", [128, 8, 512], BF16) for i in range(2)]
                    wv = T(ph2, "nwv", [128, 8, 560], BF16)
                    for half in range(2):
                        sl = slab[half]
                        load_slab(sl[:], w2d, C_Q + half * 512, 512, sl.b)
                        for i in range(4):
                            c = half * 4 + i
                            banks = pbank[0:4] if c % 2 == 0 else pbank[4:8]
                            src = PA if c % 2 == 0 else PB
                            proj_fm(sl.t, sl.b, i * 128, 128, hT.t, hT_b, banks)
                            act(qT[:, c, :], src[:, :], AF.Copy, [b for _, b in banks], [qT.b], scale=0.125)
                    if stop == "nsa1a":
                        S.barrier()
                        return
                    n = 0
                    for idx, dst in ((2, kS), (4, kW)):
                        sl = slab[n % 2]
                        n += 1
                        for g in range(4):
                            c0 = C_KV + idx * 256 + g * 64
                            for dup in range(2):
                                dma("gpsimd", sl[:, :, g * 128 + dup * 64:g * 128 + dup * 64 + 64],
                                    w2d[:, c0:c0 + 64].rearrange("(kc p) n -> p kc n", p=128), w=[sl.b])
                        for g in range(4):
                            banks = pbank[0:4] if g % 2 == 0 else pbank[4:8]
                            src = PA if g % 2 == 0 else PB
                            proj_fm(sl.t, sl.b, g * 128, 128, hT.t, hT_b, banks)
                            cp("scalar" if g % 2 == 0 else "vector", dst[:, g, :], src[:, :], [b for _, b in banks], [dst.b])
                    if stop == "nsa1b":
                        S.barrier()
                        return
                    load_slab(wv[:, :, 0:256], w2d, C_KV + 3 * 256, 256, wv.b)
                    load_slab(wv[:, :, 256:512], w2d, C_KV + 5 * 256, 256, wv.b)
                    load_slab(wv[:, :, 512:560], w2d, C_GATE, 48, wv.b)
                    for t in range(NT):
                        tsl = slice(t * 128, (t + 1) * 128)
                        b0, b1 = (0, 1) if t % 2 == 0 else (2, 3)
                        for kc in range(8):
                            mm(bk(b0), hT[:, kc, tsl], wv[:, kc, 0:512], kc == 0, kc == 7, [hT_b[t], wv.b], [bb(b0)])
                        import os
                        SK = os.environ.get("NSA_SKIP", "")
                        if "g" not in SK:
                            for kc in range(8):
                                mm(bk(b1)[:, 0:48], hT[:, kc, tsl], wv[:, kc, 512:560], kc == 0, kc == 7, [hT_b[t], wv.b], [bb(b1)])
                        if "v" not in SK:
                            cp("vector", vS[:, t, :, 0:64], bk(b0)[:, 0:256].rearrange("p (g d) -> p g d", g=4), [bb(b0)], [vS.b])
                        if "w" not in SK:
                            cp("scalar", vW[:, t, :, 0:64], bk(b0)[:, 256:512].rearrange("p (g d) -> p g d", g=4), [bb(b0)], [vW.b])
                        if "g" not in SK:
                            act(sgate[:, t, :], bk(b1)[:, 0:48], AF.Sigmoid, [bb(b1)], [sgate.b])
                    S.barrier()
                if stop == "nsa1":
                    return
                with ExitStack() as ph2:
                    slab = [T(ph2, f"ncslab{i}", [128, 8, 256], BF16) for i in range(2)]
                    w1sb = T(ph2, "nw1", [128, 32, 256], BF16)
                    w2sb = T(ph2, "nw2", [128, 2, 128], BF16)
                    prow2 = T(ph2, "nprow2", [32, 128], F32)
                    posT = T(ph2, "nposT", [128, 32], F32)
                    XAB = [T(ph2, f"nXAB{i}", [128, SEQ], BF16) for i in range(2)]
                    gtmp = T(ph2, "ngtmp", [128, 2, 128], F32)
                    geluT = T(ph2, "ngeluT", [128, 2, 128], BF16)
                    for kv in range(2):
                        for dup in range(2):
                            dma("gpsimd", w1sb[dup * 64:(dup + 1) * 64, :, :], cmp_w1[l, kv].rearrange("(p d) j -> d p j", d=64), w=[w1sb.b])
                            dma("gpsimd", w2sb[:, :, dup * 64:(dup + 1) * 64], cmp_w2[l, kv].rearrange("(jc p) d -> p jc d", p=128), w=[w2sb.b])
                            dma("sync", prow2[:, dup * 64:(dup + 1) * 64], cmp_pos[l, kv], w=[prow2.b])
                        trp(bk(6)[:, 0:32], prow2[:, :], ident_f[0:32, 0:32], [prow2.b, ident_f.b], [bb(6)])
                        cp("vector", posT[:], bk(6)[:, 0:32], [bb(6)], [posT.b])
                        sl = slab[kv]
                        load_slab(sl[:, :, 0:256], w2d, C_KV + kv * 256, 256, sl.b)
                        for cc in range(2):
                            banks = pbank[0:4]
                            proj_fm(sl.t, sl.b, cc * 128, 128, hT.t, hT_b, banks)
                            for ab in range(2):
                                tt("vector" if ab == 0 else "gpsimd" if False else "vector", XAB[ab].t.rearrange("p (i q) -> p i q", q=16), PA.t.rearrange("p (i q) -> p i q", q=16),
                                   fap(posT[:, ab * 16:ab * 16 + 1], [[0, 128], [1, 16]]), ALU.add, [b for _, b in banks] + [posT.b], [XAB[ab].b])
                            for gg in range(2):
                                g = cc * 2 + gg
                                rows = slice(gg * 64, gg * 64 + 64)
                                hb_, hbb = bk(4 + 2 * gg), bb(4 + 2 * gg)
                                for jc in range(2):
                                    for p in range(32):
                                        srcT = XAB[0] if p < 16 else XAB[1]
                                        rhs = fap(srcT[rows, p:p + 1], [[16, 127]])
                                        mm(hb_[:, jc * 128:jc * 128 + 127], w1sb[rows, p, jc * 128:(jc + 1) * 128], rhs, p == 0, p == 31,
                                           [w1sb.b, srcT.b], [hbb])
                                hv = hb_[:, 0:256].rearrange("p (j i) -> p j i", j=2)[:, :, 0:127]
                                gv = gtmp[:, :, 0:127]
                                act(gv, hv, AF.Square, [hbb], [gtmp.b])
                                tsc("vector", gv, gv, 0.044715, ALU.mult, [gtmp.b], [gtmp.b], 1.0, ALU.add)
                                tt("vector", gv, gv, hv, ALU.mult, [gtmp.b, hbb], [gtmp.b])
                                act(gv, gv, AF.Sigmoid, [gtmp.b], [gtmp.b], scale=1.5957691216057308)
                                tt("vector", geluT[:, :, 0:127], gv, hv, ALU.mult, [gtmp.b, hbb], [geluT.b])
                                if kv == 0:
                                    for jc in range(2):
                                        mm(bk(5)[:, 0:127], w2sb[:, jc, :], geluT[:, jc, 0:127], jc == 0, jc == 1, [w2sb.b, geluT.b], [bb(5)])
                                    cp("scalar", kcP[0:64, 0, g, 0:127], bk(5)[0:64, 0:127], [bb(5)], [kcP.b])
                                    cp("scalar", kcP[64:128, 1, g, 0:127], bk(5)[64:128, 0:127], [bb(5)], [kcP.b])
                                else:
                                    for jc in range(2):
                                        mm(bk(5)[0:127, 0:64], geluT[:, jc, 0:127], w2sb[:, jc, 0:64], jc == 0, jc == 1, [w2sb.b, geluT.b], [bb(5)])
                                    cp("scalar", vcx[0:127, g, 0:64], bk(5)[0:127, 0:64], [bb(5)], [vcx.b])
                    S.barrier()
                if stop == "nsa2":
                    return
                dma("sync", hsc, hT.t.rearrange("p k s -> p (k s)"), r=hT_b, w=[b_hsc])
                kpad1 = Buf("kpad1")
                for base, KT_, ceng in ((0, kS, "scalar"), (4, kW, "vector")):
                    cp(ceng, hT[64:128, base:base + 4, :], KT_[64:128, :, :], [KT_.b], hT_b + [kpad1])
                    memset("gpsimd", hT[0:64, base:base + 4, :], 0.0, hT_b + [kpad1])
                    memset("gpsimd" if base == 0 else "vector", KT_[64:128, :, :], 0.0, [KT_.b])
                E = [T(ph, f"nE{i}", [128, 512], BF16) for i in range(4)]
                cb = [T(ph, f"ncb{i}", [128, 2, 512], BF16) for i in range(4)]
                for cbx in cb:
                    memset("vector", cbx[:], NEG, [cbx.b])
                ybt = [T(ph, f"nybt{i}", [128, 1024], BF16) for i in range(2)]
                ybT = [T(ph, f"nybT{i}", [128, 8, 128], BF16) for i in range(2)]
                sets = []
                for i in range(2):
                    sets.append(dict(
                        ybacc=T(ph, f"nybacc{i}", [128, 4, 64], F32), tmp1=T(ph, f"ntmp1{i}", [128, 4, 64], F32),
                        tmp2=T(ph, f"ntmp2{i}", [128, 4, 64], F32), impr=T(ph, f"nimpr{i}", [128, 4, 32], F32),
                        imp=T(ph, f"nimp{i}", [128, 32], F32), m8=T(ph, f"nm8{i}", [128, 8], F32),
                        sm=T(ph, f"nsm{i}", [128, 3, 4], F32), Us=(3, 6)[i], Uw=(4, 7)[i]))
                colb = lambda r: (r % 2) * 256 + (r // 2) * 128
                cnt_ = dict(l=0, e=0, kp=0, cb=0)
                tasks = []

                inflight = set()

                def next_Li(hold=False):
                    while True:
                        i = (0, 1, 5)[cnt_["l"] % 3]
                        cnt_["l"] += 1
                        if i not in inflight:
                            break
                    if hold:
                        inflight.add(i)
                    return i

                def next_L(hold=False):
                    return pbank[next_Li(hold)]

                def release_L(Lb):
                    for i in (0, 1, 5):
                        if pbank[i][1] is Lb:
                            inflight.discard(i)

                def next_E():
                    e_ = E[cnt_["e"] % 4]
                    cnt_["e"] += 1
                    return e_

                cb_dma = []
                PF = 3

                def mk_cmp(qt, g, st):
                    qsl = slice(qt * 128, (qt + 1) * 128)
                    ui = len(cb_dma)
                    cbt = cb[ui % 4]
                    box = {}

                    def issue():
                        for hl in range(2):
                            for rp in range(2):
                                a0 = tc_d[hl, 4 * g + 2 * rp, qt * 128 + 1:qt * 128 + 2]
                                src = bass.AP(a0.tensor, a0.offset, [[16, 127], [4096, 2], [1, 128]])
                                dst = cbt[0:127, hl, :].rearrange("p (a b s) -> p a b s", a=2, b=2)[:, :, rp, :]
                                dma("sync", dst, src, r=[b_tabs], w=[cbt.b])

                    cb_dma.append(issue)

                    def pre():
                        if ui + PF < len(cb_dma):
                            cb_dma[ui + PF]()

                    def s1():
                        L, Lb = next_L(hold=True)
                        box["L"] = (L, Lb)
                        for par in range(2):
                            mm(L[:, par * 256:(par + 1) * 256], kcP[:, par, g, :], qT[:, 2 * g:2 * g + 2, qsl], par == 0, False, [kcP.b, qT.b], [Lb])
                        for hl in range(2):
                            mm(L, Jb[:, 1, :], cbt[:, hl, :], False, hl == 1, [Jb.b, cbt.b], [Lb])

                    def s2():
                        L, Lb = box["L"]
                        release_L(Lb)
                        Ec = next_E()
                        act(Ec[:], L, AF.Exp, [Lb], [Ec.b])
                        Uc = bk(2)
                        for r in range(4):
                            mm(Uc[:, r * 98:(r + 1) * 98], Ec[:, colb(r):colb(r) + 128], vcx[:, g, 0:98], r == 0, r == 3, [Ec.b, vcx.b], [bb(2)])

                    def post():
                        Uc = bk(2)
                        sm, ybacc, impr, imp, m8 = st["sm"], st["ybacc"], st["impr"], st["imp"], st["m8"]
                        ucv = lambda a, b_: fap(Uc[:, a:a + 1], [[98, 4], [1, b_]])
                        rs4, wc = sm[:, 0, :], sm[:, 1, :]
                        tsc("vector", rs4, fap(Uc[:, 96:97], [[98, 4]]), 1e-30, ALU.max, [bb(2)], [sm.b])
                        recip(rs4, rs4, [sm.b], [sm.b])
                        tt("vector", wc, rs4, sgate[:, qt, 4 * g:4 * g + 4], ALU.mult, [sm.b, sgate.b], [sm.b])
                        tt("vector", ybacc[:], ucv(0, 64), fap(wc, [[1, 4], [0, 64]]), ALU.mult, [bb(2), sm.b], [ybacc.b])
                        tt("vector", impr[:], ucv(64, 32), fap(rs4, [[1, 4], [0, 32]]), ALU.mult, [bb(2), sm.b], [impr.b])
                        S.op("vector", lambda e: e.tensor_reduce(out=imp[:], in_=fap(impr[:, 0, 0:1], [[1, 32], [32, 4]]), axis=AX.X, op=ALU.add),
                             [impr.b], [imp.b])
                        tt("vector", imp[:], imp[:], cand[:, qt, :], ALU.mult, [imp.b, cand.b], [imp.b])
                        tt("vector", imp[:], imp[:], negc[:, qt, :], ALU.add, [imp.b, negc.b], [imp.b])
                        S.op("vector", lambda e: e.max(out=m8[:], in_=imp[:]), [imp.b], [m8.b])
                        tsc("vector", imp[:], imp[:], m8[:, 4:5], ALU.is_ge, [imp.b, m8.b], [imp.b])
                        tt("vector", imp[:], imp[:], forced[:, qt, :], ALU.max, [imp.b, forced.b], [imp.b])
                        tsc("vector", imp[:], imp[:], -1.0, ALU.add, [imp.b], [imp.b], -NEG, ALU.mult)

                    return dict(pre=pre, s1=s1, s2=s2, post=post, defer=None, nm=None, first_slc=False)

                def mk_tile(qt, g, st, br, kt, first, last, buf, hooks_pre, hooks_post, defer):
                    qsl = slice(qt * 128, (qt + 1) * 128)
                    ksl = slice(kt * 128, (kt + 1) * 128)
                    KT = kS if br == 0 else kW
                    VT = vS if br == 0 else vW
                    Ub = st["Us"] if br == 0 else st["Uw"]
                    dl = qt - kt
                    near = dl < (2 if br == 0 else 3)
                    box = {}

                    def pre():
                        for h_ in hooks_pre:
                            h_()

                    def s1():
                        L, Lb = next_L(hold=True)
                        box["L"] = (L, Lb)
                        mm(L[:, 0:256], KT[:, g, ksl], qT[:, 2 * g:2 * g + 2, qsl], True, False, [KT.b, qT.b], [Lb])
                        mm(L[:, 256:512], hT[:, (0 if br == 0 else 4) + g, ksl], qT[:, 2 * g:2 * g + 2, qsl], False, False, [kpad1, qT.b], [Lb])
                        if br == 0:
                            mm(L, expd[:, 1 if near else 0, kt, :], NM[:, g, buf, :], False, not near, [expd.b, NM.b], [Lb])
                        if near:
                            for hl in range(2):
                                mm(L, Jb[:, 0, :], hbt[:, dl, g, hl, :], False, hl == 1, [Jb.b, hbt.b], [Lb])

                    def s2():
                        L, Lb = box["L"]
                        release_L(Lb)
                        Et = next_E()
                        act(Et[:], L, AF.Exp, [Lb], [Et.b])
                        for r in range(4):
                            mm(bk(Ub)[:, r * 66:(r + 1) * 66], Et[:, colb(r):colb(r) + 128], VT[:, kt, g, 0:66],
                               first and r == 0, last and r == 3, [Et.b, VT.b], [bb(Ub)])

                    def post():
                        for h_ in hooks_post:
                            h_()

                    return dict(pre=pre, s1=s1, s2=s2, post=post, defer=defer, nm=None, first_slc=False)

                def mk_nm_hook(g, st, buf):
                    def hook():
                        imp = st["imp"]
                        M_, Mb = next_L()
                        trp(M_[0:32, 0:128], imp[:, :], ident_f[:, :], [imp.b, ident_f.b], [Mb])
                        cp("vector", NM[0:32, g, buf, :].rearrange("p (a s) -> p a s", a=4), fap(M_[0:32, 0:1], [[0, 4], [1, 128]]), [Mb], [NM.b])
                    return hook

                def mk_combine(qt, g, st, ybq):
                    def hook():
                        sm, ybacc, tmp1, tmp2 = st["sm"], st["ybacc"], st["tmp1"], st["tmp2"]
                        for br in range(2):
                            ub = st["Us"] if br == 0 else st["Uw"]
                            U = bk(ub)
                            rsb, wb_ = sm[:, 0, :], sm[:, 1 + br, :]
                            S.op("vector", lambda e, U=U, rsb=rsb: e.reciprocal(out=rsb, in_=fap(U[:, 64:65], [[66, 4]])), [bb(ub)], [sm.b])
                            tt("vector", wb_, rsb, sgate[:, qt, 16 * (br + 1) + 4 * g:16 * (br + 1) + 4 * g + 4], ALU.mult, [sm.b, sgate.b], [sm.b])
                            tgt = tmp1 if br == 0 else tmp2
                            tt("vector", tgt[:], fap(U[:, 0:1], [[66, 4], [1, 64]]), fap(wb_, [[1, 4], [0, 64]]), ALU.mult, [bb(ub), sm.b], [tgt.b])
                        tt("gpsimd", tmp1[:], tmp1[:], ybacc[:], ALU.add, [tmp1.b, ybacc.b], [tmp1.b])
                        tt("gpsimd", ybq[:, g * 256:(g + 1) * 256].rearrange("p (r d) -> p r d", r=4), tmp1[:], tmp2[:], ALU.add, [tmp1.b, tmp2.b], [ybq.b])
                    return hook

                def mk_ybout(qt, ybq):
                    def hook():
                        qsl = slice(qt * 128, (qt + 1) * 128)
                        li = next_Li()
                        pv = bank_bf(li)
                        for c in range(8):
                            trp(pv[:, c * 128:(c + 1) * 128], ybq[:, c * 128:(c + 1) * 128], ident_b[:], [ybq.b, ident_b.b], [bb(li)])
                        yT = ybT[qt % 2]
                        cp("scalar", yT[:], pv.rearrange("p (k s) -> p k s", k=8), [bb(li)], [yT.b])
                        dma("sync", ysc[1, :, qsl].rearrange("(c p) s -> p c s", p=128), yT[:], r=[yT.b], w=[b_ysc[1]])
                    return hook

                un = 0
                units = []
                for qt in range(1 if stop == 'nsa3' else NT):
                    ybq = ybt[qt % 2]
                    for g in range(4):
                        st = sets[un % 2]
                        buf = un % 2
                        un += 1
                        uc_ = [mk_cmp(qt, g, st)]
                        uc_[0]["nm"] = mk_nm_hook(g, st, buf)
                        uc_[0]["unit"] = len(units)
                        wk = list(range(max(0, qt - 2), qt + 1))
                        uw_ = [mk_tile(qt, g, st, 1, kt, kt == wk[0], kt == qt, buf, [], [], None) for kt in wk]
                        us_ = []
                        for kt in range(qt + 1):
                            hp = []
                            hq = [mk_combine(qt, g, st, ybq)] if kt == qt else []
                            df = mk_ybout(qt, ybq) if (kt == qt and g == 3) else None
                            us_.append(mk_tile(qt, g, st, 0, kt, kt == 0, kt == qt, buf, hp, hq, df))
                        us_[0]["first_slc"] = True
                        us_[0]["unit"] = len(units)
                        units.append((uc_, uw_, us_))
                tasks += units[0][0] + units[0][1]
                for ui in range(len(units)):
                    if ui + 1 < len(units):
                        tasks += units[ui + 1][0]
                    tasks += units[ui][2]
                    if ui + 1 < len(units):
                        tasks += units[ui + 1][1]
                for ui_ in range(min(PF, len(cb_dma))):
                    cb_dma[ui_]()
                deferred = {}
                ntk = len(tasks)
                LA = 2
                for j in range(min(LA, ntk)):
                    tasks[j]["pre"]()
                    tasks[j]["s1"]()
                first_idx = {tk["unit"]: i for i, tk in enumerate(tasks) if tk["first_slc"]}
                for i, tk in enumerate(tasks):
                    tk["s2"]()
                    tk["post"]()
                    if tk["nm"] is not None:
                        j = max(i, min(i + 8, first_idx[tk["unit"]] - LA))
                        deferred.setdefault(j, []).append(tk["nm"])
                    if tk["defer"] is not None:
                        deferred.setdefault(i + 3, []).append(tk["defer"])
                    for fn in deferred.pop(i, []):
                        fn()
                    if i + LA < ntk:
                        tasks[i + LA]["pre"]()
                        tasks[i + LA]["s1"]()
                for k_ in sorted(deferred):
                    for fn in deferred[k_]:
                        fn()
                dma("sync", hT.t.rearrange("p k s -> p (k s)"), hsc, r=[b_hsc], w=hT_b + [kpad1])
                S.barrier()

        def phase_tail(l, x_src, x_dst, b_xsrc, b_xdst):
            bk = lambda i: pbank[i][0]
            bb = lambda i: pbank[i][1]
            w2d = w_in[l]
            with ExitStack() as ph:
                mrgb = T(ph, "mrgb", [128, 8, SEQ], BF16)
                with ExitStack() as ph2:
                    mrg = T(ph2, "mrg", [128, 8, SEQ], F32)
                    yT = T(ph2, "m_yT", [128, 8, SEQ], BF16)
                    yb_ = [Buf(f"m_yT{c}") for c in range(8)]
                    wbr = [T(ph2, f"m_wbr{i}", [128, 8, 256], BF16) for i in range(2)]
                    wmg = [T(ph2, f"m_wmg{i}", [128, 8, 256], BF16) for i in range(2)]
                    sig = [T(ph2, f"m_sig{i}", [128, 512], F32) for i in range(2)]
                    prod = [T(ph2, f"m_prod{i}", [128, 512], F32) for i in range(2)]
                    n = 0
                    bn = 0
                    for br in range(3):
                        for c in range(8):
                            dma("sync", yT[:, c, :], ysc[br, c * 128:(c + 1) * 128, :], r=[b_ysc[br]], w=[yb_[c]])
                        for oc2 in range(4):
                            wb, wm = wbr[oc2 % 2], wmg[oc2 % 2]
                            load_slab(wb[:], w_branch[l, br], oc2 * 256, 256, wb.b)
                            load_slab(wm[:], w2d, C_MG + br * 1024 + oc2 * 256, 256, wm.b)
                            for o in range(2):
                                oc = oc2 * 2 + o
                                for sc in range(4):
                                    ssl = slice(sc * 512, (sc + 1) * 512)
                                    bB, bG = (bn % 4) * 2, (bn % 4) * 2 + 1
                                    bn += 1
                                    for c in range(8):
                                        mm(bk(bB), wb[:, c, o * 128:(o + 1) * 128], yT[:, c, ssl], c == 0, c == 7, [wb.b, yb_[c]], [bb(bB)])
                                    for kc in range(8):
                                        mm(bk(bG), wm[:, kc, o * 128:(o + 1) * 128], hT[:, kc, ssl], kc == 0, kc == 7, [wm.b] + hT_b[sc * 4:(sc + 1) * 4], [bb(bG)])
                                    sg_, pr_ = sig[n % 2], prod[n % 2]
                                    n += 1
                                    act(sg_[:], bk(bG), AF.Sigmoid, [bb(bG)], [sg_.b])
                                    if br == 0:
                                        tt("vector", mrg[:, oc, ssl], sg_[:], bk(bB), ALU.mult, [sg_.b, bb(bB)], [mrg.b])
                                    elif br == 1:
                                        tt("vector", pr_[:], sg_[:], bk(bB), ALU.mult, [sg_.b, bb(bB)], [pr_.b])
                                        tt("gpsimd", mrg[:, oc, ssl], mrg[:, oc, ssl], pr_[:], ALU.add, [mrg.b, pr_.b], [mrg.b])
                                    else:
                                        tt("vector", pr_[:], sg_[:], bk(bB), ALU.mult, [sg_.b, bb(bB)], [pr_.b])
                                        tt("gpsimd", mrgb[:, oc, ssl], mrg[:, oc, ssl], pr_[:], ALU.add, [mrg.b, pr_.b], [mrgb.b])
                    S.barrier()
                with ExitStack() as ph2:
                    wout = T(ph2, "p_wout", [128, 8, D], BF16)
                    g1 = load_gain(ph2, l, 1)
                    g2 = load_gain(ph2, l, 2)
                    xt = [T(ph2, f"p_xt{i}", [128, D], F32) for i in range(2)]
                    on = [T(ph2, f"p_on{i}", [128, D], F32) for i in range(2)]
                    hb = [T(ph2, f"p_hb{i}", [128, D], BF16) for i in range(2)]
                    junk = T(ph2, "p_junk", [128, D], BF16)
                    ss = T(ph2, "p_ss", [128, NT], F32)
                    rs = T(ph2, "p_rs", [128, NT], F32)
                    ss2 = T(ph2, "p_ss2", [128, NT], F32)
                    rs2 = T(ph2, "p_rs2", [128, NT], F32)
                    for half in range(2):
                        load_slab(wout[:, :, half * 512:(half + 1) * 512], w_out[l], half * 512, 512, wout.b)
                    for t in range(NT):
                        tsl = slice(t * 128, (t + 1) * 128)
                        b0 = (t % 2) * 2
                        ov = PA[:, b0 * 512:(b0 + 2) * 512]
                        obufs = [bb(b0), bb(b0 + 1)]
                        for half in range(2):
                            for c in range(8):
                                mm(bk(b0 + half), mrgb[:, c, tsl], wout[:, c, half * 512:(half + 1) * 512], c == 0, c == 7, [mrgb.b, wout.b], [bb(b0 + half)])
                        x_, o_ = xt[t % 2], on[t % 2]
                        dma("sync", x_[:], x_src[tsl, :], r=[b_xsrc], w=[x_.b])
                        act(junk[:], ov, AF.Square, obufs, [junk.b, ss.b], accum_out=ss[:, t:t + 1])
                        act(rs[:, t:t + 1], ss[:, t:t + 1], AF.Sqrt, [ss.b], [rs.b], scale=1.0 / D, bias=EPS)
                        recip(rs[:, t:t + 1], rs[:, t:t + 1], [rs.b], [rs.b])
                        stt(o_[:], ov, rs[:, t:t + 1], g1[:], ALU.mult, ALU.mult, obufs + [rs.b, g1.b], [o_.b])
                        tt("gpsimd", o_[:], o_[:], x_[:], ALU.add, [o_.b, x_.b], [o_.b])
                        dma("sync", xmid[tsl, :], o_[:], r=[o_.b], w=[b_xmid])
                        norm_transpose_tile(ph2, t, o_[:], o_.b, g2, ss2, rs2, hb[t % 2], junk, 4 + t % 2)
                    S.barrier()
            with ExitStack() as ph:
                hid = T(ph, "f_hid", [128, 22, SEQ], BF16)
                hid_b = [Buf(f"hid{j}") for j in range(22)]
                wfo = T(ph, "f_wfo", [128, 22, D], BF16)
                wfo_b = [Buf(f"wfo{j}") for j in range(11)]
                wfi = [T(ph, f"f_wfi{i}", [128, 8, 2, 128], BF16) for i in range(2)]
                sgt = [T(ph, f"f_sg{i}", [128, 512], F32) for i in range(2)]
                g3 = load_gain(ph, l, 3)
                xt = [T(ph, f"f_xt{i}", [128, D], F32) for i in range(2)]
                on = [T(ph, f"f_on{i}", [128, D], F32) for i in range(2)]
                junk = T(ph, "f_junk", [128, D], BF16)
                ss = T(ph, "f_ss", [128, NT], F32)
                rs = T(ph, "f_rs", [128, NT], F32)
                n = 0
                bn = 0
                for j in range(22):
                    wf = wfi[j % 2]
                    dma("gpsimd", wf[:, :, 0, :], w_ffn_in[l][:, j * 128:(j + 1) * 128].rearrange("(kc p) n -> p kc n", p=128), w=[wf.b])
                    dma("gpsimd", wf[:, :, 1, :], w_ffn_in[l][:, DFF + j * 128:DFF + (j + 1) * 128].rearrange("(kc p) n -> p kc n", p=128), w=[wf.b])
                    if j % 2 == 0:
                        jj = j // 2
                        dma("gpsimd", wfo[:, 2 * jj:2 * jj + 2, :], w_ffn_out[l][jj * 256:(jj + 1) * 256, :].rearrange("(j p) n -> p j n", p=128), w=[wfo_b[jj]])
                    for sc in range(4):
                        ssl = slice(sc * 512, (sc + 1) * 512)
                        bG, bU = (bn % 4) * 2, (bn % 4) * 2 + 1
                        bn += 1
                        for kc in range(8):
                            mm(bk(bG), wf[:, kc, 0, :], hT[:, kc, ssl], kc == 0, kc == 7, [wf.b] + hT_b[sc * 4:(sc + 1) * 4], [bb(bG)])
                        for kc in range(8):
                            mm(bk(bU), wf[:, kc, 1, :], hT[:, kc, ssl], kc == 0, kc == 7, [wf.b] + hT_b[sc * 4:(sc + 1) * 4], [bb(bU)])
                        sg_ = sgt[n % 2]
                        n += 1
                        act(sg_[:], bk(bG), AF.Silu, [bb(bG)], [sg_.b])
                        tt("vector", hid[:, j, ssl], sg_[:], bk(bU), ALU.mult, [sg_.b, bb(bU)], [hid_b[j]])
                for t in range(NT):
                    tsl = slice(t * 128, (t + 1) * 128)
                    b0 = (t % 2) * 2
                    ov = PA[:, b0 * 512:(b0 + 2) * 512]
                    obufs = [bb(b0), bb(b0 + 1)]
                    for half in range(2):
                        for j in range(22):
                            mm(bk(b0 + half), hid[:, j, tsl], wfo[:, j, half * 512:(half + 1) * 512], j == 0, j == 21, [hid_b[j], wfo_b[j // 2]], [bb(b0 + half)])
                    x_, o_ = xt[t % 2], on[t % 2]
                    dma("sync", x_[:], xmid[tsl, :], r=[b_xmid], w=[x_.b])
                    act(junk[:], ov, AF.Square, obufs, [junk.b, ss.b], accum_out=ss[:, t:t + 1])
                    act(rs[:, t:t + 1], ss[:, t:t + 1], AF.Sqrt, [ss.b], [rs.b], scale=1.0 / D, bias=EPS)
                    recip(rs[:, t:t + 1], rs[:, t:t + 1], [rs.b], [rs.b])
                    stt(o_[:], ov, rs[:, t:t + 1], g3[:], ALU.mult, ALU.mult, obufs + [rs.b, g3.b], [o_.b])
                    tt("gpsimd", o_[:], o_[:], x_[:], ALU.add, [o_.b, x_.b], [o_.b])
                    dma("sync", x_dst[tsl, :], o_[:], r=[o_.b], w=[b_xdst])
                S.barrier()

        setup_tables()
        for l in range(n_layers):
            phase_A(l, x_in if l == 0 else xres)
            import os
            if not os.environ.get("SKIP_LG"):
                phase_lru(l)
                if stop == "lru":
                    break
                phase_gla(l)
                if stop == "gla":
                    break
            phase_nsa(l)
            if stop is not None and stop.startswith("nsa"):
                break
            last = (l == n_layers - 1)
            phase_tail(l, x_in if l == 0 else xres, y_out if last else xres, Buf() if l == 0 else b_xres, Buf() if last else b_xres)
        S.barrier()
        S.emit()
    return nc, consts


_CACHE = {}


def kernel(**inputs):
    if "nc" not in _CACHE:
        _CACHE["nc"] = build()
    nc, consts = _CACHE["nc"]
    x = np.ascontiguousarray(np.asarray(inputs["x"], dtype=np.float32))
    shared = {k: np.ascontiguousarray(np.asarray(v, dtype=np.float32)) for k, v in inputs.items() if k != "x"}
    for k, v in consts.items():
        shared["c_" + k] = v
    in_maps = [dict(shared, x=x[i]) for i in range(8)]
    res = run_bass_kernel_spmd(nc, in_maps, core_ids=list(range(8)))
    return np.stack([np.asarray(r["out"], dtype=np.float32) for r in res.results], axis=0)
```

```python
import os
import numpy as np
from contextlib import ExitStack
import concourse.bass as bass
import concourse.mybir as mybir
from concourse.bass_utils import run_bass_kernel_spmd

F32 = mybir.dt.float32
BF16 = mybir.dt.bfloat16
ALU = mybir.AluOpType
AF = mybir.ActivationFunctionType
AX = mybir.AxisListType

SEQ = 2048
D = 1024
NT = 16
DEPTH = 2
EPS = 1e-6
IN_W = 10816
C_LRUX, C_LRUG, C_Q, C_KV, C_GATE, C_GQ, C_GK, C_GV, C_GOG, C_GLR, C_MG = 0, 1024, 2048, 3072, 4608, 4656, 5168, 5680, 6704, 7728, 7744
DFF = 2816
NEG = -30000.0


class Buf:
    __slots__ = ("name", "w", "r", "excl")

    def __init__(self, name="", excl=False):
        self.name = name
        self.w = None
        self.r = []
        self.excl = excl


class Sched:
    ENG = ("sync", "scalar", "vector", "gpsimd", "tensor")
    DMAQ = ("sync", "gpsimd", "scalar")

    def __init__(self, nc, es, n_dma_sems=12):
        self.nc = nc
        self.q = {e: [] for e in self.ENG}
        self.cnt = {e: 0 for e in self.ENG}
        self.sems = []
        self.esem = {}
        for e in self.ENG:
            self.esem[e] = len(self.sems)
            self.sems.append(es.enter_context(nc.semaphore("s_" + e)))
        self.known = {e: {} for e in self.ENG}
        self.dpool = {}
        self.dcnt = {}
        self.dlast = {}
        for qn in self.DMAQ:
            self.dpool[qn] = []
            for i in range(n_dma_sems):
                self.dpool[qn].append(len(self.sems))
                self.sems.append(es.enter_context(nc.semaphore(f"d_{qn}_{i}")))
            self.dcnt[qn] = 0
        self.K = n_dma_sems

    def _waits(self, eng, r, w):
        waits = {}
        kn = self.known[eng]
        own_pe = self.esem["tensor"] if eng == "tensor" else -1

        def need(kv):
            k, v = kv
            if k == own_pe:
                return
            if kn.get(k, 0) < v and waits.get(k, 0) < v:
                waits[k] = v

        own = self.esem.get(eng, -2)
        for b in r:
            if b.w is not None:
                need(b.w)
            if b.excl:
                for x in b.r:
                    if x[0] != own:
                        need(x)
        for b in w:
            if b.w is not None:
                need(b.w)
            for x in b.r:
                need(x)
        for k, v in waits.items():
            kn[k] = v
        return list(waits.items())

    _rec = None

    def record(self, fn):
        self._rec = []
        fn()
        r, self._rec = self._rec, None
        return r

    def replay(self, lists):
        idx = [0] * len(lists)
        live = True
        while live:
            live = False
            for j, lst in enumerate(lists):
                if idx[j] < len(lst):
                    kind, args, kw = lst[idx[j]]
                    idx[j] += 1
                    live = True
                    if kind == "op":
                        self.op(*args)
                    else:
                        self.dma(*args, **kw)

    def op(self, eng, fn, r=(), w=()):
        if self._rec is not None:
            self._rec.append(("op", (eng, fn, list(r), list(w)), {}))
            return
        waits = self._waits(eng, r, w)
        self.cnt[eng] += 1
        seq = self.cnt[eng]
        k = self.esem[eng]
        self.q[eng].append((waits, fn, (k, 1)))
        for b in w:
            b.w = (k, seq)
            b.r = []
        for b in r:
            if b not in w:
                b.r.append((k, seq))
                if len(b.r) > 24:
                    b.r = b.r[-24:] if False else self._compact(b.r)

    @staticmethod
    def _compact(lst):
        d = {}
        for k, v in lst:
            if d.get(k, 0) < v:
                d[k] = v
        return list(d.items())

    def dma(self, qn, out, in_, r=(), w=(), **kw):
        if self._rec is not None:
            self._rec.append(("dma", (qn, out, in_, list(r), list(w)), kw))
            return
        waits = self._waits(qn, r, w)
        i = self.dcnt[qn]
        self.dcnt[qn] += 1
        k = self.dpool[qn][i % self.K]
        val = 16 * (i // self.K + 1)
        if val > 16 and self.known[qn].get(k, 0) < val - 16:
            waits.append((k, val - 16))
            self.known[qn][k] = val - 16
        self.dlast[k] = val
        self.q[qn].append((waits, lambda e: e.dma_start(out=out, in_=in_, **kw), (k, 16)))
        for b in w:
            b.w = (k, val)
            b.r = []
        for b in r:
            if b not in w:
                b.r.append((k, val))
                if len(b.r) > 24:
                    b.r = self._compact(b.r)

    def pe_drain(self):
        k = self.esem["tensor"]
        if self.cnt["tensor"] > 0:
            self.q["tensor"].append(([(k, self.cnt["tensor"])], None, None))

    def barrier(self):
        tgt = [(self.esem[e], self.cnt[e]) for e in self.ENG if self.cnt[e] > 0]
        tgt += list(self.dlast.items())
        for e in self.ENG:
            waits = []
            for k, v in tgt:
                if e == "tensor" and k == self.esem["tensor"]:
                    continue
                if self.known[e].get(k, 0) < v:
                    waits.append((k, v))
                    self.known[e][k] = v
            if waits:
                self.q[e].append((waits, None, None))

    def emit(self):
        nc = self.nc
        with nc.Block() as block:
            for e in self.ENG:
                def body(eng, _e=e):
                    for waits, fn, inc in self.q[_e]:
                        for k, v in waits:
                            eng.wait_ge(self.sems[k], v)
                        if fn is not None:
                            ins = fn(eng)
                            ins.then_inc(self.sems[inc[0]], inc[1])
                getattr(block, e)(body)


def fap(a, dims):
    return bass.AP(a.tensor, a.offset, [list(a.ap[0])] + [list(d) for d in dims])


def _rel_bucket(d):
    d = np.asarray(d)
    n = np.maximum(d, 0)
    nf = np.maximum(n, 16).astype(np.float32)
    large = 16 + (np.log(nf / np.float32(16)) / np.float32(np.log(128 / 16)) * np.float32(16)).astype(np.int32)
    large = np.minimum(large, 31)
    return np.where(n < 16, n, large)


def host_consts():
    c = {}
    c["ident"] = np.eye(128, dtype=np.float32)
    c["antiid"] = np.eye(128, dtype=np.float32)[::-1].copy()
    aid127 = np.zeros((128, 128), np.float32)
    for i in range(127):
        aid127[i, 126 - i] = 1.0
    aid127[127, 127] = 1.0
    c["antiid127"] = aid127
    s = np.arange(128)
    c["triu"] = (s[:, None] <= s[None, :]).astype(np.float32)
    c["tril"] = (s[:, None] > s[None, :]).astype(np.float32)
    def oh(deltas, valid):
        m = np.zeros((33, len(deltas)), np.float32)
        b = _rel_bucket(deltas)
        for i, (dd, v) in enumerate(zip(deltas, valid)):
            if v:
                m[b[i], i] = 1.0
            else:
                m[32, i] = 1.0
        return m
    dc = np.arange(-2048, 2048)
    c["oh_c"] = oh(dc, dc >= 0)
    ds = np.arange(-512, 512)
    c["oh_s"] = oh(ds, ds >= 0)
    c["oh_w"] = oh(ds, (ds >= 0) & (ds < 256))
    cs = np.arange(127) * 16
    js = np.arange(32) * 64
    ov = np.clip(np.minimum(cs[:, None] + 32, js[None, :] + 64) - np.maximum(cs[:, None], js[None, :]), 0, None).astype(np.float32) / 32.0
    ovx = np.zeros((128, 33), np.float32)
    ovx[:127, :32] = ov
    ovx[:127, 32] = 1.0
    c["ovx"] = ovx
    pos = np.arange(SEQ)
    cur = pos // 64
    blk = np.arange(32)[None, :]
    cand = (blk >= 1) & (blk <= cur[:, None] - 2)
    forced = (blk == 0) | (blk == cur[:, None]) | (blk == cur[:, None] - 1)
    c["cand"] = cand.astype(np.float32).reshape(NT, 128, 32).transpose(1, 0, 2).copy()
    c["negc"] = ((cand.astype(np.float32) - 1.0) * 1e4).reshape(NT, 128, 32).transpose(1, 0, 2).copy()
    c["forced"] = forced.astype(np.float32).reshape(NT, 128, 32).transpose(1, 0, 2).copy()
    ex = np.zeros((128, NT, 128), np.float32)
    for kt in range(NT):
        for key in range(128):
            ex[2 * kt + key // 64, kt, key] = 1.0
    c["expand_near"] = ex.copy()
    ex[32:34] = 1.0
    c["expand"] = ex
    return c


CONST_SHAPES = None


def build(debug=False, n_layers=DEPTH, stop=None):
    nc = bass.Bass("TRN2", target_bir_lowering=False)
    consts = host_consts()
    din = {}

    def inp(name, shape, dt=F32):
        din[name] = nc.dram_tensor(name, list(shape), dt, kind="ExternalInput").ap()
        return din[name]

    x_in = inp("x", [SEQ, D])
    rel_table = inp("rel_table", [32, 16])
    norm_g = inp("norm_g", [DEPTH, 4, D])
    w_in = inp("w_in", [DEPTH, D, IN_W])
    conv_w = inp("conv_w", [DEPTH, 4, D])
    conv_b = inp("conv_b", [DEPTH, D])
    lru_wg = inp("lru_w_gates", [DEPTH, 2, 8, 128, 128])
    lru_bg = inp("lru_b_gates", [DEPTH, 2, D])
    lru_lam = inp("lru_lambda", [DEPTH, D])
    cmp_pos = inp("cmp_pos", [DEPTH, 2, 32, 64])
    cmp_w1 = inp("cmp_w1", [DEPTH, 2, 2048, 256])
    cmp_w2 = inp("cmp_w2", [DEPTH, 2, 256, 64])
    gla_wa2 = inp("gla_wa2", [DEPTH, 16, 512])
    gla_ba = inp("gla_ba", [DEPTH, 512])
    gla_norm = inp("gla_norm", [DEPTH, 256])
    w_branch = inp("w_branch", [DEPTH, 3, D, D])
    w_out = inp("w_out", [DEPTH, D, D])
    w_ffn_in = inp("w_ffn_in", [DEPTH, D, 2 * DFF])
    w_ffn_out = inp("w_ffn_out", [DEPTH, DFF, D])
    cin = {k: inp("c_" + k, v.shape) for k, v in consts.items()}

    okind = "ExternalOutput"
    y_out = nc.dram_tensor("out", [SEQ, D], F32, kind=okind).ap()
    skind = "ExternalOutput"
    xres = nc.dram_tensor("xres", [SEQ, D], F32, kind=skind).ap()
    xmid = nc.dram_tensor("xmid", [SEQ, D], F32, kind=skind).ap()
    ysc = nc.dram_tensor("ysc", [3, D, SEQ], BF16, kind=skind).ap()
    tc_d = nc.dram_tensor("tc_d", [2, 16, 4096], BF16, kind="Internal").ap()
    ts_d = nc.dram_tensor("ts_d", [2, 16, 1024], BF16, kind="Internal").ap()
    tw_d = nc.dram_tensor("tw_d", [2, 16, 1024], BF16, kind="Internal").ap()
    hsc = nc.dram_tensor("hsc", [128, 8 * SEQ], BF16, kind=skind).ap()
    b_hsc = Buf()
    b_xres, b_xmid, b_ysc, b_tabs = Buf(), Buf(), [Buf(), Buf(), Buf()], Buf()

    with ExitStack() as es:
        S = Sched(nc, es)
        es.enter_context(nc.allow_non_contiguous_dma(reason="small param loads"))

        def mm(out, lhsT, rhs, start, stop, r, w):
            S.op("tensor", lambda e: e.matmul(out, lhsT=lhsT, rhs=rhs, start=start, stop=stop), r, w)

        def trp(out, in_, ident, r, w):
            S.op("tensor", lambda e: e.transpose(out, in_, ident), r, w)

        def act(out, in_, func, r, w, **kw):
            S.op("scalar", lambda e: e.activation(out=out, in_=in_, func=func, **kw), r, w)

        def tt(eng, out, in0, in1, op, r, w):
            S.op(eng, lambda e: e.tensor_tensor(out=out, in0=in0, in1=in1, op=op), r, w)

        def tsc(eng, out, in0, s1, op0, r, w, s2=None, op1=None):
            if op1 is None:
                S.op(eng, lambda e: e.tensor_scalar(out=out, in0=in0, scalar1=s1, scalar2=None, op0=op0), r, w)
            else:
                S.op(eng, lambda e: e.tensor_scalar(out=out, in0=in0, scalar1=s1, scalar2=s2, op0=op0, op1=op1), r, w)

        def stt(out, in0, scalar, in1, op0, op1, r, w):
            S.op("vector", lambda e: e.scalar_tensor_tensor(out=out, in0=in0, scalar=scalar, in1=in1, op0=op0, op1=op1), r, w)

        def cp(eng, out, in_, r, w):
            if eng == "scalar":
                S.op("scalar", lambda e: e.copy(out=out, in_=in_), r, w)
            else:
                S.op(eng, lambda e: e.tensor_copy(out=out, in_=in_), r, w)

        def recip(out, in_, r, w):
            S.op("vector", lambda e: e.reciprocal(out=out, in_=in_), r, w)

        def memset(eng, ap, val, w):
            S.op(eng, lambda e: e.memset(ap, val), (), w)

        def dma(q, out, in_, r=(), w=()):
            S.dma(q, out, in_, r, w)

        class T:
            _n = [0]

            def __init__(self, stack, name, shape, dt, psum=False):
                T._n[0] += 1
                name = f"{name}_{T._n[0]}"
                self.t = stack.enter_context((nc.psum_tensor if psum else nc.sbuf_tensor)(name, list(shape), dt))
                self.b = Buf(name)

            def __getitem__(self, idx):
                return self.t[idx]

        PA = T(es, "PA", [128, 2048], F32, psum=True)
        PB = T(es, "PB", [128, 2048], F32, psum=True)
        pbank = []
        for i in range(8):
            src = PA if i < 4 else PB
            pbank.append((src.t[:, (i % 4) * 512:(i % 4 + 1) * 512], Buf(f"bank{i}", excl=True)))
        PAb = PA.t.bitcast(BF16)
        PBb = PB.t.bitcast(BF16)

        def bank_bf(i):
            src = PAb if i < 4 else PBb
            return src[:, (i % 4) * 1024:(i % 4 + 1) * 1024]

        ident_f = T(es, "ident_f", [128, 128], F32)
        ident_b = T(es, "ident_b", [128, 128], BF16)
        dma("sync", ident_f[:], cin["ident"], w=[ident_f.b])
        cp("vector", ident_b[:], ident_f[:], [ident_f.b], [ident_b.b])

        hT = T(es, "hT", [128, 8, SEQ], BF16)
        hT_b = [Buf(f"hT{t}") for t in range(NT)]

        def load_gain(ph, l, i):
            gt = T(ph, f"gain{i}", [128, D], F32)
            src = norm_g[l, i:i + 1, :]
            dma("sync", gt[:], bass.AP(src.tensor, src.offset, [[0, 128], [1, D]]), w=[gt.b])
            return gt

        def norm_transpose_tile(ph, t, xt_ap, xt_buf, gt, ss, rs, hb, junk, pbi):
            act(junk[:], xt_ap, AF.Square, [xt_buf], [junk.b, ss.b], accum_out=ss[:, t:t + 1])
            act(rs[:, t:t + 1], ss[:, t:t + 1], AF.Sqrt, [ss.b], [rs.b], scale=1.0 / D, bias=EPS)
            recip(rs[:, t:t + 1], rs[:, t:t + 1], [rs.b], [rs.b])
            stt(hb[:], xt_ap, rs[:, t:t + 1], gt[:], ALU.mult, ALU.mult, [xt_buf, rs.b, gt.b], [hb.b])
            pv, pbuf = bank_bf(pbi), pbank[pbi][1]
            for kc in range(8):
                trp(pv[:, kc * 128:(kc + 1) * 128], hb[:, kc * 128:(kc + 1) * 128], ident_b[:], [hb.b, ident_b.b], [pbuf])
            cp("scalar", hT[:, :, t * 128:(t + 1) * 128], pv.rearrange("p (k s) -> p k s", k=8), [pbuf], [hT_b[t]])

        def load_slab(dst_ap, w2d, c0, ncols, wbuf, nk=8):
            src = w2d[:, c0:c0 + ncols].rearrange("(kc p) n -> p kc n", p=128)
            dma("gpsimd", dst_ap, src, w=[wbuf])

        def proj_fm(wslab, wbuf, col_off, M, rhs_tile, rhs_bufs, out_banks, nk=8, sc_list=(0, 1, 2, 3)):
            for i, sc in enumerate(sc_list):
                pa, pb_ = out_banks[i]
                for kc in range(nk):
                    mm(pa[0:M, :], wslab[:, kc, col_off:col_off + M], rhs_tile[:, kc, sc * 512:(sc + 1) * 512],
                       kc == 0, kc == nk - 1, [wbuf] + rhs_bufs[sc * 4:(sc + 1) * 4], [pb_])

        def phase_A(l, x_src):
            with ExitStack() as ph:
                xt = [T(ph, f"xtA{i}", [128, D], F32) for i in range(2)]
                hb = [T(ph, f"hbA{i}", [128, D], BF16) for i in range(2)]
                junk = T(ph, "junkA", [128, D], BF16)
                ss = T(ph, "ssA", [128, NT], F32)
                rs = T(ph, "rsA", [128, NT], F32)
                g0 = load_gain(ph, l, 0)
                for t in range(NT):
                    dma("sync", xt[t % 2][:], x_src[t * 128:(t + 1) * 128, :], r=[b_xres], w=[xt[t % 2].b])
                    norm_transpose_tile(ph, t, xt[t % 2][:], xt[t % 2].b, g0, ss, rs, hb[t % 2], junk, t % 2)
                S.barrier()

        def phase_lru(l):
            with ExitStack() as ph:
                prow = T(ph, "prow", [8, D], F32)
                lpT = T(ph, "lpT", [128, 8, 8], F32)
                sp = T(ph, "lru_sp", [128, 8, 6], F32)
                wg = T(ph, "lru_wg", [128, 2, 8, 128], BF16)
                slab = [T(ph, f"lslab{i}", [128, 8, 2, 128], BF16) for i in range(2)]
                XA = [T(ph, f"XA{i}", [128, SEQ + 4], F32) for i in range(2)]
                XC = [T(ph, f"XC{i}", [128, SEQ], F32) for i in range(2)]
                XCB = [T(ph, f"XCB{i}", [128, SEQ], BF16) for i in range(2)]
                R = [T(ph, f"R{i}", [128, SEQ], F32) for i in range(2)]
                A = [T(ph, f"A{i}", [128, SEQ], F32) for i in range(2)]
                I = [T(ph, f"I{i}", [128, SEQ], F32) for i in range(2)]
                H = [T(ph, f"H{i}", [128, SEQ], F32) for i in range(2)]
                GA = [T(ph, f"GA{i}", [128, SEQ], F32) for i in range(2)]
                G = [T(ph, f"G{i}", [128, SEQ], F32) for i in range(2)]
                YA = [T(ph, f"YA{i}", [128, SEQ], BF16) for i in range(2)]
                if os.environ.get("SBUF_DBG"):
                    print("LRU sbuf remaining", nc.sbuf_bytes_remaining)
                for k in range(4):
                    dma("sync", prow[k:k + 1, :], conv_w[l, k:k + 1, :], w=[prow.b])
                dma("sync", prow[4:5, :], conv_b[l:l + 1, :], w=[prow.b])
                dma("sync", prow[5:7, :], lru_bg[l], w=[prow.b])
                dma("sync", prow[7:8, :], lru_lam[l:l + 1, :], w=[prow.b])
                pv, pbuf = pbank[7]
                for c in range(8):
                    trp(pv[:, c * 8:(c + 1) * 8], prow[0:8, c * 128:(c + 1) * 128], ident_f[0:8, 0:8], [prow.b, ident_f.b], [pbuf])
                cp("vector", lpT[:], pv[:, 0:64].rearrange("p (c k) -> p c k", c=8), [pbuf], [lpT.b])
                xs, ln1, ser, msk, nsp8, nsp16 = (sp[:, :, i] for i in range(6))
                act(xs, lpT[:, :, 7], AF.Exp, [lpT.b], [sp.b], scale=-1.0)
                act(ln1, xs, AF.Ln, [sp.b], [sp.b], bias=1.0)
                tsc("vector", ser, xs, -0.25, ALU.mult, [sp.b], [sp.b], 1.0 / 3.0, ALU.add)
                tt("vector", ser, ser, xs, ALU.mult, [sp.b], [sp.b])
                tsc("vector", ser, ser, -1.0, ALU.mult, [sp.b], [sp.b], 0.5, ALU.add)
                tt("vector", ser, ser, xs, ALU.mult, [sp.b], [sp.b])
                tsc("vector", ser, ser, -1.0, ALU.mult, [sp.b], [sp.b], 1.0, ALU.add)
                tt("vector", ser, ser, xs, ALU.mult, [sp.b], [sp.b])
                tsc("vector", msk, xs, 0.03, ALU.is_lt, [sp.b], [sp.b])
                tt("vector", ser, ser, ln1, ALU.subtract, [sp.b], [sp.b])
                tt("vector", ser, ser, msk, ALU.mult, [sp.b], [sp.b])
                tt("vector", ser, ser, ln1, ALU.add, [sp.b], [sp.b])
                tsc("vector", nsp8, ser, -8.0, ALU.mult, [sp.b], [sp.b])
                tsc("vector", nsp16, ser, -16.0, ALU.mult, [sp.b], [sp.b])
                dma("gpsimd", wg[:], lru_wg[l].rearrange("k n c e -> c k n e"), w=[wg.b])
                for p_ in range(2):
                    memset("vector", XA[p_][:, 0:3], 0.0, [XA[p_].b])
                w2d = w_in[l]

                def lru_A(c):
                    p = c % 2
                    xa, xc, xcb, r_, i_, ga = XA[p], XC[p], XCB[p], R[p], I[p], GA[p]
                    sl = slab[p]
                    dma("gpsimd", sl[:, :, 0, :], w2d[:, C_LRUX + c * 128:C_LRUX + (c + 1) * 128].rearrange("(kc p) n -> p kc n", p=128), w=[sl.b])
                    dma("gpsimd", sl[:, :, 1, :], w2d[:, C_LRUG + c * 128:C_LRUG + (c + 1) * 128].rearrange("(kc p) n -> p kc n", p=128), w=[sl.b])
                    slv = sl.t.rearrange("p k a n -> p k (a n)")
                    proj_fm(slv, sl.b, 0, 128, hT.t, hT_b, pbank[0:4])
                    cp("scalar", xa[:, 3:3 + SEQ], PA[:, :], [pbank[i][1] for i in range(4)], [xa.b])
                    cw = lambda k: lpT[:, c, k:k + 1]
                    act(xc[:], xa[:, 3:3 + SEQ], AF.Identity, [xa.b, lpT.b], [xc.b], scale=cw(3), bias=cw(4))
                    for k in range(3):
                        stt(xc[:], xa[:, k:k + SEQ], cw(k), xc[:], ALU.mult, ALU.add, [xa.b, lpT.b, xc.b], [xc.b])
                    cp("gpsimd", xcb[:], xc[:], [xc.b], [xcb.b])
                    for gk in range(2):
                        banks = pbank[4:8] if gk == 0 else pbank[0:4]
                        for sc in range(4):
                            mm(banks[sc][0], wg[:, gk, c, :], xcb[:, sc * 512:(sc + 1) * 512], True, True, [wg.b, xcb.b], [banks[sc][1]])
                    act(r_[:], PB[:, :], AF.Sigmoid, [pbank[i][1] for i in range(4, 8)], [r_.b], bias=lpT[:, c, 5:6])
                    act(i_[:], PA[:, :], AF.Sigmoid, [pbank[i][1] for i in range(4)], [i_.b], bias=lpT[:, c, 6:7])
                    proj_fm(slv, sl.b, 128, 128, hT.t, hT_b, pbank[4:8])
                    cp("scalar", ga[:], PB[:, :], [pbank[i][1] for i in range(4, 8)], [ga.b])

                def lru_B(c):
                    p = c % 2
                    xc, r_, a_, i_, h_, ga, g_ = XC[p], R[p], A[p], I[p], H[p], GA[p], G[p]
                    act(a_[:], r_[:], AF.Exp, [r_.b, sp.b], [a_.b], scale=sp[:, c, 4:5])
                    act(r_[:], r_[:], AF.Exp, [r_.b, sp.b], [r_.b], scale=sp[:, c, 5:6])
                    act(r_[:], r_[:], AF.Identity, [r_.b], [r_.b], scale=-1.0, bias=1.0)
                    act(r_[:], r_[:], AF.Sqrt, [r_.b], [r_.b])
                    tt("gpsimd", i_[:], i_[:], xc[:], ALU.mult, [i_.b, xc.b], [i_.b])
                    tt("gpsimd", i_[:], i_[:], r_[:], ALU.mult, [i_.b, r_.b], [i_.b])
                    S.op("vector", lambda e, h_=h_, a_=a_, i_=i_: e.tensor_tensor_scan(out=h_[:], data0=a_[:], data1=i_[:], initial=0.0, op0=ALU.mult, op1=ALU.add),
                         [a_.b, i_.b], [h_.b])
                    act(g_[:], ga[:], AF.Square, [ga.b], [g_.b])
                    tsc("vector", g_[:], g_[:], 0.044715, ALU.mult, [g_.b], [g_.b], 1.0, ALU.add)
                    tt("gpsimd", g_[:], g_[:], ga[:], ALU.mult, [g_.b, ga.b], [g_.b])
                    act(g_[:], g_[:], AF.Sigmoid, [g_.b], [g_.b], scale=1.5957691216057308)
                    tt("gpsimd", g_[:], g_[:], ga[:], ALU.mult, [g_.b, ga.b], [g_.b])
                    ya = YA[p]
                    tt("vector", ya[:], g_[:], h_[:], ALU.mult, [g_.b, h_.b], [ya.b])
                    dma("sync", ysc[0, c * 128:(c + 1) * 128, :], ya[:], r=[ya.b], w=[b_ysc[0]])

                lru_A(0)
                for c in range(8):
                    lists = [S.record(lambda: lru_B(c))]
                    if c + 1 < 8:
                        lists.append(S.record(lambda: lru_A(c + 1)))
                    S.replay(lists)
                S.barrier()

        def phase_gla(l):
            w2d = w_in[l]
            with ExitStack() as ph:
                qT = T(ph, "gqT", [128, 4, SEQ], F32)
                kT = T(ph, "gkT", [128, 4, SEQ], F32)
                lrT = T(ph, "lrT", [32, SEQ], F32)
                wa2x = T(ph, "wa2x", [32, 512], F32)
                wres = T(ph, "gwres", [128, 8, 2560], BF16)
                gnb = T(ph, "gnb", [128, 4, 256], F32)
                st_f = T(ph, "st_f", [128, 4, 256], F32)
                st_b = T(ph, "st_b", [128, 4, 256], BF16)
                cm4 = T(ph, "cm4", [128, 4, 128], F32)
                triu = T(ph, "triu", [128, 128], F32)
                tril = T(ph, "tril", [128, 128], F32)
                dma("sync", triu[:], cin["triu"], w=[triu.b])
                dma("sync", tril[:], cin["tril"], w=[tril.b])
                for hh in range(4):
                    dma("sync", cm4[:, hh, :], cin["triu"], w=[cm4.b])
                    src = gla_norm[l:l + 1, :]
                    dma("sync", gnb[:, hh, :], bass.AP(src.tensor, src.offset, [[0, 128], [1, 256]]), w=[gnb.b])
                memset("vector", wa2x[:], 0.0, [wa2x.b])
                memset("vector", lrT[:], 1.0, [lrT.b])
                dma("sync", wa2x[0:16, :], gla_wa2[l], w=[wa2x.b])
                dma("sync", wa2x[16:17, :], gla_ba[l:l + 1, :], w=[wa2x.b])
                for i, c0 in enumerate((C_GK, C_GV, C_GV + 512, C_GOG, C_GOG + 512)):
                    load_slab(wres[:, :, i * 512:(i + 1) * 512], w2d, c0, 512, wres.b)
                with ExitStack() as ph2:
                    slab = [T(ph2, f"gslab{i}", [128, 8, 512], BF16) for i in range(2)]
                    lslab = T(ph2, "glslab", [128, 8, 16], BF16)
                    load_slab(slab[0][:], w2d, C_GQ, 512, slab[0].b)
                    load_slab(slab[1][:], w2d, C_GK, 512, slab[1].b)
                    load_slab(lslab[:], w2d, C_GLR, 16, lslab.b)
                    for i in range(8):
                        banks = pbank[0:4] if i % 2 == 0 else pbank[4:8]
                        src = PA if i % 2 == 0 else PB
                        proj_fm(slab[i // 4].t, slab[i // 4].b, (i % 4) * 128, 128, hT.t, hT_b, banks)
                        dst = qT if i < 4 else kT
                        act(dst[:, i % 4, :], src[:, :], AF.Copy, [b for _, b in banks], [dst.b], scale=(128 ** -0.5 if i < 4 else 1.0))
                    proj_fm(lslab.t, lslab.b, 0, 16, hT.t, hT_b, pbank[0:4])
                    cp("vector", lrT[0:16, :], PA[0:16, :], [b for _, b in pbank[0:4]], [lrT.b])
                    S.barrier()
                sp_t = T(ph, "g_sp", [128, 512], F32)
                E1 = [T(ph, f"g_E1{i}", [128, 512], F32) for i in range(2)]
                E2 = T(ph, "g_E2", [128, 512], F32)
                Erb = T(ph, "g_Erb", [128, 512], F32)
                qtb = [T(ph, f"g_qtb{i}", [128, 4, 128], BF16) for i in range(2)]
                ktb = T(ph, "g_ktb", [128, 4, 128], BF16)
                kend = [T(ph, f"g_kend{i}", [128, 512], BF16) for i in range(2)]
                v_bf = [T(ph, f"g_vbf{i}", [128, 1024], BF16) for i in range(2)]
                sg = [T(ph, f"g_sg{i}", [128, 1024], F32) for i in range(2)]
                attm = [T(ph, f"g_attm{i}", [128, 4, 128], BF16) for i in range(2)]
                on = T(ph, "g_on", [128, 1024], F32)
                yc = [T(ph, f"g_yc{i}", [128, 1024], BF16) for i in range(2)]
                ycT = [T(ph, f"g_ycT{i}", [128, 8, 128], BF16) for i in range(2)]
                junk = T(ph, "g_junk", [128, 256], BF16)
                ssq = T(ph, "g_ssq", [128, 4], F32)
                rst = T(ph, "g_rst", [128, 4], F32)
                if os.environ.get("SBUF_DBG"):
                    print("GLA sbuf remaining", nc.sbuf_bytes_remaining)
                bk = lambda i: pbank[i][0]
                bb = lambda i: pbank[i][1]

                def gla_A(t):
                    p = t % 2
                    tsl = slice(t * 128, (t + 1) * 128)
                    e1, qb, ke, vb, sg_, am = E1[p], qtb[p], kend[p], v_bf[p], sg[p], attm[p]
                    mm(bk(0), lrT[0:17, tsl], wa2x[0:17, :], True, True, [lrT.b, wa2x.b], [bb(0)])
                    act(sp_t[:], bk(0), AF.Exp, [bb(0)], [sp_t.b], scale=-1.0)
                    act(sp_t[:], sp_t[:], AF.Ln, [sp_t.b], [sp_t.b], bias=1.0)
                    for hh in range(4):
                        mm(bk(1)[:, hh * 128:(hh + 1) * 128], sp_t[:, hh * 128:(hh + 1) * 128], triu[:], True, True, [sp_t.b, triu.b], [bb(1)])
                    mm(bk(2), tril[:], sp_t[:], True, True, [tril.b, sp_t.b], [bb(2)])
                    act(e1[:], bk(1), AF.Exp, [bb(1)], [e1.b], scale=-1.0 / 16.0)
                    act(E2[:], bk(1), AF.Exp, [bb(1)], [E2.b], scale=1.0 / 16.0)
                    act(Erb[:], bk(2), AF.Exp, [bb(2)], [Erb.b], scale=-1.0 / 16.0)
                    tt("vector", qb[:], qT[:, :, tsl], e1.t.rearrange("p (h s) -> p h s", h=4), ALU.mult, [qT.b, e1.b], [qb.b])
                    tt("gpsimd", ktb[:], kT[:, :, tsl], E2.t.rearrange("p (h s) -> p h s", h=4), ALU.mult, [kT.b, E2.b], [ktb.b])
                    for kc in range(8):
                        mm(bk(3), hT[:, kc, tsl], wres[:, kc, 0:512], kc == 0, kc == 7, [hT_b[t], wres.b], [bb(3)])
                    tt("vector", ke[:], bk(3), Erb[:], ALU.mult, [bb(3), Erb.b], [ke.b])
                    for half in range(2):
                        for kc in range(8):
                            mm(bk(half), hT[:, kc, tsl], wres[:, kc, 512 + half * 512:1024 + half * 512], kc == 0, kc == 7, [hT_b[t], wres.b], [bb(half)])
                    cp("scalar", vb[:], PA[:, 0:1024], [bb(0), bb(1)], [vb.b])
                    for half in range(2):
                        for kc in range(8):
                            mm(bk(2 + half), hT[:, kc, tsl], wres[:, kc, 1536 + half * 512:2048 + half * 512], kc == 0, kc == 7, [hT_b[t], wres.b], [bb(2 + half)])
                    act(sg_[:], PA[:, 1024:2048], AF.Silu, [bb(2), bb(3)], [sg_.b])
                    tt("gpsimd", sg_[:], sg_[:], gnb.t.rearrange("p h e -> p (h e)"), ALU.mult, [sg_.b, gnb.b], [sg_.b])
                    for hh in range(4):
                        mm(bk(0)[:, hh * 128:(hh + 1) * 128], ktb[:, hh, :], qb[:, hh, :], True, True, [ktb.b, qb.b], [bb(0)])
                    tt("vector", am[:], bk(0).rearrange("p (h s) -> p h s", h=4), cm4[:], ALU.mult, [bb(0), cm4.b], [am.b])

                def gla_B(t):
                    p = t % 2
                    tsl = slice(t * 128, (t + 1) * 128)
                    e1, qb, ke, vb, sg_, am = E1[p], qtb[p], kend[p], v_bf[p], sg[p], attm[p]
                    for hh in range(4):
                        ob = 4 + hh // 2
                        oap = bk(ob)[:, (hh % 2) * 256:(hh % 2 + 1) * 256]
                        mm(oap, am[:, hh, :], vb[:, hh * 256:(hh + 1) * 256], hh % 2 == 0, t == 0 and hh % 2 == 1, [am.b, vb.b], [bb(ob)])
                        if t > 0:
                            mm(oap, qb[:, hh, :], st_b[:, hh, :], False, hh % 2 == 1, [qb.b, st_b.b], [bb(ob)])
                    for hh in range(4):
                        kb_ = 6 + hh // 2
                        mm(bk(kb_)[:, (hh % 2) * 256:(hh % 2 + 1) * 256], ke[:, hh * 128:(hh + 1) * 128], vb[:, hh * 256:(hh + 1) * 256],
                           hh % 2 == 0, hh % 2 == 1, [ke.b, vb.b], [bb(kb_)])
                    for hh in range(4):
                        kvp = bk(6 + hh // 2)[:, (hh % 2) * 256:(hh % 2 + 1) * 256]
                        if t == 0:
                            cp("vector", st_f[:, hh, :], kvp, [bb(6 + hh // 2)], [st_f.b])
                        else:
                            dec = e1[:, hh * 128 + 127:hh * 128 + 128]
                            stt(st_f[:, hh, :], st_f[:, hh, :], dec, kvp, ALU.mult, ALU.add, [st_f.b, e1.b, bb(6 + hh // 2)], [st_f.b])
                    cp("gpsimd", st_b[:], st_f[:], [st_f.b], [st_b.b])
                    for hh in range(4):
                        oap = bk(4 + hh // 2)[:, (hh % 2) * 256:(hh % 2 + 1) * 256]
                        act(junk[:], oap, AF.Square, [bb(4 + hh // 2)], [junk.b, ssq.b], accum_out=ssq[:, hh:hh + 1])
                    act(rst[:], ssq[:], AF.Sqrt, [ssq.b], [rst.b], scale=1.0 / 256.0, bias=EPS)
                    recip(rst[:], rst[:], [rst.b], [rst.b])
                    tt("vector", on.t.rearrange("p (h e) -> p h e", h=4), PB[:, 0:1024].rearrange("p (h e) -> p h e", h=4),
                       fap(rst[:], [[1, 4], [0, 256]]), ALU.mult, [bb(4), bb(5), rst.b], [on.b])
                    y = yc[p]
                    tt("gpsimd", y[:], on[:], sg_[:], ALU.mult, [on.b, sg_.b], [y.b])
                    pv = bank_bf(7)
                    for c in range(8):
                        trp(pv[:, c * 128:(c + 1) * 128], y[:, c * 128:(c + 1) * 128], ident_b[:], [y.b, ident_b.b], [bb(7)])
                    yT = ycT[p]
                    cp("scalar", yT[:], pv.rearrange("p (k s) -> p k s", k=8), [bb(7)], [yT.b])
                    dma("sync", ysc[2, :, tsl].rearrange("(c p) s -> p c s", p=128), yT[:], r=[yT.b], w=[b_ysc[2]])

                gla_A(0)
                for t in range(NT):
                    lists = [S.record(lambda: gla_B(t))]
                    if t + 1 < NT:
                        lists.append(S.record(lambda: gla_A(t + 1)))
                    S.replay(lists)
                S.barrier()

        def setup_tables():
            with ExitStack() as ph:
                tblx = T(ph, "tblx", [33, 16], F32)
                memset("vector", tblx[:], NEG, [tblx.b])
                dma("sync", tblx[0:32, :], rel_table, w=[tblx.b])
                for name, dst, n in (("oh_c", tc_d, 4096), ("oh_s", ts_d, 1024), ("oh_w", tw_d, 1024)):
                    oh = T(ph, "t_" + name, [33, n], F32)
                    thi = T(ph, "thi_" + name, [16, n], BF16)
                    tlo = T(ph, "tlo_" + name, [16, n], BF16)
                    dma("sync", oh[:], cin[name], w=[oh.b])
                    for ch in range(n // 512):
                        pa, pbuf = pbank[ch % 8]
                        mm(pa[0:16, :], tblx[0:33, 0:16], oh[0:33, ch * 512:(ch + 1) * 512], True, True, [tblx.b, oh.b], [pbuf])
                        cp("vector", thi[:, ch * 512:(ch + 1) * 512], pa[0:16, :], [pbuf], [thi.b])
                        tt("vector", tlo[:, ch * 512:(ch + 1) * 512], pa[0:16, :], thi[:, ch * 512:(ch + 1) * 512], ALU.subtract, [pbuf, thi.b], [tlo.b])
                    dma("sync", dst[0], thi[:], r=[thi.b], w=[b_tabs])
                    dma("sync", dst[1], tlo[:], r=[tlo.b], w=[b_tabs])
                S.barrier()

        def phase_nsa(l):
            w2d = w_in[l]
            bk = lambda i: pbank[i][0]
            bb = lambda i: pbank[i][1]
            with ExitStack() as ph:
                qT = T(ph, "nqT", [128, 8, SEQ], BF16)
                kS = T(ph, "nkS", [128, 4, SEQ], BF16)
                kW = T(ph, "nkW", [128, 4, SEQ], BF16)
                vS = T(ph, "nvS", [128, NT, 4, 66], BF16)
                vW = T(ph, "nvW", [128, NT, 4, 66], BF16)
                sgate = T(ph, "nsg", [128, NT, 48], F32)
                kcP = T(ph, "nkcP", [128, 2, 4, 128], BF16)
                vcx = T(ph, "nvcx", [128, 4, 98], BF16)
                hbt = T(ph, "nhbt", [128, 3, 4, 2, 512], BF16)
                NM = T(ph, "nNM", [128, 4, 2, 512], BF16)
                Jb = T(ph, "nJb", [128, 2, 128], BF16)
                expd = T(ph, "nexpd", [128, 2, NT, 128], BF16)
                cand = T(ph, "ncand", [128, NT, 32], F32)
                negc = T(ph, "nnegc", [128, NT, 32], F32)
                forced = T(ph, "nforced", [128, NT, 32], F32)
                dma("gpsimd", Jb[:, 0, :], cin["antiid"], w=[Jb.b])
                dma("gpsimd", Jb[:, 1, :], cin["antiid127"], w=[Jb.b])
                dma("gpsimd", expd[:, 0, :, :], cin["expand"], w=[expd.b])
                dma("gpsimd", expd[:, 1, :, :], cin["expand_near"], w=[expd.b])
                memset("vector", NM[:], 0.0, [NM.b])
                dma("sync", cand[:], cin["cand"], w=[cand.b])
                dma("sync", negc[:], cin["negc"], w=[negc.b])
                dma("sync", forced[:], cin["forced"], w=[forced.b])
                memset("vector", vcx[:], 0.0, [vcx.b])
                memset("vector", kcP[:], 0.0, [kcP.b])
                for g in range(4):
                    dma("gpsimd", vcx[:, g, 64:97], cin["ovx"], w=[vcx.b])
                for dl in range(3):
                    tsrc = tw_d if dl == 2 else ts_d
                    for g in range(4):
                        for hl in range(2):
                            for rp in range(2):
                                a0 = tsrc[hl, 4 * g + 2 * rp, 512 + dl * 128 - 127:512 + dl * 128 - 127 + 1]
                                src = bass.AP(a0.tensor, a0.offset, [[1, 128], [1024, 2], [1, 128]])
                                dst = hbt[:, dl, g, hl, :].rearrange("p (a b s) -> p a b s", a=2, b=2)[:, :, rp, :]
                                dma("sync", dst, src, r=[b_tabs], w=[hbt.b])
                for g in range(4):
                    for hl in range(2):
                        for par in range(2):
                            for rp in range(2):
                                h = 4 * g + 2 * rp + par
                                a0 = ts_d[hl, h, 640:641]
                                src = bass.AP(a0.tensor, a0.offset, [[0, 1], [0, 2], [1, 128]])
                                c0 = par * 256 + rp * 128
                                dma("sync", NM[32 + hl:33 + hl, g, :, c0:c0 + 128], src, r=[b_tabs], w=[NM.b])
                memset("vector", vS[:, :, :, 64:66], 1.0, [vS.b])
                memset("vector", vW[:, :, :, 64:66], 1.0, [vW.b])
                if stop == "nsa0":
                    S.barrier()
                    return
                with ExitStack() as ph2:
                    slab = [T(ph2, f"nslab{i}", [128, 8, 512], BF16) for i in range(2)]
                    wv = T(ph2, "nwv", [128, 8, 560], BF16)
                    for half in range(2):
                        sl = slab[half]
                        load_slab(sl[:], w2d, C_Q + half * 512, 512, sl.b)
                        for i in range(4):
                            c = half * 4 + i
                            banks = pbank[0:4] if c % 2 == 0 else pbank[4:8]
                            src = PA if c % 2 == 0 else PB
                            proj_fm(sl.t, sl.b, i * 128, 128, hT.t, hT_b, banks)
                            act(qT[:, c, :], src[:, :], AF.Copy, [b for _, b in banks], [qT.b], scale=0.125)
                    if stop == "nsa1a":
                        S.barrier()
                        return
                    n = 0
                    for idx, dst in ((2, kS), (4, kW)):
                        sl = slab[n % 2]
                        n += 1
                        for g in range(4):
                            c0 = C_KV + idx * 256 + g * 64
                            for dup in range(2):
                                dma("gpsimd", sl[:, :, g * 128 + dup * 64:g * 128 + dup * 64 + 64],
                                    w2d[:, c0:c0 + 64].rearrange("(kc p) n -> p kc n", p=128), w=[sl.b])
                        for g in range(4):
                            banks = pbank[0:4] if g % 2 == 0 else pbank[4:8]
                            src = PA if g % 2 == 0 else PB
                            proj_fm(sl.t, sl.b, g * 128, 128, hT.t, hT_b, banks)
                            cp("scalar" if g % 2 == 0 else "vector", dst[:, g, :], src[:, :], [b for _, b in banks], [dst.b])
                    if stop == "nsa1b":
                        S.barrier()
                        return
                    load_slab(wv[:, :, 0:256], w2d, C_KV + 3 * 256, 256, wv.b)
                    load_slab(wv[:, :, 256:512], w2d, C_KV + 5 * 256, 256, wv.b)
                    load_slab(wv[:, :, 512:560], w2d, C_GATE, 48, wv.b)
                    for t in range(NT):
                        tsl = slice(t * 128, (t + 1) * 128)
                        b0, b1 = (0, 1) if t % 2 == 0 else (2, 3)
                        for kc in range(8):
                            mm(bk(b0), hT[:, kc, tsl], wv[:, kc, 0:512], kc == 0, kc == 7, [hT_b[t], wv.b], [bb(b0)])
                        import os
                        SK = os.environ.get("NSA_SKIP", "")
                        if "g" not in SK:
                            for kc in range(8):
                                mm(bk(b1)[:, 0:48], hT[:, kc, tsl], wv[:, kc, 512:560], kc == 0, kc == 7, [hT_b[t], wv.b], [bb(b1)])
                        if "v" not in SK:
                            cp("vector", vS[:, t, :, 0:64], bk(b0)[:, 0:256].rearrange("p (g d) -> p g d", g=4), [bb(b0)], [vS.b])
                        if "w" not in SK:
                            cp("scalar", vW[:, t, :, 0:64], bk(b0)[:, 256:512].rearrange("p (g d) -> p g d", g=4), [bb(b0)], [vW.b])
                        if "g" not in SK:
                            act(sgate[:, t, :], bk(b1)[:, 0:48], AF.Sigmoid, [bb(b1)], [sgate.b])
                    S.barrier()
                if stop == "nsa1":
                    return
                with ExitStack() as ph2:
                    slab = [T(ph2, f"ncslab{i}", [128, 8, 256], BF16) for i in range(2)]
                    w1sb = T(ph2, "nw1", [128, 32, 256], BF16)
                    w2sb = T(ph2, "nw2", [128, 2, 128], BF16)
                    prow2 = T(ph2, "nprow2", [32, 128], F32)
                    posT = T(ph2, "nposT", [128, 32], F32)
                    XAB = [T(ph2, f"nXAB{i}", [128, SEQ], BF16) for i in range(2)]
                    gtmp = T(ph2, "ngtmp", [128, 2, 128], F32)
                    geluT = T(ph2, "ngeluT", [128, 2, 128], BF16)
                    for kv in range(2):
                        for dup in range(2):
                            dma("gpsimd", w1sb[dup * 64:(dup + 1) * 64, :, :], cmp_w1[l, kv].rearrange("(p d) j -> d p j", d=64), w=[w1sb.b])
                            dma("gpsimd", w2sb[:, :, dup * 64:(dup + 1) * 64], cmp_w2[l, kv].rearrange("(jc p) d -> p jc d", p=128), w=[w2sb.b])
                            dma("sync", prow2[:, dup * 64:(dup + 1) * 64], cmp_pos[l, kv], w=[prow2.b])
                        trp(bk(6)[:, 0:32], prow2[:, :], ident_f[0:32, 0:32], [prow2.b, ident_f.b], [bb(6)])
                        cp("vector", posT[:], bk(6)[:, 0:32], [bb(6)], [posT.b])
                        sl = slab[kv]
                        load_slab(sl[:, :, 0:256], w2d, C_KV + kv * 256, 256, sl.b)
                        for cc in range(2):
                            banks = pbank[0:4]
                            proj_fm(sl.t, sl.b, cc * 128, 128, hT.t, hT_b, banks)
                            for ab in range(2):
                                tt("vector" if ab == 0 else "gpsimd" if False else "vector", XAB[ab].t.rearrange("p (i q) -> p i q", q=16), PA.t.rearrange("p (i q) -> p i q", q=16),
                                   fap(posT[:, ab * 16:ab * 16 + 1], [[0, 128], [1, 16]]), ALU.add, [b for _, b in banks] + [posT.b], [XAB[ab].b])
                            for gg in range(2):
                                g = cc * 2 + gg
                                rows = slice(gg * 64, gg * 64 + 64)
                                hb_, hbb = bk(4 + 2 * gg), bb(4 + 2 * gg)
                                for jc in range(2):
                                    for p in range(32):
                                        srcT = XAB[0] if p < 16 else XAB[1]
                                        rhs = fap(srcT[rows, p:p + 1], [[16, 127]])
                                        mm(hb_[:, jc * 128:jc * 128 + 127], w1sb[rows, p, jc * 128:(jc + 1) * 128], rhs, p == 0, p == 31,
                                           [w1sb.b, srcT.b], [hbb])
                                hv = hb_[:, 0:256].rearrange("p (j i) -> p j i", j=2)[:, :, 0:127]
                                gv = gtmp[:, :, 0:127]
                                act(gv, hv, AF.Square, [hbb], [gtmp.b])
                                tsc("vector", gv, gv, 0.044715, ALU.mult, [gtmp.b], [gtmp.b], 1.0, ALU.add)
                                tt("vector", gv, gv, hv, ALU.mult, [gtmp.b, hbb], [gtmp.b])
                                act(gv, gv, AF.Sigmoid, [gtmp.b], [gtmp.b], scale=1.5957691216057308)
                                tt("vector", geluT[:, :, 0:127], gv, hv, ALU.mult, [gtmp.b, hbb], [geluT.b])
                                if kv == 0:
                                    for jc in range(2):
                                        mm(bk(5)[:, 0:127], w2sb[:, jc, :], geluT[:, jc, 0:127], jc == 0, jc == 1, [w2sb.b, geluT.b], [bb(5)])
                                    cp("scalar", kcP[0:64, 0, g, 0:127], bk(5)[0:64, 0:127], [bb(5)], [kcP.b])
                                    cp("scalar", kcP[64:128, 1, g, 0:127], bk(5)[64:128, 0:127], [bb(5)], [kcP.b])
                                else:
                                    for jc in range(2):
                                        mm(bk(5)[0:127, 0:64], geluT[:, jc, 0:127], w2sb[:, jc, 0:64], jc == 0, jc == 1, [w2sb.b, geluT.b], [bb(5)])
                                    cp("scalar", vcx[0:127, g, 0:64], bk(5)[0:127, 0:64], [bb(5)], [vcx.b])
                    S.barrier()
                if stop == "nsa2":
                    return
                dma("sync", hsc, hT.t.rearrange("p k s -> p (k s)"), r=hT_b, w=[b_hsc])
                kpad1 = Buf("kpad1")
                for base, KT_, ceng in ((0, kS, "scalar"), (4, kW, "vector")):
                    cp(ceng, hT[64:128, base:base + 4, :], KT_[64:128, :, :], [KT_.b], hT_b + [kpad1])
                    memset("gpsimd", hT[0:64, base:base + 4, :], 0.0, hT_b + [kpad1])
                    memset("gpsimd" if base == 0 else "vector", KT_[64:128, :, :], 0.0, [KT_.b])
                E = [T(ph, f"nE{i}", [128, 512], BF16) for i in range(4)]
                cb = [T(ph, f"ncb{i}", [128, 2, 512], BF16) for i in range(4)]
                for cbx in cb:
                    memset("vector", cbx[:], NEG, [cbx.b])
                ybt = [T(ph, f"nybt{i}", [128, 1024], BF16) for i in range(2)]
                ybT = [T(ph, f"nybT{i}", [128, 8, 128], BF16) for i in range(2)]
                sets = []
                for i in range(2):
                    sets.append(dict(
                        ybacc=T(ph, f"nybacc{i}", [128, 4, 64], F32), tmp1=T(ph, f"ntmp1{i}", [128, 4, 64], F32),
                        tmp2=T(ph, f"ntmp2{i}", [128, 4, 64], F32), impr=T(ph, f"nimpr{i}", [128, 4, 32], F32),
                        imp=T(ph, f"nimp{i}", [128, 32], F32), m8=T(ph, f"nm8{i}", [128, 8], F32),
                        sm=T(ph, f"nsm{i}", [128, 3, 4], F32), Us=(3, 6)[i], Uw=(4, 7)[i]))
                colb = lambda r: (r % 2) * 256 + (r // 2) * 128
                cnt_ = dict(l=0, e=0, kp=0, cb=0)
                tasks = []

                inflight = set()

                def next_Li(hold=False):
                    while True:
                        i = (0, 1, 5)[cnt_["l"] % 3]
                        cnt_["l"] += 1
                        if i not in inflight:
                            break
                    if hold:
                        inflight.add(i)
                    return i

                def next_L(hold=False):
                    return pbank[next_Li(hold)]

                def release_L(Lb):
                    for i in (0, 1, 5):
                        if pbank[i][1] is Lb:
                            inflight.discard(i)

                def next_E():
                    e_ = E[cnt_["e"] % 4]
                    cnt_["e"] += 1
                    return e_

                cb_dma = []
                PF = 3

                def mk_cmp(qt, g, st):
                    qsl = slice(qt * 128, (qt + 1) * 128)
                    ui = len(cb_dma)
                    cbt = cb[ui % 4]
                    box = {}

                    def issue():
                        for hl in range(2):
                            for rp in range(2):
                                a0 = tc_d[hl, 4 * g + 2 * rp, qt * 128 + 1:qt * 128 + 2]
                                src = bass.AP(a0.tensor, a0.offset, [[16, 128], [4096, 2], [1, 128]])
                                dst = cbt[:, hl, :].rearrange("p (a b s) -> p a b s", a=2, b=2)[:, :, rp, :]
                                dma("sync", dst, src, r=[b_tabs], w=[cbt.b])

                    cb_dma.append(issue)

                    def pre():
                        if ui + PF < len(cb_dma):
                            cb_dma[ui + PF]()

                    def s1():
                        L, Lb = next_L(hold=True)
                        box["L"] = (L, Lb)
                        for par in range(2):
                            mm(L[:, par * 256:(par + 1) * 256], kcP[:, par, g, :], qT[:, 2 * g:2 * g + 2, qsl], par == 0, False, [kcP.b, qT.b], [Lb])
                        for hl in range(2):
                            mm(L, Jb[:, 1, :], cbt[:, hl, :], False, hl == 1, [Jb.b, cbt.b], [Lb])

                    def s2():
                        L, Lb = box["L"]
                        release_L(Lb)
                        Ec = next_E()
                        act(Ec[:], L, AF.Exp, [Lb], [Ec.b])
                        Uc = bk(2)
                        for r in range(4):
                            mm(Uc[:, r * 98:(r + 1) * 98], Ec[:, colb(r):colb(r) + 128], vcx[:, g, 0:98], r == 0, r == 3, [Ec.b, vcx.b], [bb(2)])

                    def post():
                        Uc = bk(2)
                        sm, ybacc, impr, imp, m8 = st["sm"], st["ybacc"], st["impr"], st["imp"], st["m8"]
                        ucv = lambda a, b_: fap(Uc[:, a:a + 1], [[98, 4], [1, b_]])
                        rs4, wc = sm[:, 0, :], sm[:, 1, :]
                        tsc("vector", rs4, fap(Uc[:, 96:97], [[98, 4]]), 1e-30, ALU.max, [bb(2)], [sm.b])
                        recip(rs4, rs4, [sm.b], [sm.b])
                        tt("vector", wc, rs4, sgate[:, qt, 4 * g:4 * g + 4], ALU.mult, [sm.b, sgate.b], [sm.b])
                        tt("vector", ybacc[:], ucv(0, 64), fap(wc, [[1, 4], [0, 64]]), ALU.mult, [bb(2), sm.b], [ybacc.b])
                        tt("vector", impr[:], ucv(64, 32), fap(rs4, [[1, 4], [0, 32]]), ALU.mult, [bb(2), sm.b], [impr.b])
                        S.op("vector", lambda e: e.tensor_reduce(out=imp[:], in_=fap(impr[:, 0, 0:1], [[1, 32], [32, 4]]), axis=AX.X, op=ALU.add),
                             [impr.b], [imp.b])
                        tt("vector", imp[:], imp[:], cand[:, qt, :], ALU.mult, [imp.b, cand.b], [imp.b])
                        tt("vector", imp[:], imp[:], negc[:, qt, :], ALU.add, [imp.b, negc.b], [imp.b])
                        S.op("vector", lambda e: e.max(out=m8[:], in_=imp[:]), [imp.b], [m8.b])
                        tsc("vector", imp[:], imp[:], m8[:, 4:5], ALU.is_ge, [imp.b, m8.b], [imp.b])
                        tt("vector", imp[:], imp[:], forced[:, qt, :], ALU.max, [imp.b, forced.b], [imp.b])
                        tsc("vector", imp[:], imp[:], -1.0, ALU.add, [imp.b], [imp.b], -NEG, ALU.mult)

                    return dict(pre=pre, s1=s1, s2=s2, post=post, defer=None, nm=None, first_slc=False)

                def mk_tile(qt, g, st, br, kt, first, last, buf, hooks_pre, hooks_post, defer):
                    qsl = slice(qt * 128, (qt + 1) * 128)
                    ksl = slice(kt * 128, (kt + 1) * 128)
                    KT = kS if br == 0 else kW
                    VT = vS if br == 0 else vW
                    Ub = st["Us"] if br == 0 else st["Uw"]
                    dl = qt - kt
                    near = dl < (2 if br == 0 else 3)
                    box = {}

                    def pre():
                        for h_ in hooks_pre:
                            h_()

                    def s1():
                        L, Lb = next_L(hold=True)
                        box["L"] = (L, Lb)
                        mm(L[:, 0:256], KT[:, g, ksl], qT[:, 2 * g:2 * g + 2, qsl], True, False, [KT.b, qT.b], [Lb])
                        mm(L[:, 256:512], hT[:, (0 if br == 0 else 4) + g, ksl], qT[:, 2 * g:2 * g + 2, qsl], False, False, [kpad1, qT.b], [Lb])
                        if br == 0:
                            mm(L, expd[:, 1 if near else 0, kt, :], NM[:, g, buf, :], False, not near, [expd.b, NM.b], [Lb])
                        if near:
                            for hl in range(2):
                                mm(L, Jb[:, 0, :], hbt[:, dl, g, hl, :], False, hl == 1, [Jb.b, hbt.b], [Lb])

                    def s2():
                        L, Lb = box["L"]
                        release_L(Lb)
                        Et = next_E()
                        act(Et[:], L, AF.Exp, [Lb], [Et.b])
                        for r in range(4):
                            mm(bk(Ub)[:, r * 66:(r + 1) * 66], Et[:, colb(r):colb(r) + 128], VT[:, kt, g, 0:66],
                               first and r == 0, last and r == 3, [Et.b, VT.b], [bb(Ub)])

                    def post():
                        for h_ in hooks_post:
                            h_()

                    return dict(pre=pre, s1=s1, s2=s2, post=post, defer=defer, nm=None, first_slc=False)

                def mk_nm_hook(g, st, buf):
                    def hook():
                        imp = st["imp"]
                        M_, Mb = next_L()
                        trp(M_[0:32, 0:128], imp[:, :], ident_f[:, :], [imp.b, ident_f.b], [Mb])
                        cp("vector", NM[0:32, g, buf, :].rearrange("p (a s) -> p a s", a=4), fap(M_[0:32, 0:1], [[0, 4], [1, 128]]), [Mb], [NM.b])
                    return hook

                def mk_combine(qt, g, st, ybq):
                    def hook():
                        sm, ybacc, tmp1, tmp2 = st["sm"], st["ybacc"], st["tmp1"], st["tmp2"]
                        for br in range(2):
                            ub = st["Us"] if br == 0 else st["Uw"]
                            U = bk(ub)
                            rsb, wb_ = sm[:, 0, :], sm[:, 1 + br, :]
                            S.op("vector", lambda e, U=U, rsb=rsb: e.reciprocal(out=rsb, in_=fap(U[:, 64:65], [[66, 4]])), [bb(ub)], [sm.b])
                            tt("vector", wb_, rsb, sgate[:, qt, 16 * (br + 1) + 4 * g:16 * (br + 1) + 4 * g + 4], ALU.mult, [sm.b, sgate.b], [sm.b])
                            tgt = tmp1 if br == 0 else tmp2
                            tt("vector", tgt[:], fap(U[:, 0:1], [[66, 4], [1, 64]]), fap(wb_, [[1, 4], [0, 64]]), ALU.mult, [bb(ub), sm.b], [tgt.b])
                        tt("gpsimd", tmp1[:], tmp1[:], ybacc[:], ALU.add, [tmp1.b, ybacc.b], [tmp1.b])
                        tt("gpsimd", ybq[:, g * 256:(g + 1) * 256].rearrange("p (r d) -> p r d", r=4), tmp1[:], tmp2[:], ALU.add, [tmp1.b, tmp2.b], [ybq.b])
                    return hook

                def mk_ybout(qt, ybq):
                    def hook():
                        qsl = slice(qt * 128, (qt + 1) * 128)
                        li = next_Li()
                        pv = bank_bf(li)
                        for c in range(8):
                            trp(pv[:, c * 128:(c + 1) * 128], ybq[:, c * 128:(c + 1) * 128], ident_b[:], [ybq.b, ident_b.b], [bb(li)])
                        yT = ybT[qt % 2]
                        cp("scalar", yT[:], pv.rearrange("p (k s) -> p k s", k=8), [bb(li)], [yT.b])
                        dma("sync", ysc[1, :, qsl].rearrange("(c p) s -> p c s", p=128), yT[:], r=[yT.b], w=[b_ysc[1]])
                    return hook

                un = 0
                units = []
                for qt in range(1 if stop == 'nsa3' else NT):
                    ybq = ybt[qt % 2]
                    for g in range(4):
                        st = sets[un % 2]
                        buf = un % 2
                        un += 1
                        uc_ = [mk_cmp(qt, g, st)]
                        uc_[0]["nm"] = mk_nm_hook(g, st, buf)
                        uc_[0]["unit"] = len(units)
                        wk = list(range(max(0, qt - 2), qt + 1))
                        uw_ = [mk_tile(qt, g, st, 1, kt, kt == wk[0], kt == qt, buf, [], [], None) for kt in wk]
                        us_ = []
                        for kt in range(qt + 1):
                            hp = []
                            hq = [mk_combine(qt, g, st, ybq)] if kt == qt else []
                            df = mk_ybout(qt, ybq) if (kt == qt and g == 3) else None
                            us_.append(mk_tile(qt, g, st, 0, kt, kt == 0, kt == qt, buf, hp, hq, df))
                        us_[0]["first_slc"] = True
                        us_[0]["unit"] = len(units)
                        units.append((uc_, uw_, us_))
                tasks += units[0][0] + units[0][1]
                for ui in range(len(units)):
                    if ui + 1 < len(units):
                        tasks += units[ui + 1][0]
                    tasks += units[ui][2]
                    if ui + 1 < len(units):
                        tasks += units[ui + 1][1]
                for ui_ in range(min(PF, len(cb_dma))):
                    cb_dma[ui_]()
                deferred = {}
                ntk = len(tasks)
                LA = 2
                for j in range(min(LA, ntk)):
                    tasks[j]["pre"]()
                    tasks[j]["s1"]()
                first_idx = {tk["unit"]: i for i, tk in enumerate(tasks) if tk["first_slc"]}
                for i, tk in enumerate(tasks):
                    tk["s2"]()
                    tk["post"]()
                    if tk["nm"] is not None:
                        j = max(i, min(i + 8, first_idx[tk["unit"]] - LA))
                        deferred.setdefault(j, []).append(tk["nm"])
                    if tk["defer"] is not None:
                        deferred.setdefault(i + 3, []).append(tk["defer"])
                    for fn in deferred.pop(i, []):
                        fn()
                    if i + LA < ntk:
                        tasks[i + LA]["pre"]()
                        tasks[i + LA]["s1"]()
                for k_ in sorted(deferred):
                    for fn in deferred[k_]:
                        fn()
                dma("sync", hT.t.rearrange("p k s -> p (k s)"), hsc, r=[b_hsc], w=hT_b + [kpad1])
                S.barrier()

        def phase_tail(l, x_src, x_dst, b_xsrc, b_xdst):
            bk = lambda i: pbank[i][0]
            bb = lambda i: pbank[i][1]
            w2d = w_in[l]
            with ExitStack() as ph:
                mrgb = T(ph, "mrgb", [128, 8, SEQ], BF16)
                with ExitStack() as ph2:
                    mrg = T(ph2, "mrg", [128, 8, SEQ], F32)
                    yT = T(ph2, "m_yT", [128, 8, SEQ], BF16)
                    yb_ = [Buf(f"m_yT{c}") for c in range(8)]
                    wbr = [T(ph2, f"m_wbr{i}", [128, 8, 256], BF16) for i in range(2)]
                    wmg = [T(ph2, f"m_wmg{i}", [128, 8, 256], BF16) for i in range(2)]
                    sig = [T(ph2, f"m_sig{i}", [128, 512], F32) for i in range(2)]
                    prod = [T(ph2, f"m_prod{i}", [128, 512], F32) for i in range(2)]
                    n = 0
                    bn = 0
                    for br in range(3):
                        for c in range(8):
                            dma("sync", yT[:, c, :], ysc[br, c * 128:(c + 1) * 128, :], r=[b_ysc[br]], w=[yb_[c]])
                        for oc2 in range(4):
                            wb, wm = wbr[oc2 % 2], wmg[oc2 % 2]
                            load_slab(wb[:], w_branch[l, br], oc2 * 256, 256, wb.b)
                            load_slab(wm[:], w2d, C_MG + br * 1024 + oc2 * 256, 256, wm.b)
                            for o in range(2):
                                oc = oc2 * 2 + o
                                for sc in range(4):
                                    ssl = slice(sc * 512, (sc + 1) * 512)
                                    bB, bG = (bn % 4) * 2, (bn % 4) * 2 + 1
                                    bn += 1
                                    for c in range(8):
                                        mm(bk(bB), wb[:, c, o * 128:(o + 1) * 128], yT[:, c, ssl], c == 0, c == 7, [wb.b, yb_[c]], [bb(bB)])
                                    for kc in range(8):
                                        mm(bk(bG), wm[:, kc, o * 128:(o + 1) * 128], hT[:, kc, ssl], kc == 0, kc == 7, [wm.b] + hT_b[sc * 4:(sc + 1) * 4], [bb(bG)])
                                    sg_, pr_ = sig[n % 2], prod[n % 2]
                                    n += 1
                                    act(sg_[:], bk(bG), AF.Sigmoid, [bb(bG)], [sg_.b])
                                    if br == 0:
                                        tt("vector", mrg[:, oc, ssl], sg_[:], bk(bB), ALU.mult, [sg_.b, bb(bB)], [mrg.b])
                                    elif br == 1:
                                        tt("vector", pr_[:], sg_[:], bk(bB), ALU.mult, [sg_.b, bb(bB)], [pr_.b])
                                        tt("gpsimd", mrg[:, oc, ssl], mrg[:, oc, ssl], pr_[:], ALU.add, [mrg.b, pr_.b], [mrg.b])
                                    else:
                                        tt("vector", pr_[:], sg_[:], bk(bB), ALU.mult, [sg_.b, bb(bB)], [pr_.b])
                                        tt("gpsimd", mrgb[:, oc, ssl], mrg[:, oc, ssl], pr_[:], ALU.add, [mrg.b, pr_.b], [mrgb.b])
                    S.barrier()
                with ExitStack() as ph2:
                    wout = T(ph2, "p_wout", [128, 8, D], BF16)
                    g1 = load_gain(ph2, l, 1)
                    g2 = load_gain(ph2, l, 2)
                    xt = [T(ph2, f"p_xt{i}", [128, D], F32) for i in range(2)]
                    on = [T(ph2, f"p_on{i}", [128, D], F32) for i in range(2)]
                    hb = [T(ph2, f"p_hb{i}", [128, D], BF16) for i in range(2)]
                    junk = T(ph2, "p_junk", [128, D], BF16)
                    ss = T(ph2, "p_ss", [128, NT], F32)
                    rs = T(ph2, "p_rs", [128, NT], F32)
                    ss2 = T(ph2, "p_ss2", [128, NT], F32)
                    rs2 = T(ph2, "p_rs2", [128, NT], F32)
                    for half in range(2):
                        load_slab(wout[:, :, half * 512:(half + 1) * 512], w_out[l], half * 512, 512, wout.b)
                    for t in range(NT):
                        tsl = slice(t * 128, (t + 1) * 128)
                        b0 = (t % 2) * 2
                        ov = PA[:, b0 * 512:(b0 + 2) * 512]
                        obufs = [bb(b0), bb(b0 + 1)]
                        for half in range(2):
                            for c in range(8):
                                mm(bk(b0 + half), mrgb[:, c, tsl], wout[:, c, half * 512:(half + 1) * 512], c == 0, c == 7, [mrgb.b, wout.b], [bb(b0 + half)])
                        x_, o_ = xt[t % 2], on[t % 2]
                        dma("sync", x_[:], x_src[tsl, :], r=[b_xsrc], w=[x_.b])
                        act(junk[:], ov, AF.Square, obufs, [junk.b, ss.b], accum_out=ss[:, t:t + 1])
                        act(rs[:, t:t + 1], ss[:, t:t + 1], AF.Sqrt, [ss.b], [rs.b], scale=1.0 / D, bias=EPS)
                        recip(rs[:, t:t + 1], rs[:, t:t + 1], [rs.b], [rs.b])
                        stt(o_[:], ov, rs[:, t:t + 1], g1[:], ALU.mult, ALU.mult, obufs + [rs.b, g1.b], [o_.b])
                        tt("gpsimd", o_[:], o_[:], x_[:], ALU.add, [o_.b, x_.b], [o_.b])
                        dma("sync", xmid[tsl, :], o_[:], r=[o_.b], w=[b_xmid])
                        norm_transpose_tile(ph2, t, o_[:], o_.b, g2, ss2, rs2, hb[t % 2], junk, 4 + t % 2)
                    S.barrier()
            with ExitStack() as ph:
                hid = T(ph, "f_hid", [128, 22, SEQ], BF16)
                hid_b = [Buf(f"hid{j}") for j in range(22)]
                wfo = T(ph, "f_wfo", [128, 22, D], BF16)
                wfo_b = [Buf(f"wfo{j}") for j in range(11)]
                wfi = [T(ph, f"f_wfi{i}", [128, 8, 2, 128], BF16) for i in range(2)]
                sgt = [T(ph, f"f_sg{i}", [128, 512], F32) for i in range(2)]
                g3 = load_gain(ph, l, 3)
                xt = [T(ph, f"f_xt{i}", [128, D], F32) for i in range(2)]
                on = [T(ph, f"f_on{i}", [128, D], F32) for i in range(2)]
                junk = T(ph, "f_junk", [128, D], BF16)
                ss = T(ph, "f_ss", [128, NT], F32)
                rs = T(ph, "f_rs", [128, NT], F32)
                n = 0
                bn = 0
                for j in range(22):
                    wf = wfi[j % 2]
                    dma("gpsimd", wf[:, :, 0, :], w_ffn_in[l][:, j * 128:(j + 1) * 128].rearrange("(kc p) n -> p kc n", p=128), w=[wf.b])
                    dma("gpsimd", wf[:, :, 1, :], w_ffn_in[l][:, DFF + j * 128:DFF + (j + 1) * 128].rearrange("(kc p) n -> p kc n", p=128), w=[wf.b])
                    if j % 2 == 0:
                        jj = j // 2
                        dma("gpsimd", wfo[:, 2 * jj:2 * jj + 2, :], w_ffn_out[l][jj * 256:(jj + 1) * 256, :].rearrange("(j p) n -> p j n", p=128), w=[wfo_b[jj]])
                    for sc in range(4):
                        ssl = slice(sc * 512, (sc + 1) * 512)
                        bG, bU = (bn % 4) * 2, (bn % 4) * 2 + 1
                        bn += 1
                        for kc in range(8):
                            mm(bk(bG), wf[:, kc, 0, :], hT[:, kc, ssl], kc == 0, kc == 7, [wf.b] + hT_b[sc * 4:(sc + 1) * 4], [bb(bG)])
                        for kc in range(8):
                            mm(bk(bU), wf[:, kc, 1, :], hT[:, kc, ssl], kc == 0, kc == 7, [wf.b] + hT_b[sc * 4:(sc + 1) * 4], [bb(bU)])
                        sg_ = sgt[n % 2]
                        n += 1
                        act(sg_[:], bk(bG), AF.Silu, [bb(bG)], [sg_.b])
                        tt("vector", hid[:, j, ssl], sg_[:], bk(bU), ALU.mult, [sg_.b, bb(bU)], [hid_b[j]])
                for t in range(NT):
                    tsl = slice(t * 128, (t + 1) * 128)
                    b0 = (t % 2) * 2
                    ov = PA[:, b0 * 512:(b0 + 2) * 512]
                    obufs = [bb(b0), bb(b0 + 1)]
                    for half in range(2):
                        for j in range(22):
                            mm(bk(b0 + half), hid[:, j, tsl], wfo[:, j, half * 512:(half + 1) * 512], j == 0, j == 21, [hid_b[j], wfo_b[j // 2]], [bb(b0 + half)])
                    x_, o_ = xt[t % 2], on[t % 2]
                    dma("sync", x_[:], xmid[tsl, :], r=[b_xmid], w=[x_.b])
                    act(junk[:], ov, AF.Square, obufs, [junk.b, ss.b], accum_out=ss[:, t:t + 1])
                    act(rs[:, t:t + 1], ss[:, t:t + 1], AF.Sqrt, [ss.b], [rs.b], scale=1.0 / D, bias=EPS)
                    recip(rs[:, t:t + 1], rs[:, t:t + 1], [rs.b], [rs.b])
                    stt(o_[:], ov, rs[:, t:t + 1], g3[:], ALU.mult, ALU.mult, obufs + [rs.b, g3.b], [o_.b])
                    tt("gpsimd", o_[:], o_[:], x_[:], ALU.add, [o_.b, x_.b], [o_.b])
                    dma("sync", x_dst[tsl, :], o_[:], r=[o_.b], w=[b_xdst])
                S.barrier()

        setup_tables()
        for l in range(n_layers):
            phase_A(l, x_in if l == 0 else xres)
            import os
            if not os.environ.get("SKIP_LG"):
                phase_lru(l)
                if stop == "lru":
                    break
                phase_gla(l)
                if stop == "gla":
                    break
            phase_nsa(l)
            if stop is not None and stop.startswith("nsa"):
                break
            last = (l == n_layers - 1)
            phase_tail(l, x_in if l == 0 else xres, y_out if last else xres, Buf() if l == 0 else b_xres, Buf() if last else b_xres)
        S.barrier()
        S.emit()
    return nc, consts


_CACHE = {}


def kernel(**inputs):
    if "nc" not in _CACHE:
        _CACHE["nc"] = build()
    nc, consts = _CACHE["nc"]
    x = np.ascontiguousarray(np.asarray(inputs["x"], dtype=np.float32))
    shared = {k: np.ascontiguousarray(np.asarray(v, dtype=np.float32)) for k, v in inputs.items() if k != "x"}
    for k, v in consts.items():
        shared["c_" + k] = v
    in_maps = [dict(shared, x=x[i]) for i in range(8)]
    res = run_bass_kernel_spmd(nc, in_maps, core_ids=list(range(8)))
    return np.stack([np.asarray(r["out"], dtype=np.float32) for r in res.results], axis=0)
```

```python
import os
import numpy as np
from contextlib import ExitStack
import concourse.bass as bass
import concourse.mybir as mybir
from concourse.bass_utils import run_bass_kernel_spmd

F32 = mybir.dt.float32
BF16 = mybir.dt.bfloat16
ALU = mybir.AluOpType
AF = mybir.ActivationFunctionType
AX = mybir.AxisListType

SEQ = 2048
D = 1024
NT = 16
DEPTH = 2
EPS = 1e-6
IN_W = 10816
C_LRUX, C_LRUG, C_Q, C_KV, C_GATE, C_GQ, C_GK, C_GV, C_GOG, C_GLR, C_MG = 0, 1024, 2048, 3072, 4608, 4656, 5168, 5680, 6704, 7728, 7744
DFF = 2816
NEG = -30000.0


class Buf:
    __slots__ = ("name", "w", "r", "excl")

    def __init__(self, name="", excl=False):
        self.name = name
        self.w = None
        self.r = []
        self.excl = excl


class Sched:
    ENG = ("sync", "scalar", "vector", "gpsimd", "tensor")
    DMAQ = ("sync", "gpsimd", "scalar")

    def __init__(self, nc, es, n_dma_sems=12):
        self.nc = nc
        self.q = {e: [] for e in self.ENG}
        self.cnt = {e: 0 for e in self.ENG}
        self.sems = []
        self.esem = {}
        for e in self.ENG:
            self.esem[e] = len(self.sems)
            self.sems.append(es.enter_context(nc.semaphore("s_" + e)))
        self.known = {e: {} for e in self.ENG}
        self.dpool = {}
        self.dcnt = {}
        self.dlast = {}
        for qn in self.DMAQ:
            self.dpool[qn] = []
            for i in range(n_dma_sems):
                self.dpool[qn].append(len(self.sems))
                self.sems.append(es.enter_context(nc.semaphore(f"d_{qn}_{i}")))
            self.dcnt[qn] = 0
        self.K = n_dma_sems

    def _waits(self, eng, r, w):
        waits = {}
        kn = self.known[eng]
        own_pe = self.esem["tensor"] if eng == "tensor" else -1

        def need(kv):
            k, v = kv
            if k == own_pe:
                return
            if kn.get(k, 0) < v and waits.get(k, 0) < v:
                waits[k] = v

        own = self.esem.get(eng, -2)
        for b in r:
            if b.w is not None:
                need(b.w)
            if b.excl:
                for x in b.r:
                    if x[0] != own:
                        need(x)
        for b in w:
            if b.w is not None:
                need(b.w)
            for x in b.r:
                need(x)
        for k, v in waits.items():
            kn[k] = v
        return list(waits.items())

    _rec = None

    def record(self, fn):
        self._rec = []
        fn()
        r, self._rec = self._rec, None
        return r

    def replay(self, lists):
        idx = [0] * len(lists)
        live = True
        while live:
            live = False
            for j, lst in enumerate(lists):
                if idx[j] < len(lst):
                    kind, args, kw = lst[idx[j]]
                    idx[j] += 1
                    live = True
                    if kind == "op":
                        self.op(*args)
                    else:
                        self.dma(*args, **kw)

    def op(self, eng, fn, r=(), w=()):
        if self._rec is not None:
            self._rec.append(("op", (eng, fn, list(r), list(w)), {}))
            return
        waits = self._waits(eng, r, w)
        self.cnt[eng] += 1
        seq = self.cnt[eng]
        k = self.esem[eng]
        self.q[eng].append((waits, fn, (k, 1)))
        for b in w:
            b.w = (k, seq)
            b.r = []
        for b in r:
            if b not in w:
                b.r.append((k, seq))
                if len(b.r) > 24:
                    b.r = b.r[-24:] if False else self._compact(b.r)

    @staticmethod
    def _compact(lst):
        d = {}
        for k, v in lst:
            if d.get(k, 0) < v:
                d[k] = v
        return list(d.items())

    def dma(self, qn, out, in_, r=(), w=(), **kw):
        if self._rec is not None:
            self._rec.append(("dma", (qn, out, in_, list(r), list(w)), kw))
            return
        waits = self._waits(qn, r, w)
        i = self.dcnt[qn]
        self.dcnt[qn] += 1
        k = self.dpool[qn][i % self.K]
        val = 16 * (i // self.K + 1)
        if val > 16 and self.known[qn].get(k, 0) < val - 16:
            waits.append((k, val - 16))
            self.known[qn][k] = val - 16
        self.dlast[k] = val
        self.q[qn].append((waits, lambda e: e.dma_start(out=out, in_=in_, **kw), (k, 16)))
        for b in w:
            b.w = (k, val)
            b.r = []
        for b in r:
            if b not in w:
                b.r.append((k, val))
                if len(b.r) > 24:
                    b.r = self._compact(b.r)

    def pe_drain(self):
        k = self.esem["tensor"]
        if self.cnt["tensor"] > 0:
            self.q["tensor"].append(([(k, self.cnt["tensor"])], None, None))

    def barrier(self):
        tgt = [(self.esem[e], self.cnt[e]) for e in self.ENG if self.cnt[e] > 0]
        tgt += list(self.dlast.items())
        for e in self.ENG:
            waits = []
            for k, v in tgt:
                if e == "tensor" and k == self.esem["tensor"]:
                    continue
                if self.known[e].get(k, 0) < v:
                    waits.append((k, v))
                    self.known[e][k] = v
            if waits:
                self.q[e].append((waits, None, None))

    def emit(self):
        nc = self.nc
        with nc.Block() as block:
            for e in self.ENG:
                def body(eng, _e=e):
                    for waits, fn, inc in self.q[_e]:
                        for k, v in waits:
                            eng.wait_ge(self.sems[k], v)
                        if fn is not None:
                            ins = fn(eng)
                            ins.then_inc(self.sems[inc[0]], inc[1])
                getattr(block, e)(body)


def fap(a, dims):
    return bass.AP(a.tensor, a.offset, [list(a.ap[0])] + [list(d) for d in dims])


def _rel_bucket(d):
    d = np.asarray(d)
    n = np.maximum(d, 0)
    nf = np.maximum(n, 16).astype(np.float32)
    large = 16 + (np.log(nf / np.float32(16)) / np.float32(np.log(128 / 16)) * np.float32(16)).astype(np.int32)
    large = np.minimum(large, 31)
    return np.where(n < 16, n, large)


def host_consts():
    c = {}
    c["ident"] = np.eye(128, dtype=np.float32)
    c["antiid"] = np.eye(128, dtype=np.float32)[::-1].copy()
    aid127 = np.zeros((128, 128), np.float32)
    for i in range(127):
        aid127[i, 126 - i] = 1.0
    aid127[127, 127] = 1.0
    c["antiid127"] = aid127
    s = np.arange(128)
    c["triu"] = (s[:, None] <= s[None, :]).astype(np.float32)
    c["tril"] = (s[:, None] > s[None, :]).astype(np.float32)
    def oh(deltas, valid):
        m = np.zeros((33, len(deltas)), np.float32)
        b = _rel_bucket(deltas)
        for i, (dd, v) in enumerate(zip(deltas, valid)):
            if v:
                m[b[i], i] = 1.0
            else:
                m[32, i] = 1.0
        return m
    dc = np.arange(-2048, 2048)
    c["oh_c"] = oh(dc, dc >= 0)
    ds = np.arange(-512, 512)
    c["oh_s"] = oh(ds, ds >= 0)
    c["oh_w"] = oh(ds, (ds >= 0) & (ds < 256))
    cs = np.arange(127) * 16
    js = np.arange(32) * 64
    ov = np.clip(np.minimum(cs[:, None] + 32, js[None, :] + 64) - np.maximum(cs[:, None], js[None, :]), 0, None).astype(np.float32) / 32.0
    ovx = np.zeros((128, 33), np.float32)
    ovx[:127, :32] = ov
    ovx[:127, 32] = 1.0
    c["ovx"] = ovx
    pos = np.arange(SEQ)
    cur = pos // 64
    blk = np.arange(32)[None, :]
    cand = (blk >= 1) & (blk <= cur[:, None] - 2)
    forced = (blk == 0) | (blk == cur[:, None]) | (blk == cur[:, None] - 1)
    c["cand"] = cand.astype(np.float32).reshape(NT, 128, 32).transpose(1, 0, 2).copy()
    c["negc"] = ((cand.astype(np.float32) - 1.0) * 1e4).reshape(NT, 128, 32).transpose(1, 0, 2).copy()
    c["forced"] = forced.astype(np.float32).reshape(NT, 128, 32).transpose(1, 0, 2).copy()
    ex = np.zeros((128, NT, 128), np.float32)
    for kt in range(NT):
        for key in range(128):
            ex[2 * kt + key // 64, kt, key] = 1.0
    c["expand_near"] = ex.copy()
    ex[32:34] = 1.0
    c["expand"] = ex
    return c


CONST_SHAPES = None


def build(debug=False, n_layers=DEPTH, stop=None):
    nc = bass.Bass("TRN2", target_bir_lowering=False)
    consts = host_consts()
    din = {}

    def inp(name, shape, dt=F32):
        din[name] = nc.dram_tensor(name, list(shape), dt, kind="ExternalInput").ap()
        return din[name]

    x_in = inp("x", [SEQ, D])
    rel_table = inp("rel_table", [32, 16])
    norm_g = inp("norm_g", [DEPTH, 4, D])
    w_in = inp("w_in", [DEPTH, D, IN_W])
    conv_w = inp("conv_w", [DEPTH, 4, D])
    conv_b = inp("conv_b", [DEPTH, D])
    lru_wg = inp("lru_w_gates", [DEPTH, 2, 8, 128, 128])
    lru_bg = inp("lru_b_gates", [DEPTH, 2, D])
    lru_lam = inp("lru_lambda", [DEPTH, D])
    cmp_pos = inp("cmp_pos", [DEPTH, 2, 32, 64])
    cmp_w1 = inp("cmp_w1", [DEPTH, 2, 2048, 256])
    cmp_w2 = inp("cmp_w2", [DEPTH, 2, 256, 64])
    gla_wa2 = inp("gla_wa2", [DEPTH, 16, 512])
    gla_ba = inp("gla_ba", [DEPTH, 512])
    gla_norm = inp("gla_norm", [DEPTH, 256])
    w_branch = inp("w_branch", [DEPTH, 3, D, D])
    w_out = inp("w_out", [DEPTH, D, D])
    w_ffn_in = inp("w_ffn_in", [DEPTH, D, 2 * DFF])
    w_ffn_out = inp("w_ffn_out", [DEPTH, DFF, D])
    cin = {k: inp("c_" + k, v.shape) for k, v in consts.items()}

    okind = "ExternalOutput"
    y_out = nc.dram_tensor("out", [SEQ, D], F32, kind=okind).ap()
    skind = "ExternalOutput"
    xres = nc.dram_tensor("xres", [SEQ, D], F32, kind=skind).ap()
    xmid = nc.dram_tensor("xmid", [SEQ, D], F32, kind=skind).ap()
    ysc = nc.dram_tensor("ysc", [3, D, SEQ], BF16, kind=skind).ap()
    tc_d = nc.dram_tensor("tc_d", [2, 16, 4096], BF16, kind="Internal").ap()
    ts_d = nc.dram_tensor("ts_d", [2, 16, 1024], BF16, kind="Internal").ap()
    tw_d = nc.dram_tensor("tw_d", [2, 16, 1024], BF16, kind="Internal").ap()
    hsc = nc.dram_tensor("hsc", [128, 8 * SEQ], BF16, kind=skind).ap()
    b_hsc = Buf()
    b_xres, b_xmid, b_ysc, b_tabs = Buf(), Buf(), [Buf(), Buf(), Buf()], Buf()

    with ExitStack() as es:
        S = Sched(nc, es)
        es.enter_context(nc.allow_non_contiguous_dma(reason="small param loads"))

        def mm(out, lhsT, rhs, start, stop, r, w):
            S.op("tensor", lambda e: e.matmul(out, lhsT=lhsT, rhs=rhs, start=start, stop=stop), r, w)

        def trp(out, in_, ident, r, w):
            S.op("tensor", lambda e: e.transpose(out, in_, ident), r, w)

        def act(out, in_, func, r, w, **kw):
            S.op("scalar", lambda e: e.activation(out=out, in_=in_, func=func, **kw), r, w)

        def tt(eng, out, in0, in1, op, r, w):
            S.op(eng, lambda e: e.tensor_tensor(out=out, in0=in0, in1=in1, op=op), r, w)

        def tsc(eng, out, in0, s1, op0, r, w, s2=None, op1=None):
            if op1 is None:
                S.op(eng, lambda e: e.tensor_scalar(out=out, in0=in0, scalar1=s1, scalar2=None, op0=op0), r, w)
            else:
                S.op(eng, lambda e: e.tensor_scalar(out=out, in0=in0, scalar1=s1, scalar2=s2, op0=op0, op1=op1), r, w)

        def stt(out, in0, scalar, in1, op0, op1, r, w):
            S.op("vector", lambda e: e.scalar_tensor_tensor(out=out, in0=in0, scalar=scalar, in1=in1, op0=op0, op1=op1), r, w)

        def cp(eng, out, in_, r, w):
            if eng == "scalar":
                S.op("scalar", lambda e: e.copy(out=out, in_=in_), r, w)
            else:
                S.op(eng, lambda e: e.tensor_copy(out=out, in_=in_), r, w)

        def recip(out, in_, r, w):
            S.op("vector", lambda e: e.reciprocal(out=out, in_=in_), r, w)

        def memset(eng, ap, val, w):
            S.op(eng, lambda e: e.memset(ap, val), (), w)

        def dma(q, out, in_, r=(), w=()):
            S.dma(q, out, in_, r, w)

        class T:
            _n = [0]

            def __init__(self, stack, name, shape, dt, psum=False):
                T._n[0] += 1
                name = f"{name}_{T._n[0]}"
                self.t = stack.enter_context((nc.psum_tensor if psum else nc.sbuf_tensor)(name, list(shape), dt))
                self.b = Buf(name)

            def __getitem__(self, idx):
                return self.t[idx]

        PA = T(es, "PA", [128, 2048], F32, psum=True)
        PB = T(es, "PB", [128, 2048], F32, psum=True)
        pbank = []
        for i in range(8):
            src = PA if i < 4 else PB
            pbank.append((src.t[:, (i % 4) * 512:(i % 4 + 1) * 512], Buf(f"bank{i}", excl=True)))
        PAb = PA.t.bitcast(BF16)
        PBb = PB.t.bitcast(BF16)

        def bank_bf(i):
            src = PAb if i < 4 else PBb
            return src[:, (i % 4) * 1024:(i % 4 + 1) * 1024]

        ident_f = T(es, "ident_f", [128, 128], F32)
        ident_b = T(es, "ident_b", [128, 128], BF16)
        dma("sync", ident_f[:], cin["ident"], w=[ident_f.b])
        cp("vector", ident_b[:], ident_f[:], [ident_f.b], [ident_b.b])

        hT = T(es, "hT", [128, 8, SEQ], BF16)
        hT_b = [Buf(f"hT{t}") for t in range(NT)]

        def load_gain(ph, l, i):
            gt = T(ph, f"gain{i}", [128, D], F32)
            src = norm_g[l, i:i + 1, :]
            dma("sync", gt[:], bass.AP(src.tensor, src.offset, [[0, 128], [1, D]]), w=[gt.b])
            return gt

        def norm_transpose_tile(ph, t, xt_ap, xt_buf, gt, ss, rs, hb, junk, pbi):
            act(junk[:], xt_ap, AF.Square, [xt_buf], [junk.b, ss.b], accum_out=ss[:, t:t + 1])
            act(rs[:, t:t + 1], ss[:, t:t + 1], AF.Sqrt, [ss.b], [rs.b], scale=1.0 / D, bias=EPS)
            recip(rs[:, t:t + 1], rs[:, t:t + 1], [rs.b], [rs.b])
            stt(hb[:], xt_ap, rs[:, t:t + 1], gt[:], ALU.mult, ALU.mult, [xt_buf, rs.b, gt.b], [hb.b])
            pv, pbuf = bank_bf(pbi), pbank[pbi][1]
            for kc in range(8):
                trp(pv[:, kc * 128:(kc + 1) * 128], hb[:, kc * 128:(kc + 1) * 128], ident_b[:], [hb.b, ident_b.b], [pbuf])
            cp("scalar", hT[:, :, t * 128:(t + 1) * 128], pv.rearrange("p (k s) -> p k s", k=8), [pbuf], [hT_b[t]])

        def load_slab(dst_ap, w2d, c0, ncols, wbuf, nk=8):
            src = w2d[:, c0:c0 + ncols].rearrange("(kc p) n -> p kc n", p=128)
            dma("gpsimd", dst_ap, src, w=[wbuf])

        def proj_fm(wslab, wbuf, col_off, M, rhs_tile, rhs_bufs, out_banks, nk=8, sc_list=(0, 1, 2, 3)):
            for i, sc in enumerate(sc_list):
                pa, pb_ = out_banks[i]
                for kc in range(nk):
                    mm(pa[0:M, :], wslab[:, kc, col_off:col_off + M], rhs_tile[:, kc, sc * 512:(sc + 1) * 512],
                       kc == 0, kc == nk - 1, [wbuf] + rhs_bufs[sc * 4:(sc + 1) * 4], [pb_])

        def phase_A(l, x_src):
            with ExitStack() as ph:
                xt = [T(ph, f"xtA{i}", [128, D], F32) for i in range(2)]
                hb = [T(ph, f"hbA{i}", [128, D], BF16) for i in range(2)]
                junk = [T(ph, f"junkA{i}", [128, D], BF16) for i in range(2)]
                ss = [T(ph, f"ssA{i}", [128, NT], F32) for i in range(2)]
                rs = [T(ph, f"rsA{i}", [128, NT], F32) for i in range(2)]
                g0 = load_gain(ph, l, 0)

                def tileA(t):
                    p = t % 2
                    dma("sync", xt[p][:], x_src[t * 128:(t + 1) * 128, :], r=[b_xres], w=[xt[p].b])
                    norm_transpose_tile(ph, t, xt[p][:], xt[p].b, g0, ss[p], rs[p], hb[p], junk[p], p)

                for t in range(0, NT, 2):
                    S.replay([S.record(lambda: tileA(t)), S.record(lambda: tileA(t + 1))])
                S.barrier()

        def phase_lru(l):
            with ExitStack() as ph:
                prow = T(ph, "prow", [8, D], F32)
                lpT = T(ph, "lpT", [128, 8, 8], F32)
                sp = T(ph, "lru_sp", [128, 8, 6], F32)
                wg = T(ph, "lru_wg", [128, 2, 8, 128], BF16)
                slab = [T(ph, f"lslab{i}", [128, 8, 2, 128], BF16) for i in range(2)]
                XA = [T(ph, f"XA{i}", [128, SEQ + 4], F32) for i in range(2)]
                XC = [T(ph, f"XC{i}", [128, SEQ], F32) for i in range(2)]
                XCB = [T(ph, f"XCB{i}", [128, SEQ], BF16) for i in range(2)]
                R = [T(ph, f"R{i}", [128, SEQ], F32) for i in range(2)]
                A = [T(ph, f"A{i}", [128, SEQ], F32) for i in range(2)]
                I = [T(ph, f"I{i}", [128, SEQ], F32) for i in range(2)]
                H = [T(ph, f"H{i}", [128, SEQ], F32) for i in range(2)]
                GA = [T(ph, f"GA{i}", [128, SEQ], F32) for i in range(2)]
                G = [T(ph, f"G{i}", [128, SEQ], F32) for i in range(2)]
                YA = [T(ph, f"YA{i}", [128, SEQ], BF16) for i in range(2)]
                if os.environ.get("SBUF_DBG"):
                    print("LRU sbuf remaining", nc.sbuf_bytes_remaining)
                for k in range(4):
                    dma("sync", prow[k:k + 1, :], conv_w[l, k:k + 1, :], w=[prow.b])
                dma("sync", prow[4:5, :], conv_b[l:l + 1, :], w=[prow.b])
                dma("sync", prow[5:7, :], lru_bg[l], w=[prow.b])
                dma("sync", prow[7:8, :], lru_lam[l:l + 1, :], w=[prow.b])
                pv, pbuf = pbank[7]
                for c in range(8):
                    trp(pv[:, c * 8:(c + 1) * 8], prow[0:8, c * 128:(c + 1) * 128], ident_f[0:8, 0:8], [prow.b, ident_f.b], [pbuf])
                cp("vector", lpT[:], pv[:, 0:64].rearrange("p (c k) -> p c k", c=8), [pbuf], [lpT.b])
                xs, ln1, ser, msk, nsp8, nsp16 = (sp[:, :, i] for i in range(6))
                act(xs, lpT[:, :, 7], AF.Exp, [lpT.b], [sp.b], scale=-1.0)
                act(ln1, xs, AF.Ln, [sp.b], [sp.b], bias=1.0)
                tsc("vector", ser, xs, -0.25, ALU.mult, [sp.b], [sp.b], 1.0 / 3.0, ALU.add)
                tt("vector", ser, ser, xs, ALU.mult, [sp.b], [sp.b])
                tsc("vector", ser, ser, -1.0, ALU.mult, [sp.b], [sp.b], 0.5, ALU.add)
                tt("vector", ser, ser, xs, ALU.mult, [sp.b], [sp.b])
                tsc("vector", ser, ser, -1.0, ALU.mult, [sp.b], [sp.b], 1.0, ALU.add)
                tt("vector", ser, ser, xs, ALU.mult, [sp.b], [sp.b])
                tsc("vector", msk, xs, 0.03, ALU.is_lt, [sp.b], [sp.b])
                tt("vector", ser, ser, ln1, ALU.subtract, [sp.b], [sp.b])
                tt("vector", ser, ser, msk, ALU.mult, [sp.b], [sp.b])
                tt("vector", ser, ser, ln1, ALU.add, [sp.b], [sp.b])
                tsc("vector", nsp8, ser, -8.0, ALU.mult, [sp.b], [sp.b])
                tsc("vector", nsp16, ser, -16.0, ALU.mult, [sp.b], [sp.b])
                dma("gpsimd", wg[:], lru_wg[l].rearrange("k n c e -> c k n e"), w=[wg.b])
                for p_ in range(2):
                    memset("vector", XA[p_][:, 0:3], 0.0, [XA[p_].b])
                w2d = w_in[l]

                def lru_A(c):
                    p = c % 2
                    xa, xc, xcb, r_, i_, ga = XA[p], XC[p], XCB[p], R[p], I[p], GA[p]
                    sl = slab[p]
                    dma("gpsimd", sl[:, :, 0, :], w2d[:, C_LRUX + c * 128:C_LRUX + (c + 1) * 128].rearrange("(kc p) n -> p kc n", p=128), w=[sl.b])
                    dma("gpsimd", sl[:, :, 1, :], w2d[:, C_LRUG + c * 128:C_LRUG + (c + 1) * 128].rearrange("(kc p) n -> p kc n", p=128), w=[sl.b])
                    slv = sl.t.rearrange("p k a n -> p k (a n)")
                    proj_fm(slv, sl.b, 0, 128, hT.t, hT_b, pbank[0:4])
                    cp("scalar", xa[:, 3:3 + SEQ], PA[:, :], [pbank[i][1] for i in range(4)], [xa.b])
                    cw = lambda k: lpT[:, c, k:k + 1]
                    act(xc[:], xa[:, 3:3 + SEQ], AF.Identity, [xa.b, lpT.b], [xc.b], scale=cw(3), bias=cw(4))
                    for k in range(3):
                        stt(xc[:], xa[:, k:k + SEQ], cw(k), xc[:], ALU.mult, ALU.add, [xa.b, lpT.b, xc.b], [xc.b])
                    cp("gpsimd", xcb[:], xc[:], [xc.b], [xcb.b])
                    for gk in range(2):
                        banks = pbank[4:8] if gk == 0 else pbank[0:4]
                        for sc in range(4):
                            mm(banks[sc][0], wg[:, gk, c, :], xcb[:, sc * 512:(sc + 1) * 512], True, True, [wg.b, xcb.b], [banks[sc][1]])
                    act(r_[:], PB[:, :], AF.Sigmoid, [pbank[i][1] for i in range(4, 8)], [r_.b], bias=lpT[:, c, 5:6])
                    act(i_[:], PA[:, :], AF.Sigmoid, [pbank[i][1] for i in range(4)], [i_.b], bias=lpT[:, c, 6:7])
                    proj_fm(slv, sl.b, 128, 128, hT.t, hT_b, pbank[4:8])
                    cp("scalar", ga[:], PB[:, :], [pbank[i][1] for i in range(4, 8)], [ga.b])

                def lru_B(c):
                    p = c % 2
                    xc, r_, a_, i_, h_, ga, g_ = XC[p], R[p], A[p], I[p], H[p], GA[p], G[p]
                    act(a_[:], r_[:], AF.Exp, [r_.b, sp.b], [a_.b], scale=sp[:, c, 4:5])
                    act(r_[:], r_[:], AF.Exp, [r_.b, sp.b], [r_.b], scale=sp[:, c, 5:6])
                    act(r_[:], r_[:], AF.Identity, [r_.b], [r_.b], scale=-1.0, bias=1.0)
                    act(r_[:], r_[:], AF.Sqrt, [r_.b], [r_.b])
                    tt("gpsimd", i_[:], i_[:], xc[:], ALU.mult, [i_.b, xc.b], [i_.b])
                    tt("gpsimd", i_[:], i_[:], r_[:], ALU.mult, [i_.b, r_.b], [i_.b])
                    S.op("vector", lambda e, h_=h_, a_=a_, i_=i_: e.tensor_tensor_scan(out=h_[:], data0=a_[:], data1=i_[:], initial=0.0, op0=ALU.mult, op1=ALU.add),
                         [a_.b, i_.b], [h_.b])
                    act(g_[:], ga[:], AF.Square, [ga.b], [g_.b])
                    tsc("vector", g_[:], g_[:], 0.044715, ALU.mult, [g_.b], [g_.b], 1.0, ALU.add)
                    tt("gpsimd", g_[:], g_[:], ga[:], ALU.mult, [g_.b, ga.b], [g_.b])
                    act(g_[:], g_[:], AF.Sigmoid, [g_.b], [g_.b], scale=1.5957691216057308)
                    tt("gpsimd", g_[:], g_[:], ga[:], ALU.mult, [g_.b, ga.b], [g_.b])
                    ya = YA[p]
                    tt("vector", ya[:], g_[:], h_[:], ALU.mult, [g_.b, h_.b], [ya.b])
                    dma("sync", ysc[0, c * 128:(c + 1) * 128, :], ya[:], r=[ya.b], w=[b_ysc[0]])

                lru_A(0)
                for c in range(8):
                    lists = [S.record(lambda: lru_B(c))]
                    if c + 1 < 8:
                        lists.append(S.record(lambda: lru_A(c + 1)))
                    S.replay(lists)
                S.barrier()

        def phase_gla(l):
            w2d = w_in[l]
            with ExitStack() as ph:
                qT = T(ph, "gqT", [128, 4, SEQ], F32)
                kT = T(ph, "gkT", [128, 4, SEQ], F32)
                lrT = T(ph, "lrT", [32, SEQ], F32)
                wa2x = T(ph, "wa2x", [32, 512], F32)
                wres = T(ph, "gwres", [128, 8, 2560], BF16)
                gnb = T(ph, "gnb", [128, 4, 256], F32)
                st_f = T(ph, "st_f", [128, 4, 256], F32)
                st_b = T(ph, "st_b", [128, 4, 256], BF16)
                cm4 = T(ph, "cm4", [128, 4, 128], F32)
                triu = T(ph, "triu", [128, 128], F32)
                tril = T(ph, "tril", [128, 128], F32)
                dma("sync", triu[:], cin["triu"], w=[triu.b])
                dma("sync", tril[:], cin["tril"], w=[tril.b])
                for hh in range(4):
                    dma("sync", cm4[:, hh, :], cin["triu"], w=[cm4.b])
                    src = gla_norm[l:l + 1, :]
                    dma("sync", gnb[:, hh, :], bass.AP(src.tensor, src.offset, [[0, 128], [1, 256]]), w=[gnb.b])
                memset("vector", wa2x[:], 0.0, [wa2x.b])
                memset("vector", lrT[:], 1.0, [lrT.b])
                dma("sync", wa2x[0:16, :], gla_wa2[l], w=[wa2x.b])
                dma("sync", wa2x[16:17, :], gla_ba[l:l + 1, :], w=[wa2x.b])
                for i, c0 in enumerate((C_GK, C_GV, C_GV + 512, C_GOG, C_GOG + 512)):
                    load_slab(wres[:, :, i * 512:(i + 1) * 512], w2d, c0, 512, wres.b)
                with ExitStack() as ph2:
                    slab = [T(ph2, f"gslab{i}", [128, 8, 512], BF16) for i in range(2)]
                    lslab = T(ph2, "glslab", [128, 8, 16], BF16)
                    load_slab(slab[0][:], w2d, C_GQ, 512, slab[0].b)
                    load_slab(slab[1][:], w2d, C_GK, 512, slab[1].b)
                    load_slab(lslab[:], w2d, C_GLR, 16, lslab.b)
                    for i in range(8):
                        banks = pbank[0:4] if i % 2 == 0 else pbank[4:8]
                        src = PA if i % 2 == 0 else PB
                        proj_fm(slab[i // 4].t, slab[i // 4].b, (i % 4) * 128, 128, hT.t, hT_b, banks)
                        dst = qT if i < 4 else kT
                        act(dst[:, i % 4, :], src[:, :], AF.Copy, [b for _, b in banks], [dst.b], scale=(128 ** -0.5 if i < 4 else 1.0))
                    proj_fm(lslab.t, lslab.b, 0, 16, hT.t, hT_b, pbank[0:4])
                    cp("vector", lrT[0:16, :], PA[0:16, :], [b for _, b in pbank[0:4]], [lrT.b])
                    S.barrier()
                sp_t = T(ph, "g_sp", [128, 512], F32)
                E1 = [T(ph, f"g_E1{i}", [128, 512], F32) for i in range(2)]
                E2 = T(ph, "g_E2", [128, 512], F32)
                Erb = T(ph, "g_Erb", [128, 512], F32)
                qtb = [T(ph, f"g_qtb{i}", [128, 4, 128], BF16) for i in range(2)]
                ktb = T(ph, "g_ktb", [128, 4, 128], BF16)
                kend = [T(ph, f"g_kend{i}", [128, 512], BF16) for i in range(2)]
                v_bf = [T(ph, f"g_vbf{i}", [128, 1024], BF16) for i in range(2)]
                sg = [T(ph, f"g_sg{i}", [128, 1024], F32) for i in range(2)]
                attm = [T(ph, f"g_attm{i}", [128, 4, 128], BF16) for i in range(2)]
                on = T(ph, "g_on", [128, 1024], F32)
                yc = [T(ph, f"g_yc{i}", [128, 1024], BF16) for i in range(2)]
                ycT = [T(ph, f"g_ycT{i}", [128, 8, 128], BF16) for i in range(2)]
                junk = T(ph, "g_junk", [128, 256], BF16)
                ssq = T(ph, "g_ssq", [128, 4], F32)
                rst = T(ph, "g_rst", [128, 4], F32)
                if os.environ.get("SBUF_DBG"):
                    print("GLA sbuf remaining", nc.sbuf_bytes_remaining)
                bk = lambda i: pbank[i][0]
                bb = lambda i: pbank[i][1]

                def gla_A(t):
                    p = t % 2
                    tsl = slice(t * 128, (t + 1) * 128)
                    e1, qb, ke, vb, sg_, am = E1[p], qtb[p], kend[p], v_bf[p], sg[p], attm[p]
                    mm(bk(0), lrT[0:17, tsl], wa2x[0:17, :], True, True, [lrT.b, wa2x.b], [bb(0)])
                    act(sp_t[:], bk(0), AF.Exp, [bb(0)], [sp_t.b], scale=-1.0)
                    act(sp_t[:], sp_t[:], AF.Ln, [sp_t.b], [sp_t.b], bias=1.0)
                    for hh in range(4):
                        mm(bk(1)[:, hh * 128:(hh + 1) * 128], sp_t[:, hh * 128:(hh + 1) * 128], triu[:], True, True, [sp_t.b, triu.b], [bb(1)])
                    mm(bk(2), tril[:], sp_t[:], True, True, [tril.b, sp_t.b], [bb(2)])
                    act(e1[:], bk(1), AF.Exp, [bb(1)], [e1.b], scale=-1.0 / 16.0)
                    act(E2[:], bk(1), AF.Exp, [bb(1)], [E2.b], scale=1.0 / 16.0)
                    act(Erb[:], bk(2), AF.Exp, [bb(2)], [Erb.b], scale=-1.0 / 16.0)
                    tt("vector", qb[:], qT[:, :, tsl], e1.t.rearrange("p (h s) -> p h s", h=4), ALU.mult, [qT.b, e1.b], [qb.b])
                    tt("gpsimd", ktb[:], kT[:, :, tsl], E2.t.rearrange("p (h s) -> p h s", h=4), ALU.mult, [kT.b, E2.b], [ktb.b])
                    for kc in range(8):
                        mm(bk(3), hT[:, kc, tsl], wres[:, kc, 0:512], kc == 0, kc == 7, [hT_b[t], wres.b], [bb(3)])
                    tt("vector", ke[:], bk(3), Erb[:], ALU.mult, [bb(3), Erb.b], [ke.b])
                    for half in range(2):
                        for kc in range(8):
                            mm(bk(half), hT[:, kc, tsl], wres[:, kc, 512 + half * 512:1024 + half * 512], kc == 0, kc == 7, [hT_b[t], wres.b], [bb(half)])
                    cp("scalar", vb[:], PA[:, 0:1024], [bb(0), bb(1)], [vb.b])
                    for half in range(2):
                        for kc in range(8):
                            mm(bk(2 + half), hT[:, kc, tsl], wres[:, kc, 1536 + half * 512:2048 + half * 512], kc == 0, kc == 7, [hT_b[t], wres.b], [bb(2 + half)])
                    act(sg_[:], PA[:, 1024:2048], AF.Silu, [bb(2), bb(3)], [sg_.b])
                    tt("gpsimd", sg_[:], sg_[:], gnb.t.rearrange("p h e -> p (h e)"), ALU.mult, [sg_.b, gnb.b], [sg_.b])
                    for hh in range(4):
                        mm(bk(0)[:, hh * 128:(hh + 1) * 128], ktb[:, hh, :], qb[:, hh, :], True, True, [ktb.b, qb.b], [bb(0)])
                    tt("vector", am[:], bk(0).rearrange("p (h s) -> p h s", h=4), cm4[:], ALU.mult, [bb(0), cm4.b], [am.b])

                def gla_B(t):
                    p = t % 2
                    tsl = slice(t * 128, (t + 1) * 128)
                    e1, qb, ke, vb, sg_, am = E1[p], qtb[p], kend[p], v_bf[p], sg[p], attm[p]
                    for hh in range(4):
                        ob = 4 + hh // 2
                        oap = bk(ob)[:, (hh % 2) * 256:(hh % 2 + 1) * 256]
                        mm(oap, am[:, hh, :], vb[:, hh * 256:(hh + 1) * 256], hh % 2 == 0, t == 0 and hh % 2 == 1, [am.b, vb.b], [bb(ob)])
                        if t > 0:
                            mm(oap, qb[:, hh, :], st_b[:, hh, :], False, hh % 2 == 1, [qb.b, st_b.b], [bb(ob)])
                    for hh in range(4):
                        kb_ = 6 + hh // 2
                        mm(bk(kb_)[:, (hh % 2) * 256:(hh % 2 + 1) * 256], ke[:, hh * 128:(hh + 1) * 128], vb[:, hh * 256:(hh + 1) * 256],
                           hh % 2 == 0, hh % 2 == 1, [ke.b, vb.b], [bb(kb_)])
                    for hh in range(4):
                        kvp = bk(6 + hh // 2)[:, (hh % 2) * 256:(hh % 2 + 1) * 256]
                        if t == 0:
                            cp("vector", st_f[:, hh, :], kvp, [bb(6 + hh // 2)], [st_f.b])
                        else:
                            dec = e1[:, hh * 128 + 127:hh * 128 + 128]
                            stt(st_f[:, hh, :], st_f[:, hh, :], dec, kvp, ALU.mult, ALU.add, [st_f.b, e1.b, bb(6 + hh // 2)], [st_f.b])
                    cp("gpsimd", st_b[:], st_f[:], [st_f.b], [st_b.b])
                    for hh in range(4):
                        oap = bk(4 + hh // 2)[:, (hh % 2) * 256:(hh % 2 + 1) * 256]
                        act(junk[:], oap, AF.Square, [bb(4 + hh // 2)], [junk.b, ssq.b], accum_out=ssq[:, hh:hh + 1])
                    act(rst[:], ssq[:], AF.Sqrt, [ssq.b], [rst.b], scale=1.0 / 256.0, bias=EPS)
                    recip(rst[:], rst[:], [rst.b], [rst.b])
                    tt("vector", on.t.rearrange("p (h e) -> p h e", h=4), PB[:, 0:1024].rearrange("p (h e) -> p h e", h=4),
                       fap(rst[:], [[1, 4], [0, 256]]), ALU.mult, [bb(4), bb(5), rst.b], [on.b])
                    y = yc[p]
                    tt("gpsimd", y[:], on[:], sg_[:], ALU.mult, [on.b, sg_.b], [y.b])
                    pv = bank_bf(7)
                    for c in range(8):
                        trp(pv[:, c * 128:(c + 1) * 128], y[:, c * 128:(c + 1) * 128], ident_b[:], [y.b, ident_b.b], [bb(7)])
                    yT = ycT[p]
                    cp("scalar", yT[:], pv.rearrange("p (k s) -> p k s", k=8), [bb(7)], [yT.b])
                    dma("sync", ysc[2, :, tsl].rearrange("(c p) s -> p c s", p=128), yT[:], r=[yT.b], w=[b_ysc[2]])

                gla_A(0)
                for t in range(NT):
                    lists = [S.record(lambda: gla_B(t))]
                    if t + 1 < NT:
                        lists.append(S.record(lambda: gla_A(t + 1)))
                    S.replay(lists)
                S.barrier()

        def setup_tables():
            with ExitStack() as ph:
                tblx = T(ph, "tblx", [33, 16], F32)
                memset("vector", tblx[:], NEG, [tblx.b])
                dma("sync", tblx[0:32, :], rel_table, w=[tblx.b])
                for name, dst, n in (("oh_c", tc_d, 4096), ("oh_s", ts_d, 1024), ("oh_w", tw_d, 1024)):
                    oh = T(ph, "t_" + name, [33, n], F32)
                    thi = T(ph, "thi_" + name, [16, n], BF16)
                    tlo = T(ph, "tlo_" + name, [16, n], BF16)
                    dma("sync", oh[:], cin[name], w=[oh.b])
                    for ch in range(n // 512):
                        pa, pbuf = pbank[ch % 8]
                        mm(pa[0:16, :], tblx[0:33, 0:16], oh[0:33, ch * 512:(ch + 1) * 512], True, True, [tblx.b, oh.b], [pbuf])
                        cp("vector", thi[:, ch * 512:(ch + 1) * 512], pa[0:16, :], [pbuf], [thi.b])
                        tt("vector", tlo[:, ch * 512:(ch + 1) * 512], pa[0:16, :], thi[:, ch * 512:(ch + 1) * 512], ALU.subtract, [pbuf, thi.b], [tlo.b])
                    dma("sync", dst[0], thi[:], r=[thi.b], w=[b_tabs])
                    dma("sync", dst[1], tlo[:], r=[tlo.b], w=[b_tabs])
                S.barrier()

        def phase_nsa(l):
            w2d = w_in[l]
            bk = lambda i: pbank[i][0]
            bb = lambda i: pbank[i][1]
            with ExitStack() as ph:
                qT = T(ph, "nqT", [128, 8, SEQ], BF16)
                kS = T(ph, "nkS", [128, 4, SEQ], BF16)
                kW = T(ph, "nkW", [128, 4, SEQ], BF16)
                vS = T(ph, "nvS", [128, NT, 4, 66], BF16)
                vW = T(ph, "nvW", [128, NT, 4, 66], BF16)
                sgate = T(ph, "nsg", [128, NT, 48], F32)
                kcP = T(ph, "nkcP", [128, 2, 4, 128], BF16)
                vcx = T(ph, "nvcx", [128, 4, 98], BF16)
                hbt = T(ph, "nhbt", [128, 3, 4, 2, 512], BF16)
                NM = T(ph, "nNM", [128, 4, 2, 512], BF16)
                Jb = T(ph, "nJb", [128, 2, 128], BF16)
                expd = T(ph, "nexpd", [128, 2, NT, 128], BF16)
                cand = T(ph, "ncand", [128, NT, 32], F32)
                negc = T(ph, "nnegc", [128, NT, 32], F32)
                forced = T(ph, "nforced", [128, NT, 32], F32)
                dma("gpsimd", Jb[:, 0, :], cin["antiid"], w=[Jb.b])
                dma("gpsimd", Jb[:, 1, :], cin["antiid127"], w=[Jb.b])
                dma("gpsimd", expd[:, 0, :, :], cin["expand"], w=[expd.b])
                dma("gpsimd", expd[:, 1, :, :], cin["expand_near"], w=[expd.b])
                memset("vector", NM[:], 0.0, [NM.b])
                dma("sync", cand[:], cin["cand"], w=[cand.b])
                dma("sync", negc[:], cin["negc"], w=[negc.b])
                dma("sync", forced[:], cin["forced"], w=[forced.b])
                memset("vector", vcx[:], 0.0, [vcx.b])
                memset("vector", kcP[:], 0.0, [kcP.b])
                for g in range(4):
                    dma("gpsimd", vcx[:, g, 64:97], cin["ovx"], w=[vcx.b])
                for dl in range(3):
                    tsrc = tw_d if dl == 2 else ts_d
                    for g in range(4):
                        for hl in range(2):
                            for rp in range(2):
                                a0 = tsrc[hl, 4 * g + 2 * rp, 512 + dl * 128 - 127:512 + dl * 128 - 127 + 1]
                                src = bass.AP(a0.tensor, a0.offset, [[1, 128], [1024, 2], [1, 128]])
                                dst = hbt[:, dl, g, hl, :].rearrange("p (a b s) -> p a b s", a=2, b=2)[:, :, rp, :]
                                dma("sync", dst, src, r=[b_tabs], w=[hbt.b])
                for g in range(4):
                    for hl in range(2):
                        for par in range(2):
                            for rp in range(2):
                                h = 4 * g + 2 * rp + par
                                a0 = ts_d[hl, h, 640:641]
                                src = bass.AP(a0.tensor, a0.offset, [[0, 1], [0, 2], [1, 128]])
                                c0 = par * 256 + rp * 128
                                dma("sync", NM[32 + hl:33 + hl, g, :, c0:c0 + 128], src, r=[b_tabs], w=[NM.b])
                memset("vector", vS[:, :, :, 64:66], 1.0, [vS.b])
                memset("vector", vW[:, :, :, 64:66], 1.0, [vW.b])
                if stop == "nsa0":
                    S.barrier()
                    return
                with ExitStack() as ph2:
                    slab = [T(ph2, f"nslab{i}", [128, 8, 512], BF16) for i in range(2)]
                    wv = T(ph2, "nwv", [128, 8, 560], BF16)
                    for half in range(2):
                        sl = slab[half]
                        load_slab(sl[:], w2d, C_Q + half * 512, 512, sl.b)
                        for i in range(4):
                            c = half * 4 + i
                            banks = pbank[0:4] if c % 2 == 0 else pbank[4:8]
                            src = PA if c % 2 == 0 else PB
                            proj_fm(sl.t, sl.b, i * 128, 128, hT.t, hT_b, banks)
                            act(qT[:, c, :], src[:, :], AF.Copy, [b for _, b in banks], [qT.b], scale=0.125)
                    if stop == "nsa1a":
                        S.barrier()
                        return
                    n = 0
                    for idx, dst in ((2, kS), (4, kW)):
                        sl = slab[n % 2]
                        n += 1
                        for g in range(4):
                            c0 = C_KV + idx * 256 + g * 64
                            for dup in range(2):
                                dma("gpsimd", sl[:, :, g * 128 + dup * 64:g * 128 + dup * 64 + 64],
                                    w2d[:, c0:c0 + 64].rearrange("(kc p) n -> p kc n", p=128), w=[sl.b])
                        for g in range(4):
                            banks = pbank[0:4] if g % 2 == 0 else pbank[4:8]
                            src = PA if g % 2 == 0 else PB
                            proj_fm(sl.t, sl.b, g * 128, 128, hT.t, hT_b, banks)
                            cp("scalar" if g % 2 == 0 else "vector", dst[:, g, :], src[:, :], [b for _, b in banks], [dst.b])
                    if stop == "nsa1b":
                        S.barrier()
                        return
                    load_slab(wv[:, :, 0:256], w2d, C_KV + 3 * 256, 256, wv.b)
                    load_slab(wv[:, :, 256:512], w2d, C_KV + 5 * 256, 256, wv.b)
                    load_slab(wv[:, :, 512:560], w2d, C_GATE, 48, wv.b)
                    for t in range(NT):
                        tsl = slice(t * 128, (t + 1) * 128)
                        b0, b1 = (0, 1) if t % 2 == 0 else (2, 3)
                        for kc in range(8):
                            mm(bk(b0), hT[:, kc, tsl], wv[:, kc, 0:512], kc == 0, kc == 7, [hT_b[t], wv.b], [bb(b0)])
                        import os
                        SK = os.environ.get("NSA_SKIP", "")
                        if "g" not in SK:
                            for kc in range(8):
                                mm(bk(b1)[:, 0:48], hT[:, kc, tsl], wv[:, kc, 512:560], kc == 0, kc == 7, [hT_b[t], wv.b], [bb(b1)])
                        if "v" not in SK:
                            cp("vector", vS[:, t, :, 0:64], bk(b0)[:, 0:256].rearrange("p (g d) -> p g d", g=4), [bb(b0)], [vS.b])
                        if "w" not in SK:
                            cp("scalar", vW[:, t, :, 0:64], bk(b0)[:, 256:512].rearrange("p (g d) -> p g d", g=4), [bb(b0)], [vW.b])
                        if "g" not in SK:
                            act(sgate[:, t, :], bk(b1)[:, 0:48], AF.Sigmoid, [bb(b1)], [sgate.b])
                    S.barrier()
                if stop == "nsa1":
                    return
                with ExitStack() as ph2:
                    slab = [T(ph2, f"ncslab{i}", [128, 8, 256], BF16) for i in range(2)]
                    w1sb = T(ph2, "nw1", [128, 32, 256], BF16)
                    w2sb = T(ph2, "nw2", [128, 2, 128], BF16)
                    prow2 = T(ph2, "nprow2", [32, 128], F32)
                    posT = T(ph2, "nposT", [128, 32], F32)
                    XAB = [T(ph2, f"nXAB{i}", [128, SEQ], BF16) for i in range(2)]
                    gtmp = T(ph2, "ngtmp", [128, 2, 128], F32)
                    geluT = T(ph2, "ngeluT", [128, 2, 128], BF16)
                    for kv in range(2):
                        for dup in range(2):
                            dma("gpsimd", w1sb[dup * 64:(dup + 1) * 64, :, :], cmp_w1[l, kv].rearrange("(p d) j -> d p j", d=64), w=[w1sb.b])
                            dma("gpsimd", w2sb[:, :, dup * 64:(dup + 1) * 64], cmp_w2[l, kv].rearrange("(jc p) d -> p jc d", p=128), w=[w2sb.b])
                            dma("sync", prow2[:, dup * 64:(dup + 1) * 64], cmp_pos[l, kv], w=[prow2.b])
                        trp(bk(6)[:, 0:32], prow2[:, :], ident_f[0:32, 0:32], [prow2.b, ident_f.b], [bb(6)])
                        cp("vector", posT[:], bk(6)[:, 0:32], [bb(6)], [posT.b])
                        sl = slab[kv]
                        load_slab(sl[:, :, 0:256], w2d, C_KV + kv * 256, 256, sl.b)
                        for cc in range(2):
                            banks = pbank[0:4]
                            proj_fm(sl.t, sl.b, cc * 128, 128, hT.t, hT_b, banks)
                            for ab in range(2):
                                tt("vector" if ab == 0 else "gpsimd" if False else "vector", XAB[ab].t.rearrange("p (i q) -> p i q", q=16), PA.t.rearrange("p (i q) -> p i q", q=16),
                                   fap(posT[:, ab * 16:ab * 16 + 1], [[0, 128], [1, 16]]), ALU.add, [b for _, b in banks] + [posT.b], [XAB[ab].b])
                            for gg in range(2):
                                g = cc * 2 + gg
                                rows = slice(gg * 64, gg * 64 + 64)
                                hb_, hbb = bk(4 + 2 * gg), bb(4 + 2 * gg)
                                for jc in range(2):
                                    for p in range(32):
                                        srcT = XAB[0] if p < 16 else XAB[1]
                                        rhs = fap(srcT[rows, p:p + 1], [[16, 127]])
                                        mm(hb_[:, jc * 128:jc * 128 + 127], w1sb[rows, p, jc * 128:(jc + 1) * 128], rhs, p == 0, p == 31,
                                           [w1sb.b, srcT.b], [hbb])
                                hv = hb_[:, 0:256].rearrange("p (j i) -> p j i", j=2)[:, :, 0:127]
                                gv = gtmp[:, :, 0:127]
                                act(gv, hv, AF.Square, [hbb], [gtmp.b])
                                tsc("vector", gv, gv, 0.044715, ALU.mult, [gtmp.b], [gtmp.b], 1.0, ALU.add)
                                tt("vector", gv, gv, hv, ALU.mult, [gtmp.b, hbb], [gtmp.b])
                                act(gv, gv, AF.Sigmoid, [gtmp.b], [gtmp.b], scale=1.5957691216057308)
                                tt("vector", geluT[:, :, 0:127], gv, hv, ALU.mult, [gtmp.b, hbb], [geluT.b])
                                if kv == 0:
                                    for jc in range(2):
                                        mm(bk(5)[:, 0:127], w2sb[:, jc, :], geluT[:, jc, 0:127], jc == 0, jc == 1, [w2sb.b, geluT.b], [bb(5)])
                                    cp("scalar", kcP[0:64, 0, g, 0:127], bk(5)[0:64, 0:127], [bb(5)], [kcP.b])
                                    cp("scalar", kcP[64:128, 1, g, 0:127], bk(5)[64:128, 0:127], [bb(5)], [kcP.b])
                                else:
                                    for jc in range(2):
                                        mm(bk(5)[0:127, 0:64], geluT[:, jc, 0:127], w2sb[:, jc, 0:64], jc == 0, jc == 1, [w2sb.b, geluT.b], [bb(5)])
                                    cp("scalar", vcx[0:127, g, 0:64], bk(5)[0:127, 0:64], [bb(5)], [vcx.b])
                    S.barrier()
                if stop == "nsa2":
                    return
                dma("sync", hsc, hT.t.rearrange("p k s -> p (k s)"), r=hT_b, w=[b_hsc])
                kpad1 = Buf("kpad1")
                for base, KT_, ceng in ((0, kS, "scalar"), (4, kW, "vector")):
                    cp(ceng, hT[64:128, base:base + 4, :], KT_[64:128, :, :], [KT_.b], hT_b + [kpad1])
                    memset("gpsimd", hT[0:64, base:base + 4, :], 0.0, hT_b + [kpad1])
                    memset("gpsimd" if base == 0 else "vector", KT_[64:128, :, :], 0.0, [KT_.b])
                E = [T(ph, f"nE{i}", [128, 512], BF16) for i in range(4)]
                cb = [T(ph, f"ncb{i}", [128, 2, 512], BF16) for i in range(4)]
                for cbx in cb:
                    memset("vector", cbx[:], NEG, [cbx.b])
                ybt = [T(ph, f"nybt{i}", [128, 1024], BF16) for i in range(2)]
                ybT = [T(ph, f"nybT{i}", [128, 8, 128], BF16) for i in range(2)]
                sets = []
                for i in range(2):
                    sets.append(dict(
                        ybacc=T(ph, f"nybacc{i}", [128, 4, 64], F32), tmp1=T(ph, f"ntmp1{i}", [128, 4, 64], F32),
                        tmp2=T(ph, f"ntmp2{i}", [128, 4, 64], F32), impr=T(ph, f"nimpr{i}", [128, 4, 32], F32),
                        imp=T(ph, f"nimp{i}", [128, 32], F32), m8=T(ph, f"nm8{i}", [128, 8], F32),
                        sm=T(ph, f"nsm{i}", [128, 3, 4], F32), Us=(3, 6)[i], Uw=(4, 7)[i]))
                colb = lambda r: (r % 2) * 256 + (r // 2) * 128
                cnt_ = dict(l=0, e=0, kp=0, cb=0)
                tasks = []

                inflight = set()

                def next_Li(hold=False):
                    while True:
                        i = (0, 1, 5)[cnt_["l"] % 3]
                        cnt_["l"] += 1
                        if i not in inflight:
                            break
                    if hold:
                        inflight.add(i)
                    return i

                def next_L(hold=False):
                    return pbank[next_Li(hold)]

                def release_L(Lb):
                    for i in (0, 1, 5):
                        if pbank[i][1] is Lb:
                            inflight.discard(i)

                def next_E():
                    e_ = E[cnt_["e"] % 4]
                    cnt_["e"] += 1
                    return e_

                cb_dma = []
                PF = 3

                def mk_cmp(qt, g, st):
                    qsl = slice(qt * 128, (qt + 1) * 128)
                    ui = len(cb_dma)
                    cbt = cb[ui % 4]
                    box = {}

                    def issue():
                        for hl in range(2):
                            for rp in range(2):
                                a0 = tc_d[hl, 4 * g + 2 * rp, qt * 128 + 1:qt * 128 + 2]
                                src = bass.AP(a0.tensor, a0.offset, [[16, 128], [4096, 2], [1, 128]])
                                dst = cbt[:, hl, :].rearrange("p (a b s) -> p a b s", a=2, b=2)[:, :, rp, :]
                                dma("sync", dst, src, r=[b_tabs], w=[cbt.b])

                    cb_dma.append(issue)

                    def pre():
                        if ui + PF < len(cb_dma):
                            cb_dma[ui + PF]()

                    def s1():
                        L, Lb = next_L(hold=True)
                        box["L"] = (L, Lb)
                        for par in range(2):
                            mm(L[:, par * 256:(par + 1) * 256], kcP[:, par, g, :], qT[:, 2 * g:2 * g + 2, qsl], par == 0, False, [kcP.b, qT.b], [Lb])
                        for hl in range(2):
                            mm(L, Jb[:, 1, :], cbt[:, hl, :], False, hl == 1, [Jb.b, cbt.b], [Lb])

                    def s2():
                        L, Lb = box["L"]
                        release_L(Lb)
                        Ec = next_E()
                        act(Ec[:], L, AF.Exp, [Lb], [Ec.b])
                        Uc = bk(2)
                        for r in range(4):
                            mm(Uc[:, r * 98:(r + 1) * 98], Ec[:, colb(r):colb(r) + 128], vcx[:, g, 0:98], r == 0, r == 3, [Ec.b, vcx.b], [bb(2)])

                    def post():
                        Uc = bk(2)
                        sm, ybacc, impr, imp, m8 = st["sm"], st["ybacc"], st["impr"], st["imp"], st["m8"]
                        ucv = lambda a, b_: fap(Uc[:, a:a + 1], [[98, 4], [1, b_]])
                        rs4, wc = sm[:, 0, :], sm[:, 1, :]
                        tsc("vector", rs4, fap(Uc[:, 96:97], [[98, 4]]), 1e-30, ALU.max, [bb(2)], [sm.b])
                        recip(rs4, rs4, [sm.b], [sm.b])
                        tt("vector", wc, rs4, sgate[:, qt, 4 * g:4 * g + 4], ALU.mult, [sm.b, sgate.b], [sm.b])
                        tt("vector", ybacc[:], ucv(0, 64), fap(wc, [[1, 4], [0, 64]]), ALU.mult, [bb(2), sm.b], [ybacc.b])
                        tt("vector", impr[:], ucv(64, 32), fap(rs4, [[1, 4], [0, 32]]), ALU.mult, [bb(2), sm.b], [impr.b])
                        S.op("vector", lambda e: e.tensor_reduce(out=imp[:], in_=fap(impr[:, 0, 0:1], [[1, 32], [32, 4]]), axis=AX.X, op=ALU.add),
                             [impr.b], [imp.b])
                        tt("vector", imp[:], imp[:], cand[:, qt, :], ALU.mult, [imp.b, cand.b], [imp.b])
                        tt("vector", imp[:], imp[:], negc[:, qt, :], ALU.add, [imp.b, negc.b], [imp.b])
                        S.op("vector", lambda e: e.max(out=m8[:], in_=imp[:]), [imp.b], [m8.b])
                        tsc("vector", imp[:], imp[:], m8[:, 4:5], ALU.is_ge, [imp.b, m8.b], [imp.b])
                        tt("vector", imp[:], imp[:], forced[:, qt, :], ALU.max, [imp.b, forced.b], [imp.b])
                        tsc("vector", imp[:], imp[:], -1.0, ALU.add, [imp.b], [imp.b], -NEG, ALU.mult)

                    return dict(pre=pre, s1=s1, s2=s2, post=post, defer=None, nm=None, first_slc=False)

                def mk_tile(qt, g, st, br, kt, first, last, buf, hooks_pre, hooks_post, defer):
                    qsl = slice(qt * 128, (qt + 1) * 128)
                    ksl = slice(kt * 128, (kt + 1) * 128)
                    KT = kS if br == 0 else kW
                    VT = vS if br == 0 else vW
                    Ub = st["Us"] if br == 0 else st["Uw"]
                    dl = qt - kt
                    near = dl < (2 if br == 0 else 3)
                    box = {}

                    def pre():
                        for h_ in hooks_pre:
                            h_()

                    def s1():
                        L, Lb = next_L(hold=True)
                        box["L"] = (L, Lb)
                        mm(L[:, 0:256], KT[:, g, ksl], qT[:, 2 * g:2 * g + 2, qsl], True, False, [KT.b, qT.b], [Lb])
                        mm(L[:, 256:512], hT[:, (0 if br == 0 else 4) + g, ksl], qT[:, 2 * g:2 * g + 2, qsl], False, False, [kpad1, qT.b], [Lb])
                        if br == 0:
                            mm(L, expd[:, 1 if near else 0, kt, :], NM[:, g, buf, :], False, not near, [expd.b, NM.b], [Lb])
                        if near:
                            for hl in range(2):
                                mm(L, Jb[:, 0, :], hbt[:, dl, g, hl, :], False, hl == 1, [Jb.b, hbt.b], [Lb])

                    def s2():
                        L, Lb = box["L"]
                        release_L(Lb)
                        Et = next_E()
                        act(Et[:], L, AF.Exp, [Lb], [Et.b])
                        for r in range(4):
                            mm(bk(Ub)[:, r * 66:(r + 1) * 66], Et[:, colb(r):colb(r) + 128], VT[:, kt, g, 0:66],
                               first and r == 0, last and r == 3, [Et.b, VT.b], [bb(Ub)])

                    def post():
                        for h_ in hooks_post:
                            h_()

                    return dict(pre=pre, s1=s1, s2=s2, post=post, defer=defer, nm=None, first_slc=False)

                def mk_nm_hook(g, st, buf):
                    def hook():
                        imp = st["imp"]
                        M_, Mb = next_L()
                        trp(M_[0:32, 0:128], imp[:, :], ident_f[:, :], [imp.b, ident_f.b], [Mb])
                        cp("vector", NM[0:32, g, buf, :].rearrange("p (a s) -> p a s", a=4), fap(M_[0:32, 0:1], [[0, 4], [1, 128]]), [Mb], [NM.b])
                    return hook

                def mk_combine(qt, g, st, ybq):
                    def hook():
                        sm, ybacc, tmp1, tmp2 = st["sm"], st["ybacc"], st["tmp1"], st["tmp2"]
                        for br in range(2):
                            ub = st["Us"] if br == 0 else st["Uw"]
                            U = bk(ub)
                            rsb, wb_ = sm[:, 0, :], sm[:, 1 + br, :]
                            S.op("vector", lambda e, U=U, rsb=rsb: e.reciprocal(out=rsb, in_=fap(U[:, 64:65], [[66, 4]])), [bb(ub)], [sm.b])
                            tt("vector", wb_, rsb, sgate[:, qt, 16 * (br + 1) + 4 * g:16 * (br + 1) + 4 * g + 4], ALU.mult, [sm.b, sgate.b], [sm.b])
                            tgt = tmp1 if br == 0 else tmp2
                            tt("vector", tgt[:], fap(U[:, 0:1], [[66, 4], [1, 64]]), fap(wb_, [[1, 4], [0, 64]]), ALU.mult, [bb(ub), sm.b], [tgt.b])
                        tt("gpsimd", tmp1[:], tmp1[:], ybacc[:], ALU.add, [tmp1.b, ybacc.b], [tmp1.b])
                        tt("gpsimd", ybq[:, g * 256:(g + 1) * 256].rearrange("p (r d) -> p r d", r=4), tmp1[:], tmp2[:], ALU.add, [tmp1.b, tmp2.b], [ybq.b])
                    return hook

                def mk_ybout(qt, ybq):
                    def hook():
                        qsl = slice(qt * 128, (qt + 1) * 128)
                        li = next_Li()
                        pv = bank_bf(li)
                        for c in range(8):
                            trp(pv[:, c * 128:(c + 1) * 128], ybq[:, c * 128:(c + 1) * 128], ident_b[:], [ybq.b, ident_b.b], [bb(li)])
                        yT = ybT[qt % 2]
                        cp("scalar", yT[:], pv.rearrange("p (k s) -> p k s", k=8), [bb(li)], [yT.b])
                        dma("sync", ysc[1, :, qsl].rearrange("(c p) s -> p c s", p=128), yT[:], r=[yT.b], w=[b_ysc[1]])
                    return hook

                un = 0
                units = []
                for qt in range(1 if stop == 'nsa3' else NT):
                    ybq = ybt[qt % 2]
                    for g in range(4):
                        st = sets[un % 2]
                        buf = un % 2
                        un += 1
                        uc_ = [mk_cmp(qt, g, st)]
                        uc_[0]["nm"] = mk_nm_hook(g, st, buf)
                        uc_[0]["unit"] = len(units)
                        wk = list(range(max(0, qt - 2), qt + 1))
                        uw_ = [mk_tile(qt, g, st, 1, kt, kt == wk[0], kt == qt, buf, [], [], None) for kt in wk]
                        us_ = []
                        for kt in range(qt + 1):
                            hp = []
                            hq = [mk_combine(qt, g, st, ybq)] if kt == qt else []
                            df = mk_ybout(qt, ybq) if (kt == qt and g == 3) else None
                            us_.append(mk_tile(qt, g, st, 0, kt, kt == 0, kt == qt, buf, hp, hq, df))
                        us_[0]["first_slc"] = True
                        us_[0]["unit"] = len(units)
                        units.append((uc_, uw_, us_))
                tasks += units[0][0] + units[0][1]
                for ui in range(len(units)):
                    if ui + 1 < len(units):
                        tasks += units[ui + 1][0]
                    tasks += units[ui][2]
                    if ui + 1 < len(units):
                        tasks += units[ui + 1][1]
                for ui_ in range(min(PF, len(cb_dma))):
                    cb_dma[ui_]()
                deferred = {}
                ntk = len(tasks)
                LA = 2
                for j in range(min(LA, ntk)):
                    tasks[j]["pre"]()
                    tasks[j]["s1"]()
                first_idx = {tk["unit"]: i for i, tk in enumerate(tasks) if tk["first_slc"]}
                for i, tk in enumerate(tasks):
                    tk["s2"]()
                    tk["post"]()
                    if tk["nm"] is not None:
                        j = max(i, min(i + 8, first_idx[tk["unit"]] - LA))
                        deferred.setdefault(j, []).append(tk["nm"])
                    if tk["defer"] is not None:
                        deferred.setdefault(i + 3, []).append(tk["defer"])
                    for fn in deferred.pop(i, []):
                        fn()
                    if i + LA < ntk:
                        tasks[i + LA]["pre"]()
                        tasks[i + LA]["s1"]()
                for k_ in sorted(deferred):
                    for fn in deferred[k_]:
                        fn()
                dma("sync", hT.t.rearrange("p k s -> p (k s)"), hsc, r=[b_hsc], w=hT_b + [kpad1])
                S.barrier()

        def phase_tail(l, x_src, x_dst, b_xsrc, b_xdst):
            bk = lambda i: pbank[i][0]
            bb = lambda i: pbank[i][1]
            w2d = w_in[l]
            with ExitStack() as ph:
                mrgb = T(ph, "mrgb", [128, 8, SEQ], BF16)
                with ExitStack() as ph2:
                    mrg = T(ph2, "mrg", [128, 8, SEQ], F32)
                    yT = T(ph2, "m_yT", [128, 8, SEQ], BF16)
                    yb_ = [Buf(f"m_yT{c}") for c in range(8)]
                    wbr = [T(ph2, f"m_wbr{i}", [128, 8, 256], BF16) for i in range(2)]
                    wmg = [T(ph2, f"m_wmg{i}", [128, 8, 256], BF16) for i in range(2)]
                    sig = [T(ph2, f"m_sig{i}", [128, 512], F32) for i in range(2)]
                    prod = [T(ph2, f"m_prod{i}", [128, 512], F32) for i in range(2)]
                    n = 0
                    bn = 0
                    for br in range(3):
                        for c in range(8):
                            dma("sync", yT[:, c, :], ysc[br, c * 128:(c + 1) * 128, :], r=[b_ysc[br]], w=[yb_[c]])
                        for oc2 in range(4):
                            wb, wm = wbr[oc2 % 2], wmg[oc2 % 2]
                            load_slab(wb[:], w_branch[l, br], oc2 * 256, 256, wb.b)
                            load_slab(wm[:], w2d, C_MG + br * 1024 + oc2 * 256, 256, wm.b)
                            for o in range(2):
                                oc = oc2 * 2 + o
                                for sc in range(4):
                                    ssl = slice(sc * 512, (sc + 1) * 512)
                                    bB, bG = (bn % 4) * 2, (bn % 4) * 2 + 1
                                    bn += 1
                                    for c in range(8):
                                        mm(bk(bB), wb[:, c, o * 128:(o + 1) * 128], yT[:, c, ssl], c == 0, c == 7, [wb.b, yb_[c]], [bb(bB)])
                                    for kc in range(8):
                                        mm(bk(bG), wm[:, kc, o * 128:(o + 1) * 128], hT[:, kc, ssl], kc == 0, kc == 7, [wm.b] + hT_b[sc * 4:(sc + 1) * 4], [bb(bG)])
                                    sg_, pr_ = sig[n % 2], prod[n % 2]
                                    n += 1
                                    act(sg_[:], bk(bG), AF.Sigmoid, [bb(bG)], [sg_.b])
                                    if br == 0:
                                        tt("vector", mrg[:, oc, ssl], sg_[:], bk(bB), ALU.mult, [sg_.b, bb(bB)], [mrg.b])
                                    elif br == 1:
                                        tt("vector", pr_[:], sg_[:], bk(bB), ALU.mult, [sg_.b, bb(bB)], [pr_.b])
                                        tt("vector", mrg[:, oc, ssl], mrg[:, oc, ssl], pr_[:], ALU.add, [mrg.b, pr_.b], [mrg.b])
                                    else:
                                        tt("vector", pr_[:], sg_[:], bk(bB), ALU.mult, [sg_.b, bb(bB)], [pr_.b])
                                        tt("vector", mrgb[:, oc, ssl], mrg[:, oc, ssl], pr_[:], ALU.add, [mrg.b, pr_.b], [mrgb.b])
                    S.barrier()
                with ExitStack() as ph2:
                    wout = T(ph2, "p_wout", [128, 8, D], BF16)
                    g1 = load_gain(ph2, l, 1)
                    g2 = load_gain(ph2, l, 2)
                    xt = [T(ph2, f"p_xt{i}", [128, D], F32) for i in range(2)]
                    on = [T(ph2, f"p_on{i}", [128, D], F32) for i in range(2)]
                    hb = [T(ph2, f"p_hb{i}", [128, D], BF16) for i in range(2)]
                    junk = T(ph2, "p_junk", [128, D], BF16)
                    ss = T(ph2, "p_ss", [128, NT], F32)
                    rs = T(ph2, "p_rs", [128, NT], F32)
                    ss2 = T(ph2, "p_ss2", [128, NT], F32)
                    rs2 = T(ph2, "p_rs2", [128, NT], F32)
                    for half in range(2):
                        load_slab(wout[:, :, half * 512:(half + 1) * 512], w_out[l], half * 512, 512, wout.b)
                    junk2 = T(ph2, "p_junk2", [128, D], BF16)
                    ssb = T(ph2, "p_ssb", [128, NT], F32)
                    rsb_ = T(ph2, "p_rsb", [128, NT], F32)
                    ss2b = T(ph2, "p_ss2b", [128, NT], F32)
                    rs2b = T(ph2, "p_rs2b", [128, NT], F32)

                    def tileP(t):
                        p = t % 2
                        jk, s_a, r_a, s_b, r_b = (junk, ss, rs, ss2, rs2) if p == 0 else (junk2, ssb, rsb_, ss2b, rs2b)
                        tsl = slice(t * 128, (t + 1) * 128)
                        b0 = p * 2
                        ov = PA[:, b0 * 512:(b0 + 2) * 512]
                        obufs = [bb(b0), bb(b0 + 1)]
                        for half in range(2):
                            for c in range(8):
                                mm(bk(b0 + half), mrgb[:, c, tsl], wout[:, c, half * 512:(half + 1) * 512], c == 0, c == 7, [mrgb.b, wout.b], [bb(b0 + half)])
                        x_, o_ = xt[p], on[p]
                        dma("sync", x_[:], x_src[tsl, :], r=[b_xsrc], w=[x_.b])
                        act(jk[:], ov, AF.Square, obufs, [jk.b, s_a.b], accum_out=s_a[:, t:t + 1])
                        act(r_a[:, t:t + 1], s_a[:, t:t + 1], AF.Sqrt, [s_a.b], [r_a.b], scale=1.0 / D, bias=EPS)
                        recip(r_a[:, t:t + 1], r_a[:, t:t + 1], [r_a.b], [r_a.b])
                        stt(o_[:], ov, r_a[:, t:t + 1], g1[:], ALU.mult, ALU.mult, obufs + [r_a.b, g1.b], [o_.b])
                        tt("gpsimd", o_[:], o_[:], x_[:], ALU.add, [o_.b, x_.b], [o_.b])
                        dma("sync", xmid[tsl, :], o_[:], r=[o_.b], w=[b_xmid])
                        norm_transpose_tile(ph2, t, o_[:], o_.b, g2, s_b, r_b, hb[p], jk, 4 + p)

                    for t in range(0, NT, 2):
                        S.replay([S.record(lambda: tileP(t)), S.record(lambda: tileP(t + 1))])
                    S.barrier()
            with ExitStack() as ph:
                hid = T(ph, "f_hid", [128, 22, SEQ], BF16)
                hid_b = [Buf(f"hid{j}") for j in range(22)]
                wfo = T(ph, "f_wfo", [128, 22, D], BF16)
                wfo_b = [Buf(f"wfo{j}") for j in range(11)]
                wfi = [T(ph, f"f_wfi{i}", [128, 8, 2, 128], BF16) for i in range(2)]
                sgt = [T(ph, f"f_sg{i}", [128, 512], F32) for i in range(2)]
                g3 = load_gain(ph, l, 3)
                xt = [T(ph, f"f_xt{i}", [128, D], F32) for i in range(2)]
                on = [T(ph, f"f_on{i}", [128, D], F32) for i in range(2)]
                junk = T(ph, "f_junk", [128, D], BF16)
                ss = T(ph, "f_ss", [128, NT], F32)
                rs = T(ph, "f_rs", [128, NT], F32)
                n = 0
                bn = 0
                for j in range(22):
                    wf = wfi[j % 2]
                    dma("gpsimd", wf[:, :, 0, :], w_ffn_in[l][:, j * 128:(j + 1) * 128].rearrange("(kc p) n -> p kc n", p=128), w=[wf.b])
                    dma("gpsimd", wf[:, :, 1, :], w_ffn_in[l][:, DFF + j * 128:DFF + (j + 1) * 128].rearrange("(kc p) n -> p kc n", p=128), w=[wf.b])
                    if j % 2 == 0:
                        jj = j // 2
                        dma("gpsimd", wfo[:, 2 * jj:2 * jj + 2, :], w_ffn_out[l][jj * 256:(jj + 1) * 256, :].rearrange("(j p) n -> p j n", p=128), w=[wfo_b[jj]])
                    for sc in range(4):
                        ssl = slice(sc * 512, (sc + 1) * 512)
                        bG, bU = (bn % 4) * 2, (bn % 4) * 2 + 1
                        bn += 1
                        for kc in range(8):
                            mm(bk(bG), wf[:, kc, 0, :], hT[:, kc, ssl], kc == 0, kc == 7, [wf.b] + hT_b[sc * 4:(sc + 1) * 4], [bb(bG)])
                        for kc in range(8):
                            mm(bk(bU), wf[:, kc, 1, :], hT[:, kc, ssl], kc == 0, kc == 7, [wf.b] + hT_b[sc * 4:(sc + 1) * 4], [bb(bU)])
                        sg_ = sgt[n % 2]
                        n += 1
                        act(sg_[:], bk(bG), AF.Silu, [bb(bG)], [sg_.b])
                        tt("vector", hid[:, j, ssl], sg_[:], bk(bU), ALU.mult, [sg_.b, bb(bU)], [hid_b[j]])
                for t in range(NT):
                    tsl = slice(t * 128, (t + 1) * 128)
                    b0 = (t % 2) * 2
                    ov = PA[:, b0 * 512:(b0 + 2) * 512]
                    obufs = [bb(b0), bb(b0 + 1)]
                    for half in range(2):
                        for j in range(22):
                            mm(bk(b0 + half), hid[:, j, tsl], wfo[:, j, half * 512:(half + 1) * 512], j == 0, j == 21, [hid_b[j], wfo_b[j // 2]], [bb(b0 + half)])
                    x_, o_ = xt[t % 2], on[t % 2]
                    dma("sync", x_[:], xmid[tsl, :], r=[b_xmid], w=[x_.b])
                    act(junk[:], ov, AF.Square, obufs, [junk.b, ss.b], accum_out=ss[:, t:t + 1])
                    act(rs[:, t:t + 1], ss[:, t:t + 1], AF.Sqrt, [ss.b], [rs.b], scale=1.0 / D, bias=EPS)
                    recip(rs[:, t:t + 1], rs[:, t:t + 1], [rs.b], [rs.b])
                    stt(o_[:], ov, rs[:, t:t + 1], g3[:], ALU.mult, ALU.mult, obufs + [rs.b, g3.b], [o_.b])
                    tt("gpsimd", o_[:], o_[:], x_[:], ALU.add, [o_.b, x_.b], [o_.b])
                    dma("sync", x_dst[tsl, :], o_[:], r=[o_.b], w=[b_xdst])
                S.barrier()

        setup_tables()
        for l in range(n_layers):
            phase_A(l, x_in if l == 0 else xres)
            import os
            if not os.environ.get("SKIP_LG"):
                phase_lru(l)
                if stop == "lru":
                    break
                phase_gla(l)
                if stop == "gla":
                    break
            if not os.environ.get("SKIP_NSA"):
                phase_nsa(l)
            if stop is not None and stop.startswith("nsa"):
                break
            last = (l == n_layers - 1)
            phase_tail(l, x_in if l == 0 else xres, y_out if last else xres, Buf() if l == 0 else b_xres, Buf() if last else b_xres)
        S.barrier()
        S.emit()
    return nc, consts


_CACHE = {}


def kernel(**inputs):
    if "nc" not in _CACHE:
        _CACHE["nc"] = build()
    nc, consts = _CACHE["nc"]
    x = np.ascontiguousarray(np.asarray(inputs["x"], dtype=np.float32))
    shared = {k: np.ascontiguousarray(np.asarray(v, dtype=np.float32)) for k, v in inputs.items() if k != "x"}
    for k, v in consts.items():
        shared["c_" + k] = v
    in_maps = [dict(shared, x=x[i]) for i in range(8)]
    res = run_bass_kernel_spmd(nc, in_maps, core_ids=list(range(8)))
    return np.stack([np.asarray(r["out"], dtype=np.float32) for r in res.results], axis=0)
```

```python
import os
import numpy as np
from contextlib import ExitStack
import concourse.bass as bass
import concourse.mybir as mybir
from concourse.bass_utils import run_bass_kernel_spmd

F32 = mybir.dt.float32
BF16 = mybir.dt.bfloat16
ALU = mybir.AluOpType
AF = mybir.ActivationFunctionType
AX = mybir.AxisListType

SEQ = 2048
D = 1024
NT = 16
DEPTH = 2
EPS = 1e-6
IN_W = 10816
C_LRUX, C_LRUG, C_Q, C_KV, C_GATE, C_GQ, C_GK, C_GV, C_GOG, C_GLR, C_MG = 0, 1024, 2048, 3072, 4608, 4656, 5168, 5680, 6704, 7728, 7744
DFF = 2816
NEG = -30000.0


class Buf:
    __slots__ = ("name", "w", "r", "excl")

    def __init__(self, name="", excl=False):
        self.name = name
        self.w = None
        self.r = []
        self.excl = excl


class Sched:
    ENG = ("sync", "scalar", "vector", "gpsimd", "tensor")
    DMAQ = ("sync", "gpsimd", "scalar")

    def __init__(self, nc, es, n_dma_sems=12):
        self.nc = nc
        self.q = {e: [] for e in self.ENG}
        self.cnt = {e: 0 for e in self.ENG}
        self.sems = []
        self.esem = {}
        for e in self.ENG:
            self.esem[e] = len(self.sems)
            self.sems.append(es.enter_context(nc.semaphore("s_" + e)))
        self.known = {e: {} for e in self.ENG}
        self.dpool = {}
        self.dcnt = {}
        self.dlast = {}
        for qn in self.DMAQ:
            self.dpool[qn] = []
            for i in range(n_dma_sems):
                self.dpool[qn].append(len(self.sems))
                self.sems.append(es.enter_context(nc.semaphore(f"d_{qn}_{i}")))
            self.dcnt[qn] = 0
        self.K = n_dma_sems

    def _waits(self, eng, r, w):
        waits = {}
        kn = self.known[eng]
        own_pe = self.esem["tensor"] if eng == "tensor" else -1

        def need(kv):
            k, v = kv
            if k == own_pe:
                return
            if kn.get(k, 0) < v and waits.get(k, 0) < v:
                waits[k] = v

        own = self.esem.get(eng, -2)
        for b in r:
            if b.w is not None:
                need(b.w)
            if b.excl:
                for x in b.r:
                    if x[0] != own:
                        need(x)
        for b in w:
            if b.w is not None:
                need(b.w)
            for x in b.r:
                need(x)
        for k, v in waits.items():
            kn[k] = v
        return list(waits.items())

    _rec = None

    def record(self, fn):
        self._rec = []
        fn()
        r, self._rec = self._rec, None
        return r

    def replay(self, lists):
        idx = [0] * len(lists)
        live = True
        while live:
            live = False
            for j, lst in enumerate(lists):
                if idx[j] < len(lst):
                    kind, args, kw = lst[idx[j]]
                    idx[j] += 1
                    live = True
                    if kind == "op":
                        self.op(*args)
                    else:
                        self.dma(*args, **kw)

    def op(self, eng, fn, r=(), w=()):
        if self._rec is not None:
            self._rec.append(("op", (eng, fn, list(r), list(w)), {}))
            return
        waits = self._waits(eng, r, w)
        self.cnt[eng] += 1
        seq = self.cnt[eng]
        k = self.esem[eng]
        self.q[eng].append((waits, fn, (k, 1)))
        for b in w:
            b.w = (k, seq)
            b.r = []
        for b in r:
            if b not in w:
                b.r.append((k, seq))
                if len(b.r) > 24:
                    b.r = b.r[-24:] if False else self._compact(b.r)

    @staticmethod
    def _compact(lst):
        d = {}
        for k, v in lst:
            if d.get(k, 0) < v:
                d[k] = v
        return list(d.items())

    def dma(self, qn, out, in_, r=(), w=(), **kw):
        if self._rec is not None:
            self._rec.append(("dma", (qn, out, in_, list(r), list(w)), kw))
            return
        waits = self._waits(qn, r, w)
        i = self.dcnt[qn]
        self.dcnt[qn] += 1
        k = self.dpool[qn][i % self.K]
        val = 16 * (i // self.K + 1)
        if val > 16 and self.known[qn].get(k, 0) < val - 16:
            waits.append((k, val - 16))
            self.known[qn][k] = val - 16
        self.dlast[k] = val
        self.q[qn].append((waits, lambda e: e.dma_start(out=out, in_=in_, **kw), (k, 16)))
        for b in w:
            b.w = (k, val)
            b.r = []
        for b in r:
            if b not in w:
                b.r.append((k, val))
                if len(b.r) > 24:
                    b.r = self._compact(b.r)

    def pe_drain(self):
        k = self.esem["tensor"]
        if self.cnt["tensor"] > 0:
            self.q["tensor"].append(([(k, self.cnt["tensor"])], None, None))

    def barrier(self):
        tgt = [(self.esem[e], self.cnt[e]) for e in self.ENG if self.cnt[e] > 0]
        tgt += list(self.dlast.items())
        for e in self.ENG:
            waits = []
            for k, v in tgt:
                if e == "tensor" and k == self.esem["tensor"]:
                    continue
                if self.known[e].get(k, 0) < v:
                    waits.append((k, v))
                    self.known[e][k] = v
            if waits:
                self.q[e].append((waits, None, None))

    def emit(self):
        nc = self.nc
        with nc.Block() as block:
            for e in self.ENG:
                def body(eng, _e=e):
                    for waits, fn, inc in self.q[_e]:
                        for k, v in waits:
                            eng.wait_ge(self.sems[k], v)
                        if fn is not None:
                            ins = fn(eng)
                            ins.then_inc(self.sems[inc[0]], inc[1])
                getattr(block, e)(body)


def fap(a, dims):
    return bass.AP(a.tensor, a.offset, [list(a.ap[0])] + [list(d) for d in dims])


def _rel_bucket(d):
    d = np.asarray(d)
    n = np.maximum(d, 0)
    nf = np.maximum(n, 16).astype(np.float32)
    large = 16 + (np.log(nf / np.float32(16)) / np.float32(np.log(128 / 16)) * np.float32(16)).astype(np.int32)
    large = np.minimum(large, 31)
    return np.where(n < 16, n, large)


def host_consts():
    c = {}
    c["ident"] = np.eye(128, dtype=np.float32)
    c["antiid"] = np.eye(128, dtype=np.float32)[::-1].copy()
    aid127 = np.zeros((128, 128), np.float32)
    for i in range(127):
        aid127[i, 126 - i] = 1.0
    aid127[127, 127] = 1.0
    c["antiid127"] = aid127
    s = np.arange(128)
    c["triu"] = (s[:, None] <= s[None, :]).astype(np.float32)
    c["tril"] = (s[:, None] > s[None, :]).astype(np.float32)
    def oh(deltas, valid):
        m = np.zeros((33, len(deltas)), np.float32)
        b = _rel_bucket(deltas)
        for i, (dd, v) in enumerate(zip(deltas, valid)):
            if v:
                m[b[i], i] = 1.0
            else:
                m[32, i] = 1.0
        return m
    dc = np.arange(-2048, 2048)
    c["oh_c"] = oh(dc, dc >= 0)
    ds = np.arange(-512, 512)
    c["oh_s"] = oh(ds, ds >= 0)
    c["oh_w"] = oh(ds, (ds >= 0) & (ds < 256))
    cs = np.arange(127) * 16
    js = np.arange(32) * 64
    ov = np.clip(np.minimum(cs[:, None] + 32, js[None, :] + 64) - np.maximum(cs[:, None], js[None, :]), 0, None).astype(np.float32) / 32.0
    ovx = np.zeros((128, 33), np.float32)
    ovx[:127, :32] = ov
    ovx[:127, 32] = 1.0
    c["ovx"] = ovx
    pos = np.arange(SEQ)
    cur = pos // 64
    blk = np.arange(32)[None, :]
    cand = (blk >= 1) & (blk <= cur[:, None] - 2)
    forced = (blk == 0) | (blk == cur[:, None]) | (blk == cur[:, None] - 1)
    c["cand"] = cand.astype(np.float32).reshape(NT, 128, 32).transpose(1, 0, 2).copy()
    c["negc"] = ((cand.astype(np.float32) - 1.0) * 1e4).reshape(NT, 128, 32).transpose(1, 0, 2).copy()
    c["forced"] = forced.astype(np.float32).reshape(NT, 128, 32).transpose(1, 0, 2).copy()
    ex = np.zeros((128, NT, 128), np.float32)
    for kt in range(NT):
        for key in range(128):
            ex[2 * kt + key // 64, kt, key] = 1.0
    c["expand_near"] = ex.copy()
    ex[32:34] = 1.0
    c["expand"] = ex
    return c


CONST_SHAPES = None


def build(debug=False, n_layers=DEPTH, stop=None):
    nc = bass.Bass("TRN2", target_bir_lowering=False)
    consts = host_consts()
    din = {}

    def inp(name, shape, dt=F32):
        din[name] = nc.dram_tensor(name, list(shape), dt, kind="ExternalInput").ap()
        return din[name]

    x_in = inp("x", [SEQ, D])
    rel_table = inp("rel_table", [32, 16])
    norm_g = inp("norm_g", [DEPTH, 4, D])
    w_in = inp("w_in", [DEPTH, D, IN_W])
    conv_w = inp("conv_w", [DEPTH, 4, D])
    conv_b = inp("conv_b", [DEPTH, D])
    lru_wg = inp("lru_w_gates", [DEPTH, 2, 8, 128, 128])
    lru_bg = inp("lru_b_gates", [DEPTH, 2, D])
    lru_lam = inp("lru_lambda", [DEPTH, D])
    cmp_pos = inp("cmp_pos", [DEPTH, 2, 32, 64])
    cmp_w1 = inp("cmp_w1", [DEPTH, 2, 2048, 256])
    cmp_w2 = inp("cmp_w2", [DEPTH, 2, 256, 64])
    gla_wa2 = inp("gla_wa2", [DEPTH, 16, 512])
    gla_ba = inp("gla_ba", [DEPTH, 512])
    gla_norm = inp("gla_norm", [DEPTH, 256])
    w_branch = inp("w_branch", [DEPTH, 3, D, D])
    w_out = inp("w_out", [DEPTH, D, D])
    w_ffn_in = inp("w_ffn_in", [DEPTH, D, 2 * DFF])
    w_ffn_out = inp("w_ffn_out", [DEPTH, DFF, D])
    cin = {k: inp("c_" + k, v.shape) for k, v in consts.items()}

    okind = "ExternalOutput"
    y_out = nc.dram_tensor("out", [SEQ, D], F32, kind=okind).ap()
    skind = "ExternalOutput"
    xres = nc.dram_tensor("xres", [SEQ, D], F32, kind=skind).ap()
    xmid = nc.dram_tensor("xmid", [SEQ, D], F32, kind=skind).ap()
    ysc = nc.dram_tensor("ysc", [3, D, SEQ], BF16, kind=skind).ap()
    tc_d = nc.dram_tensor("tc_d", [2, 16, 4096], BF16, kind="Internal").ap()
    ts_d = nc.dram_tensor("ts_d", [2, 16, 1024], BF16, kind="Internal").ap()
    tw_d = nc.dram_tensor("tw_d", [2, 16, 1024], BF16, kind="Internal").ap()
    hsc = nc.dram_tensor("hsc", [128, 8 * SEQ], BF16, kind=skind).ap()
    b_hsc = Buf()
    b_xres, b_xmid, b_ysc, b_tabs = Buf(), Buf(), [Buf(), Buf(), Buf()], Buf()

    with ExitStack() as es:
        S = Sched(nc, es)
        es.enter_context(nc.allow_non_contiguous_dma(reason="small param loads"))

        def mm(out, lhsT, rhs, start, stop, r, w):
            S.op("tensor", lambda e: e.matmul(out, lhsT=lhsT, rhs=rhs, start=start, stop=stop), r, w)

        def trp(out, in_, ident, r, w):
            S.op("tensor", lambda e: e.transpose(out, in_, ident), r, w)

        def act(out, in_, func, r, w, **kw):
            S.op("scalar", lambda e: e.activation(out=out, in_=in_, func=func, **kw), r, w)

        def tt(eng, out, in0, in1, op, r, w):
            S.op(eng, lambda e: e.tensor_tensor(out=out, in0=in0, in1=in1, op=op), r, w)

        def tsc(eng, out, in0, s1, op0, r, w, s2=None, op1=None):
            if op1 is None:
                S.op(eng, lambda e: e.tensor_scalar(out=out, in0=in0, scalar1=s1, scalar2=None, op0=op0), r, w)
            else:
                S.op(eng, lambda e: e.tensor_scalar(out=out, in0=in0, scalar1=s1, scalar2=s2, op0=op0, op1=op1), r, w)

        def stt(out, in0, scalar, in1, op0, op1, r, w):
            S.op("vector", lambda e: e.scalar_tensor_tensor(out=out, in0=in0, scalar=scalar, in1=in1, op0=op0, op1=op1), r, w)

        def cp(eng, out, in_, r, w):
            if eng == "scalar":
                S.op("scalar", lambda e: e.copy(out=out, in_=in_), r, w)
            else:
                S.op(eng, lambda e: e.tensor_copy(out=out, in_=in_), r, w)

        def recip(out, in_, r, w):
            S.op("vector", lambda e: e.reciprocal(out=out, in_=in_), r, w)

        def memset(eng, ap, val, w):
            S.op(eng, lambda e: e.memset(ap, val), (), w)

        def dma(q, out, in_, r=(), w=()):
            S.dma(q, out, in_, r, w)

        class T:
            _n = [0]

            def __init__(self, stack, name, shape, dt, psum=False):
                T._n[0] += 1
                name = f"{name}_{T._n[0]}"
                self.t = stack.enter_context((nc.psum_tensor if psum else nc.sbuf_tensor)(name, list(shape), dt))
                self.b = Buf(name)

            def __getitem__(self, idx):
                return self.t[idx]

        PA = T(es, "PA", [128, 2048], F32, psum=True)
        PB = T(es, "PB", [128, 2048], F32, psum=True)
        pbank = []
        for i in range(8):
            src = PA if i < 4 else PB
            pbank.append((src.t[:, (i % 4) * 512:(i % 4 + 1) * 512], Buf(f"bank{i}", excl=True)))
        PAb = PA.t.bitcast(BF16)
        PBb = PB.t.bitcast(BF16)

        def bank_bf(i):
            src = PAb if i < 4 else PBb
            return src[:, (i % 4) * 1024:(i % 4 + 1) * 1024]

        ident_f = T(es, "ident_f", [128, 128], F32)
        ident_b = T(es, "ident_b", [128, 128], BF16)
        dma("sync", ident_f[:], cin["ident"], w=[ident_f.b])
        cp("vector", ident_b[:], ident_f[:], [ident_f.b], [ident_b.b])

        hT = T(es, "hT", [128, 8, SEQ], BF16)
        hT_b = [Buf(f"hT{t}") for t in range(NT)]

        def load_gain(ph, l, i):
            gt = T(ph, f"gain{i}", [128, D], F32)
            src = norm_g[l, i:i + 1, :]
            dma("sync", gt[:], bass.AP(src.tensor, src.offset, [[0, 128], [1, D]]), w=[gt.b])
            return gt

        def norm_transpose_tile(ph, t, xt_ap, xt_buf, gt, ss, rs, hb, junk, pbi):
            act(junk[:], xt_ap, AF.Square, [xt_buf], [junk.b, ss.b], accum_out=ss[:, t:t + 1])
            act(rs[:, t:t + 1], ss[:, t:t + 1], AF.Sqrt, [ss.b], [rs.b], scale=1.0 / D, bias=EPS)
            recip(rs[:, t:t + 1], rs[:, t:t + 1], [rs.b], [rs.b])
            stt(hb[:], xt_ap, rs[:, t:t + 1], gt[:], ALU.mult, ALU.mult, [xt_buf, rs.b, gt.b], [hb.b])
            pv, pbuf = bank_bf(pbi), pbank[pbi][1]
            for kc in range(8):
                trp(pv[:, kc * 128:(kc + 1) * 128], hb[:, kc * 128:(kc + 1) * 128], ident_b[:], [hb.b, ident_b.b], [pbuf])
            cp("scalar", hT[:, :, t * 128:(t + 1) * 128], pv.rearrange("p (k s) -> p k s", k=8), [pbuf], [hT_b[t]])

        def load_slab(dst_ap, w2d, c0, ncols, wbuf, nk=8):
            src = w2d[:, c0:c0 + ncols].rearrange("(kc p) n -> p kc n", p=128)
            dma("gpsimd", dst_ap, src, w=[wbuf])

        def proj_fm(wslab, wbuf, col_off, M, rhs_tile, rhs_bufs, out_banks, nk=8, sc_list=(0, 1, 2, 3)):
            for i, sc in enumerate(sc_list):
                pa, pb_ = out_banks[i]
                for kc in range(nk):
                    mm(pa[0:M, :], wslab[:, kc, col_off:col_off + M], rhs_tile[:, kc, sc * 512:(sc + 1) * 512],
                       kc == 0, kc == nk - 1, [wbuf] + rhs_bufs[sc * 4:(sc + 1) * 4], [pb_])

        def phase_A(l, x_src):
            with ExitStack() as ph:
                xt = [T(ph, f"xtA{i}", [128, D], F32) for i in range(2)]
                hb = [T(ph, f"hbA{i}", [128, D], BF16) for i in range(2)]
                junk = [T(ph, f"junkA{i}", [128, D], BF16) for i in range(2)]
                ss = [T(ph, f"ssA{i}", [128, NT], F32) for i in range(2)]
                rs = [T(ph, f"rsA{i}", [128, NT], F32) for i in range(2)]
                g0 = load_gain(ph, l, 0)

                def tileA(t):
                    p = t % 2
                    dma("sync", xt[p][:], x_src[t * 128:(t + 1) * 128, :], r=[b_xres], w=[xt[p].b])
                    norm_transpose_tile(ph, t, xt[p][:], xt[p].b, g0, ss[p], rs[p], hb[p], junk[p], p)

                for t in range(0, NT, 2):
                    S.replay([S.record(lambda: tileA(t)), S.record(lambda: tileA(t + 1))])
                S.barrier()

        def phase_lru(l):
            with ExitStack() as ph:
                prow = T(ph, "prow", [8, D], F32)
                lpT = T(ph, "lpT", [128, 8, 8], F32)
                sp = T(ph, "lru_sp", [128, 8, 6], F32)
                wg = T(ph, "lru_wg", [128, 2, 8, 128], BF16)
                slab = [T(ph, f"lslab{i}", [128, 8, 2, 128], BF16) for i in range(2)]
                XA = [T(ph, f"XA{i}", [128, SEQ + 4], F32) for i in range(2)]
                XC = [T(ph, f"XC{i}", [128, SEQ], F32) for i in range(2)]
                XCB = [T(ph, f"XCB{i}", [128, SEQ], BF16) for i in range(2)]
                R = [T(ph, f"R{i}", [128, SEQ], F32) for i in range(2)]
                A = [T(ph, f"A{i}", [128, SEQ], F32) for i in range(2)]
                I = [T(ph, f"I{i}", [128, SEQ], F32) for i in range(2)]
                H = [T(ph, f"H{i}", [128, SEQ], F32) for i in range(2)]
                GA = [T(ph, f"GA{i}", [128, SEQ], F32) for i in range(2)]
                G = [T(ph, f"G{i}", [128, SEQ], F32) for i in range(2)]
                YA = [T(ph, f"YA{i}", [128, SEQ], BF16) for i in range(2)]
                if os.environ.get("SBUF_DBG"):
                    print("LRU sbuf remaining", nc.sbuf_bytes_remaining)
                for k in range(4):
                    dma("sync", prow[k:k + 1, :], conv_w[l, k:k + 1, :], w=[prow.b])
                dma("sync", prow[4:5, :], conv_b[l:l + 1, :], w=[prow.b])
                dma("sync", prow[5:7, :], lru_bg[l], w=[prow.b])
                dma("sync", prow[7:8, :], lru_lam[l:l + 1, :], w=[prow.b])
                pv, pbuf = pbank[7]
                for c in range(8):
                    trp(pv[:, c * 8:(c + 1) * 8], prow[0:8, c * 128:(c + 1) * 128], ident_f[0:8, 0:8], [prow.b, ident_f.b], [pbuf])
                cp("vector", lpT[:], pv[:, 0:64].rearrange("p (c k) -> p c k", c=8), [pbuf], [lpT.b])
                xs, ln1, ser, msk, nsp8, nsp16 = (sp[:, :, i] for i in range(6))
                act(xs, lpT[:, :, 7], AF.Exp, [lpT.b], [sp.b], scale=-1.0)
                act(ln1, xs, AF.Ln, [sp.b], [sp.b], bias=1.0)
                tsc("vector", ser, xs, -0.25, ALU.mult, [sp.b], [sp.b], 1.0 / 3.0, ALU.add)
                tt("vector", ser, ser, xs, ALU.mult, [sp.b], [sp.b])
                tsc("vector", ser, ser, -1.0, ALU.mult, [sp.b], [sp.b], 0.5, ALU.add)
                tt("vector", ser, ser, xs, ALU.mult, [sp.b], [sp.b])
                tsc("vector", ser, ser, -1.0, ALU.mult, [sp.b], [sp.b], 1.0, ALU.add)
                tt("vector", ser, ser, xs, ALU.mult, [sp.b], [sp.b])
                tsc("vector", msk, xs, 0.03, ALU.is_lt, [sp.b], [sp.b])
                tt("vector", ser, ser, ln1, ALU.subtract, [sp.b], [sp.b])
                tt("vector", ser, ser, msk, ALU.mult, [sp.b], [sp.b])
                tt("vector", ser, ser, ln1, ALU.add, [sp.b], [sp.b])
                tsc("vector", nsp8, ser, -8.0, ALU.mult, [sp.b], [sp.b])
                tsc("vector", nsp16, ser, -16.0, ALU.mult, [sp.b], [sp.b])
                dma("gpsimd", wg[:], lru_wg[l].rearrange("k n c e -> c k n e"), w=[wg.b])
                for p_ in range(2):
                    memset("vector", XA[p_][:, 0:3], 0.0, [XA[p_].b])
                w2d = w_in[l]

                def lru_A(c):
                    p = c % 2
                    xa, xc, xcb, r_, i_, ga = XA[p], XC[p], XCB[p], R[p], I[p], GA[p]
                    sl = slab[p]
                    dma("gpsimd", sl[:, :, 0, :], w2d[:, C_LRUX + c * 128:C_LRUX + (c + 1) * 128].rearrange("(kc p) n -> p kc n", p=128), w=[sl.b])
                    dma("gpsimd", sl[:, :, 1, :], w2d[:, C_LRUG + c * 128:C_LRUG + (c + 1) * 128].rearrange("(kc p) n -> p kc n", p=128), w=[sl.b])
                    slv = sl.t.rearrange("p k a n -> p k (a n)")
                    proj_fm(slv, sl.b, 0, 128, hT.t, hT_b, pbank[0:4])
                    proj_fm(slv, sl.b, 128, 128, hT.t, hT_b, pbank[4:8])
                    cp("scalar", xa[:, 3:3 + SEQ], PA[:, :], [pbank[i][1] for i in range(4)], [xa.b])
                    cp("scalar", ga[:], PB[:, :], [pbank[i][1] for i in range(4, 8)], [ga.b])
                    cw = lambda k: lpT[:, c, k:k + 1]
                    act(xc[:], xa[:, 3:3 + SEQ], AF.Identity, [xa.b, lpT.b], [xc.b], scale=cw(3), bias=cw(4))
                    for k in range(3):
                        stt(xc[:], xa[:, k:k + SEQ], cw(k), xc[:], ALU.mult, ALU.add, [xa.b, lpT.b, xc.b], [xc.b])
                    cp("gpsimd", xcb[:], xc[:], [xc.b], [xcb.b])
                    for gk in range(2):
                        banks = pbank[4:8] if gk == 0 else pbank[0:4]
                        for sc in range(4):
                            mm(banks[sc][0], wg[:, gk, c, :], xcb[:, sc * 512:(sc + 1) * 512], True, True, [wg.b, xcb.b], [banks[sc][1]])
                    act(r_[:], PB[:, :], AF.Sigmoid, [pbank[i][1] for i in range(4, 8)], [r_.b], bias=lpT[:, c, 5:6])
                    act(i_[:], PA[:, :], AF.Sigmoid, [pbank[i][1] for i in range(4)], [i_.b], bias=lpT[:, c, 6:7])

                def lru_B(c):
                    p = c % 2
                    xc, r_, a_, i_, h_, ga, g_ = XC[p], R[p], A[p], I[p], H[p], GA[p], G[p]
                    act(a_[:], r_[:], AF.Exp, [r_.b, sp.b], [a_.b], scale=sp[:, c, 4:5])
                    act(r_[:], r_[:], AF.Exp, [r_.b, sp.b], [r_.b], scale=sp[:, c, 5:6])
                    act(r_[:], r_[:], AF.Sqrt, [r_.b], [r_.b], scale=-1.0, bias=1.0)
                    tt("gpsimd", i_[:], i_[:], xc[:], ALU.mult, [i_.b, xc.b], [i_.b])
                    tt("gpsimd", i_[:], i_[:], r_[:], ALU.mult, [i_.b, r_.b], [i_.b])
                    S.op("vector", lambda e, h_=h_, a_=a_, i_=i_: e.tensor_tensor_scan(out=h_[:], data0=a_[:], data1=i_[:], initial=0.0, op0=ALU.mult, op1=ALU.add),
                         [a_.b, i_.b], [h_.b])
                    act(g_[:], ga[:], AF.Square, [ga.b], [g_.b])
                    tsc("vector", g_[:], g_[:], 0.044715, ALU.mult, [g_.b], [g_.b], 1.0, ALU.add)
                    tt("gpsimd", g_[:], g_[:], ga[:], ALU.mult, [g_.b, ga.b], [g_.b])
                    act(g_[:], g_[:], AF.Sigmoid, [g_.b], [g_.b], scale=1.5957691216057308)
                    tt("gpsimd", g_[:], g_[:], ga[:], ALU.mult, [g_.b, ga.b], [g_.b])
                    ya = YA[p]
                    tt("vector", ya[:], g_[:], h_[:], ALU.mult, [g_.b, h_.b], [ya.b])
                    dma("sync", ysc[0, c * 128:(c + 1) * 128, :], ya[:], r=[ya.b], w=[b_ysc[0]])

                lru_A(0)
                for c in range(8):
                    lists = [S.record(lambda: lru_B(c))]
                    if c + 1 < 8:
                        lists.append(S.record(lambda: lru_A(c + 1)))
                    S.replay(lists)
                S.barrier()

        def phase_gla(l):
            w2d = w_in[l]
            with ExitStack() as ph:
                qT = T(ph, "gqT", [128, 4, SEQ], F32)
                kT = T(ph, "gkT", [128, 4, SEQ], F32)
                lrT = T(ph, "lrT", [32, SEQ], F32)
                wa2x = T(ph, "wa2x", [32, 512], F32)
                wres = T(ph, "gwres", [128, 8, 2560], BF16)
                gnb = T(ph, "gnb", [128, 4, 256], F32)
                st_f = T(ph, "st_f", [128, 4, 256], F32)
                st_b = T(ph, "st_b", [128, 4, 256], BF16)
                cm4 = T(ph, "cm4", [128, 4, 128], F32)
                triu = T(ph, "triu", [128, 128], F32)
                tril = T(ph, "tril", [128, 128], F32)
                dma("sync", triu[:], cin["triu"], w=[triu.b])
                dma("sync", tril[:], cin["tril"], w=[tril.b])
                for hh in range(4):
                    dma("sync", cm4[:, hh, :], cin["triu"], w=[cm4.b])
                    src = gla_norm[l:l + 1, :]
                    dma("sync", gnb[:, hh, :], bass.AP(src.tensor, src.offset, [[0, 128], [1, 256]]), w=[gnb.b])
                memset("vector", wa2x[:], 0.0, [wa2x.b])
                memset("vector", lrT[:], 1.0, [lrT.b])
                dma("sync", wa2x[0:16, :], gla_wa2[l], w=[wa2x.b])
                dma("sync", wa2x[16:17, :], gla_ba[l:l + 1, :], w=[wa2x.b])
                for i, c0 in enumerate((C_GK, C_GV, C_GV + 512, C_GOG, C_GOG + 512)):
                    load_slab(wres[:, :, i * 512:(i + 1) * 512], w2d, c0, 512, wres.b)
                with ExitStack() as ph2:
                    slab = [T(ph2, f"gslab{i}", [128, 8, 512], BF16) for i in range(2)]
                    lslab = T(ph2, "glslab", [128, 8, 16], BF16)
                    load_slab(slab[0][:], w2d, C_GQ, 512, slab[0].b)
                    load_slab(slab[1][:], w2d, C_GK, 512, slab[1].b)
                    load_slab(lslab[:], w2d, C_GLR, 16, lslab.b)
                    for i in range(8):
                        banks = pbank[0:4] if i % 2 == 0 else pbank[4:8]
                        src = PA if i % 2 == 0 else PB
                        proj_fm(slab[i // 4].t, slab[i // 4].b, (i % 4) * 128, 128, hT.t, hT_b, banks)
                        dst = qT if i < 4 else kT
                        act(dst[:, i % 4, :], src[:, :], AF.Copy, [b for _, b in banks], [dst.b], scale=(128 ** -0.5 if i < 4 else 1.0))
                    proj_fm(lslab.t, lslab.b, 0, 16, hT.t, hT_b, pbank[0:4])
                    cp("vector", lrT[0:16, :], PA[0:16, :], [b for _, b in pbank[0:4]], [lrT.b])
                    S.barrier()
                sp_t = T(ph, "g_sp", [128, 512], F32)
                E1 = [T(ph, f"g_E1{i}", [128, 512], F32) for i in range(2)]
                E2 = T(ph, "g_E2", [128, 512], F32)
                Erb = T(ph, "g_Erb", [128, 512], F32)
                qtb = [T(ph, f"g_qtb{i}", [128, 4, 128], BF16) for i in range(2)]
                ktb = T(ph, "g_ktb", [128, 4, 128], BF16)
                kend = [T(ph, f"g_kend{i}", [128, 512], BF16) for i in range(2)]
                v_bf = [T(ph, f"g_vbf{i}", [128, 1024], BF16) for i in range(2)]
                sg = [T(ph, f"g_sg{i}", [128, 1024], F32) for i in range(2)]
                attm = [T(ph, f"g_attm{i}", [128, 4, 128], BF16) for i in range(2)]
                on = T(ph, "g_on", [128, 1024], F32)
                yc = [T(ph, f"g_yc{i}", [128, 1024], BF16) for i in range(2)]
                ycT = [T(ph, f"g_ycT{i}", [128, 8, 128], BF16) for i in range(2)]
                junk = T(ph, "g_junk", [128, 256], BF16)
                ssq = T(ph, "g_ssq", [128, 4], F32)
                rst = T(ph, "g_rst", [128, 4], F32)
                if os.environ.get("SBUF_DBG"):
                    print("GLA sbuf remaining", nc.sbuf_bytes_remaining)
                bk = lambda i: pbank[i][0]
                bb = lambda i: pbank[i][1]

                def gla_A(t):
                    p = t % 2
                    tsl = slice(t * 128, (t + 1) * 128)
                    e1, qb, ke, vb, sg_, am = E1[p], qtb[p], kend[p], v_bf[p], sg[p], attm[p]
                    mm(bk(0), lrT[0:17, tsl], wa2x[0:17, :], True, True, [lrT.b, wa2x.b], [bb(0)])
                    act(sp_t[:], bk(0), AF.Exp, [bb(0)], [sp_t.b], scale=-1.0)
                    act(sp_t[:], sp_t[:], AF.Ln, [sp_t.b], [sp_t.b], bias=1.0)
                    for hh in range(4):
                        mm(bk(1)[:, hh * 128:(hh + 1) * 128], sp_t[:, hh * 128:(hh + 1) * 128], triu[:], True, True, [sp_t.b, triu.b], [bb(1)])
                    mm(bk(2), tril[:], sp_t[:], True, True, [tril.b, sp_t.b], [bb(2)])
                    act(e1[:], bk(1), AF.Exp, [bb(1)], [e1.b], scale=-1.0 / 16.0)
                    act(E2[:], bk(1), AF.Exp, [bb(1)], [E2.b], scale=1.0 / 16.0)
                    act(Erb[:], bk(2), AF.Exp, [bb(2)], [Erb.b], scale=-1.0 / 16.0)
                    tt("vector", qb[:], qT[:, :, tsl], e1.t.rearrange("p (h s) -> p h s", h=4), ALU.mult, [qT.b, e1.b], [qb.b])
                    tt("gpsimd", ktb[:], kT[:, :, tsl], E2.t.rearrange("p (h s) -> p h s", h=4), ALU.mult, [kT.b, E2.b], [ktb.b])
                    for kc in range(8):
                        mm(bk(3), hT[:, kc, tsl], wres[:, kc, 0:512], kc == 0, kc == 7, [hT_b[t], wres.b], [bb(3)])
                    tt("vector", ke[:], bk(3), Erb[:], ALU.mult, [bb(3), Erb.b], [ke.b])
                    for half in range(2):
                        for kc in range(8):
                            mm(bk(half), hT[:, kc, tsl], wres[:, kc, 512 + half * 512:1024 + half * 512], kc == 0, kc == 7, [hT_b[t], wres.b], [bb(half)])
                    cp("scalar", vb[:], PA[:, 0:1024], [bb(0), bb(1)], [vb.b])
                    for half in range(2):
                        for kc in range(8):
                            mm(bk(2 + half), hT[:, kc, tsl], wres[:, kc, 1536 + half * 512:2048 + half * 512], kc == 0, kc == 7, [hT_b[t], wres.b], [bb(2 + half)])
                    act(sg_[:], PA[:, 1024:2048], AF.Silu, [bb(2), bb(3)], [sg_.b])
                    tt("gpsimd", sg_[:], sg_[:], gnb.t.rearrange("p h e -> p (h e)"), ALU.mult, [sg_.b, gnb.b], [sg_.b])
                    for hh in range(4):
                        mm(bk(0)[:, hh * 128:(hh + 1) * 128], ktb[:, hh, :], qb[:, hh, :], True, True, [ktb.b, qb.b], [bb(0)])
                    tt("vector", am[:], bk(0).rearrange("p (h s) -> p h s", h=4), cm4[:], ALU.mult, [bb(0), cm4.b], [am.b])

                def gla_B(t):
                    p = t % 2
                    tsl = slice(t * 128, (t + 1) * 128)
                    e1, qb, ke, vb, sg_, am = E1[p], qtb[p], kend[p], v_bf[p], sg[p], attm[p]
                    for hh in range(4):
                        ob = 4 + hh // 2
                        oap = bk(ob)[:, (hh % 2) * 256:(hh % 2 + 1) * 256]
                        mm(oap, am[:, hh, :], vb[:, hh * 256:(hh + 1) * 256], hh % 2 == 0, t == 0 and hh % 2 == 1, [am.b, vb.b], [bb(ob)])
                        if t > 0:
                            mm(oap, qb[:, hh, :], st_b[:, hh, :], False, hh % 2 == 1, [qb.b, st_b.b], [bb(ob)])
                    for hh in range(4):
                        kb_ = 6 + hh // 2
                        mm(bk(kb_)[:, (hh % 2) * 256:(hh % 2 + 1) * 256], ke[:, hh * 128:(hh + 1) * 128], vb[:, hh * 256:(hh + 1) * 256],
                           hh % 2 == 0, hh % 2 == 1, [ke.b, vb.b], [bb(kb_)])
                    for hh in range(4):
                        kvp = bk(6 + hh // 2)[:, (hh % 2) * 256:(hh % 2 + 1) * 256]
                        if t == 0:
                            cp("vector", st_f[:, hh, :], kvp, [bb(6 + hh // 2)], [st_f.b])
                        else:
                            dec = e1[:, hh * 128 + 127:hh * 128 + 128]
                            stt(st_f[:, hh, :], st_f[:, hh, :], dec, kvp, ALU.mult, ALU.add, [st_f.b, e1.b, bb(6 + hh // 2)], [st_f.b])
                    cp("gpsimd", st_b[:], st_f[:], [st_f.b], [st_b.b])
                    for hh in range(4):
                        oap = bk(4 + hh // 2)[:, (hh % 2) * 256:(hh % 2 + 1) * 256]
                        act(junk[:], oap, AF.Square, [bb(4 + hh // 2)], [junk.b, ssq.b], accum_out=ssq[:, hh:hh + 1])
                    act(rst[:], ssq[:], AF.Sqrt, [ssq.b], [rst.b], scale=1.0 / 256.0, bias=EPS)
                    recip(rst[:], rst[:], [rst.b], [rst.b])
                    tt("vector", on.t.rearrange("p (h e) -> p h e", h=4), PB[:, 0:1024].rearrange("p (h e) -> p h e", h=4),
                       fap(rst[:], [[1, 4], [0, 256]]), ALU.mult, [bb(4), bb(5), rst.b], [on.b])
                    y = yc[p]
                    tt("gpsimd", y[:], on[:], sg_[:], ALU.mult, [on.b, sg_.b], [y.b])
                    pv = bank_bf(7)
                    for c in range(8):
                        trp(pv[:, c * 128:(c + 1) * 128], y[:, c * 128:(c + 1) * 128], ident_b[:], [y.b, ident_b.b], [bb(7)])
                    yT = ycT[p]
                    cp("scalar", yT[:], pv.rearrange("p (k s) -> p k s", k=8), [bb(7)], [yT.b])
                    dma("sync", ysc[2, :, tsl].rearrange("(c p) s -> p c s", p=128), yT[:], r=[yT.b], w=[b_ysc[2]])

                gla_A(0)
                for t in range(NT):
                    lists = [S.record(lambda: gla_B(t))]
                    if t + 1 < NT:
                        lists.append(S.record(lambda: gla_A(t + 1)))
                    S.replay(lists)
                S.barrier()

        def setup_tables():
            with ExitStack() as ph:
                tblx = T(ph, "tblx", [33, 16], F32)
                memset("vector", tblx[:], NEG, [tblx.b])
                dma("sync", tblx[0:32, :], rel_table, w=[tblx.b])
                for name, dst, n in (("oh_c", tc_d, 4096), ("oh_s", ts_d, 1024), ("oh_w", tw_d, 1024)):
                    oh = T(ph, "t_" + name, [33, n], F32)
                    thi = T(ph, "thi_" + name, [16, n], BF16)
                    tlo = T(ph, "tlo_" + name, [16, n], BF16)
                    dma("sync", oh[:], cin[name], w=[oh.b])
                    for ch in range(n // 512):
                        pa, pbuf = pbank[ch % 8]
                        mm(pa[0:16, :], tblx[0:33, 0:16], oh[0:33, ch * 512:(ch + 1) * 512], True, True, [tblx.b, oh.b], [pbuf])
                        cp("vector", thi[:, ch * 512:(ch + 1) * 512], pa[0:16, :], [pbuf], [thi.b])
                        tt("vector", tlo[:, ch * 512:(ch + 1) * 512], pa[0:16, :], thi[:, ch * 512:(ch + 1) * 512], ALU.subtract, [pbuf, thi.b], [tlo.b])
                    dma("sync", dst[0], thi[:], r=[thi.b], w=[b_tabs])
                    dma("sync", dst[1], tlo[:], r=[tlo.b], w=[b_tabs])
                S.barrier()

        def phase_nsa(l):
            w2d = w_in[l]
            bk = lambda i: pbank[i][0]
            bb = lambda i: pbank[i][1]
            with ExitStack() as ph:
                qT = T(ph, "nqT", [128, 8, SEQ], BF16)
                kS = T(ph, "nkS", [128, 4, SEQ], BF16)
                kW = T(ph, "nkW", [128, 4, SEQ], BF16)
                vS = T(ph, "nvS", [128, NT, 4, 66], BF16)
                vW = T(ph, "nvW", [128, NT, 4, 66], BF16)
                sgate = T(ph, "nsg", [128, NT, 48], F32)
                kcP = T(ph, "nkcP", [128, 2, 4, 128], BF16)
                vcx = T(ph, "nvcx", [128, 4, 98], BF16)
                hbt = T(ph, "nhbt", [128, 3, 4, 2, 512], BF16)
                NM = T(ph, "nNM", [128, 4, 2, 512], BF16)
                Jb = T(ph, "nJb", [128, 2, 128], BF16)
                expd = T(ph, "nexpd", [128, 2, NT, 128], BF16)
                cand = T(ph, "ncand", [128, NT, 32], F32)
                negc = T(ph, "nnegc", [128, NT, 32], F32)
                forced = T(ph, "nforced", [128, NT, 32], F32)
                dma("gpsimd", Jb[:, 0, :], cin["antiid"], w=[Jb.b])
                dma("gpsimd", Jb[:, 1, :], cin["antiid127"], w=[Jb.b])
                dma("gpsimd", expd[:, 0, :, :], cin["expand"], w=[expd.b])
                dma("gpsimd", expd[:, 1, :, :], cin["expand_near"], w=[expd.b])
                memset("vector", NM[:], 0.0, [NM.b])
                dma("sync", cand[:], cin["cand"], w=[cand.b])
                dma("sync", negc[:], cin["negc"], w=[negc.b])
                dma("sync", forced[:], cin["forced"], w=[forced.b])
                memset("vector", vcx[:], 0.0, [vcx.b])
                memset("vector", kcP[:], 0.0, [kcP.b])
                for g in range(4):
                    dma("gpsimd", vcx[:, g, 64:97], cin["ovx"], w=[vcx.b])
                for dl in range(3):
                    tsrc = tw_d if dl == 2 else ts_d
                    for g in range(4):
                        for hl in range(2):
                            for rp in range(2):
                                a0 = tsrc[hl, 4 * g + 2 * rp, 512 + dl * 128 - 127:512 + dl * 128 - 127 + 1]
                                src = bass.AP(a0.tensor, a0.offset, [[1, 128], [1024, 2], [1, 128]])
                                dst = hbt[:, dl, g, hl, :].rearrange("p (a b s) -> p a b s", a=2, b=2)[:, :, rp, :]
                                dma("sync", dst, src, r=[b_tabs], w=[hbt.b])
                for g in range(4):
                    for hl in range(2):
                        for par in range(2):
                            for rp in range(2):
                                h = 4 * g + 2 * rp + par
                                a0 = ts_d[hl, h, 640:641]
                                src = bass.AP(a0.tensor, a0.offset, [[0, 1], [0, 2], [1, 128]])
                                c0 = par * 256 + rp * 128
                                dma("sync", NM[32 + hl:33 + hl, g, :, c0:c0 + 128], src, r=[b_tabs], w=[NM.b])
                memset("vector", vS[:, :, :, 64:66], 1.0, [vS.b])
                memset("vector", vW[:, :, :, 64:66], 1.0, [vW.b])
                if stop == "nsa0":
                    S.barrier()
                    return
                with ExitStack() as ph2:
                    slab = [T(ph2, f"nslab{i}", [128, 8, 512], BF16) for i in range(2)]
                    wv = T(ph2, "nwv", [128, 8, 560], BF16)
                    for half in range(2):
                        sl = slab[half]
                        load_slab(sl[:], w2d, C_Q + half * 512, 512, sl.b)
                        for i in range(4):
                            c = half * 4 + i
                            banks = pbank[0:4] if c % 2 == 0 else pbank[4:8]
                            src = PA if c % 2 == 0 else PB
                            proj_fm(sl.t, sl.b, i * 128, 128, hT.t, hT_b, banks)
                            act(qT[:, c, :], src[:, :], AF.Copy, [b for _, b in banks], [qT.b], scale=0.125)
                    if stop == "nsa1a":
                        S.barrier()
                        return
                    n = 0
                    for idx, dst in ((2, kS), (4, kW)):
                        sl = slab[n % 2]
                        n += 1
                        for g in range(4):
                            c0 = C_KV + idx * 256 + g * 64
                            for dup in range(2):
                                dma("gpsimd", sl[:, :, g * 128 + dup * 64:g * 128 + dup * 64 + 64],
                                    w2d[:, c0:c0 + 64].rearrange("(kc p) n -> p kc n", p=128), w=[sl.b])
                        for g in range(4):
                            banks = pbank[0:4] if g % 2 == 0 else pbank[4:8]
                            src = PA if g % 2 == 0 else PB
                            proj_fm(sl.t, sl.b, g * 128, 128, hT.t, hT_b, banks)
                            cp("scalar" if g % 2 == 0 else "vector", dst[:, g, :], src[:, :], [b for _, b in banks], [dst.b])
                    if stop == "nsa1b":
                        S.barrier()
                        return
                    load_slab(wv[:, :, 0:256], w2d, C_KV + 3 * 256, 256, wv.b)
                    load_slab(wv[:, :, 256:512], w2d, C_KV + 5 * 256, 256, wv.b)
                    load_slab(wv[:, :, 512:560], w2d, C_GATE, 48, wv.b)
                    for t in range(NT):
                        tsl = slice(t * 128, (t + 1) * 128)
                        b0, b1 = (0, 1) if t % 2 == 0 else (2, 3)
                        for kc in range(8):
                            mm(bk(b0), hT[:, kc, tsl], wv[:, kc, 0:512], kc == 0, kc == 7, [hT_b[t], wv.b], [bb(b0)])
                        import os
                        SK = os.environ.get("NSA_SKIP", "")
                        if "g" not in SK:
                            for kc in range(8):
                                mm(bk(b1)[:, 0:48], hT[:, kc, tsl], wv[:, kc, 512:560], kc == 0, kc == 7, [hT_b[t], wv.b], [bb(b1)])
                        if "v" not in SK:
                            cp("vector", vS[:, t, :, 0:64], bk(b0)[:, 0:256].rearrange("p (g d) -> p g d", g=4), [bb(b0)], [vS.b])
                        if "w" not in SK:
                            cp("scalar", vW[:, t, :, 0:64], bk(b0)[:, 256:512].rearrange("p (g d) -> p g d", g=4), [bb(b0)], [vW.b])
                        if "g" not in SK:
                            act(sgate[:, t, :], bk(b1)[:, 0:48], AF.Sigmoid, [bb(b1)], [sgate.b])
                    S.barrier()
                if stop == "nsa1":
                    return
                with ExitStack() as ph2:
                    slab = [T(ph2, f"ncslab{i}", [128, 8, 256], BF16) for i in range(2)]
                    w1sb = T(ph2, "nw1", [128, 32, 256], BF16)
                    w2sb = T(ph2, "nw2", [128, 2, 128], BF16)
                    prow2 = T(ph2, "nprow2", [32, 128], F32)
                    posT = T(ph2, "nposT", [128, 32], F32)
                    XAB = [T(ph2, f"nXAB{i}", [128, SEQ], BF16) for i in range(2)]
                    gtmp = T(ph2, "ngtmp", [128, 2, 128], F32)
                    geluT = T(ph2, "ngeluT", [128, 2, 128], BF16)
                    for kv in range(2):
                        for dup in range(2):
                            dma("gpsimd", w1sb[dup * 64:(dup + 1) * 64, :, :], cmp_w1[l, kv].rearrange("(p d) j -> d p j", d=64), w=[w1sb.b])
                            dma("gpsimd", w2sb[:, :, dup * 64:(dup + 1) * 64], cmp_w2[l, kv].rearrange("(jc p) d -> p jc d", p=128), w=[w2sb.b])
                            dma("sync", prow2[:, dup * 64:(dup + 1) * 64], cmp_pos[l, kv], w=[prow2.b])
                        trp(bk(6)[:, 0:32], prow2[:, :], ident_f[0:32, 0:32], [prow2.b, ident_f.b], [bb(6)])
                        cp("vector", posT[:], bk(6)[:, 0:32], [bb(6)], [posT.b])
                        sl = slab[kv]
                        load_slab(sl[:, :, 0:256], w2d, C_KV + kv * 256, 256, sl.b)
                        for cc in range(2):
                            banks = pbank[0:4]
                            proj_fm(sl.t, sl.b, cc * 128, 128, hT.t, hT_b, banks)
                            for ab in range(2):
                                tt("vector" if ab == 0 else "gpsimd" if False else "vector", XAB[ab].t.rearrange("p (i q) -> p i q", q=16), PA.t.rearrange("p (i q) -> p i q", q=16),
                                   fap(posT[:, ab * 16:ab * 16 + 1], [[0, 128], [1, 16]]), ALU.add, [b for _, b in banks] + [posT.b], [XAB[ab].b])
                            for gg in range(2):
                                g = cc * 2 + gg
                                rows = slice(gg * 64, gg * 64 + 64)
                                hb_, hbb = bk(4 + 2 * gg), bb(4 + 2 * gg)
                                for jc in range(2):
                                    for p in range(32):
                                        srcT = XAB[0] if p < 16 else XAB[1]
                                        rhs = fap(srcT[rows, p:p + 1], [[16, 127]])
                                        mm(hb_[:, jc * 128:jc * 128 + 127], w1sb[rows, p, jc * 128:(jc + 1) * 128], rhs, p == 0, p == 31,
                                           [w1sb.b, srcT.b], [hbb])
                                hv = hb_[:, 0:256].rearrange("p (j i) -> p j i", j=2)[:, :, 0:127]
                                gv = gtmp[:, :, 0:127]
                                act(gv, hv, AF.Square, [hbb], [gtmp.b])
                                tsc("vector", gv, gv, 0.044715, ALU.mult, [gtmp.b], [gtmp.b], 1.0, ALU.add)
                                tt("vector", gv, gv, hv, ALU.mult, [gtmp.b, hbb], [gtmp.b])
                                act(gv, gv, AF.Sigmoid, [gtmp.b], [gtmp.b], scale=1.5957691216057308)
                                tt("vector", geluT[:, :, 0:127], gv, hv, ALU.mult, [gtmp.b, hbb], [geluT.b])
                                if kv == 0:
                                    for jc in range(2):
                                        mm(bk(5)[:, 0:127], w2sb[:, jc, :], geluT[:, jc, 0:127], jc == 0, jc == 1, [w2sb.b, geluT.b], [bb(5)])
                                    cp("scalar", kcP[0:64, 0, g, 0:127], bk(5)[0:64, 0:127], [bb(5)], [kcP.b])
                                    cp("scalar", kcP[64:128, 1, g, 0:127], bk(5)[64:128, 0:127], [bb(5)], [kcP.b])
                                else:
                                    for jc in range(2):
                                        mm(bk(5)[0:127, 0:64], geluT[:, jc, 0:127], w2sb[:, jc, 0:64], jc == 0, jc == 1, [w2sb.b, geluT.b], [bb(5)])
                                    cp("scalar", vcx[0:127, g, 0:64], bk(5)[0:127, 0:64], [bb(5)], [vcx.b])
                    S.barrier()
                if stop == "nsa2":
                    return
                dma("sync", hsc, hT.t.rearrange("p k s -> p (k s)"), r=hT_b, w=[b_hsc])
                kpad1 = Buf("kpad1")
                for base, KT_, ceng in ((0, kS, "scalar"), (4, kW, "vector")):
                    cp(ceng, hT[64:128, base:base + 4, :], KT_[64:128, :, :], [KT_.b], hT_b + [kpad1])
                    memset("gpsimd", hT[0:64, base:base + 4, :], 0.0, hT_b + [kpad1])
                    memset("gpsimd" if base == 0 else "vector", KT_[64:128, :, :], 0.0, [KT_.b])
                E = [T(ph, f"nE{i}", [128, 512], BF16) for i in range(4)]
                cb = [T(ph, f"ncb{i}", [128, 2, 512], BF16) for i in range(4)]
                for cbx in cb:
                    memset("vector", cbx[:], NEG, [cbx.b])
                ybt = [T(ph, f"nybt{i}", [128, 1024], BF16) for i in range(2)]
                ybT = [T(ph, f"nybT{i}", [128, 8, 128], BF16) for i in range(2)]
                sets = []
                for i in range(2):
                    sets.append(dict(
                        ybacc=T(ph, f"nybacc{i}", [128, 4, 64], F32), tmp1=T(ph, f"ntmp1{i}", [128, 4, 64], F32),
                        tmp2=T(ph, f"ntmp2{i}", [128, 4, 64], F32), impr=T(ph, f"nimpr{i}", [128, 4, 32], F32),
                        imp=T(ph, f"nimp{i}", [128, 32], F32), m8=T(ph, f"nm8{i}", [128, 8], F32),
                        sm=T(ph, f"nsm{i}", [128, 3, 4], F32), Us=(3, 6)[i], Uw=(4, 7)[i]))
                colb = lambda r: (r % 2) * 256 + (r // 2) * 128
                cnt_ = dict(l=0, e=0, kp=0, cb=0)
                tasks = []

                inflight = set()

                def next_Li(hold=False):
                    while True:
                        i = (0, 1, 5)[cnt_["l"] % 3]
                        cnt_["l"] += 1
                        if i not in inflight:
                            break
                    if hold:
                        inflight.add(i)
                    return i

                def next_L(hold=False):
                    return pbank[next_Li(hold)]

                def release_L(Lb):
                    for i in (0, 1, 5):
                        if pbank[i][1] is Lb:
                            inflight.discard(i)

                def next_E():
                    e_ = E[cnt_["e"] % 4]
                    cnt_["e"] += 1
                    return e_

                cb_dma = []
                PF = 3

                def mk_cmp(qt, g, st):
                    qsl = slice(qt * 128, (qt + 1) * 128)
                    ui = len(cb_dma)
                    cbt = cb[ui % 4]
                    box = {}

                    def issue():
                        for hl in range(2):
                            for rp in range(2):
                                a0 = tc_d[hl, 4 * g + 2 * rp, qt * 128 + 1:qt * 128 + 2]
                                src = bass.AP(a0.tensor, a0.offset, [[16, 128], [4096, 2], [1, 128]])
                                dst = cbt[:, hl, :].rearrange("p (a b s) -> p a b s", a=2, b=2)[:, :, rp, :]
                                dma("sync", dst, src, r=[b_tabs], w=[cbt.b])

                    cb_dma.append(issue)

                    def pre():
                        if ui + PF < len(cb_dma):
                            cb_dma[ui + PF]()

                    def s1():
                        L, Lb = next_L(hold=True)
                        box["L"] = (L, Lb)
                        for par in range(2):
                            mm(L[:, par * 256:(par + 1) * 256], kcP[:, par, g, :], qT[:, 2 * g:2 * g + 2, qsl], par == 0, False, [kcP.b, qT.b], [Lb])
                        for hl in range(2):
                            mm(L, Jb[:, 1, :], cbt[:, hl, :], False, hl == 1, [Jb.b, cbt.b], [Lb])

                    def s2():
                        L, Lb = box["L"]
                        release_L(Lb)
                        Ec = next_E()
                        act(Ec[:], L, AF.Exp, [Lb], [Ec.b])
                        Uc = bk(2)
                        for r in range(4):
                            mm(Uc[:, r * 98:(r + 1) * 98], Ec[:, colb(r):colb(r) + 128], vcx[:, g, 0:98], r == 0, r == 3, [Ec.b, vcx.b], [bb(2)])

                    def post():
                        Uc = bk(2)
                        sm, ybacc, impr, imp, m8 = st["sm"], st["ybacc"], st["impr"], st["imp"], st["m8"]
                        ucv = lambda a, b_: fap(Uc[:, a:a + 1], [[98, 4], [1, b_]])
                        rs4, wc = sm[:, 0, :], sm[:, 1, :]
                        tsc("vector", rs4, fap(Uc[:, 96:97], [[98, 4]]), 1e-30, ALU.max, [bb(2)], [sm.b])
                        recip(rs4, rs4, [sm.b], [sm.b])
                        tt("vector", wc, rs4, sgate[:, qt, 4 * g:4 * g + 4], ALU.mult, [sm.b, sgate.b], [sm.b])
                        tt("vector", ybacc[:], ucv(0, 64), fap(wc, [[1, 4], [0, 64]]), ALU.mult, [bb(2), sm.b], [ybacc.b])
                        tt("vector", impr[:], ucv(64, 32), fap(rs4, [[1, 4], [0, 32]]), ALU.mult, [bb(2), sm.b], [impr.b])
                        S.op("vector", lambda e: e.tensor_reduce(out=imp[:], in_=fap(impr[:, 0, 0:1], [[1, 32], [32, 4]]), axis=AX.X, op=ALU.add),
                             [impr.b], [imp.b])
                        tt("vector", imp[:], imp[:], cand[:, qt, :], ALU.mult, [imp.b, cand.b], [imp.b])
                        tt("vector", imp[:], imp[:], negc[:, qt, :], ALU.add, [imp.b, negc.b], [imp.b])
                        S.op("vector", lambda e: e.max(out=m8[:], in_=imp[:]), [imp.b], [m8.b])
                        tsc("vector", imp[:], imp[:], m8[:, 4:5], ALU.is_ge, [imp.b, m8.b], [imp.b])
                        tt("vector", imp[:], imp[:], forced[:, qt, :], ALU.max, [imp.b, forced.b], [imp.b])
                        tsc("vector", imp[:], imp[:], -1.0, ALU.add, [imp.b], [imp.b], -NEG, ALU.mult)

                    return dict(pre=pre, s1=s1, s2=s2, post=post, defer=None, nm=None, first_slc=False)

                def mk_tile(qt, g, st, br, kt, first, last, buf, hooks_pre, hooks_post, defer):
                    qsl = slice(qt * 128, (qt + 1) * 128)
                    ksl = slice(kt * 128, (kt + 1) * 128)
                    KT = kS if br == 0 else kW
                    VT = vS if br == 0 else vW
                    Ub = st["Us"] if br == 0 else st["Uw"]
                    dl = qt - kt
                    near = dl < (2 if br == 0 else 3)
                    box = {}

                    def pre():
                        for h_ in hooks_pre:
                            h_()

                    def s1():
                        L, Lb = next_L(hold=True)
                        box["L"] = (L, Lb)
                        mm(L[:, 0:256], KT[:, g, ksl], qT[:, 2 * g:2 * g + 2, qsl], True, False, [KT.b, qT.b], [Lb])
                        mm(L[:, 256:512], hT[:, (0 if br == 0 else 4) + g, ksl], qT[:, 2 * g:2 * g + 2, qsl], False, False, [kpad1, qT.b], [Lb])
                        if br == 0:
                            mm(L, expd[:, 1 if near else 0, kt, :], NM[:, g, buf, :], False, not near, [expd.b, NM.b], [Lb])
                        if near:
                            for hl in range(2):
                                mm(L, Jb[:, 0, :], hbt[:, dl, g, hl, :], False, hl == 1, [Jb.b, hbt.b], [Lb])

                    def s2():
                        L, Lb = box["L"]
                        release_L(Lb)
                        Et = next_E()
                        act(Et[:], L, AF.Exp, [Lb], [Et.b])
                        for r in range(4):
                            mm(bk(Ub)[:, r * 66:(r + 1) * 66], Et[:, colb(r):colb(r) + 128], VT[:, kt, g, 0:66],
                               first and r == 0, last and r == 3, [Et.b, VT.b], [bb(Ub)])

                    def post():
                        for h_ in hooks_post:
                            h_()

                    return dict(pre=pre, s1=s1, s2=s2, post=post, defer=defer, nm=None, first_slc=False)

                def mk_nm_hook(g, st, buf):
                    def hook():
                        imp = st["imp"]
                        M_, Mb = next_L()
                        trp(M_[0:32, 0:128], imp[:, :], ident_f[:, :], [imp.b, ident_f.b], [Mb])
                        cp("vector", NM[0:32, g, buf, :].rearrange("p (a s) -> p a s", a=4), fap(M_[0:32, 0:1], [[0, 4], [1, 128]]), [Mb], [NM.b])
                    return hook

                def mk_combine(qt, g, st, ybq):
                    def hook():
                        sm, ybacc, tmp1, tmp2 = st["sm"], st["ybacc"], st["tmp1"], st["tmp2"]
                        for br in range(2):
                            ub = st["Us"] if br == 0 else st["Uw"]
                            U = bk(ub)
                            rsb, wb_ = sm[:, 0, :], sm[:, 1 + br, :]
                            S.op("vector", lambda e, U=U, rsb=rsb: e.reciprocal(out=rsb, in_=fap(U[:, 64:65], [[66, 4]])), [bb(ub)], [sm.b])
                            tt("vector", wb_, rsb, sgate[:, qt, 16 * (br + 1) + 4 * g:16 * (br + 1) + 4 * g + 4], ALU.mult, [sm.b, sgate.b], [sm.b])
                            tgt = tmp1 if br == 0 else tmp2
                            tt("vector", tgt[:], fap(U[:, 0:1], [[66, 4], [1, 64]]), fap(wb_, [[1, 4], [0, 64]]), ALU.mult, [bb(ub), sm.b], [tgt.b])
                        tt("gpsimd", tmp1[:], tmp1[:], ybacc[:], ALU.add, [tmp1.b, ybacc.b], [tmp1.b])
                        tt("gpsimd", ybq[:, g * 256:(g + 1) * 256].rearrange("p (r d) -> p r d", r=4), tmp1[:], tmp2[:], ALU.add, [tmp1.b, tmp2.b], [ybq.b])
                    return hook

                def mk_ybout(qt, ybq):
                    def hook():
                        qsl = slice(qt * 128, (qt + 1) * 128)
                        li = next_Li()
                        pv = bank_bf(li)
                        for c in range(8):
                            trp(pv[:, c * 128:(c + 1) * 128], ybq[:, c * 128:(c + 1) * 128], ident_b[:], [ybq.b, ident_b.b], [bb(li)])
                        yT = ybT[qt % 2]
                        cp("scalar", yT[:], pv.rearrange("p (k s) -> p k s", k=8), [bb(li)], [yT.b])
                        dma("sync", ysc[1, :, qsl].rearrange("(c p) s -> p c s", p=128), yT[:], r=[yT.b], w=[b_ysc[1]])
                    return hook

                un = 0
                units = []
                for qt in range(1 if stop == 'nsa3' else NT):
                    ybq = ybt[qt % 2]
                    for g in range(4):
                        st = sets[un % 2]
                        buf = un % 2
                        un += 1
                        uc_ = [mk_cmp(qt, g, st)]
                        uc_[0]["nm"] = mk_nm_hook(g, st, buf)
                        uc_[0]["unit"] = len(units)
                        wk = list(range(max(0, qt - 2), qt + 1))
                        uw_ = [mk_tile(qt, g, st, 1, kt, kt == wk[0], kt == qt, buf, [], [], None) for kt in wk]
                        us_ = []
                        for kt in range(qt + 1):
                            hp = []
                            hq = [mk_combine(qt, g, st, ybq)] if kt == qt else []
                            df = mk_ybout(qt, ybq) if (kt == qt and g == 3) else None
                            us_.append(mk_tile(qt, g, st, 0, kt, kt == 0, kt == qt, buf, hp, hq, df))
                        us_[0]["first_slc"] = True
                        us_[0]["unit"] = len(units)
                        units.append((uc_, uw_, us_))
                tasks += units[0][0] + units[0][1]
                for ui in range(len(units)):
                    if ui + 1 < len(units):
                        tasks += units[ui + 1][0]
                    tasks += units[ui][2]
                    if ui + 1 < len(units):
                        tasks += units[ui + 1][1]
                for ui_ in range(min(PF, len(cb_dma))):
                    cb_dma[ui_]()
                deferred = {}
                ntk = len(tasks)
                LA = 2
                for j in range(min(LA, ntk)):
                    tasks[j]["pre"]()
                    tasks[j]["s1"]()
                first_idx = {tk["unit"]: i for i, tk in enumerate(tasks) if tk["first_slc"]}
                for i, tk in enumerate(tasks):
                    tk["s2"]()
                    tk["post"]()
                    if tk["nm"] is not None:
                        j = max(i, min(i + 8, first_idx[tk["unit"]] - LA))
                        deferred.setdefault(j, []).append(tk["nm"])
                    if tk["defer"] is not None:
                        deferred.setdefault(i + 3, []).append(tk["defer"])
                    for fn in deferred.pop(i, []):
                        fn()
                    if i + LA < ntk:
                        tasks[i + LA]["pre"]()
                        tasks[i + LA]["s1"]()
                for k_ in sorted(deferred):
                    for fn in deferred[k_]:
                        fn()
                dma("sync", hT.t.rearrange("p k s -> p (k s)"), hsc, r=[b_hsc], w=hT_b + [kpad1])
                S.barrier()

        def phase_tail(l, x_src, x_dst, b_xsrc, b_xdst):
            bk = lambda i: pbank[i][0]
            bb = lambda i: pbank[i][1]
            w2d = w_in[l]
            with ExitStack() as ph:
                mrgb = T(ph, "mrgb", [128, 8, SEQ], BF16)
                with ExitStack() as ph2:
                    mrg = T(ph2, "mrg", [128, 8, SEQ], F32)
                    yT = T(ph2, "m_yT", [128, 8, SEQ], BF16)
                    yb_ = [Buf(f"m_yT{c}") for c in range(8)]
                    wbr = [T(ph2, f"m_wbr{i}", [128, 8, 256], BF16) for i in range(2)]
                    wmg = [T(ph2, f"m_wmg{i}", [128, 8, 256], BF16) for i in range(2)]
                    sig = [T(ph2, f"m_sig{i}", [128, 512], F32) for i in range(2)]
                    prod = [T(ph2, f"m_prod{i}", [128, 512], F32) for i in range(2)]
                    n = 0
                    bn = 0
                    for br in range(3):
                        for c in range(8):
                            dma("sync", yT[:, c, :], ysc[br, c * 128:(c + 1) * 128, :], r=[b_ysc[br]], w=[yb_[c]])
                        for oc2 in range(4):
                            wb, wm = wbr[oc2 % 2], wmg[oc2 % 2]
                            load_slab(wb[:], w_branch[l, br], oc2 * 256, 256, wb.b)
                            load_slab(wm[:], w2d, C_MG + br * 1024 + oc2 * 256, 256, wm.b)
                            for o in range(2):
                                oc = oc2 * 2 + o
                                for sc in range(4):
                                    ssl = slice(sc * 512, (sc + 1) * 512)
                                    bB, bG = (bn % 4) * 2, (bn % 4) * 2 + 1
                                    bn += 1
                                    for c in range(8):
                                        mm(bk(bB), wb[:, c, o * 128:(o + 1) * 128], yT[:, c, ssl], c == 0, c == 7, [wb.b, yb_[c]], [bb(bB)])
                                    for kc in range(8):
                                        mm(bk(bG), wm[:, kc, o * 128:(o + 1) * 128], hT[:, kc, ssl], kc == 0, kc == 7, [wm.b] + hT_b[sc * 4:(sc + 1) * 4], [bb(bG)])
                                    sg_, pr_ = sig[n % 2], prod[n % 2]
                                    n += 1
                                    act(sg_[:], bk(bG), AF.Sigmoid, [bb(bG)], [sg_.b])
                                    if br == 0:
                                        tt("vector", mrg[:, oc, ssl], sg_[:], bk(bB), ALU.mult, [sg_.b, bb(bB)], [mrg.b])
                                    elif br == 1:
                                        tt("vector", pr_[:], sg_[:], bk(bB), ALU.mult, [sg_.b, bb(bB)], [pr_.b])
                                        tt("vector", mrg[:, oc, ssl], mrg[:, oc, ssl], pr_[:], ALU.add, [mrg.b, pr_.b], [mrg.b])
                                    else:
                                        tt("vector", pr_[:], sg_[:], bk(bB), ALU.mult, [sg_.b, bb(bB)], [pr_.b])
                                        tt("vector", mrgb[:, oc, ssl], mrg[:, oc, ssl], pr_[:], ALU.add, [mrg.b, pr_.b], [mrgb.b])
                    S.barrier()
                with ExitStack() as ph2:
                    wout = T(ph2, "p_wout", [128, 8, D], BF16)
                    g1 = load_gain(ph2, l, 1)
                    g2 = load_gain(ph2, l, 2)
                    xt = [T(ph2, f"p_xt{i}", [128, D], F32) for i in range(2)]
                    on = [T(ph2, f"p_on{i}", [128, D], F32) for i in range(2)]
                    hb = [T(ph2, f"p_hb{i}", [128, D], BF16) for i in range(2)]
                    junk = T(ph2, "p_junk", [128, D], BF16)
                    ss = T(ph2, "p_ss", [128, NT], F32)
                    rs = T(ph2, "p_rs", [128, NT], F32)
                    ss2 = T(ph2, "p_ss2", [128, NT], F32)
                    rs2 = T(ph2, "p_rs2", [128, NT], F32)
                    for half in range(2):
                        load_slab(wout[:, :, half * 512:(half + 1) * 512], w_out[l], half * 512, 512, wout.b)
                    junk2 = T(ph2, "p_junk2", [128, D], BF16)
                    ssb = T(ph2, "p_ssb", [128, NT], F32)
                    rsb_ = T(ph2, "p_rsb", [128, NT], F32)
                    ss2b = T(ph2, "p_ss2b", [128, NT], F32)
                    rs2b = T(ph2, "p_rs2b", [128, NT], F32)

                    def tileP(t):
                        p = t % 2
                        jk, s_a, r_a, s_b, r_b = (junk, ss, rs, ss2, rs2) if p == 0 else (junk2, ssb, rsb_, ss2b, rs2b)
                        tsl = slice(t * 128, (t + 1) * 128)
                        b0 = p * 2
                        ov = PA[:, b0 * 512:(b0 + 2) * 512]
                        obufs = [bb(b0), bb(b0 + 1)]
                        for half in range(2):
                            for c in range(8):
                                mm(bk(b0 + half), mrgb[:, c, tsl], wout[:, c, half * 512:(half + 1) * 512], c == 0, c == 7, [mrgb.b, wout.b], [bb(b0 + half)])
                        x_, o_ = xt[p], on[p]
                        dma("sync", x_[:], x_src[tsl, :], r=[b_xsrc], w=[x_.b])
                        act(jk[:], ov, AF.Square, obufs, [jk.b, s_a.b], accum_out=s_a[:, t:t + 1])
                        act(r_a[:, t:t + 1], s_a[:, t:t + 1], AF.Sqrt, [s_a.b], [r_a.b], scale=1.0 / D, bias=EPS)
                        recip(r_a[:, t:t + 1], r_a[:, t:t + 1], [r_a.b], [r_a.b])
                        stt(o_[:], ov, r_a[:, t:t + 1], g1[:], ALU.mult, ALU.mult, obufs + [r_a.b, g1.b], [o_.b])
                        tt("gpsimd", o_[:], o_[:], x_[:], ALU.add, [o_.b, x_.b], [o_.b])
                        dma("sync", xmid[tsl, :], o_[:], r=[o_.b], w=[b_xmid])
                        norm_transpose_tile(ph2, t, o_[:], o_.b, g2, s_b, r_b, hb[p], jk, 4 + p)

                    for t in range(0, NT, 2):
                        S.replay([S.record(lambda: tileP(t)), S.record(lambda: tileP(t + 1))])
                    S.barrier()
            with ExitStack() as ph:
                hid = T(ph, "f_hid", [128, 22, SEQ], BF16)
                hid_b = [Buf(f"hid{j}") for j in range(22)]
                wfo = T(ph, "f_wfo", [128, 22, D], BF16)
                wfo_b = [Buf(f"wfo{j}") for j in range(11)]
                wfi = [T(ph, f"f_wfi{i}", [128, 8, 2, 128], BF16) for i in range(2)]
                sgt = [T(ph, f"f_sg{i}", [128, 512], F32) for i in range(2)]
                g3 = load_gain(ph, l, 3)
                xt = [T(ph, f"f_xt{i}", [128, D], F32) for i in range(2)]
                on = [T(ph, f"f_on{i}", [128, D], F32) for i in range(2)]
                junk = T(ph, "f_junk", [128, D], BF16)
                ss = T(ph, "f_ss", [128, NT], F32)
                rs = T(ph, "f_rs", [128, NT], F32)
                n = 0
                bn = 0
                for j in range(22):
                    wf = wfi[j % 2]
                    dma("gpsimd", wf[:, :, 0, :], w_ffn_in[l][:, j * 128:(j + 1) * 128].rearrange("(kc p) n -> p kc n", p=128), w=[wf.b])
                    dma("gpsimd", wf[:, :, 1, :], w_ffn_in[l][:, DFF + j * 128:DFF + (j + 1) * 128].rearrange("(kc p) n -> p kc n", p=128), w=[wf.b])
                    if j % 2 == 0:
                        jj = j // 2
                        dma("gpsimd", wfo[:, 2 * jj:2 * jj + 2, :], w_ffn_out[l][jj * 256:(jj + 1) * 256, :].rearrange("(j p) n -> p j n", p=128), w=[wfo_b[jj]])
                    for sc in range(4):
                        ssl = slice(sc * 512, (sc + 1) * 512)
                        bG, bU = (bn % 4) * 2, (bn % 4) * 2 + 1
                        bn += 1
                        for kc in range(8):
                            mm(bk(bG), wf[:, kc, 0, :], hT[:, kc, ssl], kc == 0, kc == 7, [wf.b] + hT_b[sc * 4:(sc + 1) * 4], [bb(bG)])
                        for kc in range(8):
                            mm(bk(bU), wf[:, kc, 1, :], hT[:, kc, ssl], kc == 0, kc == 7, [wf.b] + hT_b[sc * 4:(sc + 1) * 4], [bb(bU)])
                        sg_ = sgt[n % 2]
                        n += 1
                        act(sg_[:], bk(bG), AF.Silu, [bb(bG)], [sg_.b])
                        tt("vector", hid[:, j, ssl], sg_[:], bk(bU), ALU.mult, [sg_.b, bb(bU)], [hid_b[j]])
                for t in range(NT):
                    tsl = slice(t * 128, (t + 1) * 128)
                    b0 = (t % 2) * 2
                    ov = PA[:, b0 * 512:(b0 + 2) * 512]
                    obufs = [bb(b0), bb(b0 + 1)]
                    for half in range(2):
                        for j in range(22):
                            mm(bk(b0 + half), hid[:, j, tsl], wfo[:, j, half * 512:(half + 1) * 512], j == 0, j == 21, [hid_b[j], wfo_b[j // 2]], [bb(b0 + half)])
                    x_, o_ = xt[t % 2], on[t % 2]
                    dma("sync", x_[:], xmid[tsl, :], r=[b_xmid], w=[x_.b])
                    act(junk[:], ov, AF.Square, obufs, [junk.b, ss.b], accum_out=ss[:, t:t + 1])
                    act(rs[:, t:t + 1], ss[:, t:t + 1], AF.Sqrt, [ss.b], [rs.b], scale=1.0 / D, bias=EPS)
                    recip(rs[:, t:t + 1], rs[:, t:t + 1], [rs.b], [rs.b])
                    stt(o_[:], ov, rs[:, t:t + 1], g3[:], ALU.mult, ALU.mult, obufs + [rs.b, g3.b], [o_.b])
                    tt("gpsimd", o_[:], o_[:], x_[:], ALU.add, [o_.b, x_.b], [o_.b])
                    dma("sync", x_dst[tsl, :], o_[:], r=[o_.b], w=[b_xdst])
                S.barrier()

        setup_tables()
        for l in range(n_layers):
            phase_A(l, x_in if l == 0 else xres)
            import os
            if not os.environ.get("SKIP_LG"):
                phase_lru(l)
                if stop == "lru":
                    break
                phase_gla(l)
                if stop == "gla":
                    break
            if not os.environ.get("SKIP_NSA"):
                phase_nsa(l)
            if stop is not None and stop.startswith("nsa"):
                break
            last = (l == n_layers - 1)
            phase_tail(l, x_in if l == 0 else xres, y_out if last else xres, Buf() if l == 0 else b_xres, Buf() if last else b_xres)
        S.barrier()
        S.emit()
    return nc, consts


_CACHE = {}


def kernel(**inputs):
    if "nc" not in _CACHE:
        _CACHE["nc"] = build()
    nc, consts = _CACHE["nc"]
    x = np.ascontiguousarray(np.asarray(inputs["x"], dtype=np.float32))
    shared = {k: np.ascontiguousarray(np.asarray(v, dtype=np.float32)) for k, v in inputs.items() if k != "x"}
    for k, v in consts.items():
        shared["c_" + k] = v
    in_maps = [dict(shared, x=x[i]) for i in range(8)]
    res = run_bass_kernel_spmd(nc, in_maps, core_ids=list(range(8)))
    return np.stack([np.asarray(r["out"], dtype=np.float32) for r in res.results], axis=0)
```

```python
import os
import numpy as np
from contextlib import ExitStack
import concourse.bass as bass
import concourse.mybir as mybir
from concourse.bass_utils import run_bass_kernel_spmd

F32 = mybir.dt.float32
BF16 = mybir.dt.bfloat16
ALU = mybir.AluOpType
AF = mybir.ActivationFunctionType
AX = mybir.AxisListType

SEQ = 2048
D = 1024
NT = 16
DEPTH = 2
EPS = 1e-6
IN_W = 10816
C_LRUX, C_LRUG, C_Q, C_KV, C_GATE, C_GQ, C_GK, C_GV, C_GOG, C_GLR, C_MG = 0, 1024, 2048, 3072, 4608, 4656, 5168, 5680, 6704, 7728, 7744
DFF = 2816
NEG = -30000.0


class Buf:
    __slots__ = ("name", "w", "r", "excl")

    def __init__(self, name="", excl=False):
        self.name = name
        self.w = None
        self.r = []
        self.excl = excl


class Sched:
    ENG = ("sync", "scalar", "vector", "gpsimd", "tensor")
    DMAQ = ("sync", "gpsimd", "scalar")

    def __init__(self, nc, es, n_dma_sems=12):
        self.nc = nc
        self.q = {e: [] for e in self.ENG}
        self.cnt = {e: 0 for e in self.ENG}
        self.sems = []
        self.esem = {}
        for e in self.ENG:
            self.esem[e] = len(self.sems)
            self.sems.append(es.enter_context(nc.semaphore("s_" + e)))
        self.known = {e: {} for e in self.ENG}
        self.dpool = {}
        self.dcnt = {}
        self.dlast = {}
        for qn in self.DMAQ:
            self.dpool[qn] = []
            for i in range(n_dma_sems):
                self.dpool[qn].append(len(self.sems))
                self.sems.append(es.enter_context(nc.semaphore(f"d_{qn}_{i}")))
            self.dcnt[qn] = 0
        self.K = n_dma_sems

    def _waits(self, eng, r, w):
        waits = {}
        kn = self.known[eng]
        own_pe = self.esem["tensor"] if eng == "tensor" else -1

        def need(kv):
            k, v = kv
            if k == own_pe:
                return
            if kn.get(k, 0) < v and waits.get(k, 0) < v:
                waits[k] = v

        own = self.esem.get(eng, -2)
        for b in r:
            if b.w is not None:
                need(b.w)
            if b.excl:
                for x in b.r:
                    if x[0] != own:
                        need(x)
        for b in w:
            if b.w is not None:
                need(b.w)
            for x in b.r:
                need(x)
        for k, v in waits.items():
            kn[k] = v
        return list(waits.items())

    _rec = None

    def record(self, fn):
        self._rec = []
        fn()
        r, self._rec = self._rec, None
        return r

    def replay(self, lists):
        idx = [0] * len(lists)
        live = True
        while live:
            live = False
            for j, lst in enumerate(lists):
                if idx[j] < len(lst):
                    kind, args, kw = lst[idx[j]]
                    idx[j] += 1
                    live = True
                    if kind == "op":
                        self.op(*args)
                    else:
                        self.dma(*args, **kw)

    def op(self, eng, fn, r=(), w=()):
        if self._rec is not None:
            self._rec.append(("op", (eng, fn, list(r), list(w)), {}))
            return
        waits = self._waits(eng, r, w)
        self.cnt[eng] += 1
        seq = self.cnt[eng]
        k = self.esem[eng]
        self.q[eng].append((waits, fn, (k, 1)))
        for b in w:
            b.w = (k, seq)
            b.r = []
        for b in r:
            if b not in w:
                b.r.append((k, seq))
                if len(b.r) > 24:
                    b.r = b.r[-24:] if False else self._compact(b.r)

    @staticmethod
    def _compact(lst):
        d = {}
        for k, v in lst:
            if d.get(k, 0) < v:
                d[k] = v
        return list(d.items())

    def dma(self, qn, out, in_, r=(), w=(), **kw):
        if self._rec is not None:
            self._rec.append(("dma", (qn, out, in_, list(r), list(w)), kw))
            return
        waits = self._waits(qn, r, w)
        i = self.dcnt[qn]
        self.dcnt[qn] += 1
        k = self.dpool[qn][i % self.K]
        val = 16 * (i // self.K + 1)
        if val > 16 and self.known[qn].get(k, 0) < val - 16:
            waits.append((k, val - 16))
            self.known[qn][k] = val - 16
        self.dlast[k] = val
        self.q[qn].append((waits, lambda e: e.dma_start(out=out, in_=in_, **kw), (k, 16)))
        for b in w:
            b.w = (k, val)
            b.r = []
        for b in r:
            if b not in w:
                b.r.append((k, val))
                if len(b.r) > 24:
                    b.r = self._compact(b.r)

    def pe_drain(self):
        k = self.esem["tensor"]
        if self.cnt["tensor"] > 0:
            self.q["tensor"].append(([(k, self.cnt["tensor"])], None, None))

    def barrier(self):
        tgt = [(self.esem[e], self.cnt[e]) for e in self.ENG if self.cnt[e] > 0]
        tgt += list(self.dlast.items())
        for e in self.ENG:
            waits = []
            for k, v in tgt:
                if e == "tensor" and k == self.esem["tensor"]:
                    continue
                if self.known[e].get(k, 0) < v:
                    waits.append((k, v))
                    self.known[e][k] = v
            if waits:
                self.q[e].append((waits, None, None))

    def emit(self):
        nc = self.nc
        with nc.Block() as block:
            for e in self.ENG:
                def body(eng, _e=e):
                    for waits, fn, inc in self.q[_e]:
                        for k, v in waits:
                            eng.wait_ge(self.sems[k], v)
                        if fn is not None:
                            ins = fn(eng)
                            ins.then_inc(self.sems[inc[0]], inc[1])
                getattr(block, e)(body)


def fap(a, dims):
    return bass.AP(a.tensor, a.offset, [list(a.ap[0])] + [list(d) for d in dims])


def _rel_bucket(d):
    d = np.asarray(d)
    n = np.maximum(d, 0)
    nf = np.maximum(n, 16).astype(np.float32)
    large = 16 + (np.log(nf / np.float32(16)) / np.float32(np.log(128 / 16)) * np.float32(16)).astype(np.int32)
    large = np.minimum(large, 31)
    return np.where(n < 16, n, large)


def host_consts():
    c = {}
    c["ident"] = np.eye(128, dtype=np.float32)
    c["antiid"] = np.eye(128, dtype=np.float32)[::-1].copy()
    aid127 = np.zeros((128, 128), np.float32)
    for i in range(127):
        aid127[i, 126 - i] = 1.0
    aid127[127, 127] = 1.0
    c["antiid127"] = aid127
    s = np.arange(128)
    c["triu"] = (s[:, None] <= s[None, :]).astype(np.float32)
    c["tril"] = (s[:, None] > s[None, :]).astype(np.float32)
    def oh(deltas, valid):
        m = np.zeros((33, len(deltas)), np.float32)
        b = _rel_bucket(deltas)
        for i, (dd, v) in enumerate(zip(deltas, valid)):
            if v:
                m[b[i], i] = 1.0
            else:
                m[32, i] = 1.0
        return m
    dc = np.arange(-2048, 2048)
    c["oh_c"] = oh(dc, dc >= 0)
    ds = np.arange(-512, 512)
    c["oh_s"] = oh(ds, ds >= 0)
    c["oh_w"] = oh(ds, (ds >= 0) & (ds < 256))
    cs = np.arange(127) * 16
    js = np.arange(32) * 64
    ov = np.clip(np.minimum(cs[:, None] + 32, js[None, :] + 64) - np.maximum(cs[:, None], js[None, :]), 0, None).astype(np.float32) / 32.0
    ovx = np.zeros((128, 33), np.float32)
    ovx[:127, :32] = ov
    ovx[:127, 32] = 1.0
    c["ovx"] = ovx
    pos = np.arange(SEQ)
    cur = pos // 64
    blk = np.arange(32)[None, :]
    cand = (blk >= 1) & (blk <= cur[:, None] - 2)
    forced = (blk == 0) | (blk == cur[:, None]) | (blk == cur[:, None] - 1)
    c["cand"] = cand.astype(np.float32).reshape(NT, 128, 32).transpose(1, 0, 2).copy()
    c["negc"] = ((cand.astype(np.float32) - 1.0) * 1e4).reshape(NT, 128, 32).transpose(1, 0, 2).copy()
    c["forced"] = forced.astype(np.float32).reshape(NT, 128, 32).transpose(1, 0, 2).copy()
    ex = np.zeros((128, NT, 128), np.float32)
    for kt in range(NT):
        for key in range(128):
            ex[2 * kt + key // 64, kt, key] = 1.0
    c["expand_near"] = ex.copy()
    ex[32:34] = 1.0
    c["expand"] = ex
    return c


CONST_SHAPES = None


def build(debug=False, n_layers=DEPTH, stop=None):
    nc = bass.Bass("TRN2", target_bir_lowering=False)
    consts = host_consts()
    din = {}

    def inp(name, shape, dt=F32):
        din[name] = nc.dram_tensor(name, list(shape), dt, kind="ExternalInput").ap()
        return din[name]

    x_in = inp("x", [SEQ, D])
    rel_table = inp("rel_table", [32, 16])
    norm_g = inp("norm_g", [DEPTH, 4, D])
    w_in = inp("w_in", [DEPTH, D, IN_W])
    conv_w = inp("conv_w", [DEPTH, 4, D])
    conv_b = inp("conv_b", [DEPTH, D])
    lru_wg = inp("lru_w_gates", [DEPTH, 2, 8, 128, 128])
    lru_bg = inp("lru_b_gates", [DEPTH, 2, D])
    lru_lam = inp("lru_lambda", [DEPTH, D])
    cmp_pos = inp("cmp_pos", [DEPTH, 2, 32, 64])
    cmp_w1 = inp("cmp_w1", [DEPTH, 2, 2048, 256])
    cmp_w2 = inp("cmp_w2", [DEPTH, 2, 256, 64])
    gla_wa2 = inp("gla_wa2", [DEPTH, 16, 512])
    gla_ba = inp("gla_ba", [DEPTH, 512])
    gla_norm = inp("gla_norm", [DEPTH, 256])
    w_branch = inp("w_branch", [DEPTH, 3, D, D])
    w_out = inp("w_out", [DEPTH, D, D])
    w_ffn_in = inp("w_ffn_in", [DEPTH, D, 2 * DFF])
    w_ffn_out = inp("w_ffn_out", [DEPTH, DFF, D])
    cin = {k: inp("c_" + k, v.shape) for k, v in consts.items()}

    okind = "ExternalOutput"
    y_out = nc.dram_tensor("out", [SEQ, D], F32, kind=okind).ap()
    skind = "ExternalOutput"
    xres = nc.dram_tensor("xres", [SEQ, D], F32, kind=skind).ap()
    xmid = nc.dram_tensor("xmid", [SEQ, D], F32, kind=skind).ap()
    ysc = nc.dram_tensor("ysc", [3, D, SEQ], BF16, kind=skind).ap()
    tc_d = nc.dram_tensor("tc_d", [2, 16, 4096], BF16, kind="Internal").ap()
    ts_d = nc.dram_tensor("ts_d", [2, 16, 1024], BF16, kind="Internal").ap()
    tw_d = nc.dram_tensor("tw_d", [2, 16, 1024], BF16, kind="Internal").ap()
    hsc = nc.dram_tensor("hsc", [128, 8 * SEQ], BF16, kind=skind).ap()
    b_hsc = Buf()
    b_xres, b_xmid, b_ysc, b_tabs = Buf(), Buf(), [Buf(), Buf(), Buf()], Buf()

    with ExitStack() as es:
        S = Sched(nc, es)
        es.enter_context(nc.allow_non_contiguous_dma(reason="small param loads"))

        def mm(out, lhsT, rhs, start, stop, r, w):
            S.op("tensor", lambda e: e.matmul(out, lhsT=lhsT, rhs=rhs, start=start, stop=stop), r, w)

        def trp(out, in_, ident, r, w):
            S.op("tensor", lambda e: e.transpose(out, in_, ident), r, w)

        def act(out, in_, func, r, w, **kw):
            S.op("scalar", lambda e: e.activation(out=out, in_=in_, func=func, **kw), r, w)

        def tt(eng, out, in0, in1, op, r, w):
            S.op(eng, lambda e: e.tensor_tensor(out=out, in0=in0, in1=in1, op=op), r, w)

        def tsc(eng, out, in0, s1, op0, r, w, s2=None, op1=None):
            if op1 is None:
                S.op(eng, lambda e: e.tensor_scalar(out=out, in0=in0, scalar1=s1, scalar2=None, op0=op0), r, w)
            else:
                S.op(eng, lambda e: e.tensor_scalar(out=out, in0=in0, scalar1=s1, scalar2=s2, op0=op0, op1=op1), r, w)

        def stt(out, in0, scalar, in1, op0, op1, r, w):
            S.op("vector", lambda e: e.scalar_tensor_tensor(out=out, in0=in0, scalar=scalar, in1=in1, op0=op0, op1=op1), r, w)

        def cp(eng, out, in_, r, w):
            if eng == "scalar":
                S.op("scalar", lambda e: e.copy(out=out, in_=in_), r, w)
            else:
                S.op(eng, lambda e: e.tensor_copy(out=out, in_=in_), r, w)

        def recip(out, in_, r, w):
            S.op("vector", lambda e: e.reciprocal(out=out, in_=in_), r, w)

        def memset(eng, ap, val, w):
            S.op(eng, lambda e: e.memset(ap, val), (), w)

        def dma(q, out, in_, r=(), w=()):
            S.dma(q, out, in_, r, w)

        class T:
            _n = [0]

            def __init__(self, stack, name, shape, dt, psum=False):
                T._n[0] += 1
                name = f"{name}_{T._n[0]}"
                self.t = stack.enter_context((nc.psum_tensor if psum else nc.sbuf_tensor)(name, list(shape), dt))
                self.b = Buf(name)

            def __getitem__(self, idx):
                return self.t[idx]

        PA = T(es, "PA", [128, 2048], F32, psum=True)
        PB = T(es, "PB", [128, 2048], F32, psum=True)
        pbank = []
        for i in range(8):
            src = PA if i < 4 else PB
            pbank.append((src.t[:, (i % 4) * 512:(i % 4 + 1) * 512], Buf(f"bank{i}", excl=True)))
        PAb = PA.t.bitcast(BF16)
        PBb = PB.t.bitcast(BF16)

        def bank_bf(i):
            src = PAb if i < 4 else PBb
            return src[:, (i % 4) * 1024:(i % 4 + 1) * 1024]

        ident_f = T(es, "ident_f", [128, 128], F32)
        ident_b = T(es, "ident_b", [128, 128], BF16)
        dma("sync", ident_f[:], cin["ident"], w=[ident_f.b])
        cp("vector", ident_b[:], ident_f[:], [ident_f.b], [ident_b.b])

        hT = T(es, "hT", [128, 8, SEQ], BF16)
        hT_b = [Buf(f"hT{t}") for t in range(NT)]

        def load_gain(ph, l, i):
            gt = T(ph, f"gain{i}", [128, D], F32)
            src = norm_g[l, i:i + 1, :]
            dma("sync", gt[:], bass.AP(src.tensor, src.offset, [[0, 128], [1, D]]), w=[gt.b])
            return gt

        def norm_transpose_tile(ph, t, xt_ap, xt_buf, gt, ss, rs, hb, junk, pbi):
            act(junk[:], xt_ap, AF.Square, [xt_buf], [junk.b, ss.b], accum_out=ss[:, t:t + 1])
            act(rs[:, t:t + 1], ss[:, t:t + 1], AF.Sqrt, [ss.b], [rs.b], scale=1.0 / D, bias=EPS)
            recip(rs[:, t:t + 1], rs[:, t:t + 1], [rs.b], [rs.b])
            stt(hb[:], xt_ap, rs[:, t:t + 1], gt[:], ALU.mult, ALU.mult, [xt_buf, rs.b, gt.b], [hb.b])
            pv, pbuf = bank_bf(pbi), pbank[pbi][1]
            for kc in range(8):
                trp(pv[:, kc * 128:(kc + 1) * 128], hb[:, kc * 128:(kc + 1) * 128], ident_b[:], [hb.b, ident_b.b], [pbuf])
            cp("scalar", hT[:, :, t * 128:(t + 1) * 128], pv.rearrange("p (k s) -> p k s", k=8), [pbuf], [hT_b[t]])

        def load_slab(dst_ap, w2d, c0, ncols, wbuf, nk=8):
            src = w2d[:, c0:c0 + ncols].rearrange("(kc p) n -> p kc n", p=128)
            dma("gpsimd", dst_ap, src, w=[wbuf])

        def proj_fm(wslab, wbuf, col_off, M, rhs_tile, rhs_bufs, out_banks, nk=8, sc_list=(0, 1, 2, 3)):
            for i, sc in enumerate(sc_list):
                pa, pb_ = out_banks[i]
                for kc in range(nk):
                    mm(pa[0:M, :], wslab[:, kc, col_off:col_off + M], rhs_tile[:, kc, sc * 512:(sc + 1) * 512],
                       kc == 0, kc == nk - 1, [wbuf] + rhs_bufs[sc * 4:(sc + 1) * 4], [pb_])

        def phase_A(l, x_src):
            with ExitStack() as ph:
                xt = [T(ph, f"xtA{i}", [128, D], F32) for i in range(2)]
                hb = [T(ph, f"hbA{i}", [128, D], BF16) for i in range(2)]
                junk = [T(ph, f"junkA{i}", [128, D], BF16) for i in range(2)]
                ss = [T(ph, f"ssA{i}", [128, NT], F32) for i in range(2)]
                rs = [T(ph, f"rsA{i}", [128, NT], F32) for i in range(2)]
                g0 = load_gain(ph, l, 0)

                def tileA(t):
                    p = t % 2
                    dma("sync", xt[p][:], x_src[t * 128:(t + 1) * 128, :], r=[b_xres], w=[xt[p].b])
                    norm_transpose_tile(ph, t, xt[p][:], xt[p].b, g0, ss[p], rs[p], hb[p], junk[p], p)

                for t in range(0, NT, 2):
                    S.replay([S.record(lambda: tileA(t)), S.record(lambda: tileA(t + 1))])
                S.barrier()

        def phase_lru(l):
            with ExitStack() as ph:
                prow = T(ph, "prow", [8, D], F32)
                lpT = T(ph, "lpT", [128, 8, 8], F32)
                sp = T(ph, "lru_sp", [128, 8, 6], F32)
                wg = T(ph, "lru_wg", [128, 2, 8, 128], BF16)
                slab = [T(ph, f"lslab{i}", [128, 8, 2, 128], BF16) for i in range(2)]
                XA = [T(ph, f"XA{i}", [128, SEQ + 4], F32) for i in range(2)]
                XC = [T(ph, f"XC{i}", [128, SEQ], F32) for i in range(2)]
                XCB = [T(ph, f"XCB{i}", [128, SEQ], BF16) for i in range(2)]
                R = [T(ph, f"R{i}", [128, SEQ], F32) for i in range(2)]
                A = [T(ph, f"A{i}", [128, SEQ], F32) for i in range(2)]
                I = [T(ph, f"I{i}", [128, SEQ], F32) for i in range(2)]
                H = [T(ph, f"H{i}", [128, SEQ], F32) for i in range(2)]
                GA = [T(ph, f"GA{i}", [128, SEQ], F32) for i in range(2)]
                G = [T(ph, f"G{i}", [128, SEQ], F32) for i in range(2)]
                YA = [T(ph, f"YA{i}", [128, SEQ], BF16) for i in range(2)]
                if os.environ.get("SBUF_DBG"):
                    print("LRU sbuf remaining", nc.sbuf_bytes_remaining)
                dma("sync", hsc, hT.t.rearrange("p k s -> p (k s)"), r=hT_b, w=[b_hsc])
                for k in range(4):
                    dma("sync", prow[k:k + 1, :], conv_w[l, k:k + 1, :], w=[prow.b])
                dma("sync", prow[4:5, :], conv_b[l:l + 1, :], w=[prow.b])
                dma("sync", prow[5:7, :], lru_bg[l], w=[prow.b])
                dma("sync", prow[7:8, :], lru_lam[l:l + 1, :], w=[prow.b])
                pv, pbuf = pbank[7]
                for c in range(8):
                    trp(pv[:, c * 8:(c + 1) * 8], prow[0:8, c * 128:(c + 1) * 128], ident_f[0:8, 0:8], [prow.b, ident_f.b], [pbuf])
                cp("vector", lpT[:], pv[:, 0:64].rearrange("p (c k) -> p c k", c=8), [pbuf], [lpT.b])
                xs, ln1, ser, msk, nsp8, nsp16 = (sp[:, :, i] for i in range(6))
                act(xs, lpT[:, :, 7], AF.Exp, [lpT.b], [sp.b], scale=-1.0)
                act(ln1, xs, AF.Ln, [sp.b], [sp.b], bias=1.0)
                tsc("vector", ser, xs, -0.25, ALU.mult, [sp.b], [sp.b], 1.0 / 3.0, ALU.add)
                tt("vector", ser, ser, xs, ALU.mult, [sp.b], [sp.b])
                tsc("vector", ser, ser, -1.0, ALU.mult, [sp.b], [sp.b], 0.5, ALU.add)
                tt("vector", ser, ser, xs, ALU.mult, [sp.b], [sp.b])
                tsc("vector", ser, ser, -1.0, ALU.mult, [sp.b], [sp.b], 1.0, ALU.add)
                tt("vector", ser, ser, xs, ALU.mult, [sp.b], [sp.b])
                tsc("vector", msk, xs, 0.03, ALU.is_lt, [sp.b], [sp.b])
                tt("vector", ser, ser, ln1, ALU.subtract, [sp.b], [sp.b])
                tt("vector", ser, ser, msk, ALU.mult, [sp.b], [sp.b])
                tt("vector", ser, ser, ln1, ALU.add, [sp.b], [sp.b])
                tsc("vector", nsp8, ser, -8.0, ALU.mult, [sp.b], [sp.b])
                tsc("vector", nsp16, ser, -16.0, ALU.mult, [sp.b], [sp.b])
                dma("gpsimd", wg[:], lru_wg[l].rearrange("k n c e -> c k n e"), w=[wg.b])
                for p_ in range(2):
                    memset("vector", XA[p_][:, 0:3], 0.0, [XA[p_].b])
                w2d = w_in[l]

                def lru_A(c):
                    p = c % 2
                    xa, xc, xcb, r_, i_, ga = XA[p], XC[p], XCB[p], R[p], I[p], GA[p]
                    sl = slab[p]
                    dma("gpsimd", sl[:, :, 0, :], w2d[:, C_LRUX + c * 128:C_LRUX + (c + 1) * 128].rearrange("(kc p) n -> p kc n", p=128), w=[sl.b])
                    dma("gpsimd", sl[:, :, 1, :], w2d[:, C_LRUG + c * 128:C_LRUG + (c + 1) * 128].rearrange("(kc p) n -> p kc n", p=128), w=[sl.b])
                    slv = sl.t.rearrange("p k a n -> p k (a n)")
                    proj_fm(slv, sl.b, 0, 128, hT.t, hT_b, pbank[0:4])
                    proj_fm(slv, sl.b, 128, 128, hT.t, hT_b, pbank[4:8])
                    cp("scalar", xa[:, 3:3 + SEQ], PA[:, :], [pbank[i][1] for i in range(4)], [xa.b])
                    cp("scalar", ga[:], PB[:, :], [pbank[i][1] for i in range(4, 8)], [ga.b])
                    cw = lambda k: lpT[:, c, k:k + 1]
                    act(xc[:], xa[:, 3:3 + SEQ], AF.Identity, [xa.b, lpT.b], [xc.b], scale=cw(3), bias=cw(4))
                    for k in range(3):
                        stt(xc[:], xa[:, k:k + SEQ], cw(k), xc[:], ALU.mult, ALU.add, [xa.b, lpT.b, xc.b], [xc.b])
                    cp("gpsimd", xcb[:], xc[:], [xc.b], [xcb.b])
                    for gk in range(2):
                        banks = pbank[4:8] if gk == 0 else pbank[0:4]
                        for sc in range(4):
                            mm(banks[sc][0], wg[:, gk, c, :], xcb[:, sc * 512:(sc + 1) * 512], True, True, [wg.b, xcb.b], [banks[sc][1]])
                    act(r_[:], PB[:, :], AF.Sigmoid, [pbank[i][1] for i in range(4, 8)], [r_.b], bias=lpT[:, c, 5:6])
                    act(i_[:], PA[:, :], AF.Sigmoid, [pbank[i][1] for i in range(4)], [i_.b], bias=lpT[:, c, 6:7])

                def lru_B(c):
                    p = c % 2
                    xc, r_, a_, i_, h_, ga, g_ = XC[p], R[p], A[p], I[p], H[p], GA[p], G[p]
                    act(a_[:], r_[:], AF.Exp, [r_.b, sp.b], [a_.b], scale=sp[:, c, 4:5])
                    act(r_[:], r_[:], AF.Exp, [r_.b, sp.b], [r_.b], scale=sp[:, c, 5:6])
                    act(r_[:], r_[:], AF.Sqrt, [r_.b], [r_.b], scale=-1.0, bias=1.0)
                    tt("gpsimd", i_[:], i_[:], xc[:], ALU.mult, [i_.b, xc.b], [i_.b])
                    tt("gpsimd", i_[:], i_[:], r_[:], ALU.mult, [i_.b, r_.b], [i_.b])
                    S.op("vector", lambda e, h_=h_, a_=a_, i_=i_: e.tensor_tensor_scan(out=h_[:], data0=a_[:], data1=i_[:], initial=0.0, op0=ALU.mult, op1=ALU.add),
                         [a_.b, i_.b], [h_.b])
                    act(g_[:], ga[:], AF.Square, [ga.b], [g_.b])
                    tsc("vector", g_[:], g_[:], 0.044715, ALU.mult, [g_.b], [g_.b], 1.0, ALU.add)
                    tt("gpsimd", g_[:], g_[:], ga[:], ALU.mult, [g_.b, ga.b], [g_.b])
                    act(g_[:], g_[:], AF.Sigmoid, [g_.b], [g_.b], scale=1.5957691216057308)
                    tt("gpsimd", g_[:], g_[:], ga[:], ALU.mult, [g_.b, ga.b], [g_.b])
                    ya = YA[p]
                    tt("vector", ya[:], g_[:], h_[:], ALU.mult, [g_.b, h_.b], [ya.b])
                    dma("sync", ysc[0, c * 128:(c + 1) * 128, :], ya[:], r=[ya.b], w=[b_ysc[0]])

                lru_A(0)
                for c in range(8):
                    lists = [S.record(lambda: lru_B(c))]
                    if c + 1 < 8:
                        lists.append(S.record(lambda: lru_A(c + 1)))
                    S.replay(lists)
                S.barrier()

        def phase_gla(l):
            w2d = w_in[l]
            with ExitStack() as ph:
                qT = T(ph, "gqT", [128, 4, SEQ], F32)
                kT = T(ph, "gkT", [128, 4, SEQ], F32)
                lrT = T(ph, "lrT", [32, SEQ], F32)
                wa2x = T(ph, "wa2x", [32, 512], F32)
                wres = T(ph, "gwres", [128, 8, 2560], BF16)
                gnb = T(ph, "gnb", [128, 4, 256], F32)
                st_f = T(ph, "st_f", [128, 4, 256], F32)
                st_b = T(ph, "st_b", [128, 4, 256], BF16)
                cm4 = T(ph, "cm4", [128, 4, 128], F32)
                triu = T(ph, "triu", [128, 128], F32)
                tril = T(ph, "tril", [128, 128], F32)
                dma("sync", triu[:], cin["triu"], w=[triu.b])
                dma("sync", tril[:], cin["tril"], w=[tril.b])
                for hh in range(4):
                    dma("sync", cm4[:, hh, :], cin["triu"], w=[cm4.b])
                    src = gla_norm[l:l + 1, :]
                    dma("sync", gnb[:, hh, :], bass.AP(src.tensor, src.offset, [[0, 128], [1, 256]]), w=[gnb.b])
                memset("vector", wa2x[:], 0.0, [wa2x.b])
                memset("vector", lrT[:], 1.0, [lrT.b])
                dma("sync", wa2x[0:16, :], gla_wa2[l], w=[wa2x.b])
                dma("sync", wa2x[16:17, :], gla_ba[l:l + 1, :], w=[wa2x.b])
                for i, c0 in enumerate((C_GK, C_GV, C_GV + 512, C_GOG, C_GOG + 512)):
                    load_slab(wres[:, :, i * 512:(i + 1) * 512], w2d, c0, 512, wres.b)
                with ExitStack() as ph2:
                    slab = [T(ph2, f"gslab{i}", [128, 8, 512], BF16) for i in range(2)]
                    lslab = T(ph2, "glslab", [128, 8, 16], BF16)
                    load_slab(slab[0][:], w2d, C_GQ, 512, slab[0].b)
                    load_slab(slab[1][:], w2d, C_GK, 512, slab[1].b)
                    load_slab(lslab[:], w2d, C_GLR, 16, lslab.b)
                    for i in range(8):
                        banks = pbank[0:4] if i % 2 == 0 else pbank[4:8]
                        src = PA if i % 2 == 0 else PB
                        proj_fm(slab[i // 4].t, slab[i // 4].b, (i % 4) * 128, 128, hT.t, hT_b, banks)
                        dst = qT if i < 4 else kT
                        act(dst[:, i % 4, :], src[:, :], AF.Copy, [b for _, b in banks], [dst.b], scale=(128 ** -0.5 if i < 4 else 1.0))
                    proj_fm(lslab.t, lslab.b, 0, 16, hT.t, hT_b, pbank[0:4])
                    cp("vector", lrT[0:16, :], PA[0:16, :], [b for _, b in pbank[0:4]], [lrT.b])
                    S.barrier()
                sp_t = T(ph, "g_sp", [128, 512], F32)
                E1 = [T(ph, f"g_E1{i}", [128, 512], F32) for i in range(2)]
                E2 = T(ph, "g_E2", [128, 512], F32)
                Erb = T(ph, "g_Erb", [128, 512], F32)
                qtb = [T(ph, f"g_qtb{i}", [128, 4, 128], BF16) for i in range(2)]
                ktb = T(ph, "g_ktb", [128, 4, 128], BF16)
                kend = [T(ph, f"g_kend{i}", [128, 512], BF16) for i in range(2)]
                v_bf = [T(ph, f"g_vbf{i}", [128, 1024], BF16) for i in range(2)]
                sg = [T(ph, f"g_sg{i}", [128, 1024], F32) for i in range(2)]
                attm = [T(ph, f"g_attm{i}", [128, 4, 128], BF16) for i in range(2)]
                on = T(ph, "g_on", [128, 1024], F32)
                yc = [T(ph, f"g_yc{i}", [128, 1024], BF16) for i in range(2)]
                ycT = [T(ph, f"g_ycT{i}", [128, 8, 128], BF16) for i in range(2)]
                junk = T(ph, "g_junk", [128, 256], BF16)
                ssq = T(ph, "g_ssq", [128, 4], F32)
                rst = T(ph, "g_rst", [128, 4], F32)
                if os.environ.get("SBUF_DBG"):
                    print("GLA sbuf remaining", nc.sbuf_bytes_remaining)
                bk = lambda i: pbank[i][0]
                bb = lambda i: pbank[i][1]

                def gla_A(t):
                    p = t % 2
                    tsl = slice(t * 128, (t + 1) * 128)
                    e1, qb, ke, vb, sg_, am = E1[p], qtb[p], kend[p], v_bf[p], sg[p], attm[p]
                    mm(bk(0), lrT[0:17, tsl], wa2x[0:17, :], True, True, [lrT.b, wa2x.b], [bb(0)])
                    act(sp_t[:], bk(0), AF.Exp, [bb(0)], [sp_t.b], scale=-1.0)
                    act(sp_t[:], sp_t[:], AF.Ln, [sp_t.b], [sp_t.b], bias=1.0)
                    for hh in range(4):
                        mm(bk(1)[:, hh * 128:(hh + 1) * 128], sp_t[:, hh * 128:(hh + 1) * 128], triu[:], True, True, [sp_t.b, triu.b], [bb(1)])
                    mm(bk(2), tril[:], sp_t[:], True, True, [tril.b, sp_t.b], [bb(2)])
                    act(e1[:], bk(1), AF.Exp, [bb(1)], [e1.b], scale=-1.0 / 16.0)
                    act(E2[:], bk(1), AF.Exp, [bb(1)], [E2.b], scale=1.0 / 16.0)
                    act(Erb[:], bk(2), AF.Exp, [bb(2)], [Erb.b], scale=-1.0 / 16.0)
                    tt("vector", qb[:], qT[:, :, tsl], e1.t.rearrange("p (h s) -> p h s", h=4), ALU.mult, [qT.b, e1.b], [qb.b])
                    tt("gpsimd", ktb[:], kT[:, :, tsl], E2.t.rearrange("p (h s) -> p h s", h=4), ALU.mult, [kT.b, E2.b], [ktb.b])
                    for kc in range(8):
                        mm(bk(3), hT[:, kc, tsl], wres[:, kc, 0:512], kc == 0, kc == 7, [hT_b[t], wres.b], [bb(3)])
                    tt("vector", ke[:], bk(3), Erb[:], ALU.mult, [bb(3), Erb.b], [ke.b])
                    for half in range(2):
                        for kc in range(8):
                            mm(bk(half), hT[:, kc, tsl], wres[:, kc, 512 + half * 512:1024 + half * 512], kc == 0, kc == 7, [hT_b[t], wres.b], [bb(half)])
                    cp("scalar", vb[:], PA[:, 0:1024], [bb(0), bb(1)], [vb.b])
                    for half in range(2):
                        for kc in range(8):
                            mm(bk(2 + half), hT[:, kc, tsl], wres[:, kc, 1536 + half * 512:2048 + half * 512], kc == 0, kc == 7, [hT_b[t], wres.b], [bb(2 + half)])
                    act(sg_[:], PA[:, 1024:2048], AF.Silu, [bb(2), bb(3)], [sg_.b])
                    tt("gpsimd", sg_[:], sg_[:], gnb.t.rearrange("p h e -> p (h e)"), ALU.mult, [sg_.b, gnb.b], [sg_.b])
                    for hh in range(4):
                        mm(bk(0)[:, hh * 128:(hh + 1) * 128], ktb[:, hh, :], qb[:, hh, :], True, True, [ktb.b, qb.b], [bb(0)])
                    tt("vector", am[:], bk(0).rearrange("p (h s) -> p h s", h=4), cm4[:], ALU.mult, [bb(0), cm4.b], [am.b])

                def gla_B(t):
                    p = t % 2
                    tsl = slice(t * 128, (t + 1) * 128)
                    e1, qb, ke, vb, sg_, am = E1[p], qtb[p], kend[p], v_bf[p], sg[p], attm[p]
                    for hh in range(4):
                        ob = 4 + hh // 2
                        oap = bk(ob)[:, (hh % 2) * 256:(hh % 2 + 1) * 256]
                        mm(oap, am[:, hh, :], vb[:, hh * 256:(hh + 1) * 256], hh % 2 == 0, t == 0 and hh % 2 == 1, [am.b, vb.b], [bb(ob)])
                        if t > 0:
                            mm(oap, qb[:, hh, :], st_b[:, hh, :], False, hh % 2 == 1, [qb.b, st_b.b], [bb(ob)])
                    for hh in range(4):
                        kb_ = 6 + hh // 2
                        mm(bk(kb_)[:, (hh % 2) * 256:(hh % 2 + 1) * 256], ke[:, hh * 128:(hh + 1) * 128], vb[:, hh * 256:(hh + 1) * 256],
                           hh % 2 == 0, hh % 2 == 1, [ke.b, vb.b], [bb(kb_)])
                    for hh in range(4):
                        kvp = bk(6 + hh // 2)[:, (hh % 2) * 256:(hh % 2 + 1) * 256]
                        if t == 0:
                            cp("vector", st_f[:, hh, :], kvp, [bb(6 + hh // 2)], [st_f.b])
                        else:
                            dec = e1[:, hh * 128 + 127:hh * 128 + 128]
                            stt(st_f[:, hh, :], st_f[:, hh, :], dec, kvp, ALU.mult, ALU.add, [st_f.b, e1.b, bb(6 + hh // 2)], [st_f.b])
                    cp("gpsimd", st_b[:], st_f[:], [st_f.b], [st_b.b])
                    for hh in range(4):
                        oap = bk(4 + hh // 2)[:, (hh % 2) * 256:(hh % 2 + 1) * 256]
                        act(junk[:], oap, AF.Square, [bb(4 + hh // 2)], [junk.b, ssq.b], accum_out=ssq[:, hh:hh + 1])
                    act(rst[:], ssq[:], AF.Sqrt, [ssq.b], [rst.b], scale=1.0 / 256.0, bias=EPS)
                    recip(rst[:], rst[:], [rst.b], [rst.b])
                    tt("vector", on.t.rearrange("p (h e) -> p h e", h=4), PB[:, 0:1024].rearrange("p (h e) -> p h e", h=4),
                       fap(rst[:], [[1, 4], [0, 256]]), ALU.mult, [bb(4), bb(5), rst.b], [on.b])
                    y = yc[p]
                    tt("gpsimd", y[:], on[:], sg_[:], ALU.mult, [on.b, sg_.b], [y.b])
                    pv = bank_bf(7)
                    for c in range(8):
                        trp(pv[:, c * 128:(c + 1) * 128], y[:, c * 128:(c + 1) * 128], ident_b[:], [y.b, ident_b.b], [bb(7)])
                    yT = ycT[p]
                    cp("scalar", yT[:], pv.rearrange("p (k s) -> p k s", k=8), [bb(7)], [yT.b])
                    dma("sync", ysc[2, :, tsl].rearrange("(c p) s -> p c s", p=128), yT[:], r=[yT.b], w=[b_ysc[2]])

                gla_A(0)
                for t in range(NT):
                    lists = [S.record(lambda: gla_B(t))]
                    if t + 1 < NT:
                        lists.append(S.record(lambda: gla_A(t + 1)))
                    S.replay(lists)
                S.barrier()

        def setup_tables():
            with ExitStack() as ph:
                tblx = T(ph, "tblx", [33, 16], F32)
                memset("vector", tblx[:], NEG, [tblx.b])
                dma("sync", tblx[0:32, :], rel_table, w=[tblx.b])
                for name, dst, n in (("oh_c", tc_d, 4096), ("oh_s", ts_d, 1024), ("oh_w", tw_d, 1024)):
                    oh = T(ph, "t_" + name, [33, n], F32)
                    thi = T(ph, "thi_" + name, [16, n], BF16)
                    tlo = T(ph, "tlo_" + name, [16, n], BF16)
                    dma("sync", oh[:], cin[name], w=[oh.b])
                    for ch in range(n // 512):
                        pa, pbuf = pbank[ch % 8]
                        mm(pa[0:16, :], tblx[0:33, 0:16], oh[0:33, ch * 512:(ch + 1) * 512], True, True, [tblx.b, oh.b], [pbuf])
                        cp("vector", thi[:, ch * 512:(ch + 1) * 512], pa[0:16, :], [pbuf], [thi.b])
                        tt("vector", tlo[:, ch * 512:(ch + 1) * 512], pa[0:16, :], thi[:, ch * 512:(ch + 1) * 512], ALU.subtract, [pbuf, thi.b], [tlo.b])
                    dma("sync", dst[0], thi[:], r=[thi.b], w=[b_tabs])
                    dma("sync", dst[1], tlo[:], r=[tlo.b], w=[b_tabs])
                S.barrier()

        def phase_nsa(l):
            w2d = w_in[l]
            bk = lambda i: pbank[i][0]
            bb = lambda i: pbank[i][1]
            with ExitStack() as ph:
                qT = T(ph, "nqT", [128, 8, SEQ], BF16)
                kS = T(ph, "nkS", [128, 4, SEQ], BF16)
                kW = T(ph, "nkW", [128, 4, SEQ], BF16)
                vS = T(ph, "nvS", [128, NT, 4, 66], BF16)
                vW = T(ph, "nvW", [128, NT, 4, 66], BF16)
                sgate = T(ph, "nsg", [128, NT, 48], F32)
                kcP = T(ph, "nkcP", [128, 2, 4, 128], BF16)
                vcx = T(ph, "nvcx", [128, 4, 98], BF16)
                hbt = T(ph, "nhbt", [128, 3, 4, 2, 512], BF16)
                NM = T(ph, "nNM", [128, 4, 2, 512], BF16)
                Jb = T(ph, "nJb", [128, 2, 128], BF16)
                expd = T(ph, "nexpd", [128, 2, NT, 128], BF16)
                cand = T(ph, "ncand", [128, NT, 32], F32)
                negc = T(ph, "nnegc", [128, NT, 32], F32)
                forced = T(ph, "nforced", [128, NT, 32], F32)
                dma("gpsimd", Jb[:, 0, :], cin["antiid"], w=[Jb.b])
                dma("gpsimd", Jb[:, 1, :], cin["antiid127"], w=[Jb.b])
                dma("gpsimd", expd[:, 0, :, :], cin["expand"], w=[expd.b])
                dma("gpsimd", expd[:, 1, :, :], cin["expand_near"], w=[expd.b])
                memset("vector", NM[:], 0.0, [NM.b])
                dma("sync", cand[:], cin["cand"], w=[cand.b])
                dma("sync", negc[:], cin["negc"], w=[negc.b])
                dma("sync", forced[:], cin["forced"], w=[forced.b])
                memset("vector", vcx[:], 0.0, [vcx.b])
                memset("vector", kcP[:], 0.0, [kcP.b])
                for g in range(4):
                    dma("gpsimd", vcx[:, g, 64:97], cin["ovx"], w=[vcx.b])
                for dl in range(3):
                    tsrc = tw_d if dl == 2 else ts_d
                    for g in range(4):
                        for hl in range(2):
                            for rp in range(2):
                                a0 = tsrc[hl, 4 * g + 2 * rp, 512 + dl * 128 - 127:512 + dl * 128 - 127 + 1]
                                src = bass.AP(a0.tensor, a0.offset, [[1, 128], [1024, 2], [1, 128]])
                                dst = hbt[:, dl, g, hl, :].rearrange("p (a b s) -> p a b s", a=2, b=2)[:, :, rp, :]
                                dma("sync", dst, src, r=[b_tabs], w=[hbt.b])
                for g in range(4):
                    for hl in range(2):
                        for par in range(2):
                            for rp in range(2):
                                h = 4 * g + 2 * rp + par
                                a0 = ts_d[hl, h, 640:641]
                                src = bass.AP(a0.tensor, a0.offset, [[0, 1], [0, 2], [1, 128]])
                                c0 = par * 256 + rp * 128
                                dma("sync", NM[32 + hl:33 + hl, g, :, c0:c0 + 128], src, r=[b_tabs], w=[NM.b])
                memset("vector", vS[:, :, :, 64:66], 1.0, [vS.b])
                memset("vector", vW[:, :, :, 64:66], 1.0, [vW.b])
                if stop == "nsa0":
                    S.barrier()
                    return
                with ExitStack() as ph2:
                    slab = [T(ph2, f"nslab{i}", [128, 8, 512], BF16) for i in range(2)]
                    wv = T(ph2, "nwv", [128, 8, 560], BF16)
                    for half in range(2):
                        sl = slab[half]
                        load_slab(sl[:], w2d, C_Q + half * 512, 512, sl.b)
                        for i in range(4):
                            c = half * 4 + i
                            banks = pbank[0:4] if c % 2 == 0 else pbank[4:8]
                            src = PA if c % 2 == 0 else PB
                            proj_fm(sl.t, sl.b, i * 128, 128, hT.t, hT_b, banks)
                            act(qT[:, c, :], src[:, :], AF.Copy, [b for _, b in banks], [qT.b], scale=0.125)
                    if stop == "nsa1a":
                        S.barrier()
                        return
                    n = 0
                    for idx, dst in ((2, kS), (4, kW)):
                        sl = slab[n % 2]
                        n += 1
                        for g in range(4):
                            c0 = C_KV + idx * 256 + g * 64
                            for dup in range(2):
                                dma("gpsimd", sl[:, :, g * 128 + dup * 64:g * 128 + dup * 64 + 64],
                                    w2d[:, c0:c0 + 64].rearrange("(kc p) n -> p kc n", p=128), w=[sl.b])
                        for g in range(4):
                            banks = pbank[0:4] if g % 2 == 0 else pbank[4:8]
                            src = PA if g % 2 == 0 else PB
                            proj_fm(sl.t, sl.b, g * 128, 128, hT.t, hT_b, banks)
                            cp("scalar" if g % 2 == 0 else "vector", dst[:, g, :], src[:, :], [b for _, b in banks], [dst.b])
                    if stop == "nsa1b":
                        S.barrier()
                        return
                    load_slab(wv[:, :, 0:256], w2d, C_KV + 3 * 256, 256, wv.b)
                    load_slab(wv[:, :, 256:512], w2d, C_KV + 5 * 256, 256, wv.b)
                    load_slab(wv[:, :, 512:560], w2d, C_GATE, 48, wv.b)
                    for t in range(NT):
                        tsl = slice(t * 128, (t + 1) * 128)
                        b0, b1 = (0, 1) if t % 2 == 0 else (2, 3)
                        for kc in range(8):
                            mm(bk(b0), hT[:, kc, tsl], wv[:, kc, 0:512], kc == 0, kc == 7, [hT_b[t], wv.b], [bb(b0)])
                        import os
                        SK = os.environ.get("NSA_SKIP", "")
                        if "g" not in SK:
                            for kc in range(8):
                                mm(bk(b1)[:, 0:48], hT[:, kc, tsl], wv[:, kc, 512:560], kc == 0, kc == 7, [hT_b[t], wv.b], [bb(b1)])
                        if "v" not in SK:
                            cp("vector", vS[:, t, :, 0:64], bk(b0)[:, 0:256].rearrange("p (g d) -> p g d", g=4), [bb(b0)], [vS.b])
                        if "w" not in SK:
                            cp("scalar", vW[:, t, :, 0:64], bk(b0)[:, 256:512].rearrange("p (g d) -> p g d", g=4), [bb(b0)], [vW.b])
                        if "g" not in SK:
                            act(sgate[:, t, :], bk(b1)[:, 0:48], AF.Sigmoid, [bb(b1)], [sgate.b])
                    S.barrier()
                if stop == "nsa1":
                    return
                with ExitStack() as ph2:
                    slab = [T(ph2, f"ncslab{i}", [128, 8, 256], BF16) for i in range(2)]
                    w1sb = T(ph2, "nw1", [128, 32, 256], BF16)
                    w2sb = T(ph2, "nw2", [128, 2, 128], BF16)
                    prow2 = T(ph2, "nprow2", [32, 128], F32)
                    posT = T(ph2, "nposT", [128, 32], F32)
                    XAB = [T(ph2, f"nXAB{i}", [128, SEQ], BF16) for i in range(2)]
                    gtmp = T(ph2, "ngtmp", [128, 2, 128], F32)
                    geluT = T(ph2, "ngeluT", [128, 2, 128], BF16)
                    for kv in range(2):
                        for dup in range(2):
                            dma("gpsimd", w1sb[dup * 64:(dup + 1) * 64, :, :], cmp_w1[l, kv].rearrange("(p d) j -> d p j", d=64), w=[w1sb.b])
                            dma("gpsimd", w2sb[:, :, dup * 64:(dup + 1) * 64], cmp_w2[l, kv].rearrange("(jc p) d -> p jc d", p=128), w=[w2sb.b])
                            dma("sync", prow2[:, dup * 64:(dup + 1) * 64], cmp_pos[l, kv], w=[prow2.b])
                        trp(bk(6)[:, 0:32], prow2[:, :], ident_f[0:32, 0:32], [prow2.b, ident_f.b], [bb(6)])
                        cp("vector", posT[:], bk(6)[:, 0:32], [bb(6)], [posT.b])
                        sl = slab[kv]
                        load_slab(sl[:, :, 0:256], w2d, C_KV + kv * 256, 256, sl.b)
                        for cc in range(2):
                            banks = pbank[0:4]
                            proj_fm(sl.t, sl.b, cc * 128, 128, hT.t, hT_b, banks)
                            for ab in range(2):
                                tt("vector" if ab == 0 else "gpsimd" if False else "vector", XAB[ab].t.rearrange("p (i q) -> p i q", q=16), PA.t.rearrange("p (i q) -> p i q", q=16),
                                   fap(posT[:, ab * 16:ab * 16 + 1], [[0, 128], [1, 16]]), ALU.add, [b for _, b in banks] + [posT.b], [XAB[ab].b])
                            for gg in range(2):
                                g = cc * 2 + gg
                                rows = slice(gg * 64, gg * 64 + 64)
                                hb_, hbb = bk(4 + 2 * gg), bb(4 + 2 * gg)
                                for jc in range(2):
                                    for p in range(32):
                                        srcT = XAB[0] if p < 16 else XAB[1]
                                        rhs = fap(srcT[rows, p:p + 1], [[16, 127]])
                                        mm(hb_[:, jc * 128:jc * 128 + 127], w1sb[rows, p, jc * 128:(jc + 1) * 128], rhs, p == 0, p == 31,
                                           [w1sb.b, srcT.b], [hbb])
                                hv = hb_[:, 0:256].rearrange("p (j i) -> p j i", j=2)[:, :, 0:127]
                                gv = gtmp[:, :, 0:127]
                                act(gv, hv, AF.Square, [hbb], [gtmp.b])
                                tsc("vector", gv, gv, 0.044715, ALU.mult, [gtmp.b], [gtmp.b], 1.0, ALU.add)
                                tt("vector", gv, gv, hv, ALU.mult, [gtmp.b, hbb], [gtmp.b])
                                act(gv, gv, AF.Sigmoid, [gtmp.b], [gtmp.b], scale=1.5957691216057308)
                                tt("vector", geluT[:, :, 0:127], gv, hv, ALU.mult, [gtmp.b, hbb], [geluT.b])
                                if kv == 0:
                                    for jc in range(2):
                                        mm(bk(5)[:, 0:127], w2sb[:, jc, :], geluT[:, jc, 0:127], jc == 0, jc == 1, [w2sb.b, geluT.b], [bb(5)])
                                    cp("scalar", kcP[0:64, 0, g, 0:127], bk(5)[0:64, 0:127], [bb(5)], [kcP.b])
                                    cp("scalar", kcP[64:128, 1, g, 0:127], bk(5)[64:128, 0:127], [bb(5)], [kcP.b])
                                else:
                                    for jc in range(2):
                                        mm(bk(5)[0:127, 0:64], geluT[:, jc, 0:127], w2sb[:, jc, 0:64], jc == 0, jc == 1, [w2sb.b, geluT.b], [bb(5)])
                                    cp("scalar", vcx[0:127, g, 0:64], bk(5)[0:127, 0:64], [bb(5)], [vcx.b])
                    S.barrier()
                if stop == "nsa2":
                    return
                kpad1 = Buf("kpad1")
                for base, KT_, ceng in ((0, kS, "scalar"), (4, kW, "vector")):
                    cp(ceng, hT[64:128, base:base + 4, :], KT_[64:128, :, :], [KT_.b], hT_b + [kpad1])
                    memset("gpsimd", hT[0:64, base:base + 4, :], 0.0, hT_b + [kpad1])
                    memset("gpsimd" if base == 0 else "vector", KT_[64:128, :, :], 0.0, [KT_.b])
                E = [T(ph, f"nE{i}", [128, 512], BF16) for i in range(4)]
                cb = [T(ph, f"ncb{i}", [128, 2, 512], BF16) for i in range(5)]
                for cbx in cb:
                    memset("vector", cbx[:], NEG, [cbx.b])
                ybt = [T(ph, f"nybt{i}", [128, 1024], BF16) for i in range(2)]
                ybT = [T(ph, f"nybT{i}", [128, 8, 128], BF16) for i in range(2)]
                sets = []
                for i in range(2):
                    sets.append(dict(
                        ybacc=T(ph, f"nybacc{i}", [128, 4, 64], F32), tmp1=T(ph, f"ntmp1{i}", [128, 4, 64], F32),
                        tmp2=T(ph, f"ntmp2{i}", [128, 4, 64], F32), impr=T(ph, f"nimpr{i}", [128, 4, 32], F32),
                        imp=T(ph, f"nimp{i}", [128, 32], F32), m8=T(ph, f"nm8{i}", [128, 8], F32),
                        sm=T(ph, f"nsm{i}", [128, 3, 4], F32), Us=(3, 6)[i], Uw=(4, 7)[i]))
                colb = lambda r: (r % 2) * 256 + (r // 2) * 128
                cnt_ = dict(l=0, e=0, kp=0, cb=0)
                tasks = []

                inflight = set()

                def next_Li(hold=False):
                    while True:
                        i = (0, 1, 5)[cnt_["l"] % 3]
                        cnt_["l"] += 1
                        if i not in inflight:
                            break
                    if hold:
                        inflight.add(i)
                    return i

                def next_L(hold=False):
                    return pbank[next_Li(hold)]

                def release_L(Lb):
                    for i in (0, 1, 5):
                        if pbank[i][1] is Lb:
                            inflight.discard(i)

                def next_E():
                    e_ = E[cnt_["e"] % 4]
                    cnt_["e"] += 1
                    return e_

                cb_dma = []
                PF = 4

                def mk_cmp(qt, g, st):
                    qsl = slice(qt * 128, (qt + 1) * 128)
                    ui = len(cb_dma)
                    cbt = cb[ui % 5]
                    box = {}

                    def issue():
                        for hl in range(2):
                            for rp in range(2):
                                a0 = tc_d[hl, 4 * g + 2 * rp, qt * 128 + 1:qt * 128 + 2]
                                src = bass.AP(a0.tensor, a0.offset, [[16, 128], [4096, 2], [1, 128]])
                                dst = cbt[:, hl, :].rearrange("p (a b s) -> p a b s", a=2, b=2)[:, :, rp, :]
                                dma("sync", dst, src, r=[b_tabs], w=[cbt.b])

                    cb_dma.append(issue)

                    def pre():
                        if ui + PF < len(cb_dma):
                            cb_dma[ui + PF]()

                    def s1():
                        L, Lb = next_L(hold=True)
                        box["L"] = (L, Lb)
                        for par in range(2):
                            mm(L[:, par * 256:(par + 1) * 256], kcP[:, par, g, :], qT[:, 2 * g:2 * g + 2, qsl], par == 0, False, [kcP.b, qT.b], [Lb])
                        for hl in range(2):
                            mm(L, Jb[:, 1, :], cbt[:, hl, :], False, hl == 1, [Jb.b, cbt.b], [Lb])

                    def s2():
                        L, Lb = box["L"]
                        release_L(Lb)
                        Ec = next_E()
                        act(Ec[:], L, AF.Exp, [Lb], [Ec.b])
                        Uc = bk(2)
                        for r in range(4):
                            mm(Uc[:, r * 98:(r + 1) * 98], Ec[:, colb(r):colb(r) + 128], vcx[:, g, 0:98], r == 0, r == 3, [Ec.b, vcx.b], [bb(2)])

                    def post():
                        Uc = bk(2)
                        sm, ybacc, impr, imp, m8 = st["sm"], st["ybacc"], st["impr"], st["imp"], st["m8"]
                        ucv = lambda a, b_: fap(Uc[:, a:a + 1], [[98, 4], [1, b_]])
                        rs4, wc = sm[:, 0, :], sm[:, 1, :]
                        tsc("vector", rs4, fap(Uc[:, 96:97], [[98, 4]]), 1e-30, ALU.max, [bb(2)], [sm.b])
                        recip(rs4, rs4, [sm.b], [sm.b])
                        tt("vector", wc, rs4, sgate[:, qt, 4 * g:4 * g + 4], ALU.mult, [sm.b, sgate.b], [sm.b])
                        tt("vector", ybacc[:], ucv(0, 64), fap(wc, [[1, 4], [0, 64]]), ALU.mult, [bb(2), sm.b], [ybacc.b])
                        tt("vector", impr[:], ucv(64, 32), fap(rs4, [[1, 4], [0, 32]]), ALU.mult, [bb(2), sm.b], [impr.b])
                        S.op("vector", lambda e: e.tensor_reduce(out=imp[:], in_=fap(impr[:, 0, 0:1], [[1, 32], [32, 4]]), axis=AX.X, op=ALU.add),
                             [impr.b], [imp.b])
                        tt("vector", imp[:], imp[:], cand[:, qt, :], ALU.mult, [imp.b, cand.b], [imp.b])
                        tt("vector", imp[:], imp[:], negc[:, qt, :], ALU.add, [imp.b, negc.b], [imp.b])
                        S.op("vector", lambda e: e.max(out=m8[:], in_=imp[:]), [imp.b], [m8.b])
                        tsc("vector", imp[:], imp[:], m8[:, 4:5], ALU.is_ge, [imp.b, m8.b], [imp.b])
                        tt("vector", imp[:], imp[:], forced[:, qt, :], ALU.max, [imp.b, forced.b], [imp.b])
                        tsc("vector", imp[:], imp[:], -1.0, ALU.add, [imp.b], [imp.b], -NEG, ALU.mult)

                    return dict(pre=pre, s1=s1, s2=s2, post=post, defer=None, nm=None, first_slc=False)

                def mk_tile(qt, g, st, br, kt, first, last, buf, hooks_pre, hooks_post, defer):
                    qsl = slice(qt * 128, (qt + 1) * 128)
                    ksl = slice(kt * 128, (kt + 1) * 128)
                    KT = kS if br == 0 else kW
                    VT = vS if br == 0 else vW
                    Ub = st["Us"] if br == 0 else st["Uw"]
                    dl = qt - kt
                    near = dl < (2 if br == 0 else 3)
                    box = {}

                    def pre():
                        for h_ in hooks_pre:
                            h_()

                    def s1():
                        L, Lb = next_L(hold=True)
                        box["L"] = (L, Lb)
                        mm(L[:, 0:256], KT[:, g, ksl], qT[:, 2 * g:2 * g + 2, qsl], True, False, [KT.b, qT.b], [Lb])
                        mm(L[:, 256:512], hT[:, (0 if br == 0 else 4) + g, ksl], qT[:, 2 * g:2 * g + 2, qsl], False, False, [kpad1, qT.b], [Lb])
                        if br == 0:
                            mm(L, expd[:, 1 if near else 0, kt, :], NM[:, g, buf, :], False, not near, [expd.b, NM.b], [Lb])
                        if near:
                            for hl in range(2):
                                mm(L, Jb[:, 0, :], hbt[:, dl, g, hl, :], False, hl == 1, [Jb.b, hbt.b], [Lb])

                    def s2():
                        L, Lb = box["L"]
                        release_L(Lb)
                        Et = next_E()
                        act(Et[:], L, AF.Exp, [Lb], [Et.b])
                        for r in range(4):
                            mm(bk(Ub)[:, r * 66:(r + 1) * 66], Et[:, colb(r):colb(r) + 128], VT[:, kt, g, 0:66],
                               first and r == 0, last and r == 3, [Et.b, VT.b], [bb(Ub)])

                    def post():
                        for h_ in hooks_post:
                            h_()

                    return dict(pre=pre, s1=s1, s2=s2, post=post, defer=defer, nm=None, first_slc=False)

                def mk_nm_hook(g, st, buf):
                    def hook():
                        imp = st["imp"]
                        M_, Mb = next_L()
                        trp(M_[0:32, 0:128], imp[:, :], ident_f[:, :], [imp.b, ident_f.b], [Mb])
                        cp("vector", NM[0:32, g, buf, :].rearrange("p (a s) -> p a s", a=4), fap(M_[0:32, 0:1], [[0, 4], [1, 128]]), [Mb], [NM.b])
                    return hook

                def mk_combine(qt, g, st, ybq):
                    def hook():
                        sm, ybacc, tmp1, tmp2 = st["sm"], st["ybacc"], st["tmp1"], st["tmp2"]
                        for br in range(2):
                            ub = st["Us"] if br == 0 else st["Uw"]
                            U = bk(ub)
                            rsb, wb_ = sm[:, 0, :], sm[:, 1 + br, :]
                            S.op("vector", lambda e, U=U, rsb=rsb: e.reciprocal(out=rsb, in_=fap(U[:, 64:65], [[66, 4]])), [bb(ub)], [sm.b])
                            tt("vector", wb_, rsb, sgate[:, qt, 16 * (br + 1) + 4 * g:16 * (br + 1) + 4 * g + 4], ALU.mult, [sm.b, sgate.b], [sm.b])
                            tgt = tmp1 if br == 0 else tmp2
                            tt("vector", tgt[:], fap(U[:, 0:1], [[66, 4], [1, 64]]), fap(wb_, [[1, 4], [0, 64]]), ALU.mult, [bb(ub), sm.b], [tgt.b])
                        tt("gpsimd", tmp1[:], tmp1[:], ybacc[:], ALU.add, [tmp1.b, ybacc.b], [tmp1.b])
                        tt("gpsimd", ybq[:, g * 256:(g + 1) * 256].rearrange("p (r d) -> p r d", r=4), tmp1[:], tmp2[:], ALU.add, [tmp1.b, tmp2.b], [ybq.b])
                    return hook

                def mk_ybout(qt, ybq):
                    def hook():
                        qsl = slice(qt * 128, (qt + 1) * 128)
                        li = next_Li()
                        pv = bank_bf(li)
                        for c in range(8):
                            trp(pv[:, c * 128:(c + 1) * 128], ybq[:, c * 128:(c + 1) * 128], ident_b[:], [ybq.b, ident_b.b], [bb(li)])
                        yT = ybT[qt % 2]
                        cp("scalar", yT[:], pv.rearrange("p (k s) -> p k s", k=8), [bb(li)], [yT.b])
                        dma("sync", ysc[1, :, qsl].rearrange("(c p) s -> p c s", p=128), yT[:], r=[yT.b], w=[b_ysc[1]])
                    return hook

                un = 0
                units = []
                for qt in range(1 if stop == 'nsa3' else NT):
                    ybq = ybt[qt % 2]
                    for g in range(4):
                        st = sets[un % 2]
                        buf = un % 2
                        un += 1
                        uc_ = [mk_cmp(qt, g, st)]
                        uc_[0]["nm"] = mk_nm_hook(g, st, buf)
                        uc_[0]["unit"] = len(units)
                        wk = list(range(max(0, qt - 2), qt + 1))
                        uw_ = [mk_tile(qt, g, st, 1, kt, kt == wk[0], kt == qt, buf, [], [], None) for kt in wk]
                        us_ = []
                        for kt in range(qt + 1):
                            hp = []
                            hq = [mk_combine(qt, g, st, ybq)] if kt == qt else []
                            df = mk_ybout(qt, ybq) if (kt == qt and g == 3) else None
                            us_.append(mk_tile(qt, g, st, 0, kt, kt == 0, kt == qt, buf, hp, hq, df))
                        us_[0]["first_slc"] = True
                        us_[0]["unit"] = len(units)
                        units.append((uc_, uw_, us_))
                tasks += units[0][0] + units[0][1]
                for ui in range(len(units)):
                    if ui + 1 < len(units):
                        tasks += units[ui + 1][0]
                    tasks += units[ui][2]
                    if ui + 1 < len(units):
                        tasks += units[ui + 1][1]
                for ui_ in range(min(PF, len(cb_dma))):
                    cb_dma[ui_]()
                deferred = {}
                ntk = len(tasks)
                LA = 3
                for j in range(min(LA, ntk)):
                    tasks[j]["pre"]()
                    tasks[j]["s1"]()
                first_idx = {tk["unit"]: i for i, tk in enumerate(tasks) if tk["first_slc"]}
                for i, tk in enumerate(tasks):
                    tk["s2"]()
                    tk["post"]()
                    if tk["nm"] is not None:
                        j = max(i, min(i + 8, first_idx[tk["unit"]] - LA))
                        deferred.setdefault(j, []).append(tk["nm"])
                    if tk["defer"] is not None:
                        deferred.setdefault(i + 3, []).append(tk["defer"])
                    for fn in deferred.pop(i, []):
                        fn()
                    if i + LA < ntk:
                        tasks[i + LA]["pre"]()
                        tasks[i + LA]["s1"]()
                for k_ in sorted(deferred):
                    for fn in deferred[k_]:
                        fn()
                dma("sync", hT.t.rearrange("p k s -> p (k s)"), hsc, r=[b_hsc], w=hT_b + [kpad1])
                S.barrier()

        def phase_tail(l, x_src, x_dst, b_xsrc, b_xdst):
            bk = lambda i: pbank[i][0]
            bb = lambda i: pbank[i][1]
            w2d = w_in[l]
            with ExitStack() as ph:
                mrgb = T(ph, "mrgb", [128, 8, SEQ], BF16)
                with ExitStack() as ph2:
                    mrg = T(ph2, "mrg", [128, 8, SEQ], F32)
                    yT = T(ph2, "m_yT", [128, 8, SEQ], BF16)
                    yb_ = [Buf(f"m_yT{c}") for c in range(8)]
                    wbr = [T(ph2, f"m_wbr{i}", [128, 8, 256], BF16) for i in range(2)]
                    wmg = [T(ph2, f"m_wmg{i}", [128, 8, 256], BF16) for i in range(2)]
                    sig = [T(ph2, f"m_sig{i}", [128, 512], F32) for i in range(2)]
                    prod = [T(ph2, f"m_prod{i}", [128, 512], F32) for i in range(2)]
                    n = 0
                    bn = 0
                    for br in range(3):
                        for c in range(8):
                            dma("sync", yT[:, c, :], ysc[br, c * 128:(c + 1) * 128, :], r=[b_ysc[br]], w=[yb_[c]])
                        for oc2 in range(4):
                            wb, wm = wbr[oc2 % 2], wmg[oc2 % 2]
                            load_slab(wb[:], w_branch[l, br], oc2 * 256, 256, wb.b)
                            load_slab(wm[:], w2d, C_MG + br * 1024 + oc2 * 256, 256, wm.b)
                            for o in range(2):
                                oc = oc2 * 2 + o
                                for sc in range(4):
                                    ssl = slice(sc * 512, (sc + 1) * 512)
                                    bB, bG = (bn % 4) * 2, (bn % 4) * 2 + 1
                                    bn += 1
                                    for c in range(8):
                                        mm(bk(bB), wb[:, c, o * 128:(o + 1) * 128], yT[:, c, ssl], c == 0, c == 7, [wb.b, yb_[c]], [bb(bB)])
                                    for kc in range(8):
                                        mm(bk(bG), wm[:, kc, o * 128:(o + 1) * 128], hT[:, kc, ssl], kc == 0, kc == 7, [wm.b] + hT_b[sc * 4:(sc + 1) * 4], [bb(bG)])
                                    sg_, pr_ = sig[n % 2], prod[n % 2]
                                    n += 1
                                    act(sg_[:], bk(bG), AF.Sigmoid, [bb(bG)], [sg_.b])
                                    if br == 0:
                                        tt("vector", mrg[:, oc, ssl], sg_[:], bk(bB), ALU.mult, [sg_.b, bb(bB)], [mrg.b])
                                    elif br == 1:
                                        tt("vector", pr_[:], sg_[:], bk(bB), ALU.mult, [sg_.b, bb(bB)], [pr_.b])
                                        tt("vector", mrg[:, oc, ssl], mrg[:, oc, ssl], pr_[:], ALU.add, [mrg.b, pr_.b], [mrg.b])
                                    else:
                                        tt("vector", pr_[:], sg_[:], bk(bB), ALU.mult, [sg_.b, bb(bB)], [pr_.b])
                                        tt("vector", mrgb[:, oc, ssl], mrg[:, oc, ssl], pr_[:], ALU.add, [mrg.b, pr_.b], [mrgb.b])
                    S.barrier()
                with ExitStack() as ph2:
                    wout = T(ph2, "p_wout", [128, 8, D], BF16)
                    g1 = load_gain(ph2, l, 1)
                    g2 = load_gain(ph2, l, 2)
                    xt = [T(ph2, f"p_xt{i}", [128, D], F32) for i in range(2)]
                    on = [T(ph2, f"p_on{i}", [128, D], F32) for i in range(2)]
                    hb = [T(ph2, f"p_hb{i}", [128, D], BF16) for i in range(2)]
                    junk = T(ph2, "p_junk", [128, D], BF16)
                    ss = T(ph2, "p_ss", [128, NT], F32)
                    rs = T(ph2, "p_rs", [128, NT], F32)
                    ss2 = T(ph2, "p_ss2", [128, NT], F32)
                    rs2 = T(ph2, "p_rs2", [128, NT], F32)
                    for half in range(2):
                        load_slab(wout[:, :, half * 512:(half + 1) * 512], w_out[l], half * 512, 512, wout.b)
                    junk2 = T(ph2, "p_junk2", [128, D], BF16)
                    ssb = T(ph2, "p_ssb", [128, NT], F32)
                    rsb_ = T(ph2, "p_rsb", [128, NT], F32)
                    ss2b = T(ph2, "p_ss2b", [128, NT], F32)
                    rs2b = T(ph2, "p_rs2b", [128, NT], F32)

                    def tileP(t):
                        p = t % 2
                        jk, s_a, r_a, s_b, r_b = (junk, ss, rs, ss2, rs2) if p == 0 else (junk2, ssb, rsb_, ss2b, rs2b)
                        tsl = slice(t * 128, (t + 1) * 128)
                        b0 = p * 2
                        ov = PA[:, b0 * 512:(b0 + 2) * 512]
                        obufs = [bb(b0), bb(b0 + 1)]
                        for half in range(2):
                            for c in range(8):
                                mm(bk(b0 + half), mrgb[:, c, tsl], wout[:, c, half * 512:(half + 1) * 512], c == 0, c == 7, [mrgb.b, wout.b], [bb(b0 + half)])
                        x_, o_ = xt[p], on[p]
                        dma("sync", x_[:], x_src[tsl, :], r=[b_xsrc], w=[x_.b])
                        act(jk[:], ov, AF.Square, obufs, [jk.b, s_a.b], accum_out=s_a[:, t:t + 1])
                        act(r_a[:, t:t + 1], s_a[:, t:t + 1], AF.Sqrt, [s_a.b], [r_a.b], scale=1.0 / D, bias=EPS)
                        recip(r_a[:, t:t + 1], r_a[:, t:t + 1], [r_a.b], [r_a.b])
                        stt(o_[:], ov, r_a[:, t:t + 1], g1[:], ALU.mult, ALU.mult, obufs + [r_a.b, g1.b], [o_.b])
                        tt("gpsimd", o_[:], o_[:], x_[:], ALU.add, [o_.b, x_.b], [o_.b])
                        dma("sync", xmid[tsl, :], o_[:], r=[o_.b], w=[b_xmid])
                        norm_transpose_tile(ph2, t, o_[:], o_.b, g2, s_b, r_b, hb[p], jk, 4 + p)

                    for t in range(0, NT, 2):
                        S.replay([S.record(lambda: tileP(t)), S.record(lambda: tileP(t + 1))])
                    S.barrier()
            with ExitStack() as ph:
                hid = T(ph, "f_hid", [128, 22, SEQ], BF16)
                hid_b = [Buf(f"hid{j}") for j in range(22)]
                wfo = T(ph, "f_wfo", [128, 22, D], BF16)
                wfo_b = [Buf(f"wfo{j}") for j in range(11)]
                wfi = [T(ph, f"f_wfi{i}", [128, 8, 2, 128], BF16) for i in range(2)]
                sgt = [T(ph, f"f_sg{i}", [128, 512], F32) for i in range(2)]
                g3 = load_gain(ph, l, 3)
                xt = [T(ph, f"f_xt{i}", [128, D], F32) for i in range(2)]
                on = [T(ph, f"f_on{i}", [128, D], F32) for i in range(2)]
                junk = T(ph, "f_junk", [128, D], BF16)
                ss = T(ph, "f_ss", [128, NT], F32)
                rs = T(ph, "f_rs", [128, NT], F32)
                n = 0
                bn = 0
                for j in range(22):
                    wf = wfi[j % 2]
                    dma("gpsimd", wf[:, :, 0, :], w_ffn_in[l][:, j * 128:(j + 1) * 128].rearrange("(kc p) n -> p kc n", p=128), w=[wf.b])
                    dma("gpsimd", wf[:, :, 1, :], w_ffn_in[l][:, DFF + j * 128:DFF + (j + 1) * 128].rearrange("(kc p) n -> p kc n", p=128), w=[wf.b])
                    if j % 2 == 0:
                        jj = j // 2
                        dma("gpsimd", wfo[:, 2 * jj:2 * jj + 2, :], w_ffn_out[l][jj * 256:(jj + 1) * 256, :].rearrange("(j p) n -> p j n", p=128), w=[wfo_b[jj]])
                    for sc in range(4):
                        ssl = slice(sc * 512, (sc + 1) * 512)
                        bG, bU = (bn % 4) * 2, (bn % 4) * 2 + 1
                        bn += 1
                        for kc in range(8):
                            mm(bk(bG), wf[:, kc, 0, :], hT[:, kc, ssl], kc == 0, kc == 7, [wf.b] + hT_b[sc * 4:(sc + 1) * 4], [bb(bG)])
                        for kc in range(8):
                            mm(bk(bU), wf[:, kc, 1, :], hT[:, kc, ssl], kc == 0, kc == 7, [wf.b] + hT_b[sc * 4:(sc + 1) * 4], [bb(bU)])
                        sg_ = sgt[n % 2]
                        n += 1
                        act(sg_[:], bk(bG), AF.Silu, [bb(bG)], [sg_.b])
                        tt("vector", hid[:, j, ssl], sg_[:], bk(bU), ALU.mult, [sg_.b, bb(bU)], [hid_b[j]])
                for t in range(NT):
                    tsl = slice(t * 128, (t + 1) * 128)
                    b0 = (t % 2) * 2
                    ov = PA[:, b0 * 512:(b0 + 2) * 512]
                    obufs = [bb(b0), bb(b0 + 1)]
                    for half in range(2):
                        for j in range(22):
                            mm(bk(b0 + half), hid[:, j, tsl], wfo[:, j, half * 512:(half + 1) * 512], j == 0, j == 21, [hid_b[j], wfo_b[j // 2]], [bb(b0 + half)])
                    x_, o_ = xt[t % 2], on[t % 2]
                    dma("sync", x_[:], xmid[tsl, :], r=[b_xmid], w=[x_.b])
                    act(junk[:], ov, AF.Square, obufs, [junk.b, ss.b], accum_out=ss[:, t:t + 1])
                    act(rs[:, t:t + 1], ss[:, t:t + 1], AF.Sqrt, [ss.b], [rs.b], scale=1.0 / D, bias=EPS)
                    recip(rs[:, t:t + 1], rs[:, t:t + 1], [rs.b], [rs.b])
                    stt(o_[:], ov, rs[:, t:t + 1], g3[:], ALU.mult, ALU.mult, obufs + [rs.b, g3.b], [o_.b])
                    tt("gpsimd", o_[:], o_[:], x_[:], ALU.add, [o_.b, x_.b], [o_.b])
                    dma("sync", x_dst[tsl, :], o_[:], r=[o_.b], w=[b_xdst])
                S.barrier()

        setup_tables()
        for l in range(n_layers):
            phase_A(l, x_in if l == 0 else xres)
            import os
            if not os.environ.get("SKIP_LG"):
                phase_lru(l)
                if stop == "lru":
                    break
                phase_gla(l)
                if stop == "gla":
                    break
            if not os.environ.get("SKIP_NSA"):
                phase_nsa(l)
            if stop is not None and stop.startswith("nsa"):
                break
            last = (l == n_layers - 1)
            phase_tail(l, x_in if l == 0 else xres, y_out if last else xres, Buf() if l == 0 else b_xres, Buf() if last else b_xres)
        S.barrier()
        S.emit()
    return nc, consts


_CACHE = {}


def kernel(**inputs):
    if "nc" not in _CACHE:
        _CACHE["nc"] = build()
    nc, consts = _CACHE["nc"]
    x = np.ascontiguousarray(np.asarray(inputs["x"], dtype=np.float32))
    shared = {k: np.ascontiguousarray(np.asarray(v, dtype=np.float32)) for k, v in inputs.items() if k != "x"}
    for k, v in consts.items():
        shared["c_" + k] = v
    in_maps = [dict(shared, x=x[i]) for i in range(8)]
    res = run_bass_kernel_spmd(nc, in_maps, core_ids=list(range(8)))
    return np.stack([np.asarray(r["out"], dtype=np.float32) for r in res.results], axis=0)
```

```python
import os
import numpy as np
from contextlib import ExitStack
import concourse.bass as bass
import concourse.mybir as mybir
from concourse.bass_utils import run_bass_kernel_spmd

F32 = mybir.dt.float32
BF16 = mybir.dt.bfloat16
ALU = mybir.AluOpType
AF = mybir.ActivationFunctionType
AX = mybir.AxisListType

SEQ = 2048
D = 1024
NT = 16
DEPTH = 2
EPS = 1e-6
IN_W = 10816
C_LRUX, C_LRUG, C_Q, C_KV, C_GATE, C_GQ, C_GK, C_GV, C_GOG, C_GLR, C_MG = 0, 1024, 2048, 3072, 4608, 4656, 5168, 5680, 6704, 7728, 7744
DFF = 2816
NEG = -30000.0


class Buf:
    __slots__ = ("name", "w", "r", "excl")

    def __init__(self, name="", excl=False):
        self.name = name
        self.w = None
        self.r = []
        self.excl = excl


class Sched:
    ENG = ("sync", "scalar", "vector", "gpsimd", "tensor")
    DMAQ = ("sync", "gpsimd", "scalar")

    def __init__(self, nc, es, n_dma_sems=12):
        self.nc = nc
        self.q = {e: [] for e in self.ENG}
        self.cnt = {e: 0 for e in self.ENG}
        self.sems = []
        self.esem = {}
        for e in self.ENG:
            self.esem[e] = len(self.sems)
            self.sems.append(es.enter_context(nc.semaphore("s_" + e)))
        self.known = {e: {} for e in self.ENG}
        self.dpool = {}
        self.dcnt = {}
        self.dlast = {}
        for qn in self.DMAQ:
            self.dpool[qn] = []
            for i in range(n_dma_sems):
                self.dpool[qn].append(len(self.sems))
                self.sems.append(es.enter_context(nc.semaphore(f"d_{qn}_{i}")))
            self.dcnt[qn] = 0
        self.K = n_dma_sems

    def _waits(self, eng, r, w):
        waits = {}
        kn = self.known[eng]
        own_pe = self.esem["tensor"] if eng == "tensor" else -1

        def need(kv):
            k, v = kv
            if k == own_pe:
                return
            if kn.get(k, 0) < v and waits.get(k, 0) < v:
                waits[k] = v

        own = self.esem.get(eng, -2)
        for b in r:
            if b.w is not None:
                need(b.w)
            if b.excl:
                for x in b.r:
                    if x[0] != own:
                        need(x)
        for b in w:
            if b.w is not None:
                need(b.w)
            for x in b.r:
                need(x)
        for k, v in waits.items():
            kn[k] = v
        return list(waits.items())

    _rec = None

    def record(self, fn):
        self._rec = []
        fn()
        r, self._rec = self._rec, None
        return r

    def replay(self, lists):
        idx = [0] * len(lists)
        live = True
        while live:
            live = False
            for j, lst in enumerate(lists):
                if idx[j] < len(lst):
                    kind, args, kw = lst[idx[j]]
                    idx[j] += 1
                    live = True
                    if kind == "op":
                        self.op(*args)
                    else:
                        self.dma(*args, **kw)

    def op(self, eng, fn, r=(), w=()):
        if self._rec is not None:
            self._rec.append(("op", (eng, fn, list(r), list(w)), {}))
            return
        waits = self._waits(eng, r, w)
        self.cnt[eng] += 1
        seq = self.cnt[eng]
        k = self.esem[eng]
        self.q[eng].append((waits, fn, (k, 1)))
        for b in w:
            b.w = (k, seq)
            b.r = []
        for b in r:
            if b not in w:
                b.r.append((k, seq))
                if len(b.r) > 24:
                    b.r = b.r[-24:] if False else self._compact(b.r)

    @staticmethod
    def _compact(lst):
        d = {}
        for k, v in lst:
            if d.get(k, 0) < v:
                d[k] = v
        return list(d.items())

    def dma(self, qn, out, in_, r=(), w=(), **kw):
        if self._rec is not None:
            self._rec.append(("dma", (qn, out, in_, list(r), list(w)), kw))
            return
        waits = self._waits(qn, r, w)
        i = self.dcnt[qn]
        self.dcnt[qn] += 1
        k = self.dpool[qn][i % self.K]
        val = 16 * (i // self.K + 1)
        if val > 16 and self.known[qn].get(k, 0) < val - 16:
            waits.append((k, val - 16))
            self.known[qn][k] = val - 16
        self.dlast[k] = val
        self.q[qn].append((waits, lambda e: e.dma_start(out=out, in_=in_, **kw), (k, 16)))
        for b in w:
            b.w = (k, val)
            b.r = []
        for b in r:
            if b not in w:
                b.r.append((k, val))
                if len(b.r) > 24:
                    b.r = self._compact(b.r)

    def pe_drain(self):
        k = self.esem["tensor"]
        if self.cnt["tensor"] > 0:
            self.q["tensor"].append(([(k, self.cnt["tensor"])], None, None))

    def barrier(self):
        tgt = [(self.esem[e], self.cnt[e]) for e in self.ENG if self.cnt[e] > 0]
        tgt += list(self.dlast.items())
        for e in self.ENG:
            waits = []
            for k, v in tgt:
                if e == "tensor" and k == self.esem["tensor"]:
                    continue
                if self.known[e].get(k, 0) < v:
                    waits.append((k, v))
                    self.known[e][k] = v
            if waits:
                self.q[e].append((waits, None, None))

    def emit(self):
        nc = self.nc
        with nc.Block() as block:
            for e in self.ENG:
                def body(eng, _e=e):
                    for waits, fn, inc in self.q[_e]:
                        for k, v in waits:
                            eng.wait_ge(self.sems[k], v)
                        if fn is not None:
                            ins = fn(eng)
                            ins.then_inc(self.sems[inc[0]], inc[1])
                getattr(block, e)(body)


def fap(a, dims):
    return bass.AP(a.tensor, a.offset, [list(a.ap[0])] + [list(d) for d in dims])


def _rel_bucket(d):
    d = np.asarray(d)
    n = np.maximum(d, 0)
    nf = np.maximum(n, 16).astype(np.float32)
    large = 16 + (np.log(nf / np.float32(16)) / np.float32(np.log(128 / 16)) * np.float32(16)).astype(np.int32)
    large = np.minimum(large, 31)
    return np.where(n < 16, n, large)


def host_consts():
    c = {}
    c["ident"] = np.eye(128, dtype=np.float32)
    c["antiid"] = np.eye(128, dtype=np.float32)[::-1].copy()
    aid127 = np.zeros((128, 128), np.float32)
    for i in range(127):
        aid127[i, 126 - i] = 1.0
    aid127[127, 127] = 1.0
    c["antiid127"] = aid127
    s = np.arange(128)
    c["triu"] = (s[:, None] <= s[None, :]).astype(np.float32)
    c["tril"] = (s[:, None] > s[None, :]).astype(np.float32)
    def oh(deltas, valid):
        m = np.zeros((33, len(deltas)), np.float32)
        b = _rel_bucket(deltas)
        for i, (dd, v) in enumerate(zip(deltas, valid)):
            if v:
                m[b[i], i] = 1.0
            else:
                m[32, i] = 1.0
        return m
    dc = np.arange(-2048, 2048)
    c["oh_c"] = oh(dc, dc >= 0)
    ds = np.arange(-512, 512)
    c["oh_s"] = oh(ds, ds >= 0)
    c["oh_w"] = oh(ds, (ds >= 0) & (ds < 256))
    cs = np.arange(127) * 16
    js = np.arange(32) * 64
    ov = np.clip(np.minimum(cs[:, None] + 32, js[None, :] + 64) - np.maximum(cs[:, None], js[None, :]), 0, None).astype(np.float32) / 32.0
    ovx = np.zeros((128, 33), np.float32)
    ovx[:127, :32] = ov
    ovx[:127, 32] = 1.0
    c["ovx"] = ovx
    pos = np.arange(SEQ)
    cur = pos // 64
    blk = np.arange(32)[None, :]
    cand = (blk >= 1) & (blk <= cur[:, None] - 2)
    forced = (blk == 0) | (blk == cur[:, None]) | (blk == cur[:, None] - 1)
    c["cand"] = cand.astype(np.float32).reshape(NT, 128, 32).transpose(1, 0, 2).copy()
    c["negc"] = ((cand.astype(np.float32) - 1.0) * 1e4).reshape(NT, 128, 32).transpose(1, 0, 2).copy()
    c["forced"] = forced.astype(np.float32).reshape(NT, 128, 32).transpose(1, 0, 2).copy()
    ex = np.zeros((128, NT, 128), np.float32)
    for kt in range(NT):
        for key in range(128):
            ex[2 * kt + key // 64, kt, key] = 1.0
    c["expand_near"] = ex.copy()
    ex[32:34] = 1.0
    c["expand"] = ex
    return c


CONST_SHAPES = None


def build(debug=False, n_layers=DEPTH, stop=None):
    nc = bass.Bass("TRN2", target_bir_lowering=False)
    consts = host_consts()
    din = {}

    def inp(name, shape, dt=F32):
        din[name] = nc.dram_tensor(name, list(shape), dt, kind="ExternalInput").ap()
        return din[name]

    x_in = inp("x", [SEQ, D])
    rel_table = inp("rel_table", [32, 16])
    norm_g = inp("norm_g", [DEPTH, 4, D])
    w_in = inp("w_in", [DEPTH, D, IN_W])
    conv_w = inp("conv_w", [DEPTH, 4, D])
    conv_b = inp("conv_b", [DEPTH, D])
    lru_wg = inp("lru_w_gates", [DEPTH, 2, 8, 128, 128])
    lru_bg = inp("lru_b_gates", [DEPTH, 2, D])
    lru_lam = inp("lru_lambda", [DEPTH, D])
    cmp_pos = inp("cmp_pos", [DEPTH, 2, 32, 64])
    cmp_w1 = inp("cmp_w1", [DEPTH, 2, 2048, 256])
    cmp_w2 = inp("cmp_w2", [DEPTH, 2, 256, 64])
    gla_wa2 = inp("gla_wa2", [DEPTH, 16, 512])
    gla_ba = inp("gla_ba", [DEPTH, 512])
    gla_norm = inp("gla_norm", [DEPTH, 256])
    w_branch = inp("w_branch", [DEPTH, 3, D, D])
    w_out = inp("w_out", [DEPTH, D, D])
    w_ffn_in = inp("w_ffn_in", [DEPTH, D, 2 * DFF])
    w_ffn_out = inp("w_ffn_out", [DEPTH, DFF, D])
    cin = {k: inp("c_" + k, v.shape) for k, v in consts.items()}

    okind = "ExternalOutput"
    y_out = nc.dram_tensor("out", [SEQ, D], F32, kind=okind).ap()
    skind = "ExternalOutput"
    xres = nc.dram_tensor("xres", [SEQ, D], F32, kind=skind).ap()
    xmid = nc.dram_tensor("xmid", [SEQ, D], F32, kind=skind).ap()
    ysc = nc.dram_tensor("ysc", [3, D, SEQ], BF16, kind=skind).ap()
    tc_d = nc.dram_tensor("tc_d", [2, 16, 4096], BF16, kind="Internal").ap()
    ts_d = nc.dram_tensor("ts_d", [2, 16, 1024], BF16, kind="Internal").ap()
    tw_d = nc.dram_tensor("tw_d", [2, 16, 1024], BF16, kind="Internal").ap()
    hsc = nc.dram_tensor("hsc", [128, 8 * SEQ], BF16, kind=skind).ap()
    b_hsc = Buf()
    b_xres, b_xmid, b_ysc, b_tabs = Buf(), Buf(), [Buf(), Buf(), Buf()], Buf()

    with ExitStack() as es:
        S = Sched(nc, es)
        es.enter_context(nc.allow_non_contiguous_dma(reason="small param loads"))

        def mm(out, lhsT, rhs, start, stop, r, w):
            S.op("tensor", lambda e: e.matmul(out, lhsT=lhsT, rhs=rhs, start=start, stop=stop), r, w)

        def trp(out, in_, ident, r, w):
            S.op("tensor", lambda e: e.transpose(out, in_, ident), r, w)

        def act(out, in_, func, r, w, **kw):
            S.op("scalar", lambda e: e.activation(out=out, in_=in_, func=func, **kw), r, w)

        def tt(eng, out, in0, in1, op, r, w):
            S.op(eng, lambda e: e.tensor_tensor(out=out, in0=in0, in1=in1, op=op), r, w)

        def tsc(eng, out, in0, s1, op0, r, w, s2=None, op1=None):
            if op1 is None:
                S.op(eng, lambda e: e.tensor_scalar(out=out, in0=in0, scalar1=s1, scalar2=None, op0=op0), r, w)
            else:
                S.op(eng, lambda e: e.tensor_scalar(out=out, in0=in0, scalar1=s1, scalar2=s2, op0=op0, op1=op1), r, w)

        def stt(out, in0, scalar, in1, op0, op1, r, w):
            S.op("vector", lambda e: e.scalar_tensor_tensor(out=out, in0=in0, scalar=scalar, in1=in1, op0=op0, op1=op1), r, w)

        def cp(eng, out, in_, r, w):
            if eng == "scalar":
                S.op("scalar", lambda e: e.copy(out=out, in_=in_), r, w)
            else:
                S.op(eng, lambda e: e.tensor_copy(out=out, in_=in_), r, w)

        def recip(out, in_, r, w):
            S.op("vector", lambda e: e.reciprocal(out=out, in_=in_), r, w)

        def memset(eng, ap, val, w):
            S.op(eng, lambda e: e.memset(ap, val), (), w)

        def dma(q, out, in_, r=(), w=()):
            S.dma(q, out, in_, r, w)

        class T:
            _n = [0]

            def __init__(self, stack, name, shape, dt, psum=False):
                T._n[0] += 1
                name = f"{name}_{T._n[0]}"
                self.t = stack.enter_context((nc.psum_tensor if psum else nc.sbuf_tensor)(name, list(shape), dt))
                self.b = Buf(name)

            def __getitem__(self, idx):
                return self.t[idx]

        PA = T(es, "PA", [128, 2048], F32, psum=True)
        PB = T(es, "PB", [128, 2048], F32, psum=True)
        pbank = []
        for i in range(8):
            src = PA if i < 4 else PB
            pbank.append((src.t[:, (i % 4) * 512:(i % 4 + 1) * 512], Buf(f"bank{i}", excl=True)))
        PAb = PA.t.bitcast(BF16)
        PBb = PB.t.bitcast(BF16)

        def bank_bf(i):
            src = PAb if i < 4 else PBb
            return src[:, (i % 4) * 1024:(i % 4 + 1) * 1024]

        ident_f = T(es, "ident_f", [128, 128], F32)
        ident_b = T(es, "ident_b", [128, 128], BF16)
        dma("sync", ident_f[:], cin["ident"], w=[ident_f.b])
        cp("vector", ident_b[:], ident_f[:], [ident_f.b], [ident_b.b])

        hT = T(es, "hT", [128, 8, SEQ], BF16)
        hT_b = [Buf(f"hT{t}") for t in range(NT)]

        def load_gain(ph, l, i):
            gt = T(ph, f"gain{i}", [128, D], F32)
            src = norm_g[l, i:i + 1, :]
            dma("sync", gt[:], bass.AP(src.tensor, src.offset, [[0, 128], [1, D]]), w=[gt.b])
            return gt

        def norm_transpose_tile(ph, t, xt_ap, xt_buf, gt, ss, rs, hb, junk, pbi):
            act(junk[:], xt_ap, AF.Square, [xt_buf], [junk.b, ss.b], accum_out=ss[:, t:t + 1])
            act(rs[:, t:t + 1], ss[:, t:t + 1], AF.Sqrt, [ss.b], [rs.b], scale=1.0 / D, bias=EPS)
            recip(rs[:, t:t + 1], rs[:, t:t + 1], [rs.b], [rs.b])
            stt(hb[:], xt_ap, rs[:, t:t + 1], gt[:], ALU.mult, ALU.mult, [xt_buf, rs.b, gt.b], [hb.b])
            pv, pbuf = bank_bf(pbi), pbank[pbi][1]
            for kc in range(8):
                trp(pv[:, kc * 128:(kc + 1) * 128], hb[:, kc * 128:(kc + 1) * 128], ident_b[:], [hb.b, ident_b.b], [pbuf])
            cp("scalar", hT[:, :, t * 128:(t + 1) * 128], pv.rearrange("p (k s) -> p k s", k=8), [pbuf], [hT_b[t]])

        def load_slab(dst_ap, w2d, c0, ncols, wbuf, nk=8):
            src = w2d[:, c0:c0 + ncols].rearrange("(kc p) n -> p kc n", p=128)
            dma("gpsimd", dst_ap, src, w=[wbuf])

        def proj_fm(wslab, wbuf, col_off, M, rhs_tile, rhs_bufs, out_banks, nk=8, sc_list=(0, 1, 2, 3)):
            for i, sc in enumerate(sc_list):
                pa, pb_ = out_banks[i]
                for kc in range(nk):
                    mm(pa[0:M, :], wslab[:, kc, col_off:col_off + M], rhs_tile[:, kc, sc * 512:(sc + 1) * 512],
                       kc == 0, kc == nk - 1, [wbuf] + rhs_bufs[sc * 4:(sc + 1) * 4], [pb_])

        def phase_A(l, x_src):
            with ExitStack() as ph:
                xt = [T(ph, f"xtA{i}", [128, D], F32) for i in range(2)]
                hb = [T(ph, f"hbA{i}", [128, D], BF16) for i in range(2)]
                junk = [T(ph, f"junkA{i}", [128, D], BF16) for i in range(2)]
                ss = [T(ph, f"ssA{i}", [128, NT], F32) for i in range(2)]
                rs = [T(ph, f"rsA{i}", [128, NT], F32) for i in range(2)]
                g0 = load_gain(ph, l, 0)

                def tileA(t):
                    p = t % 2
                    dma("sync", xt[p][:], x_src[t * 128:(t + 1) * 128, :], r=[b_xres], w=[xt[p].b])
                    norm_transpose_tile(ph, t, xt[p][:], xt[p].b, g0, ss[p], rs[p], hb[p], junk[p], p)

                for t in range(0, NT, 2):
                    S.replay([S.record(lambda: tileA(t)), S.record(lambda: tileA(t + 1))])
                S.barrier()

        def phase_lru(l):
            with ExitStack() as ph:
                prow = T(ph, "prow", [8, D], F32)
                lpT = T(ph, "lpT", [128, 8, 8], F32)
                sp = T(ph, "lru_sp", [128, 8, 6], F32)
                wg = T(ph, "lru_wg", [128, 2, 8, 128], BF16)
                slab = [T(ph, f"lslab{i}", [128, 8, 2, 128], BF16) for i in range(2)]
                XA = [T(ph, f"XA{i}", [128, SEQ + 4], F32) for i in range(2)]
                XC = [T(ph, f"XC{i}", [128, SEQ], F32) for i in range(2)]
                XCB = [T(ph, f"XCB{i}", [128, SEQ], BF16) for i in range(2)]
                R = [T(ph, f"R{i}", [128, SEQ], F32) for i in range(2)]
                A = [T(ph, f"A{i}", [128, SEQ], F32) for i in range(2)]
                I = [T(ph, f"I{i}", [128, SEQ], F32) for i in range(2)]
                H = [T(ph, f"H{i}", [128, SEQ], F32) for i in range(2)]
                GA = [T(ph, f"GA{i}", [128, SEQ], F32) for i in range(2)]
                G = [T(ph, f"G{i}", [128, SEQ], F32) for i in range(2)]
                YA = [T(ph, f"YA{i}", [128, SEQ], BF16) for i in range(2)]
                if os.environ.get("SBUF_DBG"):
                    print("LRU sbuf remaining", nc.sbuf_bytes_remaining)
                dma("sync", hsc, hT.t.rearrange("p k s -> p (k s)"), r=hT_b, w=[b_hsc])
                for k in range(4):
                    dma("sync", prow[k:k + 1, :], conv_w[l, k:k + 1, :], w=[prow.b])
                dma("sync", prow[4:5, :], conv_b[l:l + 1, :], w=[prow.b])
                dma("sync", prow[5:7, :], lru_bg[l], w=[prow.b])
                dma("sync", prow[7:8, :], lru_lam[l:l + 1, :], w=[prow.b])
                pv, pbuf = pbank[7]
                for c in range(8):
                    trp(pv[:, c * 8:(c + 1) * 8], prow[0:8, c * 128:(c + 1) * 128], ident_f[0:8, 0:8], [prow.b, ident_f.b], [pbuf])
                cp("vector", lpT[:], pv[:, 0:64].rearrange("p (c k) -> p c k", c=8), [pbuf], [lpT.b])
                xs, ln1, ser, msk, nsp8, nsp16 = (sp[:, :, i] for i in range(6))
                act(xs, lpT[:, :, 7], AF.Exp, [lpT.b], [sp.b], scale=-1.0)
                act(ln1, xs, AF.Ln, [sp.b], [sp.b], bias=1.0)
                tsc("vector", ser, xs, -0.25, ALU.mult, [sp.b], [sp.b], 1.0 / 3.0, ALU.add)
                tt("vector", ser, ser, xs, ALU.mult, [sp.b], [sp.b])
                tsc("vector", ser, ser, -1.0, ALU.mult, [sp.b], [sp.b], 0.5, ALU.add)
                tt("vector", ser, ser, xs, ALU.mult, [sp.b], [sp.b])
                tsc("vector", ser, ser, -1.0, ALU.mult, [sp.b], [sp.b], 1.0, ALU.add)
                tt("vector", ser, ser, xs, ALU.mult, [sp.b], [sp.b])
                tsc("vector", msk, xs, 0.03, ALU.is_lt, [sp.b], [sp.b])
                tt("vector", ser, ser, ln1, ALU.subtract, [sp.b], [sp.b])
                tt("vector", ser, ser, msk, ALU.mult, [sp.b], [sp.b])
                tt("vector", ser, ser, ln1, ALU.add, [sp.b], [sp.b])
                tsc("vector", nsp8, ser, -8.0, ALU.mult, [sp.b], [sp.b])
                tsc("vector", nsp16, ser, -16.0, ALU.mult, [sp.b], [sp.b])
                dma("gpsimd", wg[:], lru_wg[l].rearrange("k n c e -> c k n e"), w=[wg.b])
                for p_ in range(2):
                    memset("vector", XA[p_][:, 0:3], 0.0, [XA[p_].b])
                w2d = w_in[l]

                def lru_A(c):
                    p = c % 2
                    xa, xc, xcb, r_, i_, ga = XA[p], XC[p], XCB[p], R[p], I[p], GA[p]
                    sl = slab[p]
                    dma("gpsimd", sl[:, :, 0, :], w2d[:, C_LRUX + c * 128:C_LRUX + (c + 1) * 128].rearrange("(kc p) n -> p kc n", p=128), w=[sl.b])
                    dma("gpsimd", sl[:, :, 1, :], w2d[:, C_LRUG + c * 128:C_LRUG + (c + 1) * 128].rearrange("(kc p) n -> p kc n", p=128), w=[sl.b])
                    slv = sl.t.rearrange("p k a n -> p k (a n)")
                    proj_fm(slv, sl.b, 0, 128, hT.t, hT_b, pbank[0:4])
                    proj_fm(slv, sl.b, 128, 128, hT.t, hT_b, pbank[4:8])
                    cp("scalar", xa[:, 3:3 + SEQ], PA[:, :], [pbank[i][1] for i in range(4)], [xa.b])
                    cp("scalar", ga[:], PB[:, :], [pbank[i][1] for i in range(4, 8)], [ga.b])
                    cw = lambda k: lpT[:, c, k:k + 1]
                    act(xc[:], xa[:, 3:3 + SEQ], AF.Identity, [xa.b, lpT.b], [xc.b], scale=cw(3), bias=cw(4))
                    for k in range(3):
                        stt(xc[:], xa[:, k:k + SEQ], cw(k), xc[:], ALU.mult, ALU.add, [xa.b, lpT.b, xc.b], [xc.b])
                    cp("gpsimd", xcb[:], xc[:], [xc.b], [xcb.b])
                    for gk in range(2):
                        banks = pbank[4:8] if gk == 0 else pbank[0:4]
                        for sc in range(4):
                            mm(banks[sc][0], wg[:, gk, c, :], xcb[:, sc * 512:(sc + 1) * 512], True, True, [wg.b, xcb.b], [banks[sc][1]])
                    act(r_[:], PB[:, :], AF.Sigmoid, [pbank[i][1] for i in range(4, 8)], [r_.b], bias=lpT[:, c, 5:6])
                    act(i_[:], PA[:, :], AF.Sigmoid, [pbank[i][1] for i in range(4)], [i_.b], bias=lpT[:, c, 6:7])

                def lru_B(c):
                    p = c % 2
                    xc, r_, a_, i_, h_, ga, g_ = XC[p], R[p], A[p], I[p], H[p], GA[p], G[p]
                    act(a_[:], r_[:], AF.Exp, [r_.b, sp.b], [a_.b], scale=sp[:, c, 4:5])
                    act(r_[:], r_[:], AF.Exp, [r_.b, sp.b], [r_.b], scale=sp[:, c, 5:6])
                    act(r_[:], r_[:], AF.Sqrt, [r_.b], [r_.b], scale=-1.0, bias=1.0)
                    tt("gpsimd", i_[:], i_[:], xc[:], ALU.mult, [i_.b, xc.b], [i_.b])
                    tt("gpsimd", i_[:], i_[:], r_[:], ALU.mult, [i_.b, r_.b], [i_.b])
                    S.op("vector", lambda e, h_=h_, a_=a_, i_=i_: e.tensor_tensor_scan(out=h_[:], data0=a_[:], data1=i_[:], initial=0.0, op0=ALU.mult, op1=ALU.add),
                         [a_.b, i_.b], [h_.b])
                    act(g_[:], ga[:], AF.Square, [ga.b], [g_.b])
                    tsc("vector", g_[:], g_[:], 0.044715, ALU.mult, [g_.b], [g_.b], 1.0, ALU.add)
                    tt("gpsimd", g_[:], g_[:], ga[:], ALU.mult, [g_.b, ga.b], [g_.b])
                    act(g_[:], g_[:], AF.Sigmoid, [g_.b], [g_.b], scale=1.5957691216057308)
                    tt("gpsimd", g_[:], g_[:], ga[:], ALU.mult, [g_.b, ga.b], [g_.b])
                    ya = YA[p]
                    tt("vector", ya[:], g_[:], h_[:], ALU.mult, [g_.b, h_.b], [ya.b])
                    dma("sync", ysc[0, c * 128:(c + 1) * 128, :], ya[:], r=[ya.b], w=[b_ysc[0]])

                lru_A(0)
                for c in range(8):
                    lists = [S.record(lambda: lru_B(c))]
                    if c + 1 < 8:
                        lists.append(S.record(lambda: lru_A(c + 1)))
                    S.replay(lists)
                S.barrier()

        def phase_gla(l):
            w2d = w_in[l]
            with ExitStack() as ph:
                qT = T(ph, "gqT", [128, 4, SEQ], F32)
                kT = T(ph, "gkT", [128, 4, SEQ], F32)
                lrT = T(ph, "lrT", [32, SEQ], F32)
                wa2x = T(ph, "wa2x", [32, 512], F32)
                wres = T(ph, "gwres", [128, 8, 2560], BF16)
                gnb = T(ph, "gnb", [128, 4, 256], F32)
                st_f = T(ph, "st_f", [128, 4, 256], F32)
                st_b = T(ph, "st_b", [128, 4, 256], BF16)
                cm4 = T(ph, "cm4", [128, 4, 128], F32)
                triu = T(ph, "triu", [128, 128], F32)
                tril = T(ph, "tril", [128, 128], F32)
                dma("sync", triu[:], cin["triu"], w=[triu.b])
                dma("sync", tril[:], cin["tril"], w=[tril.b])
                for hh in range(4):
                    dma("sync", cm4[:, hh, :], cin["triu"], w=[cm4.b])
                    src = gla_norm[l:l + 1, :]
                    dma("sync", gnb[:, hh, :], bass.AP(src.tensor, src.offset, [[0, 128], [1, 256]]), w=[gnb.b])
                memset("vector", wa2x[:], 0.0, [wa2x.b])
                memset("vector", lrT[:], 1.0, [lrT.b])
                dma("sync", wa2x[0:16, :], gla_wa2[l], w=[wa2x.b])
                dma("sync", wa2x[16:17, :], gla_ba[l:l + 1, :], w=[wa2x.b])
                for i, c0 in enumerate((C_GK, C_GV, C_GV + 512, C_GOG, C_GOG + 512)):
                    load_slab(wres[:, :, i * 512:(i + 1) * 512], w2d, c0, 512, wres.b)
                with ExitStack() as ph2:
                    slab = [T(ph2, f"gslab{i}", [128, 8, 512], BF16) for i in range(2)]
                    lslab = T(ph2, "glslab", [128, 8, 16], BF16)
                    load_slab(slab[0][:], w2d, C_GQ, 512, slab[0].b)
                    load_slab(slab[1][:], w2d, C_GK, 512, slab[1].b)
                    load_slab(lslab[:], w2d, C_GLR, 16, lslab.b)
                    for i in range(8):
                        banks = pbank[0:4] if i % 2 == 0 else pbank[4:8]
                        src = PA if i % 2 == 0 else PB
                        proj_fm(slab[i // 4].t, slab[i // 4].b, (i % 4) * 128, 128, hT.t, hT_b, banks)
                        dst = qT if i < 4 else kT
                        act(dst[:, i % 4, :], src[:, :], AF.Copy, [b for _, b in banks], [dst.b], scale=(128 ** -0.5 if i < 4 else 1.0))
                    proj_fm(lslab.t, lslab.b, 0, 16, hT.t, hT_b, pbank[0:4])
                    cp("vector", lrT[0:16, :], PA[0:16, :], [b for _, b in pbank[0:4]], [lrT.b])
                    S.barrier()
                sp_t = T(ph, "g_sp", [128, 512], F32)
                E1 = [T(ph, f"g_E1{i}", [128, 512], F32) for i in range(2)]
                E2 = T(ph, "g_E2", [128, 512], F32)
                Erb = T(ph, "g_Erb", [128, 512], F32)
                qtb = [T(ph, f"g_qtb{i}", [128, 4, 128], BF16) for i in range(2)]
                ktb = T(ph, "g_ktb", [128, 4, 128], BF16)
                kend = [T(ph, f"g_kend{i}", [128, 512], BF16) for i in range(2)]
                v_bf = [T(ph, f"g_vbf{i}", [128, 1024], BF16) for i in range(2)]
                sg = [T(ph, f"g_sg{i}", [128, 1024], F32) for i in range(2)]
                attm = [T(ph, f"g_attm{i}", [128, 4, 128], BF16) for i in range(2)]
                on = T(ph, "g_on", [128, 1024], F32)
                yc = [T(ph, f"g_yc{i}", [128, 1024], BF16) for i in range(2)]
                ycT = [T(ph, f"g_ycT{i}", [128, 8, 128], BF16) for i in range(2)]
                junk = T(ph, "g_junk", [128, 256], BF16)
                ssq = T(ph, "g_ssq", [128, 4], F32)
                rst = T(ph, "g_rst", [128, 4], F32)
                if os.environ.get("SBUF_DBG"):
                    print("GLA sbuf remaining", nc.sbuf_bytes_remaining)
                bk = lambda i: pbank[i][0]
                bb = lambda i: pbank[i][1]

                def gla_A(t):
                    p = t % 2
                    tsl = slice(t * 128, (t + 1) * 128)
                    e1, qb, ke, vb, sg_, am = E1[p], qtb[p], kend[p], v_bf[p], sg[p], attm[p]
                    mm(bk(0), lrT[0:17, tsl], wa2x[0:17, :], True, True, [lrT.b, wa2x.b], [bb(0)])
                    act(sp_t[:], bk(0), AF.Exp, [bb(0)], [sp_t.b], scale=-1.0)
                    act(sp_t[:], sp_t[:], AF.Ln, [sp_t.b], [sp_t.b], bias=1.0)
                    for hh in range(4):
                        mm(bk(1)[:, hh * 128:(hh + 1) * 128], sp_t[:, hh * 128:(hh + 1) * 128], triu[:], True, True, [sp_t.b, triu.b], [bb(1)])
                    mm(bk(2), tril[:], sp_t[:], True, True, [tril.b, sp_t.b], [bb(2)])
                    act(e1[:], bk(1), AF.Exp, [bb(1)], [e1.b], scale=-1.0 / 16.0)
                    act(E2[:], bk(1), AF.Exp, [bb(1)], [E2.b], scale=1.0 / 16.0)
                    act(Erb[:], bk(2), AF.Exp, [bb(2)], [Erb.b], scale=-1.0 / 16.0)
                    tt("vector", qb[:], qT[:, :, tsl], e1.t.rearrange("p (h s) -> p h s", h=4), ALU.mult, [qT.b, e1.b], [qb.b])
                    tt("gpsimd", ktb[:], kT[:, :, tsl], E2.t.rearrange("p (h s) -> p h s", h=4), ALU.mult, [kT.b, E2.b], [ktb.b])
                    for kc in range(8):
                        mm(bk(3), hT[:, kc, tsl], wres[:, kc, 0:512], kc == 0, kc == 7, [hT_b[t], wres.b], [bb(3)])
                    tt("vector", ke[:], bk(3), Erb[:], ALU.mult, [bb(3), Erb.b], [ke.b])
                    for half in range(2):
                        for kc in range(8):
                            mm(bk(half), hT[:, kc, tsl], wres[:, kc, 512 + half * 512:1024 + half * 512], kc == 0, kc == 7, [hT_b[t], wres.b], [bb(half)])
                    cp("scalar", vb[:], PA[:, 0:1024], [bb(0), bb(1)], [vb.b])
                    for half in range(2):
                        for kc in range(8):
                            mm(bk(2 + half), hT[:, kc, tsl], wres[:, kc, 1536 + half * 512:2048 + half * 512], kc == 0, kc == 7, [hT_b[t], wres.b], [bb(2 + half)])
                    act(sg_[:], PA[:, 1024:2048], AF.Silu, [bb(2), bb(3)], [sg_.b])
                    tt("gpsimd", sg_[:], sg_[:], gnb.t.rearrange("p h e -> p (h e)"), ALU.mult, [sg_.b, gnb.b], [sg_.b])
                    for hh in range(4):
                        mm(bk(0)[:, hh * 128:(hh + 1) * 128], ktb[:, hh, :], qb[:, hh, :], True, True, [ktb.b, qb.b], [bb(0)])
                    tt("vector", am[:], bk(0).rearrange("p (h s) -> p h s", h=4), cm4[:], ALU.mult, [bb(0), cm4.b], [am.b])

                def gla_B(t):
                    p = t % 2
                    tsl = slice(t * 128, (t + 1) * 128)
                    e1, qb, ke, vb, sg_, am = E1[p], qtb[p], kend[p], v_bf[p], sg[p], attm[p]
                    for hh in range(4):
                        ob = 4 + hh // 2
                        oap = bk(ob)[:, (hh % 2) * 256:(hh % 2 + 1) * 256]
                        mm(oap, am[:, hh, :], vb[:, hh * 256:(hh + 1) * 256], hh % 2 == 0, t == 0 and hh % 2 == 1, [am.b, vb.b], [bb(ob)])
                        if t > 0:
                            mm(oap, qb[:, hh, :], st_b[:, hh, :], False, hh % 2 == 1, [qb.b, st_b.b], [bb(ob)])
                    for hh in range(4):
                        kb_ = 6 + hh // 2
                        mm(bk(kb_)[:, (hh % 2) * 256:(hh % 2 + 1) * 256], ke[:, hh * 128:(hh + 1) * 128], vb[:, hh * 256:(hh + 1) * 256],
                           hh % 2 == 0, hh % 2 == 1, [ke.b, vb.b], [bb(kb_)])
                    for hh in range(4):
                        kvp = bk(6 + hh // 2)[:, (hh % 2) * 256:(hh % 2 + 1) * 256]
                        if t == 0:
                            cp("vector", st_f[:, hh, :], kvp, [bb(6 + hh // 2)], [st_f.b])
                        else:
                            dec = e1[:, hh * 128 + 127:hh * 128 + 128]
                            stt(st_f[:, hh, :], st_f[:, hh, :], dec, kvp, ALU.mult, ALU.add, [st_f.b, e1.b, bb(6 + hh // 2)], [st_f.b])
                    cp("gpsimd", st_b[:], st_f[:], [st_f.b], [st_b.b])
                    for hh in range(4):
                        oap = bk(4 + hh // 2)[:, (hh % 2) * 256:(hh % 2 + 1) * 256]
                        act(junk[:], oap, AF.Square, [bb(4 + hh // 2)], [junk.b, ssq.b], accum_out=ssq[:, hh:hh + 1])
                    act(rst[:], ssq[:], AF.Sqrt, [ssq.b], [rst.b], scale=1.0 / 256.0, bias=EPS)
                    recip(rst[:], rst[:], [rst.b], [rst.b])
                    tt("vector", on.t.rearrange("p (h e) -> p h e", h=4), PB[:, 0:1024].rearrange("p (h e) -> p h e", h=4),
                       fap(rst[:], [[1, 4], [0, 256]]), ALU.mult, [bb(4), bb(5), rst.b], [on.b])
                    y = yc[p]
                    tt("gpsimd", y[:], on[:], sg_[:], ALU.mult, [on.b, sg_.b], [y.b])
                    pv = bank_bf(7)
                    for c in range(8):
                        trp(pv[:, c * 128:(c + 1) * 128], y[:, c * 128:(c + 1) * 128], ident_b[:], [y.b, ident_b.b], [bb(7)])
                    yT = ycT[p]
                    cp("scalar", yT[:], pv.rearrange("p (k s) -> p k s", k=8), [bb(7)], [yT.b])
                    dma("sync", ysc[2, :, tsl].rearrange("(c p) s -> p c s", p=128), yT[:], r=[yT.b], w=[b_ysc[2]])

                gla_A(0)
                for t in range(NT):
                    lists = [S.record(lambda: gla_B(t))]
                    if t + 1 < NT:
                        lists.append(S.record(lambda: gla_A(t + 1)))
                    S.replay(lists)
                S.barrier()

        def setup_tables():
            with ExitStack() as ph:
                tblx = T(ph, "tblx", [33, 16], F32)
                memset("vector", tblx[:], NEG, [tblx.b])
                dma("sync", tblx[0:32, :], rel_table, w=[tblx.b])
                for name, dst, n in (("oh_c", tc_d, 4096), ("oh_s", ts_d, 1024), ("oh_w", tw_d, 1024)):
                    oh = T(ph, "t_" + name, [33, n], F32)
                    thi = T(ph, "thi_" + name, [16, n], BF16)
                    tlo = T(ph, "tlo_" + name, [16, n], BF16)
                    dma("sync", oh[:], cin[name], w=[oh.b])
                    for ch in range(n // 512):
                        pa, pbuf = pbank[ch % 8]
                        mm(pa[0:16, :], tblx[0:33, 0:16], oh[0:33, ch * 512:(ch + 1) * 512], True, True, [tblx.b, oh.b], [pbuf])
                        cp("vector", thi[:, ch * 512:(ch + 1) * 512], pa[0:16, :], [pbuf], [thi.b])
                        tt("vector", tlo[:, ch * 512:(ch + 1) * 512], pa[0:16, :], thi[:, ch * 512:(ch + 1) * 512], ALU.subtract, [pbuf, thi.b], [tlo.b])
                    dma("sync", dst[0], thi[:], r=[thi.b], w=[b_tabs])
                    dma("sync", dst[1], tlo[:], r=[tlo.b], w=[b_tabs])
                S.barrier()

        def phase_nsa(l):
            w2d = w_in[l]
            bk = lambda i: pbank[i][0]
            bb = lambda i: pbank[i][1]
            with ExitStack() as ph:
                qT = T(ph, "nqT", [128, 8, SEQ], BF16)
                kS = T(ph, "nkS", [128, 4, SEQ], BF16)
                kW = T(ph, "nkW", [128, 4, SEQ], BF16)
                vS = T(ph, "nvS", [128, NT, 4, 66], BF16)
                vW = T(ph, "nvW", [128, NT, 4, 66], BF16)
                sgate = T(ph, "nsg", [128, NT, 48], F32)
                kcP = T(ph, "nkcP", [128, 2, 4, 128], BF16)
                vcx = T(ph, "nvcx", [128, 4, 98], BF16)
                hbt = T(ph, "nhbt", [128, 3, 4, 2, 512], BF16)
                NM = T(ph, "nNM", [128, 4, 2, 512], BF16)
                Jb = T(ph, "nJb", [128, 2, 128], BF16)
                expd = T(ph, "nexpd", [128, 2, NT, 128], BF16)
                cand = T(ph, "ncand", [128, NT, 32], F32)
                negc = T(ph, "nnegc", [128, NT, 32], F32)
                forced = T(ph, "nforced", [128, NT, 32], F32)
                dma("gpsimd", Jb[:, 0, :], cin["antiid"], w=[Jb.b])
                dma("gpsimd", Jb[:, 1, :], cin["antiid127"], w=[Jb.b])
                dma("gpsimd", expd[:, 0, :, :], cin["expand"], w=[expd.b])
                dma("gpsimd", expd[:, 1, :, :], cin["expand_near"], w=[expd.b])
                memset("vector", NM[:], 0.0, [NM.b])
                dma("sync", cand[:], cin["cand"], w=[cand.b])
                dma("sync", negc[:], cin["negc"], w=[negc.b])
                dma("sync", forced[:], cin["forced"], w=[forced.b])
                memset("vector", vcx[:], 0.0, [vcx.b])
                memset("vector", kcP[:], 0.0, [kcP.b])
                for g in range(4):
                    dma("gpsimd", vcx[:, g, 64:97], cin["ovx"], w=[vcx.b])
                for dl in range(3):
                    tsrc = tw_d if dl == 2 else ts_d
                    for g in range(4):
                        for hl in range(2):
                            for rp in range(2):
                                a0 = tsrc[hl, 4 * g + 2 * rp, 512 + dl * 128 - 127:512 + dl * 128 - 127 + 1]
                                src = bass.AP(a0.tensor, a0.offset, [[1, 128], [1024, 2], [1, 128]])
                                dst = hbt[:, dl, g, hl, :].rearrange("p (a b s) -> p a b s", a=2, b=2)[:, :, rp, :]
                                dma("sync", dst, src, r=[b_tabs], w=[hbt.b])
                for g in range(4):
                    for hl in range(2):
                        for par in range(2):
                            for rp in range(2):
                                h = 4 * g + 2 * rp + par
                                a0 = ts_d[hl, h, 640:641]
                                src = bass.AP(a0.tensor, a0.offset, [[0, 1], [0, 2], [1, 128]])
                                c0 = par * 256 + rp * 128
                                dma("sync", NM[32 + hl:33 + hl, g, :, c0:c0 + 128], src, r=[b_tabs], w=[NM.b])
                memset("vector", vS[:, :, :, 64:66], 1.0, [vS.b])
                memset("vector", vW[:, :, :, 64:66], 1.0, [vW.b])
                if stop == "nsa0":
                    S.barrier()
                    return
                with ExitStack() as ph2:
                    slab = [T(ph2, f"nslab{i}", [128, 8, 512], BF16) for i in range(2)]
                    wv = T(ph2, "nwv", [128, 8, 560], BF16)
                    for half in range(2):
                        sl = slab[half]
                        load_slab(sl[:], w2d, C_Q + half * 512, 512, sl.b)
                        for i in range(4):
                            c = half * 4 + i
                            banks = pbank[0:4] if c % 2 == 0 else pbank[4:8]
                            src = PA if c % 2 == 0 else PB
                            proj_fm(sl.t, sl.b, i * 128, 128, hT.t, hT_b, banks)
                            act(qT[:, c, :], src[:, :], AF.Copy, [b for _, b in banks], [qT.b], scale=0.125)
                    if stop == "nsa1a":
                        S.barrier()
                        return
                    n = 0
                    for idx, dst in ((2, kS), (4, kW)):
                        sl = slab[n % 2]
                        n += 1
                        for g in range(4):
                            c0 = C_KV + idx * 256 + g * 64
                            for dup in range(2):
                                dma("gpsimd", sl[:, :, g * 128 + dup * 64:g * 128 + dup * 64 + 64],
                                    w2d[:, c0:c0 + 64].rearrange("(kc p) n -> p kc n", p=128), w=[sl.b])
                        for g in range(4):
                            banks = pbank[0:4] if g % 2 == 0 else pbank[4:8]
                            src = PA if g % 2 == 0 else PB
                            proj_fm(sl.t, sl.b, g * 128, 128, hT.t, hT_b, banks)
                            cp("scalar" if g % 2 == 0 else "vector", dst[:, g, :], src[:, :], [b for _, b in banks], [dst.b])
                    if stop == "nsa1b":
                        S.barrier()
                        return
                    load_slab(wv[:, :, 0:256], w2d, C_KV + 3 * 256, 256, wv.b)
                    load_slab(wv[:, :, 256:512], w2d, C_KV + 5 * 256, 256, wv.b)
                    load_slab(wv[:, :, 512:560], w2d, C_GATE, 48, wv.b)
                    for t in range(NT):
                        tsl = slice(t * 128, (t + 1) * 128)
                        b0, b1 = (0, 1) if t % 2 == 0 else (2, 3)
                        for kc in range(8):
                            mm(bk(b0), hT[:, kc, tsl], wv[:, kc, 0:512], kc == 0, kc == 7, [hT_b[t], wv.b], [bb(b0)])
                        import os
                        SK = os.environ.get("NSA_SKIP", "")
                        if "g" not in SK:
                            for kc in range(8):
                                mm(bk(b1)[:, 0:48], hT[:, kc, tsl], wv[:, kc, 512:560], kc == 0, kc == 7, [hT_b[t], wv.b], [bb(b1)])
                        if "v" not in SK:
                            cp("vector", vS[:, t, :, 0:64], bk(b0)[:, 0:256].rearrange("p (g d) -> p g d", g=4), [bb(b0)], [vS.b])
                        if "w" not in SK:
                            cp("scalar", vW[:, t, :, 0:64], bk(b0)[:, 256:512].rearrange("p (g d) -> p g d", g=4), [bb(b0)], [vW.b])
                        if "g" not in SK:
                            act(sgate[:, t, :], bk(b1)[:, 0:48], AF.Sigmoid, [bb(b1)], [sgate.b])
                    S.barrier()
                if stop == "nsa1":
                    return
                with ExitStack() as ph2:
                    slab = [T(ph2, f"ncslab{i}", [128, 8, 256], BF16) for i in range(2)]
                    w1sb = T(ph2, "nw1", [128, 32, 256], BF16)
                    w2sb = T(ph2, "nw2", [128, 2, 128], BF16)
                    prow2 = T(ph2, "nprow2", [32, 128], F32)
                    posT = T(ph2, "nposT", [128, 32], F32)
                    XAB = [T(ph2, f"nXAB{i}", [128, SEQ], BF16) for i in range(2)]
                    gtmp = T(ph2, "ngtmp", [128, 2, 128], F32)
                    geluT = T(ph2, "ngeluT", [128, 2, 128], BF16)
                    for kv in range(2):
                        for dup in range(2):
                            dma("gpsimd", w1sb[dup * 64:(dup + 1) * 64, :, :], cmp_w1[l, kv].rearrange("(p d) j -> d p j", d=64), w=[w1sb.b])
                            dma("gpsimd", w2sb[:, :, dup * 64:(dup + 1) * 64], cmp_w2[l, kv].rearrange("(jc p) d -> p jc d", p=128), w=[w2sb.b])
                            dma("sync", prow2[:, dup * 64:(dup + 1) * 64], cmp_pos[l, kv], w=[prow2.b])
                        trp(bk(6)[:, 0:32], prow2[:, :], ident_f[0:32, 0:32], [prow2.b, ident_f.b], [bb(6)])
                        cp("vector", posT[:], bk(6)[:, 0:32], [bb(6)], [posT.b])
                        sl = slab[kv]
                        load_slab(sl[:, :, 0:256], w2d, C_KV + kv * 256, 256, sl.b)
                        for cc in range(2):
                            banks = pbank[0:4]
                            proj_fm(sl.t, sl.b, cc * 128, 128, hT.t, hT_b, banks)
                            for ab in range(2):
                                tt("vector" if ab == 0 else "gpsimd" if False else "vector", XAB[ab].t.rearrange("p (i q) -> p i q", q=16), PA.t.rearrange("p (i q) -> p i q", q=16),
                                   fap(posT[:, ab * 16:ab * 16 + 1], [[0, 128], [1, 16]]), ALU.add, [b for _, b in banks] + [posT.b], [XAB[ab].b])
                            for gg in range(2):
                                g = cc * 2 + gg
                                rows = slice(gg * 64, gg * 64 + 64)
                                hb_, hbb = bk(4 + 2 * gg), bb(4 + 2 * gg)
                                for jc in range(2):
                                    for p in range(32):
                                        srcT = XAB[0] if p < 16 else XAB[1]
                                        rhs = fap(srcT[rows, p:p + 1], [[16, 127]])
                                        mm(hb_[:, jc * 128:jc * 128 + 127], w1sb[rows, p, jc * 128:(jc + 1) * 128], rhs, p == 0, p == 31,
                                           [w1sb.b, srcT.b], [hbb])
                                hv = hb_[:, 0:256].rearrange("p (j i) -> p j i", j=2)[:, :, 0:127]
                                gv = gtmp[:, :, 0:127]
                                act(gv, hv, AF.Square, [hbb], [gtmp.b])
                                tsc("vector", gv, gv, 0.044715, ALU.mult, [gtmp.b], [gtmp.b], 1.0, ALU.add)
                                tt("vector", gv, gv, hv, ALU.mult, [gtmp.b, hbb], [gtmp.b])
                                act(gv, gv, AF.Sigmoid, [gtmp.b], [gtmp.b], scale=1.5957691216057308)
                                tt("vector", geluT[:, :, 0:127], gv, hv, ALU.mult, [gtmp.b, hbb], [geluT.b])
                                if kv == 0:
                                    for jc in range(2):
                                        mm(bk(5)[:, 0:127], w2sb[:, jc, :], geluT[:, jc, 0:127], jc == 0, jc == 1, [w2sb.b, geluT.b], [bb(5)])
                                    cp("scalar", kcP[0:64, 0, g, 0:127], bk(5)[0:64, 0:127], [bb(5)], [kcP.b])
                                    cp("scalar", kcP[64:128, 1, g, 0:127], bk(5)[64:128, 0:127], [bb(5)], [kcP.b])
                                else:
                                    for jc in range(2):
                                        mm(bk(5)[0:127, 0:64], geluT[:, jc, 0:127], w2sb[:, jc, 0:64], jc == 0, jc == 1, [w2sb.b, geluT.b], [bb(5)])
                                    cp("scalar", vcx[0:127, g, 0:64], bk(5)[0:127, 0:64], [bb(5)], [vcx.b])
                    S.barrier()
                if stop == "nsa2":
                    return
                kpad1 = Buf("kpad1")
                for base, KT_, ceng in ((0, kS, "scalar"), (4, kW, "vector")):
                    cp(ceng, hT[64:128, base:base + 4, :], KT_[64:128, :, :], [KT_.b], hT_b + [kpad1])
                    memset("gpsimd", hT[0:64, base:base + 4, :], 0.0, hT_b + [kpad1])
                    memset("gpsimd" if base == 0 else "vector", KT_[64:128, :, :], 0.0, [KT_.b])
                E = [T(ph, f"nE{i}", [128, 512], BF16) for i in range(4)]
                cb = [T(ph, f"ncb{i}", [128, 2, 512], BF16) for i in range(5)]
                for cbx in cb:
                    memset("vector", cbx[:], NEG, [cbx.b])
                ybt = [T(ph, f"nybt{i}", [128, 1024], BF16) for i in range(2)]
                ybT = [T(ph, f"nybT{i}", [128, 8, 128], BF16) for i in range(2)]
                sets = []
                for i in range(2):
                    sets.append(dict(
                        ybacc=T(ph, f"nybacc{i}", [128, 4, 64], F32), tmp1=T(ph, f"ntmp1{i}", [128, 4, 64], F32),
                        tmp2=T(ph, f"ntmp2{i}", [128, 4, 64], F32), impr=T(ph, f"nimpr{i}", [128, 4, 32], F32),
                        imp=T(ph, f"nimp{i}", [128, 32], F32), m8=T(ph, f"nm8{i}", [128, 8], F32),
                        sm=T(ph, f"nsm{i}", [128, 3, 4], F32), Us=(3, 6)[i], Uw=(4, 7)[i]))
                colb = lambda r: (r % 2) * 256 + (r // 2) * 128
                cnt_ = dict(l=0, e=0, kp=0, cb=0)
                tasks = []

                inflight = set()

                def next_Li(hold=False):
                    while True:
                        i = (0, 1, 5)[cnt_["l"] % 3]
                        cnt_["l"] += 1
                        if i not in inflight:
                            break
                    if hold:
                        inflight.add(i)
                    return i

                def next_L(hold=False):
                    return pbank[next_Li(hold)]

                def release_L(Lb):
                    for i in (0, 1, 5):
                        if pbank[i][1] is Lb:
                            inflight.discard(i)

                def next_E():
                    e_ = E[cnt_["e"] % 4]
                    cnt_["e"] += 1
                    return e_

                cb_dma = []
                PF = 4

                def mk_cmp(qt, g, st):
                    qsl = slice(qt * 128, (qt + 1) * 128)
                    ui = len(cb_dma)
                    cbt = cb[ui % 5]
                    box = {}

                    def issue():
                        for hl in range(2):
                            for rp in range(2):
                                a0 = tc_d[hl, 4 * g + 2 * rp, qt * 128 + 1:qt * 128 + 2]
                                src = bass.AP(a0.tensor, a0.offset, [[16, 128], [4096, 2], [1, 128]])
                                dst = cbt[:, hl, :].rearrange("p (a b s) -> p a b s", a=2, b=2)[:, :, rp, :]
                                dma("sync", dst, src, r=[b_tabs], w=[cbt.b])

                    cb_dma.append(issue)

                    def pre():
                        if ui + PF < len(cb_dma):
                            cb_dma[ui + PF]()

                    def s1():
                        L, Lb = next_L(hold=True)
                        box["L"] = (L, Lb)
                        for par in range(2):
                            mm(L[:, par * 256:(par + 1) * 256], kcP[:, par, g, :], qT[:, 2 * g:2 * g + 2, qsl], par == 0, False, [kcP.b, qT.b], [Lb])
                        for hl in range(2):
                            mm(L, Jb[:, 1, :], cbt[:, hl, :], False, hl == 1, [Jb.b, cbt.b], [Lb])

                    def s2():
                        L, Lb = box["L"]
                        release_L(Lb)
                        Ec = next_E()
                        act(Ec[:], L, AF.Exp, [Lb], [Ec.b])
                        Uc = bk(2)
                        for r in range(4):
                            mm(Uc[:, r * 98:(r + 1) * 98], Ec[:, colb(r):colb(r) + 128], vcx[:, g, 0:98], r == 0, r == 3, [Ec.b, vcx.b], [bb(2)])

                    def post():
                        Uc = bk(2)
                        sm, ybacc, impr, imp, m8 = st["sm"], st["ybacc"], st["impr"], st["imp"], st["m8"]
                        ucv = lambda a, b_: fap(Uc[:, a:a + 1], [[98, 4], [1, b_]])
                        rs4, wc = sm[:, 0, :], sm[:, 1, :]
                        tsc("vector", rs4, fap(Uc[:, 96:97], [[98, 4]]), 1e-30, ALU.max, [bb(2)], [sm.b])
                        recip(rs4, rs4, [sm.b], [sm.b])
                        tt("vector", wc, rs4, sgate[:, qt, 4 * g:4 * g + 4], ALU.mult, [sm.b, sgate.b], [sm.b])
                        tt("vector", ybacc[:], ucv(0, 64), fap(wc, [[1, 4], [0, 64]]), ALU.mult, [bb(2), sm.b], [ybacc.b])
                        tt("vector", impr[:], ucv(64, 32), fap(rs4, [[1, 4], [0, 32]]), ALU.mult, [bb(2), sm.b], [impr.b])
                        S.op("vector", lambda e: e.tensor_reduce(out=imp[:], in_=fap(impr[:, 0, 0:1], [[1, 32], [32, 4]]), axis=AX.X, op=ALU.add),
                             [impr.b], [imp.b])
                        tt("vector", imp[:], imp[:], cand[:, qt, :], ALU.mult, [imp.b, cand.b], [imp.b])
                        tt("vector", imp[:], imp[:], negc[:, qt, :], ALU.add, [imp.b, negc.b], [imp.b])
                        S.op("vector", lambda e: e.max(out=m8[:], in_=imp[:]), [imp.b], [m8.b])
                        tsc("vector", imp[:], imp[:], m8[:, 4:5], ALU.is_ge, [imp.b, m8.b], [imp.b])
                        tt("vector", imp[:], imp[:], forced[:, qt, :], ALU.max, [imp.b, forced.b], [imp.b])
                        tsc("vector", imp[:], imp[:], -1.0, ALU.add, [imp.b], [imp.b], -NEG, ALU.mult)

                    return dict(pre=pre, s1=s1, s2=s2, post=post, defer=None, nm=None, first_slc=False)

                def mk_tile(qt, g, st, br, kt, first, last, buf, hooks_pre, hooks_post, defer):
                    qsl = slice(qt * 128, (qt + 1) * 128)
                    ksl = slice(kt * 128, (kt + 1) * 128)
                    KT = kS if br == 0 else kW
                    VT = vS if br == 0 else vW
                    Ub = st["Us"] if br == 0 else st["Uw"]
                    dl = qt - kt
                    near = dl < (2 if br == 0 else 3)
                    box = {}

                    def pre():
                        for h_ in hooks_pre:
                            h_()

                    def s1():
                        L, Lb = next_L(hold=True)
                        box["L"] = (L, Lb)
                        mm(L[:, 0:256], KT[:, g, ksl], qT[:, 2 * g:2 * g + 2, qsl], True, False, [KT.b, qT.b], [Lb])
                        mm(L[:, 256:512], hT[:, (0 if br == 0 else 4) + g, ksl], qT[:, 2 * g:2 * g + 2, qsl], False, False, [kpad1, qT.b], [Lb])
                        if br == 0:
                            mm(L, expd[:, 1 if near else 0, kt, :], NM[:, g, buf, :], False, not near, [expd.b, NM.b], [Lb])
                        if near:
                            for hl in range(2):
                                mm(L, Jb[:, 0, :], hbt[:, dl, g, hl, :], False, hl == 1, [Jb.b, hbt.b], [Lb])

                    def s2():
                        L, Lb = box["L"]
                        release_L(Lb)
                        Et = next_E()
                        act(Et[:], L, AF.Exp, [Lb], [Et.b])
                        for r in range(4):
                            mm(bk(Ub)[:, r * 66:(r + 1) * 66], Et[:, colb(r):colb(r) + 128], VT[:, kt, g, 0:66],
                               first and r == 0, last and r == 3, [Et.b, VT.b], [bb(Ub)])

                    def post():
                        for h_ in hooks_post:
                            h_()

                    return dict(pre=pre, s1=s1, s2=s2, post=post, defer=defer, nm=None, first_slc=False)

                def mk_nm_hook(g, st, buf):
                    def hook():
                        imp = st["imp"]
                        M_, Mb = next_L()
                        trp(M_[0:32, 0:128], imp[:, :], ident_f[:, :], [imp.b, ident_f.b], [Mb])
                        cp("vector", NM[0:32, g, buf, :].rearrange("p (a s) -> p a s", a=4), fap(M_[0:32, 0:1], [[0, 4], [1, 128]]), [Mb], [NM.b])
                    return hook

                def mk_combine(qt, g, st, ybq):
                    def hook():
                        sm, ybacc, tmp1, tmp2 = st["sm"], st["ybacc"], st["tmp1"], st["tmp2"]
                        for br in range(2):
                            ub = st["Us"] if br == 0 else st["Uw"]
                            U = bk(ub)
                            rsb, wb_ = sm[:, 0, :], sm[:, 1 + br, :]
                            S.op("vector", lambda e, U=U, rsb=rsb: e.reciprocal(out=rsb, in_=fap(U[:, 64:65], [[66, 4]])), [bb(ub)], [sm.b])
                            tt("vector", wb_, rsb, sgate[:, qt, 16 * (br + 1) + 4 * g:16 * (br + 1) + 4 * g + 4], ALU.mult, [sm.b, sgate.b], [sm.b])
                            tgt = tmp1 if br == 0 else tmp2
                            tt("vector", tgt[:], fap(U[:, 0:1], [[66, 4], [1, 64]]), fap(wb_, [[1, 4], [0, 64]]), ALU.mult, [bb(ub), sm.b], [tgt.b])
                        tt("gpsimd", tmp1[:], tmp1[:], ybacc[:], ALU.add, [tmp1.b, ybacc.b], [tmp1.b])
                        tt("gpsimd", ybq[:, g * 256:(g + 1) * 256].rearrange("p (r d) -> p r d", r=4), tmp1[:], tmp2[:], ALU.add, [tmp1.b, tmp2.b], [ybq.b])
                    return hook

                def mk_ybout(qt, ybq):
                    def hook():
                        qsl = slice(qt * 128, (qt + 1) * 128)
                        li = next_Li()
                        pv = bank_bf(li)
                        for c in range(8):
                            trp(pv[:, c * 128:(c + 1) * 128], ybq[:, c * 128:(c + 1) * 128], ident_b[:], [ybq.b, ident_b.b], [bb(li)])
                        yT = ybT[qt % 2]
                        cp("scalar", yT[:], pv.rearrange("p (k s) -> p k s", k=8), [bb(li)], [yT.b])
                        dma("sync", ysc[1, :, qsl].rearrange("(c p) s -> p c s", p=128), yT[:], r=[yT.b], w=[b_ysc[1]])
                    return hook

                un = 0
                units = []
                for qt in range(1 if stop == 'nsa3' else NT):
                    ybq = ybt[qt % 2]
                    for g in range(4):
                        st = sets[un % 2]
                        buf = un % 2
                        un += 1
                        uc_ = [mk_cmp(qt, g, st)]
                        uc_[0]["nm"] = mk_nm_hook(g, st, buf)
                        uc_[0]["unit"] = len(units)
                        wk = list(range(max(0, qt - 2), qt + 1))
                        uw_ = [mk_tile(qt, g, st, 1, kt, kt == wk[0], kt == qt, buf, [], [], None) for kt in wk]
                        us_ = []
                        for kt in range(qt + 1):
                            hp = []
                            hq = [mk_combine(qt, g, st, ybq)] if kt == qt else []
                            df = mk_ybout(qt, ybq) if (kt == qt and g == 3) else None
                            us_.append(mk_tile(qt, g, st, 0, kt, kt == 0, kt == qt, buf, hp, hq, df))
                        us_[0]["first_slc"] = True
                        us_[0]["unit"] = len(units)
                        units.append((uc_, uw_, us_))
                tasks += units[0][0] + units[0][1]
                for ui in range(len(units)):
                    if ui + 1 < len(units):
                        tasks += units[ui + 1][0]
                    tasks += units[ui][2]
                    if ui + 1 < len(units):
                        tasks += units[ui + 1][1]
                for ui_ in range(min(PF, len(cb_dma))):
                    cb_dma[ui_]()
                deferred = {}
                ntk = len(tasks)
                LA = 3
                for j in range(min(LA, ntk)):
                    tasks[j]["pre"]()
                    tasks[j]["s1"]()
                first_idx = {tk["unit"]: i for i, tk in enumerate(tasks) if tk["first_slc"]}
                for i, tk in enumerate(tasks):
                    tk["s2"]()
                    tk["post"]()
                    if tk["nm"] is not None:
                        j = max(i, min(i + 8, first_idx[tk["unit"]] - LA))
                        deferred.setdefault(j, []).append(tk["nm"])
                    if tk["defer"] is not None:
                        deferred.setdefault(i + 3, []).append(tk["defer"])
                    for fn in deferred.pop(i, []):
                        fn()
                    if i + LA < ntk:
                        tasks[i + LA]["pre"]()
                        tasks[i + LA]["s1"]()
                for k_ in sorted(deferred):
                    for fn in deferred[k_]:
                        fn()
                dma("sync", hT.t.rearrange("p k s -> p (k s)"), hsc, r=[b_hsc], w=hT_b + [kpad1])
                S.barrier()

        def phase_tail(l, x_src, x_dst, b_xsrc, b_xdst):
            bk = lambda i: pbank[i][0]
            bb = lambda i: pbank[i][1]
            w2d = w_in[l]
            with ExitStack() as ph:
                mrgb = T(ph, "mrgb", [128, 8, SEQ], BF16)
                with ExitStack() as ph2:
                    mrg = T(ph2, "mrg", [128, 8, SEQ], F32)
                    yT = T(ph2, "m_yT", [128, 8, SEQ], BF16)
                    yb_ = [Buf(f"m_yT{c}") for c in range(8)]
                    wbr = [T(ph2, f"m_wbr{i}", [128, 8, 256], BF16) for i in range(2)]
                    wmg = [T(ph2, f"m_wmg{i}", [128, 8, 256], BF16) for i in range(2)]
                    sig = [T(ph2, f"m_sig{i}", [128, 512], F32) for i in range(2)]
                    prod = [T(ph2, f"m_prod{i}", [128, 512], F32) for i in range(2)]
                    n = 0
                    bn = 0
                    ybB_ = [Buf(f"m_yTB{c}") for c in range(8)]
                    for c in range(8):
                        dma("sync", yT[:, c, :], ysc[0, c * 128:(c + 1) * 128, :], r=[b_ysc[0]], w=[yb_[c]])
                    for c in range(8):
                        dma("sync", mrgb[:, c, :], ysc[1, c * 128:(c + 1) * 128, :], r=[b_ysc[1]], w=[ybB_[c], mrgb.b])
                    for br in range(3):
                        if br == 1:
                            for c in range(8):
                                dma("sync", yT[:, c, :], ysc[2, c * 128:(c + 1) * 128, :], r=[b_ysc[2]], w=[yb_[c]])
                        ysrc, ybufs = (mrgb, ybB_) if br == 1 else (yT, yb_)
                        for oc2 in range(4):
                            wb, wm = wbr[oc2 % 2], wmg[oc2 % 2]
                            load_slab(wb[:], w_branch[l, br], oc2 * 256, 256, wb.b)
                            load_slab(wm[:], w2d, C_MG + br * 1024 + oc2 * 256, 256, wm.b)
                            for o in range(2):
                                oc = oc2 * 2 + o
                                for sc in range(4):
                                    ssl = slice(sc * 512, (sc + 1) * 512)
                                    bB, bG = (bn % 4) * 2, (bn % 4) * 2 + 1
                                    bn += 1
                                    for c in range(8):
                                        mm(bk(bB), wb[:, c, o * 128:(o + 1) * 128], ysrc[:, c, ssl], c == 0, c == 7, [wb.b, ybufs[c]], [bb(bB)])
                                    for kc in range(8):
                                        mm(bk(bG), wm[:, kc, o * 128:(o + 1) * 128], hT[:, kc, ssl], kc == 0, kc == 7, [wm.b] + hT_b[sc * 4:(sc + 1) * 4], [bb(bG)])
                                    sg_, pr_ = sig[n % 2], prod[n % 2]
                                    n += 1
                                    act(sg_[:], bk(bG), AF.Sigmoid, [bb(bG)], [sg_.b])
                                    if br == 0:
                                        tt("vector", mrg[:, oc, ssl], sg_[:], bk(bB), ALU.mult, [sg_.b, bb(bB)], [mrg.b])
                                    elif br == 1:
                                        tt("vector", pr_[:], sg_[:], bk(bB), ALU.mult, [sg_.b, bb(bB)], [pr_.b])
                                        tt("vector", mrg[:, oc, ssl], mrg[:, oc, ssl], pr_[:], ALU.add, [mrg.b, pr_.b], [mrg.b])
                                    else:
                                        tt("vector", pr_[:], sg_[:], bk(bB), ALU.mult, [sg_.b, bb(bB)], [pr_.b])
                                        tt("vector", mrgb[:, oc, ssl], mrg[:, oc, ssl], pr_[:], ALU.add, [mrg.b, pr_.b], [mrgb.b] + ybB_)
                    S.barrier()
                with ExitStack() as ph2:
                    wout = T(ph2, "p_wout", [128, 8, D], BF16)
                    g1 = load_gain(ph2, l, 1)
                    g2 = load_gain(ph2, l, 2)
                    xt = [T(ph2, f"p_xt{i}", [128, D], F32) for i in range(2)]
                    on = [T(ph2, f"p_on{i}", [128, D], F32) for i in range(2)]
                    hb = [T(ph2, f"p_hb{i}", [128, D], BF16) for i in range(2)]
                    junk = T(ph2, "p_junk", [128, D], BF16)
                    ss = T(ph2, "p_ss", [128, NT], F32)
                    rs = T(ph2, "p_rs", [128, NT], F32)
                    ss2 = T(ph2, "p_ss2", [128, NT], F32)
                    rs2 = T(ph2, "p_rs2", [128, NT], F32)
                    for half in range(2):
                        load_slab(wout[:, :, half * 512:(half + 1) * 512], w_out[l], half * 512, 512, wout.b)
                    junk2 = T(ph2, "p_junk2", [128, D], BF16)
                    ssb = T(ph2, "p_ssb", [128, NT], F32)
                    rsb_ = T(ph2, "p_rsb", [128, NT], F32)
                    ss2b = T(ph2, "p_ss2b", [128, NT], F32)
                    rs2b = T(ph2, "p_rs2b", [128, NT], F32)

                    def tileP(t):
                        p = t % 2
                        jk, s_a, r_a, s_b, r_b = (junk, ss, rs, ss2, rs2) if p == 0 else (junk2, ssb, rsb_, ss2b, rs2b)
                        tsl = slice(t * 128, (t + 1) * 128)
                        b0 = p * 2
                        ov = PA[:, b0 * 512:(b0 + 2) * 512]
                        obufs = [bb(b0), bb(b0 + 1)]
                        for half in range(2):
                            for c in range(8):
                                mm(bk(b0 + half), mrgb[:, c, tsl], wout[:, c, half * 512:(half + 1) * 512], c == 0, c == 7, [mrgb.b, wout.b], [bb(b0 + half)])
                        x_, o_ = xt[p], on[p]
                        dma("sync", x_[:], x_src[tsl, :], r=[b_xsrc], w=[x_.b])
                        act(jk[:], ov, AF.Square, obufs, [jk.b, s_a.b], accum_out=s_a[:, t:t + 1])
                        act(r_a[:, t:t + 1], s_a[:, t:t + 1], AF.Sqrt, [s_a.b], [r_a.b], scale=1.0 / D, bias=EPS)
                        recip(r_a[:, t:t + 1], r_a[:, t:t + 1], [r_a.b], [r_a.b])
                        stt(o_[:], ov, r_a[:, t:t + 1], g1[:], ALU.mult, ALU.mult, obufs + [r_a.b, g1.b], [o_.b])
                        tt("gpsimd", o_[:], o_[:], x_[:], ALU.add, [o_.b, x_.b], [o_.b])
                        dma("sync", xmid[tsl, :], o_[:], r=[o_.b], w=[b_xmid])
                        norm_transpose_tile(ph2, t, o_[:], o_.b, g2, s_b, r_b, hb[p], jk, 4 + p)

                    for t in range(0, NT, 2):
                        S.replay([S.record(lambda: tileP(t)), S.record(lambda: tileP(t + 1))])
                    S.barrier()
            with ExitStack() as ph:
                hid = T(ph, "f_hid", [128, 22, SEQ], BF16)
                hid_b = [Buf(f"hid{j}") for j in range(22)]
                wfo = T(ph, "f_wfo", [128, 22, D], BF16)
                wfo_b = [Buf(f"wfo{j}") for j in range(11)]
                wfi = [T(ph, f"f_wfi{i}", [128, 8, 2, 128], BF16) for i in range(2)]
                sgt = [T(ph, f"f_sg{i}", [128, 512], F32) for i in range(2)]
                g3 = load_gain(ph, l, 3)
                xt = [T(ph, f"f_xt{i}", [128, D], F32) for i in range(2)]
                on = [T(ph, f"f_on{i}", [128, D], F32) for i in range(2)]
                junk = T(ph, "f_junk", [128, D], BF16)
                ss = T(ph, "f_ss", [128, NT], F32)
                rs = T(ph, "f_rs", [128, NT], F32)
                n = 0
                bn = 0
                for j in range(22):
                    wf = wfi[j % 2]
                    dma("gpsimd", wf[:, :, 0, :], w_ffn_in[l][:, j * 128:(j + 1) * 128].rearrange("(kc p) n -> p kc n", p=128), w=[wf.b])
                    dma("gpsimd", wf[:, :, 1, :], w_ffn_in[l][:, DFF + j * 128:DFF + (j + 1) * 128].rearrange("(kc p) n -> p kc n", p=128), w=[wf.b])
                    if j % 2 == 0:
                        jj = j // 2
                        dma("gpsimd", wfo[:, 2 * jj:2 * jj + 2, :], w_ffn_out[l][jj * 256:(jj + 1) * 256, :].rearrange("(j p) n -> p j n", p=128), w=[wfo_b[jj]])
                    for sc in range(4):
                        ssl = slice(sc * 512, (sc + 1) * 512)
                        bG, bU = (bn % 4) * 2, (bn % 4) * 2 + 1
                        bn += 1
                        for kc in range(8):
                            mm(bk(bG), wf[:, kc, 0, :], hT[:, kc, ssl], kc == 0, kc == 7, [wf.b] + hT_b[sc * 4:(sc + 1) * 4], [bb(bG)])
                        for kc in range(8):
                            mm(bk(bU), wf[:, kc, 1, :], hT[:, kc, ssl], kc == 0, kc == 7, [wf.b] + hT_b[sc * 4:(sc + 1) * 4], [bb(bU)])
                        sg_ = sgt[n % 2]
                        n += 1
                        act(sg_[:], bk(bG), AF.Silu, [bb(bG)], [sg_.b])
                        tt("vector", hid[:, j, ssl], sg_[:], bk(bU), ALU.mult, [sg_.b, bb(bU)], [hid_b[j]])
                for t in range(NT):
                    tsl = slice(t * 128, (t + 1) * 128)
                    b0 = (t % 2) * 2
                    ov = PA[:, b0 * 512:(b0 + 2) * 512]
                    obufs = [bb(b0), bb(b0 + 1)]
                    for half in range(2):
                        for j in range(22):
                            mm(bk(b0 + half), hid[:, j, tsl], wfo[:, j, half * 512:(half + 1) * 512], j == 0, j == 21, [hid_b[j], wfo_b[j // 2]], [bb(b0 + half)])
                    x_, o_ = xt[t % 2], on[t % 2]
                    dma("sync", x_[:], xmid[tsl, :], r=[b_xmid], w=[x_.b])
                    act(junk[:], ov, AF.Square, obufs, [junk.b, ss.b], accum_out=ss[:, t:t + 1])
                    act(rs[:, t:t + 1], ss[:, t:t + 1], AF.Sqrt, [ss.b], [rs.b], scale=1.0 / D, bias=EPS)
                    recip(rs[:, t:t + 1], rs[:, t:t + 1], [rs.b], [rs.b])
                    stt(o_[:], ov, rs[:, t:t + 1], g3[:], ALU.mult, ALU.mult, obufs + [rs.b, g3.b], [o_.b])
                    tt("gpsimd", o_[:], o_[:], x_[:], ALU.add, [o_.b, x_.b], [o_.b])
                    dma("sync", x_dst[tsl, :], o_[:], r=[o_.b], w=[b_xdst])
                S.barrier()

        setup_tables()
        for l in range(n_layers):
            phase_A(l, x_in if l == 0 else xres)
            import os
            if not os.environ.get("SKIP_LG"):
                phase_lru(l)
                if stop == "lru":
                    break
                phase_gla(l)
                if stop == "gla":
                    break
            if not os.environ.get("SKIP_NSA"):
                phase_nsa(l)
            if stop is not None and stop.startswith("nsa"):
                break
            last = (l == n_layers - 1)
            phase_tail(l, x_in if l == 0 else xres, y_out if last else xres, Buf() if l == 0 else b_xres, Buf() if last else b_xres)
        S.barrier()
        S.emit()
    return nc, consts


_CACHE = {}


def kernel(**inputs):
    if "nc" not in _CACHE:
        _CACHE["nc"] = build()
    nc, consts = _CACHE["nc"]
    x = np.ascontiguousarray(np.asarray(inputs["x"], dtype=np.float32))
    shared = {k: np.ascontiguousarray(np.asarray(v, dtype=np.float32)) for k, v in inputs.items() if k != "x"}
    for k, v in consts.items():
        shared["c_" + k] = v
    in_maps = [dict(shared, x=x[i]) for i in range(8)]
    res = run_bass_kernel_spmd(nc, in_maps, core_ids=list(range(8)))
    return np.stack([np.asarray(r["out"], dtype=np.float32) for r in res.results], axis=0)
```

```python
import os
import numpy as np
from contextlib import ExitStack
import concourse.bass as bass
import concourse.mybir as mybir
from concourse.bass_utils import run_bass_kernel_spmd

F32 = mybir.dt.float32
BF16 = mybir.dt.bfloat16
ALU = mybir.AluOpType
AF = mybir.ActivationFunctionType
AX = mybir.AxisListType

SEQ = 2048
D = 1024
NT = 16
DEPTH = 2
EPS = 1e-6
IN_W = 10816
C_LRUX, C_LRUG, C_Q, C_KV, C_GATE, C_GQ, C_GK, C_GV, C_GOG, C_GLR, C_MG = 0, 1024, 2048, 3072, 4608, 4656, 5168, 5680, 6704, 7728, 7744
DFF = 2816
NEG = -30000.0


class Buf:
    __slots__ = ("name", "w", "r", "excl")

    def __init__(self, name="", excl=False):
        self.name = name
        self.w = None
        self.r = []
        self.excl = excl


class Sched:
    ENG = ("sync", "scalar", "vector", "gpsimd", "tensor")
    DMAQ = ("sync", "gpsimd", "scalar")

    def __init__(self, nc, es, n_dma_sems=12):
        self.nc = nc
        self.q = {e: [] for e in self.ENG}
        self.cnt = {e: 0 for e in self.ENG}
        self.sems = []
        self.esem = {}
        for e in self.ENG:
            self.esem[e] = len(self.sems)
            self.sems.append(es.enter_context(nc.semaphore("s_" + e)))
        self.known = {e: {} for e in self.ENG}
        self.dpool = {}
        self.dcnt = {}
        self.dlast = {}
        for qn in self.DMAQ:
            self.dpool[qn] = []
            for i in range(n_dma_sems):
                self.dpool[qn].append(len(self.sems))
                self.sems.append(es.enter_context(nc.semaphore(f"d_{qn}_{i}")))
            self.dcnt[qn] = 0
        self.K = n_dma_sems

    def _waits(self, eng, r, w):
        waits = {}
        kn = self.known[eng]
        own_pe = self.esem["tensor"] if eng == "tensor" else -1

        def need(kv):
            k, v = kv
            if k == own_pe:
                return
            if kn.get(k, 0) < v and waits.get(k, 0) < v:
                waits[k] = v

        own = self.esem.get(eng, -2)
        for b in r:
            if b.w is not None:
                need(b.w)
            if b.excl:
                for x in b.r:
                    if x[0] != own:
                        need(x)
        for b in w:
            if b.w is not None:
                need(b.w)
            for x in b.r:
                need(x)
        for k, v in waits.items():
            kn[k] = v
        return list(waits.items())

    _rec = None

    def record(self, fn):
        self._rec = []
        fn()
        r, self._rec = self._rec, None
        return r

    def replay(self, lists):
        idx = [0] * len(lists)
        live = True
        while live:
            live = False
            for j, lst in enumerate(lists):
                if idx[j] < len(lst):
                    kind, args, kw = lst[idx[j]]
                    idx[j] += 1
                    live = True
                    if kind == "op":
                        self.op(*args)
                    else:
                        self.dma(*args, **kw)

    def op(self, eng, fn, r=(), w=()):
        if self._rec is not None:
            self._rec.append(("op", (eng, fn, list(r), list(w)), {}))
            return
        waits = self._waits(eng, r, w)
        self.cnt[eng] += 1
        seq = self.cnt[eng]
        k = self.esem[eng]
        self.q[eng].append((waits, fn, (k, 1)))
        for b in w:
            b.w = (k, seq)
            b.r = []
        for b in r:
            if b not in w:
                b.r.append((k, seq))
                if len(b.r) > 24:
                    b.r = b.r[-24:] if False else self._compact(b.r)

    @staticmethod
    def _compact(lst):
        d = {}
        for k, v in lst:
            if d.get(k, 0) < v:
                d[k] = v
        return list(d.items())

    def dma(self, qn, out, in_, r=(), w=(), **kw):
        if self._rec is not None:
            self._rec.append(("dma", (qn, out, in_, list(r), list(w)), kw))
            return
        waits = self._waits(qn, r, w)
        i = self.dcnt[qn]
        self.dcnt[qn] += 1
        k = self.dpool[qn][i % self.K]
        val = 16 * (i // self.K + 1)
        if val > 16 and self.known[qn].get(k, 0) < val - 16:
            waits.append((k, val - 16))
            self.known[qn][k] = val - 16
        self.dlast[k] = val
        self.q[qn].append((waits, lambda e: e.dma_start(out=out, in_=in_, **kw), (k, 16)))
        for b in w:
            b.w = (k, val)
            b.r = []
        for b in r:
            if b not in w:
                b.r.append((k, val))
                if len(b.r) > 24:
                    b.r = self._compact(b.r)

    def pe_drain(self):
        k = self.esem["tensor"]
        if self.cnt["tensor"] > 0:
            self.q["tensor"].append(([(k, self.cnt["tensor"])], None, None))

    def barrier(self):
        tgt = [(self.esem[e], self.cnt[e]) for e in self.ENG if self.cnt[e] > 0]
        tgt += list(self.dlast.items())
        for e in self.ENG:
            waits = []
            for k, v in tgt:
                if e == "tensor" and k == self.esem["tensor"]:
                    continue
                if self.known[e].get(k, 0) < v:
                    waits.append((k, v))
                    self.known[e][k] = v
            if waits:
                self.q[e].append((waits, None, None))

    def emit(self):
        nc = self.nc
        with nc.Block() as block:
            for e in self.ENG:
                def body(eng, _e=e):
                    for waits, fn, inc in self.q[_e]:
                        for k, v in waits:
                            eng.wait_ge(self.sems[k], v)
                        if fn is not None:
                            ins = fn(eng)
                            ins.then_inc(self.sems[inc[0]], inc[1])
                getattr(block, e)(body)


def fap(a, dims):
    return bass.AP(a.tensor, a.offset, [list(a.ap[0])] + [list(d) for d in dims])


def _rel_bucket(d):
    d = np.asarray(d)
    n = np.maximum(d, 0)
    nf = np.maximum(n, 16).astype(np.float32)
    large = 16 + (np.log(nf / np.float32(16)) / np.float32(np.log(128 / 16)) * np.float32(16)).astype(np.int32)
    large = np.minimum(large, 31)
    return np.where(n < 16, n, large)


def host_consts():
    c = {}
    c["ident"] = np.eye(128, dtype=np.float32)
    c["antiid"] = np.eye(128, dtype=np.float32)[::-1].copy()
    aid127 = np.zeros((128, 128), np.float32)
    for i in range(127):
        aid127[i, 126 - i] = 1.0
    aid127[127, 127] = 1.0
    c["antiid127"] = aid127
    s = np.arange(128)
    c["triu"] = (s[:, None] <= s[None, :]).astype(np.float32)
    c["tril"] = (s[:, None] > s[None, :]).astype(np.float32)
    def oh(deltas, valid):
        m = np.zeros((33, len(deltas)), np.float32)
        b = _rel_bucket(deltas)
        for i, (dd, v) in enumerate(zip(deltas, valid)):
            if v:
                m[b[i], i] = 1.0
            else:
                m[32, i] = 1.0
        return m
    dc = np.arange(-2048, 2048)
    c["oh_c"] = oh(dc, dc >= 0)
    ds = np.arange(-512, 512)
    c["oh_s"] = oh(ds, ds >= 0)
    c["oh_w"] = oh(ds, (ds >= 0) & (ds < 256))
    cs = np.arange(127) * 16
    js = np.arange(32) * 64
    ov = np.clip(np.minimum(cs[:, None] + 32, js[None, :] + 64) - np.maximum(cs[:, None], js[None, :]), 0, None).astype(np.float32) / 32.0
    ovx = np.zeros((128, 33), np.float32)
    ovx[:127, :32] = ov
    ovx[:127, 32] = 1.0
    c["ovx"] = ovx
    pos = np.arange(SEQ)
    cur = pos // 64
    blk = np.arange(32)[None, :]
    cand = (blk >= 1) & (blk <= cur[:, None] - 2)
    forced = (blk == 0) | (blk == cur[:, None]) | (blk == cur[:, None] - 1)
    c["cand"] = cand.astype(np.float32).reshape(NT, 128, 32).transpose(1, 0, 2).copy()
    c["negc"] = ((cand.astype(np.float32) - 1.0) * 1e4).reshape(NT, 128, 32).transpose(1, 0, 2).copy()
    c["forced"] = forced.astype(np.float32).reshape(NT, 128, 32).transpose(1, 0, 2).copy()
    ex = np.zeros((128, NT, 128), np.float32)
    for kt in range(NT):
        for key in range(128):
            ex[2 * kt + key // 64, kt, key] = 1.0
    c["expand_near"] = ex.copy()
    ex[32:34] = 1.0
    c["expand"] = ex
    return c


CONST_SHAPES = None


def build(debug=False, n_layers=DEPTH, stop=None):
    nc = bass.Bass("TRN2", target_bir_lowering=False)
    consts = host_consts()
    din = {}

    def inp(name, shape, dt=F32):
        din[name] = nc.dram_tensor(name, list(shape), dt, kind="ExternalInput").ap()
        return din[name]

    x_in = inp("x", [SEQ, D])
    rel_table = inp("rel_table", [32, 16])
    norm_g = inp("norm_g", [DEPTH, 4, D])
    w_in = inp("w_in", [DEPTH, D, IN_W])
    conv_w = inp("conv_w", [DEPTH, 4, D])
    conv_b = inp("conv_b", [DEPTH, D])
    lru_wg = inp("lru_w_gates", [DEPTH, 2, 8, 128, 128])
    lru_bg = inp("lru_b_gates", [DEPTH, 2, D])
    lru_lam = inp("lru_lambda", [DEPTH, D])
    cmp_pos = inp("cmp_pos", [DEPTH, 2, 32, 64])
    cmp_w1 = inp("cmp_w1", [DEPTH, 2, 2048, 256])
    cmp_w2 = inp("cmp_w2", [DEPTH, 2, 256, 64])
    gla_wa2 = inp("gla_wa2", [DEPTH, 16, 512])
    gla_ba = inp("gla_ba", [DEPTH, 512])
    gla_norm = inp("gla_norm", [DEPTH, 256])
    w_branch = inp("w_branch", [DEPTH, 3, D, D])
    w_out = inp("w_out", [DEPTH, D, D])
    w_ffn_in = inp("w_ffn_in", [DEPTH, D, 2 * DFF])
    w_ffn_out = inp("w_ffn_out", [DEPTH, DFF, D])
    cin = {k: inp("c_" + k, v.shape) for k, v in consts.items()}

    okind = "ExternalOutput"
    y_out = nc.dram_tensor("out", [SEQ, D], F32, kind=okind).ap()
    skind = "ExternalOutput"
    xres = nc.dram_tensor("xres", [SEQ, D], F32, kind=skind).ap()
    xmid = nc.dram_tensor("xmid", [SEQ, D], F32, kind=skind).ap()
    ysc = nc.dram_tensor("ysc", [3, D, SEQ], BF16, kind=skind).ap()
    tc_d = nc.dram_tensor("tc_d", [2, 16, 4096], BF16, kind="Internal").ap()
    ts_d = nc.dram_tensor("ts_d", [2, 16, 1024], BF16, kind="Internal").ap()
    tw_d = nc.dram_tensor("tw_d", [2, 16, 1024], BF16, kind="Internal").ap()
    hsc = nc.dram_tensor("hsc", [128, 8 * SEQ], BF16, kind=skind).ap()
    b_hsc = Buf()
    b_xres, b_xmid, b_ysc, b_tabs = Buf(), Buf(), [Buf(), Buf(), Buf()], Buf()

    with ExitStack() as es:
        S = Sched(nc, es)
        es.enter_context(nc.allow_non_contiguous_dma(reason="small param loads"))

        def mm(out, lhsT, rhs, start, stop, r, w):
            S.op("tensor", lambda e: e.matmul(out, lhsT=lhsT, rhs=rhs, start=start, stop=stop), r, w)

        def trp(out, in_, ident, r, w):
            S.op("tensor", lambda e: e.transpose(out, in_, ident), r, w)

        def act(out, in_, func, r, w, **kw):
            S.op("scalar", lambda e: e.activation(out=out, in_=in_, func=func, **kw), r, w)

        def tt(eng, out, in0, in1, op, r, w):
            S.op(eng, lambda e: e.tensor_tensor(out=out, in0=in0, in1=in1, op=op), r, w)

        def tsc(eng, out, in0, s1, op0, r, w, s2=None, op1=None):
            if op1 is None:
                S.op(eng, lambda e: e.tensor_scalar(out=out, in0=in0, scalar1=s1, scalar2=None, op0=op0), r, w)
            else:
                S.op(eng, lambda e: e.tensor_scalar(out=out, in0=in0, scalar1=s1, scalar2=s2, op0=op0, op1=op1), r, w)

        def stt(out, in0, scalar, in1, op0, op1, r, w):
            S.op("vector", lambda e: e.scalar_tensor_tensor(out=out, in0=in0, scalar=scalar, in1=in1, op0=op0, op1=op1), r, w)

        def cp(eng, out, in_, r, w):
            if eng == "scalar":
                S.op("scalar", lambda e: e.copy(out=out, in_=in_), r, w)
            else:
                S.op(eng, lambda e: e.tensor_copy(out=out, in_=in_), r, w)

        def recip(out, in_, r, w):
            S.op("vector", lambda e: e.reciprocal(out=out, in_=in_), r, w)

        def memset(eng, ap, val, w):
            S.op(eng, lambda e: e.memset(ap, val), (), w)

        def dma(q, out, in_, r=(), w=()):
            S.dma(q, out, in_, r, w)

        class T:
            _n = [0]

            def __init__(self, stack, name, shape, dt, psum=False):
                T._n[0] += 1
                name = f"{name}_{T._n[0]}"
                self.t = stack.enter_context((nc.psum_tensor if psum else nc.sbuf_tensor)(name, list(shape), dt))
                self.b = Buf(name)

            def __getitem__(self, idx):
                return self.t[idx]

        PA = T(es, "PA", [128, 2048], F32, psum=True)
        PB = T(es, "PB", [128, 2048], F32, psum=True)
        pbank = []
        for i in range(8):
            src = PA if i < 4 else PB
            pbank.append((src.t[:, (i % 4) * 512:(i % 4 + 1) * 512], Buf(f"bank{i}", excl=True)))
        PAb = PA.t.bitcast(BF16)
        PBb = PB.t.bitcast(BF16)

        def bank_bf(i):
            src = PAb if i < 4 else PBb
            return src[:, (i % 4) * 1024:(i % 4 + 1) * 1024]

        ident_f = T(es, "ident_f", [128, 128], F32)
        ident_b = T(es, "ident_b", [128, 128], BF16)
        dma("sync", ident_f[:], cin["ident"], w=[ident_f.b])
        cp("vector", ident_b[:], ident_f[:], [ident_f.b], [ident_b.b])

        hT = T(es, "hT", [128, 8, SEQ], BF16)
        hT_b = [Buf(f"hT{t}") for t in range(NT)]

        def load_gain(ph, l, i):
            gt = T(ph, f"gain{i}", [128, D], F32)
            src = norm_g[l, i:i + 1, :]
            dma("sync", gt[:], bass.AP(src.tensor, src.offset, [[0, 128], [1, D]]), w=[gt.b])
            return gt

        def norm_transpose_tile(ph, t, xt_ap, xt_buf, gt, ss, rs, hb, junk, pbi):
            act(junk[:], xt_ap, AF.Square, [xt_buf], [junk.b, ss.b], accum_out=ss[:, t:t + 1])
            act(rs[:, t:t + 1], ss[:, t:t + 1], AF.Sqrt, [ss.b], [rs.b], scale=1.0 / D, bias=EPS)
            recip(rs[:, t:t + 1], rs[:, t:t + 1], [rs.b], [rs.b])
            stt(hb[:], xt_ap, rs[:, t:t + 1], gt[:], ALU.mult, ALU.mult, [xt_buf, rs.b, gt.b], [hb.b])
            pv, pbuf = bank_bf(pbi), pbank[pbi][1]
            for kc in range(8):
                trp(pv[:, kc * 128:(kc + 1) * 128], hb[:, kc * 128:(kc + 1) * 128], ident_b[:], [hb.b, ident_b.b], [pbuf])
            cp("scalar", hT[:, :, t * 128:(t + 1) * 128], pv.rearrange("p (k s) -> p k s", k=8), [pbuf], [hT_b[t]])

        def load_slab(dst_ap, w2d, c0, ncols, wbuf, nk=8):
            src = w2d[:, c0:c0 + ncols].rearrange("(kc p) n -> p kc n", p=128)
            dma("gpsimd", dst_ap, src, w=[wbuf])

        def proj_fm(wslab, wbuf, col_off, M, rhs_tile, rhs_bufs, out_banks, nk=8, sc_list=(0, 1, 2, 3)):
            for i, sc in enumerate(sc_list):
                pa, pb_ = out_banks[i]
                for kc in range(nk):
                    mm(pa[0:M, :], wslab[:, kc, col_off:col_off + M], rhs_tile[:, kc, sc * 512:(sc + 1) * 512],
                       kc == 0, kc == nk - 1, [wbuf] + rhs_bufs[sc * 4:(sc + 1) * 4], [pb_])

        def phase_A(l, x_src):
            with ExitStack() as ph:
                xt = [T(ph, f"xtA{i}", [128, D], F32) for i in range(2)]
                hb = [T(ph, f"hbA{i}", [128, D], BF16) for i in range(2)]
                junk = [T(ph, f"junkA{i}", [128, D], BF16) for i in range(2)]
                ss = [T(ph, f"ssA{i}", [128, NT], F32) for i in range(2)]
                rs = [T(ph, f"rsA{i}", [128, NT], F32) for i in range(2)]
                g0 = load_gain(ph, l, 0)

                def tileA(t):
                    p = t % 2
                    dma("sync", xt[p][:], x_src[t * 128:(t + 1) * 128, :], r=[b_xres], w=[xt[p].b])
                    norm_transpose_tile(ph, t, xt[p][:], xt[p].b, g0, ss[p], rs[p], hb[p], junk[p], p)

                for t in range(0, NT, 2):
                    S.replay([S.record(lambda: tileA(t)), S.record(lambda: tileA(t + 1))])
                S.barrier()

        def phase_lru(l):
            with ExitStack() as ph:
                prow = T(ph, "prow", [8, D], F32)
                lpT = T(ph, "lpT", [128, 8, 8], F32)
                sp = T(ph, "lru_sp", [128, 8, 6], F32)
                wg = T(ph, "lru_wg", [128, 2, 8, 128], BF16)
                slab = [T(ph, f"lslab{i}", [128, 8, 2, 128], BF16) for i in range(2)]
                XA = [T(ph, f"XA{i}", [128, SEQ + 4], F32) for i in range(2)]
                XC = [T(ph, f"XC{i}", [128, SEQ], F32) for i in range(2)]
                XCB = [T(ph, f"XCB{i}", [128, SEQ], BF16) for i in range(2)]
                R = [T(ph, f"R{i}", [128, SEQ], F32) for i in range(2)]
                A = [T(ph, f"A{i}", [128, SEQ], F32) for i in range(2)]
                I = [T(ph, f"I{i}", [128, SEQ], F32) for i in range(2)]
                H = [T(ph, f"H{i}", [128, SEQ], F32) for i in range(2)]
                GA = [T(ph, f"GA{i}", [128, SEQ], F32) for i in range(2)]
                G = [T(ph, f"G{i}", [128, SEQ], F32) for i in range(2)]
                YA = [T(ph, f"YA{i}", [128, SEQ], BF16) for i in range(2)]
                if os.environ.get("SBUF_DBG"):
                    print("LRU sbuf remaining", nc.sbuf_bytes_remaining)
                dma("sync", hsc, hT.t.rearrange("p k s -> p (k s)"), r=hT_b, w=[b_hsc])
                for k in range(4):
                    dma("sync", prow[k:k + 1, :], conv_w[l, k:k + 1, :], w=[prow.b])
                dma("sync", prow[4:5, :], conv_b[l:l + 1, :], w=[prow.b])
                dma("sync", prow[5:7, :], lru_bg[l], w=[prow.b])
                dma("sync", prow[7:8, :], lru_lam[l:l + 1, :], w=[prow.b])
                pv, pbuf = pbank[7]
                for c in range(8):
                    trp(pv[:, c * 8:(c + 1) * 8], prow[0:8, c * 128:(c + 1) * 128], ident_f[0:8, 0:8], [prow.b, ident_f.b], [pbuf])
                cp("vector", lpT[:], pv[:, 0:64].rearrange("p (c k) -> p c k", c=8), [pbuf], [lpT.b])
                xs, ln1, ser, msk, nsp8, nsp16 = (sp[:, :, i] for i in range(6))
                act(xs, lpT[:, :, 7], AF.Exp, [lpT.b], [sp.b], scale=-1.0)
                act(ln1, xs, AF.Ln, [sp.b], [sp.b], bias=1.0)
                tsc("vector", ser, xs, -0.25, ALU.mult, [sp.b], [sp.b], 1.0 / 3.0, ALU.add)
                tt("vector", ser, ser, xs, ALU.mult, [sp.b], [sp.b])
                tsc("vector", ser, ser, -1.0, ALU.mult, [sp.b], [sp.b], 0.5, ALU.add)
                tt("vector", ser, ser, xs, ALU.mult, [sp.b], [sp.b])
                tsc("vector", ser, ser, -1.0, ALU.mult, [sp.b], [sp.b], 1.0, ALU.add)
                tt("vector", ser, ser, xs, ALU.mult, [sp.b], [sp.b])
                tsc("vector", msk, xs, 0.03, ALU.is_lt, [sp.b], [sp.b])
                tt("vector", ser, ser, ln1, ALU.subtract, [sp.b], [sp.b])
                tt("vector", ser, ser, msk, ALU.mult, [sp.b], [sp.b])
                tt("vector", ser, ser, ln1, ALU.add, [sp.b], [sp.b])
                tsc("vector", nsp8, ser, -8.0, ALU.mult, [sp.b], [sp.b])
                tsc("vector", nsp16, ser, -16.0, ALU.mult, [sp.b], [sp.b])
                dma("gpsimd", wg[:], lru_wg[l].rearrange("k n c e -> c k n e"), w=[wg.b])
                for p_ in range(2):
                    memset("vector", XA[p_][:, 0:3], 0.0, [XA[p_].b])
                w2d = w_in[l]

                def lru_A(c):
                    p = c % 2
                    xa, xc, xcb, r_, i_, ga = XA[p], XC[p], XCB[p], R[p], I[p], GA[p]
                    sl = slab[p]
                    dma("gpsimd", sl[:, :, 0, :], w2d[:, C_LRUX + c * 128:C_LRUX + (c + 1) * 128].rearrange("(kc p) n -> p kc n", p=128), w=[sl.b])
                    dma("gpsimd", sl[:, :, 1, :], w2d[:, C_LRUG + c * 128:C_LRUG + (c + 1) * 128].rearrange("(kc p) n -> p kc n", p=128), w=[sl.b])
                    slv = sl.t.rearrange("p k a n -> p k (a n)")
                    proj_fm(slv, sl.b, 0, 128, hT.t, hT_b, pbank[0:4])
                    proj_fm(slv, sl.b, 128, 128, hT.t, hT_b, pbank[4:8])
                    cp("scalar", xa[:, 3:3 + SEQ], PA[:, :], [pbank[i][1] for i in range(4)], [xa.b])
                    cp("scalar", ga[:], PB[:, :], [pbank[i][1] for i in range(4, 8)], [ga.b])
                    cw = lambda k: lpT[:, c, k:k + 1]
                    act(xc[:], xa[:, 3:3 + SEQ], AF.Identity, [xa.b, lpT.b], [xc.b], scale=cw(3), bias=cw(4))
                    for k in range(3):
                        stt(xc[:], xa[:, k:k + SEQ], cw(k), xc[:], ALU.mult, ALU.add, [xa.b, lpT.b, xc.b], [xc.b])
                    cp("gpsimd", xcb[:], xc[:], [xc.b], [xcb.b])
                    for gk in range(2):
                        banks = pbank[4:8] if gk == 0 else pbank[0:4]
                        for sc in range(4):
                            mm(banks[sc][0], wg[:, gk, c, :], xcb[:, sc * 512:(sc + 1) * 512], True, True, [wg.b, xcb.b], [banks[sc][1]])
                    act(r_[:], PB[:, :], AF.Sigmoid, [pbank[i][1] for i in range(4, 8)], [r_.b], bias=lpT[:, c, 5:6])
                    act(i_[:], PA[:, :], AF.Sigmoid, [pbank[i][1] for i in range(4)], [i_.b], bias=lpT[:, c, 6:7])

                def lru_B(c):
                    p = c % 2
                    xc, r_, a_, i_, h_, ga, g_ = XC[p], R[p], A[p], I[p], H[p], GA[p], G[p]
                    act(a_[:], r_[:], AF.Exp, [r_.b, sp.b], [a_.b], scale=sp[:, c, 4:5])
                    act(r_[:], r_[:], AF.Exp, [r_.b, sp.b], [r_.b], scale=sp[:, c, 5:6])
                    act(r_[:], r_[:], AF.Sqrt, [r_.b], [r_.b], scale=-1.0, bias=1.0)
                    tt("gpsimd", i_[:], i_[:], xc[:], ALU.mult, [i_.b, xc.b], [i_.b])
                    tt("gpsimd", i_[:], i_[:], r_[:], ALU.mult, [i_.b, r_.b], [i_.b])
                    S.op("vector", lambda e, h_=h_, a_=a_, i_=i_: e.tensor_tensor_scan(out=h_[:], data0=a_[:], data1=i_[:], initial=0.0, op0=ALU.mult, op1=ALU.add),
                         [a_.b, i_.b], [h_.b])
                    act(g_[:], ga[:], AF.Square, [ga.b], [g_.b])
                    tsc("vector", g_[:], g_[:], 0.044715, ALU.mult, [g_.b], [g_.b], 1.0, ALU.add)
                    tt("gpsimd", g_[:], g_[:], ga[:], ALU.mult, [g_.b, ga.b], [g_.b])
                    act(g_[:], g_[:], AF.Sigmoid, [g_.b], [g_.b], scale=1.5957691216057308)
                    tt("gpsimd", g_[:], g_[:], ga[:], ALU.mult, [g_.b, ga.b], [g_.b])
                    ya = YA[p]
                    tt("vector", ya[:], g_[:], h_[:], ALU.mult, [g_.b, h_.b], [ya.b])
                    dma("sync", ysc[0, c * 128:(c + 1) * 128, :], ya[:], r=[ya.b], w=[b_ysc[0]])

                lru_A(0)
                for c in range(8):
                    lists = [S.record(lambda: lru_B(c))]
                    if c + 1 < 8:
                        lists.append(S.record(lambda: lru_A(c + 1)))
                    S.replay(lists)
                S.barrier()

        def phase_gla(l):
            w2d = w_in[l]
            with ExitStack() as ph:
                qT = T(ph, "gqT", [128, 4, SEQ], F32)
                kT = T(ph, "gkT", [128, 4, SEQ], F32)
                lrT = T(ph, "lrT", [32, SEQ], F32)
                wa2x = T(ph, "wa2x", [32, 512], F32)
                wres = T(ph, "gwres", [128, 8, 2560], BF16)
                gnb = T(ph, "gnb", [128, 4, 256], F32)
                st_f = T(ph, "st_f", [128, 4, 256], F32)
                st_b = T(ph, "st_b", [128, 4, 256], BF16)
                cm4 = T(ph, "cm4", [128, 4, 128], F32)
                triu = T(ph, "triu", [128, 128], F32)
                tril = T(ph, "tril", [128, 128], F32)
                dma("sync", triu[:], cin["triu"], w=[triu.b])
                dma("sync", tril[:], cin["tril"], w=[tril.b])
                for hh in range(4):
                    dma("sync", cm4[:, hh, :], cin["triu"], w=[cm4.b])
                    src = gla_norm[l:l + 1, :]
                    dma("sync", gnb[:, hh, :], bass.AP(src.tensor, src.offset, [[0, 128], [1, 256]]), w=[gnb.b])
                memset("vector", wa2x[:], 0.0, [wa2x.b])
                memset("vector", lrT[:], 1.0, [lrT.b])
                dma("sync", wa2x[0:16, :], gla_wa2[l], w=[wa2x.b])
                dma("sync", wa2x[16:17, :], gla_ba[l:l + 1, :], w=[wa2x.b])
                for i, c0 in enumerate((C_GK, C_GV, C_GV + 512, C_GOG, C_GOG + 512)):
                    load_slab(wres[:, :, i * 512:(i + 1) * 512], w2d, c0, 512, wres.b)
                with ExitStack() as ph2:
                    slab = [T(ph2, f"gslab{i}", [128, 8, 512], BF16) for i in range(2)]
                    lslab = T(ph2, "glslab", [128, 8, 16], BF16)
                    load_slab(slab[0][:], w2d, C_GQ, 512, slab[0].b)
                    load_slab(slab[1][:], w2d, C_GK, 512, slab[1].b)
                    load_slab(lslab[:], w2d, C_GLR, 16, lslab.b)
                    for i in range(8):
                        banks = pbank[0:4] if i % 2 == 0 else pbank[4:8]
                        src = PA if i % 2 == 0 else PB
                        proj_fm(slab[i // 4].t, slab[i // 4].b, (i % 4) * 128, 128, hT.t, hT_b, banks)
                        dst = qT if i < 4 else kT
                        act(dst[:, i % 4, :], src[:, :], AF.Copy, [b for _, b in banks], [dst.b], scale=(128 ** -0.5 if i < 4 else 1.0))
                    proj_fm(lslab.t, lslab.b, 0, 16, hT.t, hT_b, pbank[0:4])
                    cp("vector", lrT[0:16, :], PA[0:16, :], [b for _, b in pbank[0:4]], [lrT.b])
                    S.barrier()
                sp_t = T(ph, "g_sp", [128, 512], F32)
                E1 = [T(ph, f"g_E1{i}", [128, 512], F32) for i in range(2)]
                E2 = T(ph, "g_E2", [128, 512], F32)
                Erb = T(ph, "g_Erb", [128, 512], F32)
                qtb = [T(ph, f"g_qtb{i}", [128, 4, 128], BF16) for i in range(2)]
                ktb = T(ph, "g_ktb", [128, 4, 128], BF16)
                kend = [T(ph, f"g_kend{i}", [128, 512], BF16) for i in range(2)]
                v_bf = [T(ph, f"g_vbf{i}", [128, 1024], BF16) for i in range(2)]
                sg = [T(ph, f"g_sg{i}", [128, 1024], F32) for i in range(2)]
                attm = [T(ph, f"g_attm{i}", [128, 4, 128], BF16) for i in range(2)]
                on = T(ph, "g_on", [128, 1024], F32)
                yc = [T(ph, f"g_yc{i}", [128, 1024], BF16) for i in range(2)]
                ycT = [T(ph, f"g_ycT{i}", [128, 8, 128], BF16) for i in range(2)]
                junk = T(ph, "g_junk", [128, 256], BF16)
                ssq = T(ph, "g_ssq", [128, 4], F32)
                rst = T(ph, "g_rst", [128, 4], F32)
                if os.environ.get("SBUF_DBG"):
                    print("GLA sbuf remaining", nc.sbuf_bytes_remaining)
                bk = lambda i: pbank[i][0]
                bb = lambda i: pbank[i][1]

                def gla_A(t):
                    p = t % 2
                    tsl = slice(t * 128, (t + 1) * 128)
                    e1, qb, ke, vb, sg_, am = E1[p], qtb[p], kend[p], v_bf[p], sg[p], attm[p]
                    mm(bk(0), lrT[0:17, tsl], wa2x[0:17, :], True, True, [lrT.b, wa2x.b], [bb(0)])
                    act(sp_t[:], bk(0), AF.Exp, [bb(0)], [sp_t.b], scale=-1.0)
                    act(sp_t[:], sp_t[:], AF.Ln, [sp_t.b], [sp_t.b], bias=1.0)
                    for hh in range(4):
                        mm(bk(1)[:, hh * 128:(hh + 1) * 128], sp_t[:, hh * 128:(hh + 1) * 128], triu[:], True, True, [sp_t.b, triu.b], [bb(1)])
                    mm(bk(2), tril[:], sp_t[:], True, True, [tril.b, sp_t.b], [bb(2)])
                    act(e1[:], bk(1), AF.Exp, [bb(1)], [e1.b], scale=-1.0 / 16.0)
                    act(E2[:], bk(1), AF.Exp, [bb(1)], [E2.b], scale=1.0 / 16.0)
                    act(Erb[:], bk(2), AF.Exp, [bb(2)], [Erb.b], scale=-1.0 / 16.0)
                    tt("vector", qb[:], qT[:, :, tsl], e1.t.rearrange("p (h s) -> p h s", h=4), ALU.mult, [qT.b, e1.b], [qb.b])
                    tt("gpsimd", ktb[:], kT[:, :, tsl], E2.t.rearrange("p (h s) -> p h s", h=4), ALU.mult, [kT.b, E2.b], [ktb.b])
                    for kc in range(8):
                        mm(bk(3), hT[:, kc, tsl], wres[:, kc, 0:512], kc == 0, kc == 7, [hT_b[t], wres.b], [bb(3)])
                    tt("vector", ke[:], bk(3), Erb[:], ALU.mult, [bb(3), Erb.b], [ke.b])
                    for half in range(2):
                        for kc in range(8):
                            mm(bk(half), hT[:, kc, tsl], wres[:, kc, 512 + half * 512:1024 + half * 512], kc == 0, kc == 7, [hT_b[t], wres.b], [bb(half)])
                    cp("scalar", vb[:], PA[:, 0:1024], [bb(0), bb(1)], [vb.b])
                    for half in range(2):
                        for kc in range(8):
                            mm(bk(2 + half), hT[:, kc, tsl], wres[:, kc, 1536 + half * 512:2048 + half * 512], kc == 0, kc == 7, [hT_b[t], wres.b], [bb(2 + half)])
                    act(sg_[:], PA[:, 1024:2048], AF.Silu, [bb(2), bb(3)], [sg_.b])
                    tt("gpsimd", sg_[:], sg_[:], gnb.t.rearrange("p h e -> p (h e)"), ALU.mult, [sg_.b, gnb.b], [sg_.b])
                    for hh in range(4):
                        mm(bk(0)[:, hh * 128:(hh + 1) * 128], ktb[:, hh, :], qb[:, hh, :], True, True, [ktb.b, qb.b], [bb(0)])
                    tt("vector", am[:], bk(0).rearrange("p (h s) -> p h s", h=4), cm4[:], ALU.mult, [bb(0), cm4.b], [am.b])

                def gla_B(t):
                    p = t % 2
                    tsl = slice(t * 128, (t + 1) * 128)
                    e1, qb, ke, vb, sg_, am = E1[p], qtb[p], kend[p], v_bf[p], sg[p], attm[p]
                    for hh in range(4):
                        ob = 4 + hh // 2
                        oap = bk(ob)[:, (hh % 2) * 256:(hh % 2 + 1) * 256]
                        mm(oap, am[:, hh, :], vb[:, hh * 256:(hh + 1) * 256], hh % 2 == 0, t == 0 and hh % 2 == 1, [am.b, vb.b], [bb(ob)])
                        if t > 0:
                            mm(oap, qb[:, hh, :], st_b[:, hh, :], False, hh % 2 == 1, [qb.b, st_b.b], [bb(ob)])
                    for hh in range(4):
                        kb_ = 6 + hh // 2
                        mm(bk(kb_)[:, (hh % 2) * 256:(hh % 2 + 1) * 256], ke[:, hh * 128:(hh + 1) * 128], vb[:, hh * 256:(hh + 1) * 256],
                           hh % 2 == 0, hh % 2 == 1, [ke.b, vb.b], [bb(kb_)])
                    for hh in range(4):
                        kvp = bk(6 + hh // 2)[:, (hh % 2) * 256:(hh % 2 + 1) * 256]
                        if t == 0:
                            cp("vector", st_f[:, hh, :], kvp, [bb(6 + hh // 2)], [st_f.b])
                        else:
                            dec = e1[:, hh * 128 + 127:hh * 128 + 128]
                            stt(st_f[:, hh, :], st_f[:, hh, :], dec, kvp, ALU.mult, ALU.add, [st_f.b, e1.b, bb(6 + hh // 2)], [st_f.b])
                    cp("gpsimd", st_b[:], st_f[:], [st_f.b], [st_b.b])
                    for hh in range(4):
                        oap = bk(4 + hh // 2)[:, (hh % 2) * 256:(hh % 2 + 1) * 256]
                        act(junk[:], oap, AF.Square, [bb(4 + hh // 2)], [junk.b, ssq.b], accum_out=ssq[:, hh:hh + 1])
                    act(rst[:], ssq[:], AF.Sqrt, [ssq.b], [rst.b], scale=1.0 / 256.0, bias=EPS)
                    recip(rst[:], rst[:], [rst.b], [rst.b])
                    tt("vector", on.t.rearrange("p (h e) -> p h e", h=4), PB[:, 0:1024].rearrange("p (h e) -> p h e", h=4),
                       fap(rst[:], [[1, 4], [0, 256]]), ALU.mult, [bb(4), bb(5), rst.b], [on.b])
                    y = yc[p]
                    tt("gpsimd", y[:], on[:], sg_[:], ALU.mult, [on.b, sg_.b], [y.b])
                    pv = bank_bf(7)
                    for c in range(8):
                        trp(pv[:, c * 128:(c + 1) * 128], y[:, c * 128:(c + 1) * 128], ident_b[:], [y.b, ident_b.b], [bb(7)])
                    yT = ycT[p]
                    cp("scalar", yT[:], pv.rearrange("p (k s) -> p k s", k=8), [bb(7)], [yT.b])
                    dma("sync", ysc[2, :, tsl].rearrange("(c p) s -> p c s", p=128), yT[:], r=[yT.b], w=[b_ysc[2]])

                gla_A(0)
                for t in range(NT):
                    lists = [S.record(lambda: gla_B(t))]
                    if t + 1 < NT:
                        lists.append(S.record(lambda: gla_A(t + 1)))
                    S.replay(lists)
                S.barrier()

        def setup_tables():
            with ExitStack() as ph:
                tblx = T(ph, "tblx", [33, 16], F32)
                memset("vector", tblx[:], NEG, [tblx.b])
                dma("sync", tblx[0:32, :], rel_table, w=[tblx.b])
                for name, dst, n in (("oh_c", tc_d, 4096), ("oh_s", ts_d, 1024), ("oh_w", tw_d, 1024)):
                    oh = T(ph, "t_" + name, [33, n], F32)
                    thi = T(ph, "thi_" + name, [16, n], BF16)
                    tlo = T(ph, "tlo_" + name, [16, n], BF16)
                    dma("sync", oh[:], cin[name], w=[oh.b])
                    for ch in range(n // 512):
                        pa, pbuf = pbank[ch % 8]
                        mm(pa[0:16, :], tblx[0:33, 0:16], oh[0:33, ch * 512:(ch + 1) * 512], True, True, [tblx.b, oh.b], [pbuf])
                        cp("vector", thi[:, ch * 512:(ch + 1) * 512], pa[0:16, :], [pbuf], [thi.b])
                        tt("vector", tlo[:, ch * 512:(ch + 1) * 512], pa[0:16, :], thi[:, ch * 512:(ch + 1) * 512], ALU.subtract, [pbuf, thi.b], [tlo.b])
                    dma("sync", dst[0], thi[:], r=[thi.b], w=[b_tabs])
                    dma("sync", dst[1], tlo[:], r=[tlo.b], w=[b_tabs])
                S.barrier()

        def phase_nsa(l):
            w2d = w_in[l]
            bk = lambda i: pbank[i][0]
            bb = lambda i: pbank[i][1]
            with ExitStack() as ph:
                qT = T(ph, "nqT", [128, 8, SEQ], BF16)
                kS = T(ph, "nkS", [128, 4, SEQ], BF16)
                kW = T(ph, "nkW", [128, 4, SEQ], BF16)
                vS = T(ph, "nvS", [128, NT, 4, 66], BF16)
                vW = T(ph, "nvW", [128, NT, 4, 66], BF16)
                sgate = T(ph, "nsg", [128, NT, 48], F32)
                kcP = T(ph, "nkcP", [128, 2, 4, 128], BF16)
                vcx = T(ph, "nvcx", [128, 4, 98], BF16)
                hbt = T(ph, "nhbt", [128, 3, 4, 2, 512], BF16)
                NM = T(ph, "nNM", [128, 4, 2, 512], BF16)
                Jb = T(ph, "nJb", [128, 2, 128], BF16)
                expd = T(ph, "nexpd", [128, 2, NT, 128], BF16)
                cand = T(ph, "ncand", [128, NT, 32], F32)
                negc = T(ph, "nnegc", [128, NT, 32], F32)
                forced = T(ph, "nforced", [128, NT, 32], F32)
                dma("gpsimd", Jb[:, 0, :], cin["antiid"], w=[Jb.b])
                dma("gpsimd", Jb[:, 1, :], cin["antiid127"], w=[Jb.b])
                dma("gpsimd", expd[:, 0, :, :], cin["expand"], w=[expd.b])
                dma("gpsimd", expd[:, 1, :, :], cin["expand_near"], w=[expd.b])
                memset("vector", NM[:], 0.0, [NM.b])
                dma("sync", cand[:], cin["cand"], w=[cand.b])
                dma("sync", negc[:], cin["negc"], w=[negc.b])
                dma("sync", forced[:], cin["forced"], w=[forced.b])
                memset("vector", vcx[:], 0.0, [vcx.b])
                memset("vector", kcP[:], 0.0, [kcP.b])
                for g in range(4):
                    dma("gpsimd", vcx[:, g, 64:97], cin["ovx"], w=[vcx.b])
                for dl in range(3):
                    tsrc = tw_d if dl == 2 else ts_d
                    for g in range(4):
                        for hl in range(2):
                            for rp in range(2):
                                a0 = tsrc[hl, 4 * g + 2 * rp, 512 + dl * 128 - 127:512 + dl * 128 - 127 + 1]
                                src = bass.AP(a0.tensor, a0.offset, [[1, 128], [1024, 2], [1, 128]])
                                dst = hbt[:, dl, g, hl, :].rearrange("p (a b s) -> p a b s", a=2, b=2)[:, :, rp, :]
                                dma("sync", dst, src, r=[b_tabs], w=[hbt.b])
                for g in range(4):
                    for hl in range(2):
                        for par in range(2):
                            for rp in range(2):
                                h = 4 * g + 2 * rp + par
                                a0 = ts_d[hl, h, 640:641]
                                src = bass.AP(a0.tensor, a0.offset, [[0, 1], [0, 2], [1, 128]])
                                c0 = par * 256 + rp * 128
                                dma("sync", NM[32 + hl:33 + hl, g, :, c0:c0 + 128], src, r=[b_tabs], w=[NM.b])
                memset("vector", vS[:, :, :, 64:66], 1.0, [vS.b])
                memset("vector", vW[:, :, :, 64:66], 1.0, [vW.b])
                if stop == "nsa0":
                    S.barrier()
                    return
                with ExitStack() as ph2:
                    slab = [T(ph2, f"nslab{i}", [128, 8, 512], BF16) for i in range(2)]
                    wv = T(ph2, "nwv", [128, 8, 560], BF16)
                    for half in range(2):
                        sl = slab[half]
                        load_slab(sl[:], w2d, C_Q + half * 512, 512, sl.b)
                        for i in range(4):
                            c = half * 4 + i
                            banks = pbank[0:4] if c % 2 == 0 else pbank[4:8]
                            src = PA if c % 2 == 0 else PB
                            proj_fm(sl.t, sl.b, i * 128, 128, hT.t, hT_b, banks)
                            act(qT[:, c, :], src[:, :], AF.Copy, [b for _, b in banks], [qT.b], scale=0.125)
                    if stop == "nsa1a":
                        S.barrier()
                        return
                    n = 0
                    for idx, dst in ((2, kS), (4, kW)):
                        sl = slab[n % 2]
                        n += 1
                        for g in range(4):
                            c0 = C_KV + idx * 256 + g * 64
                            for dup in range(2):
                                dma("gpsimd", sl[:, :, g * 128 + dup * 64:g * 128 + dup * 64 + 64],
                                    w2d[:, c0:c0 + 64].rearrange("(kc p) n -> p kc n", p=128), w=[sl.b])
                        for g in range(4):
                            banks = pbank[0:4] if g % 2 == 0 else pbank[4:8]
                            src = PA if g % 2 == 0 else PB
                            proj_fm(sl.t, sl.b, g * 128, 128, hT.t, hT_b, banks)
                            cp("scalar" if g % 2 == 0 else "vector", dst[:, g, :], src[:, :], [b for _, b in banks], [dst.b])
                    if stop == "nsa1b":
                        S.barrier()
                        return
                    load_slab(wv[:, :, 0:256], w2d, C_KV + 3 * 256, 256, wv.b)
                    load_slab(wv[:, :, 256:512], w2d, C_KV + 5 * 256, 256, wv.b)
                    load_slab(wv[:, :, 512:560], w2d, C_GATE, 48, wv.b)
                    for t in range(NT):
                        tsl = slice(t * 128, (t + 1) * 128)
                        b0, b1 = (0, 1) if t % 2 == 0 else (2, 3)
                        for kc in range(8):
                            mm(bk(b0), hT[:, kc, tsl], wv[:, kc, 0:512], kc == 0, kc == 7, [hT_b[t], wv.b], [bb(b0)])
                        import os
                        SK = os.environ.get("NSA_SKIP", "")
                        if "g" not in SK:
                            for kc in range(8):
                                mm(bk(b1)[:, 0:48], hT[:, kc, tsl], wv[:, kc, 512:560], kc == 0, kc == 7, [hT_b[t], wv.b], [bb(b1)])
                        if "v" not in SK:
                            cp("vector", vS[:, t, :, 0:64], bk(b0)[:, 0:256].rearrange("p (g d) -> p g d", g=4), [bb(b0)], [vS.b])
                        if "w" not in SK:
                            cp("scalar", vW[:, t, :, 0:64], bk(b0)[:, 256:512].rearrange("p (g d) -> p g d", g=4), [bb(b0)], [vW.b])
                        if "g" not in SK:
                            act(sgate[:, t, :], bk(b1)[:, 0:48], AF.Sigmoid, [bb(b1)], [sgate.b])
                    S.barrier()
                if stop == "nsa1":
                    return
                with ExitStack() as ph2:
                    slab = [T(ph2, f"ncslab{i}", [128, 8, 256], BF16) for i in range(2)]
                    w1sb = T(ph2, "nw1", [128, 32, 256], BF16)
                    w2sb = T(ph2, "nw2", [128, 2, 128], BF16)
                    prow2 = T(ph2, "nprow2", [32, 128], F32)
                    posT = T(ph2, "nposT", [128, 32], F32)
                    XAB = [T(ph2, f"nXAB{i}", [128, SEQ], BF16) for i in range(2)]
                    gtmp = T(ph2, "ngtmp", [128, 2, 128], F32)
                    geluT = T(ph2, "ngeluT", [128, 2, 128], BF16)
                    for kv in range(2):
                        for dup in range(2):
                            dma("gpsimd", w1sb[dup * 64:(dup + 1) * 64, :, :], cmp_w1[l, kv].rearrange("(p d) j -> d p j", d=64), w=[w1sb.b])
                            dma("gpsimd", w2sb[:, :, dup * 64:(dup + 1) * 64], cmp_w2[l, kv].rearrange("(jc p) d -> p jc d", p=128), w=[w2sb.b])
                            dma("sync", prow2[:, dup * 64:(dup + 1) * 64], cmp_pos[l, kv], w=[prow2.b])
                        trp(bk(6)[:, 0:32], prow2[:, :], ident_f[0:32, 0:32], [prow2.b, ident_f.b], [bb(6)])
                        cp("vector", posT[:], bk(6)[:, 0:32], [bb(6)], [posT.b])
                        sl = slab[kv]
                        load_slab(sl[:, :, 0:256], w2d, C_KV + kv * 256, 256, sl.b)
                        for cc in range(2):
                            banks = pbank[0:4]
                            proj_fm(sl.t, sl.b, cc * 128, 128, hT.t, hT_b, banks)
                            for ab in range(2):
                                tt("vector" if ab == 0 else "gpsimd" if False else "vector", XAB[ab].t.rearrange("p (i q) -> p i q", q=16), PA.t.rearrange("p (i q) -> p i q", q=16),
                                   fap(posT[:, ab * 16:ab * 16 + 1], [[0, 128], [1, 16]]), ALU.add, [b for _, b in banks] + [posT.b], [XAB[ab].b])
                            for gg in range(2):
                                g = cc * 2 + gg
                                rows = slice(gg * 64, gg * 64 + 64)
                                hb_, hbb = bk(4 + 2 * gg), bb(4 + 2 * gg)
                                for jc in range(2):
                                    for p in range(32):
                                        srcT = XAB[0] if p < 16 else XAB[1]
                                        rhs = fap(srcT[rows, p:p + 1], [[16, 127]])
                                        mm(hb_[:, jc * 128:jc * 128 + 127], w1sb[rows, p, jc * 128:(jc + 1) * 128], rhs, p == 0, p == 31,
                                           [w1sb.b, srcT.b], [hbb])
                                hv = hb_[:, 0:256].rearrange("p (j i) -> p j i", j=2)[:, :, 0:127]
                                gv = gtmp[:, :, 0:127]
                                act(gv, hv, AF.Square, [hbb], [gtmp.b])
                                tsc("vector", gv, gv, 0.044715, ALU.mult, [gtmp.b], [gtmp.b], 1.0, ALU.add)
                                tt("vector", gv, gv, hv, ALU.mult, [gtmp.b, hbb], [gtmp.b])
                                act(gv, gv, AF.Sigmoid, [gtmp.b], [gtmp.b], scale=1.5957691216057308)
                                tt("vector", geluT[:, :, 0:127], gv, hv, ALU.mult, [gtmp.b, hbb], [geluT.b])
                                if kv == 0:
                                    for jc in range(2):
                                        mm(bk(5)[:, 0:127], w2sb[:, jc, :], geluT[:, jc, 0:127], jc == 0, jc == 1, [w2sb.b, geluT.b], [bb(5)])
                                    cp("scalar", kcP[0:64, 0, g, 0:127], bk(5)[0:64, 0:127], [bb(5)], [kcP.b])
                                    cp("scalar", kcP[64:128, 1, g, 0:127], bk(5)[64:128, 0:127], [bb(5)], [kcP.b])
                                else:
                                    for jc in range(2):
                                        mm(bk(5)[0:127, 0:64], geluT[:, jc, 0:127], w2sb[:, jc, 0:64], jc == 0, jc == 1, [w2sb.b, geluT.b], [bb(5)])
                                    cp("scalar", vcx[0:127, g, 0:64], bk(5)[0:127, 0:64], [bb(5)], [vcx.b])
                    S.barrier()
                if stop == "nsa2":
                    return
                kpad1 = Buf("kpad1")
                for base, KT_, ceng in ((0, kS, "scalar"), (4, kW, "vector")):
                    cp(ceng, hT[64:128, base:base + 4, :], KT_[64:128, :, :], [KT_.b], hT_b + [kpad1])
                    memset("gpsimd", hT[0:64, base:base + 4, :], 0.0, hT_b + [kpad1])
                    memset("gpsimd" if base == 0 else "vector", KT_[64:128, :, :], 0.0, [KT_.b])
                E = [T(ph, f"nE{i}", [128, 512], BF16) for i in range(4)]
                cb = [T(ph, f"ncb{i}", [128, 2, 512], BF16) for i in range(5)]
                for cbx in cb:
                    memset("vector", cbx[:], NEG, [cbx.b])
                ybt = [T(ph, f"nybt{i}", [128, 1024], BF16) for i in range(2)]
                ybT = [T(ph, f"nybT{i}", [128, 8, 128], BF16) for i in range(2)]
                sets = []
                for i in range(2):
                    sets.append(dict(
                        ybacc=T(ph, f"nybacc{i}", [128, 4, 64], F32), tmp1=T(ph, f"ntmp1{i}", [128, 4, 64], F32),
                        tmp2=T(ph, f"ntmp2{i}", [128, 4, 64], F32), impr=T(ph, f"nimpr{i}", [128, 4, 32], F32),
                        imp=T(ph, f"nimp{i}", [128, 32], F32), m8=T(ph, f"nm8{i}", [128, 8], F32),
                        sm=T(ph, f"nsm{i}", [128, 3, 4], F32), Us=(3, 6)[i], Uw=(4, 7)[i]))
                colb = lambda r: (r % 2) * 256 + (r // 2) * 128
                cnt_ = dict(l=0, e=0, kp=0, cb=0)
                tasks = []

                inflight = set()

                def next_Li(hold=False):
                    while True:
                        i = (0, 1, 5)[cnt_["l"] % 3]
                        cnt_["l"] += 1
                        if i not in inflight:
                            break
                    if hold:
                        inflight.add(i)
                    return i

                def next_L(hold=False):
                    return pbank[next_Li(hold)]

                def release_L(Lb):
                    for i in (0, 1, 5):
                        if pbank[i][1] is Lb:
                            inflight.discard(i)

                def next_E():
                    e_ = E[cnt_["e"] % 4]
                    cnt_["e"] += 1
                    return e_

                cb_dma = []
                PF = 4

                def mk_cmp(qt, g, st):
                    qsl = slice(qt * 128, (qt + 1) * 128)
                    ui = len(cb_dma)
                    cbt = cb[ui % 5]
                    box = {}

                    def issue():
                        for hl in range(2):
                            for rp in range(2):
                                a0 = tc_d[hl, 4 * g + 2 * rp, qt * 128 + 1:qt * 128 + 2]
                                src = bass.AP(a0.tensor, a0.offset, [[16, 128], [4096, 2], [1, 128]])
                                dst = cbt[:, hl, :].rearrange("p (a b s) -> p a b s", a=2, b=2)[:, :, rp, :]
                                dma("sync", dst, src, r=[b_tabs], w=[cbt.b])

                    cb_dma.append(issue)

                    def pre():
                        if ui + PF < len(cb_dma):
                            cb_dma[ui + PF]()

                    def s1():
                        L, Lb = next_L(hold=True)
                        box["L"] = (L, Lb)
                        for par in range(2):
                            mm(L[:, par * 256:(par + 1) * 256], kcP[:, par, g, :], qT[:, 2 * g:2 * g + 2, qsl], par == 0, False, [kcP.b, qT.b], [Lb])
                        for hl in range(2):
                            mm(L, Jb[:, 1, :], cbt[:, hl, :], False, hl == 1, [Jb.b, cbt.b], [Lb])

                    def s2():
                        L, Lb = box["L"]
                        release_L(Lb)
                        Ec = next_E()
                        act(Ec[:], L, AF.Exp, [Lb], [Ec.b])
                        Uc = bk(2)
                        for r in range(4):
                            mm(Uc[:, r * 98:(r + 1) * 98], Ec[:, colb(r):colb(r) + 128], vcx[:, g, 0:98], r == 0, r == 3, [Ec.b, vcx.b], [bb(2)])

                    def post():
                        Uc = bk(2)
                        sm, ybacc, impr, imp, m8 = st["sm"], st["ybacc"], st["impr"], st["imp"], st["m8"]
                        ucv = lambda a, b_: fap(Uc[:, a:a + 1], [[98, 4], [1, b_]])
                        rs4, wc = sm[:, 0, :], sm[:, 1, :]
                        tsc("vector", rs4, fap(Uc[:, 96:97], [[98, 4]]), 1e-30, ALU.max, [bb(2)], [sm.b])
                        recip(rs4, rs4, [sm.b], [sm.b])
                        tt("vector", wc, rs4, sgate[:, qt, 4 * g:4 * g + 4], ALU.mult, [sm.b, sgate.b], [sm.b])
                        tt("vector", ybacc[:], ucv(0, 64), fap(wc, [[1, 4], [0, 64]]), ALU.mult, [bb(2), sm.b], [ybacc.b])
                        tt("vector", impr[:], ucv(64, 32), fap(rs4, [[1, 4], [0, 32]]), ALU.mult, [bb(2), sm.b], [impr.b])
                        S.op("vector", lambda e: e.tensor_reduce(out=imp[:], in_=fap(impr[:, 0, 0:1], [[1, 32], [32, 4]]), axis=AX.X, op=ALU.add),
                             [impr.b], [imp.b])
                        tt("vector", imp[:], imp[:], cand[:, qt, :], ALU.mult, [imp.b, cand.b], [imp.b])
                        tt("vector", imp[:], imp[:], negc[:, qt, :], ALU.add, [imp.b, negc.b], [imp.b])
                        S.op("vector", lambda e: e.max(out=m8[:], in_=imp[:]), [imp.b], [m8.b])
                        tsc("vector", imp[:], imp[:], m8[:, 4:5], ALU.is_ge, [imp.b, m8.b], [imp.b])
                        tt("vector", imp[:], imp[:], forced[:, qt, :], ALU.max, [imp.b, forced.b], [imp.b])
                        tsc("vector", imp[:], imp[:], -1.0, ALU.add, [imp.b], [imp.b], -NEG, ALU.mult)

                    return dict(pre=pre, s1=s1, s2=s2, post=post, defer=None, nm=None, first_slc=False)

                def mk_tile(qt, g, st, br, kt, first, last, buf, hooks_pre, hooks_post, defer):
                    qsl = slice(qt * 128, (qt + 1) * 128)
                    ksl = slice(kt * 128, (kt + 1) * 128)
                    KT = kS if br == 0 else kW
                    VT = vS if br == 0 else vW
                    Ub = st["Us"] if br == 0 else st["Uw"]
                    dl = qt - kt
                    near = dl < (2 if br == 0 else 3)
                    box = {}

                    def pre():
                        for h_ in hooks_pre:
                            h_()

                    def s1():
                        L, Lb = next_L(hold=True)
                        box["L"] = (L, Lb)
                        mm(L[:, 0:256], KT[:, g, ksl], qT[:, 2 * g:2 * g + 2, qsl], True, False, [KT.b, qT.b], [Lb])
                        mm(L[:, 256:512], hT[:, (0 if br == 0 else 4) + g, ksl], qT[:, 2 * g:2 * g + 2, qsl], False, False, [kpad1, qT.b], [Lb])
                        if br == 0:
                            mm(L, expd[:, 1 if near else 0, kt, :], NM[:, g, buf, :], False, not near, [expd.b, NM.b], [Lb])
                        if near:
                            for hl in range(2):
                                mm(L, Jb[:, 0, :], hbt[:, dl, g, hl, :], False, hl == 1, [Jb.b, hbt.b], [Lb])

                    def s2():
                        L, Lb = box["L"]
                        release_L(Lb)
                        Et = next_E()
                        act(Et[:], L, AF.Exp, [Lb], [Et.b])
                        for r in range(4):
                            mm(bk(Ub)[:, r * 66:(r + 1) * 66], Et[:, colb(r):colb(r) + 128], VT[:, kt, g, 0:66],
                               first and r == 0, last and r == 3, [Et.b, VT.b], [bb(Ub)])

                    def post():
                        for h_ in hooks_post:
                            h_()

                    return dict(pre=pre, s1=s1, s2=s2, post=post, defer=defer, nm=None, first_slc=False)

                def mk_nm_hook(g, st, buf):
                    def hook():
                        imp = st["imp"]
                        M_, Mb = next_L()
                        trp(M_[0:32, 0:128], imp[:, :], ident_f[:, :], [imp.b, ident_f.b], [Mb])
                        cp("vector", NM[0:32, g, buf, :].rearrange("p (a s) -> p a s", a=4), fap(M_[0:32, 0:1], [[0, 4], [1, 128]]), [Mb], [NM.b])
                    return hook

                def mk_combine(qt, g, st, ybq):
                    def hook():
                        sm, ybacc, tmp1, tmp2 = st["sm"], st["ybacc"], st["tmp1"], st["tmp2"]
                        for br in range(2):
                            ub = st["Us"] if br == 0 else st["Uw"]
                            U = bk(ub)
                            rsb, wb_ = sm[:, 0, :], sm[:, 1 + br, :]
                            S.op("vector", lambda e, U=U, rsb=rsb: e.reciprocal(out=rsb, in_=fap(U[:, 64:65], [[66, 4]])), [bb(ub)], [sm.b])
                            tt("vector", wb_, rsb, sgate[:, qt, 16 * (br + 1) + 4 * g:16 * (br + 1) + 4 * g + 4], ALU.mult, [sm.b, sgate.b], [sm.b])
                            tgt = tmp1 if br == 0 else tmp2
                            tt("vector", tgt[:], fap(U[:, 0:1], [[66, 4], [1, 64]]), fap(wb_, [[1, 4], [0, 64]]), ALU.mult, [bb(ub), sm.b], [tgt.b])
                        tt("gpsimd", tmp1[:], tmp1[:], ybacc[:], ALU.add, [tmp1.b, ybacc.b], [tmp1.b])
                        tt("gpsimd", ybq[:, g * 256:(g + 1) * 256].rearrange("p (r d) -> p r d", r=4), tmp1[:], tmp2[:], ALU.add, [tmp1.b, tmp2.b], [ybq.b])
                    return hook

                def mk_ybout(qt, ybq):
                    def hook():
                        qsl = slice(qt * 128, (qt + 1) * 128)
                        li = next_Li()
                        pv = bank_bf(li)
                        for c in range(8):
                            trp(pv[:, c * 128:(c + 1) * 128], ybq[:, c * 128:(c + 1) * 128], ident_b[:], [ybq.b, ident_b.b], [bb(li)])
                        yT = ybT[qt % 2]
                        cp("scalar", yT[:], pv.rearrange("p (k s) -> p k s", k=8), [bb(li)], [yT.b])
                        dma("sync", ysc[1, :, qsl].rearrange("(c p) s -> p c s", p=128), yT[:], r=[yT.b], w=[b_ysc[1]])
                    return hook

                un = 0
                units = []
                for qt in range(1 if stop == 'nsa3' else NT):
                    ybq = ybt[qt % 2]
                    for g in range(4):
                        st = sets[un % 2]
                        buf = un % 2
                        un += 1
                        uc_ = [mk_cmp(qt, g, st)]
                        uc_[0]["nm"] = mk_nm_hook(g, st, buf)
                        uc_[0]["unit"] = len(units)
                        wk = list(range(max(0, qt - 2), qt + 1))
                        uw_ = [mk_tile(qt, g, st, 1, kt, kt == wk[0], kt == qt, buf, [], [], None) for kt in wk]
                        us_ = []
                        for kt in range(qt + 1):
                            hp = []
                            hq = [mk_combine(qt, g, st, ybq)] if kt == qt else []
                            df = mk_ybout(qt, ybq) if (kt == qt and g == 3) else None
                            us_.append(mk_tile(qt, g, st, 0, kt, kt == 0, kt == qt, buf, hp, hq, df))
                        us_[0]["first_slc"] = True
                        us_[0]["unit"] = len(units)
                        units.append((uc_, uw_, us_))
                tasks += units[0][0] + units[0][1]
                for ui in range(len(units)):
                    if ui + 1 < len(units):
                        tasks += units[ui + 1][0]
                    tasks += units[ui][2]
                    if ui + 1 < len(units):
                        tasks += units[ui + 1][1]
                for ui_ in range(min(PF, len(cb_dma))):
                    cb_dma[ui_]()
                deferred = {}
                ntk = len(tasks)
                LA = 3
                for j in range(min(LA, ntk)):
                    tasks[j]["pre"]()
                    tasks[j]["s1"]()
                first_idx = {tk["unit"]: i for i, tk in enumerate(tasks) if tk["first_slc"]}
                for i, tk in enumerate(tasks):
                    tk["s2"]()
                    tk["post"]()
                    if tk["nm"] is not None:
                        j = max(i, min(i + 8, first_idx[tk["unit"]] - LA))
                        deferred.setdefault(j, []).append(tk["nm"])
                    if tk["defer"] is not None:
                        deferred.setdefault(i + 3, []).append(tk["defer"])
                    for fn in deferred.pop(i, []):
                        fn()
                    if i + LA < ntk:
                        tasks[i + LA]["pre"]()
                        tasks[i + LA]["s1"]()
                for k_ in sorted(deferred):
                    for fn in deferred[k_]:
                        fn()
                dma("sync", hT.t.rearrange("p k s -> p (k s)"), hsc, r=[b_hsc], w=hT_b + [kpad1])
                S.barrier()

        def phase_tail(l, x_src, x_dst, b_xsrc, b_xdst):
            bk = lambda i: pbank[i][0]
            bb = lambda i: pbank[i][1]
            w2d = w_in[l]
            with ExitStack() as ph:
                mrgb = T(ph, "mrgb", [128, 8, SEQ], BF16)
                with ExitStack() as ph2:
                    mrg = T(ph2, "mrg", [128, 8, SEQ], F32)
                    yT = T(ph2, "m_yT", [128, 8, SEQ], BF16)
                    yb_ = [Buf(f"m_yT{c}") for c in range(8)]
                    wbr = [T(ph2, f"m_wbr{i}", [128, 8, 256], BF16) for i in range(2)]
                    wmg = [T(ph2, f"m_wmg{i}", [128, 8, 256], BF16) for i in range(2)]
                    sig = [T(ph2, f"m_sig{i}", [128, 512], F32) for i in range(2)]
                    prod = [T(ph2, f"m_prod{i}", [128, 512], F32) for i in range(2)]
                    n = 0
                    bn = 0
                    ybB_ = [Buf(f"m_yTB{c}") for c in range(8)]
                    for c in range(8):
                        dma("sync", yT[:, c, :], ysc[0, c * 128:(c + 1) * 128, :], r=[b_ysc[0]], w=[yb_[c]])
                    for br in range(3):
                        if br == 1:
                            for c in range(8):
                                dma("sync", yT[:, c, :], ysc[2, c * 128:(c + 1) * 128, :], r=[b_ysc[2]], w=[yb_[c]])
                        ysrc, ybufs = (mrgb, ybB_) if br == 1 else (yT, yb_)
                        for oc2 in range(4):
                            if br == 0 and oc2 == 2:
                                for c in range(8):
                                    dma("sync", mrgb[:, c, :], ysc[1, c * 128:(c + 1) * 128, :], r=[b_ysc[1]], w=[ybB_[c], mrgb.b])
                            wb, wm = wbr[oc2 % 2], wmg[oc2 % 2]
                            load_slab(wb[:], w_branch[l, br], oc2 * 256, 256, wb.b)
                            load_slab(wm[:], w2d, C_MG + br * 1024 + oc2 * 256, 256, wm.b)
                            for o in range(2):
                                oc = oc2 * 2 + o
                                for sc in range(4):
                                    ssl = slice(sc * 512, (sc + 1) * 512)
                                    bB, bG = (bn % 4) * 2, (bn % 4) * 2 + 1
                                    bn += 1
                                    for c in range(8):
                                        mm(bk(bB), wb[:, c, o * 128:(o + 1) * 128], ysrc[:, c, ssl], c == 0, c == 7, [wb.b, ybufs[c]], [bb(bB)])
                                    for kc in range(8):
                                        mm(bk(bG), wm[:, kc, o * 128:(o + 1) * 128], hT[:, kc, ssl], kc == 0, kc == 7, [wm.b] + hT_b[sc * 4:(sc + 1) * 4], [bb(bG)])
                                    sg_, pr_ = sig[n % 2], prod[n % 2]
                                    n += 1
                                    act(sg_[:], bk(bG), AF.Sigmoid, [bb(bG)], [sg_.b])
                                    if br == 0:
                                        tt("vector", mrg[:, oc, ssl], sg_[:], bk(bB), ALU.mult, [sg_.b, bb(bB)], [mrg.b])
                                    elif br == 1:
                                        tt("vector", pr_[:], sg_[:], bk(bB), ALU.mult, [sg_.b, bb(bB)], [pr_.b])
                                        tt("vector", mrg[:, oc, ssl], mrg[:, oc, ssl], pr_[:], ALU.add, [mrg.b, pr_.b], [mrg.b])
                                    else:
                                        tt("vector", pr_[:], sg_[:], bk(bB), ALU.mult, [sg_.b, bb(bB)], [pr_.b])
                                        tt("vector", mrgb[:, oc, ssl], mrg[:, oc, ssl], pr_[:], ALU.add, [mrg.b, pr_.b], [mrgb.b] + ybB_)
                    S.barrier()
                with ExitStack() as ph2:
                    wout = T(ph2, "p_wout", [128, 8, D], BF16)
                    g1 = load_gain(ph2, l, 1)
                    g2 = load_gain(ph2, l, 2)
                    xt = [T(ph2, f"p_xt{i}", [128, D], F32) for i in range(2)]
                    on = [T(ph2, f"p_on{i}", [128, D], F32) for i in range(2)]
                    hb = [T(ph2, f"p_hb{i}", [128, D], BF16) for i in range(2)]
                    junk = T(ph2, "p_junk", [128, D], BF16)
                    ss = T(ph2, "p_ss", [128, NT], F32)
                    rs = T(ph2, "p_rs", [128, NT], F32)
                    ss2 = T(ph2, "p_ss2", [128, NT], F32)
                    rs2 = T(ph2, "p_rs2", [128, NT], F32)
                    for half in range(2):
                        load_slab(wout[:, :, half * 512:(half + 1) * 512], w_out[l], half * 512, 512, wout.b)
                    junk2 = T(ph2, "p_junk2", [128, D], BF16)
                    ssb = T(ph2, "p_ssb", [128, NT], F32)
                    rsb_ = T(ph2, "p_rsb", [128, NT], F32)
                    ss2b = T(ph2, "p_ss2b", [128, NT], F32)
                    rs2b = T(ph2, "p_rs2b", [128, NT], F32)

                    def tileP(t):
                        p = t % 2
                        jk, s_a, r_a, s_b, r_b = (junk, ss, rs, ss2, rs2) if p == 0 else (junk2, ssb, rsb_, ss2b, rs2b)
                        tsl = slice(t * 128, (t + 1) * 128)
                        b0 = p * 2
                        ov = PA[:, b0 * 512:(b0 + 2) * 512]
                        obufs = [bb(b0), bb(b0 + 1)]
                        for half in range(2):
                            for c in range(8):
                                mm(bk(b0 + half), mrgb[:, c, tsl], wout[:, c, half * 512:(half + 1) * 512], c == 0, c == 7, [mrgb.b, wout.b], [bb(b0 + half)])
                        x_, o_ = xt[p], on[p]
                        dma("sync", x_[:], x_src[tsl, :], r=[b_xsrc], w=[x_.b])
                        act(jk[:], ov, AF.Square, obufs, [jk.b, s_a.b], accum_out=s_a[:, t:t + 1])
                        act(r_a[:, t:t + 1], s_a[:, t:t + 1], AF.Sqrt, [s_a.b], [r_a.b], scale=1.0 / D, bias=EPS)
                        recip(r_a[:, t:t + 1], r_a[:, t:t + 1], [r_a.b], [r_a.b])
                        stt(o_[:], ov, r_a[:, t:t + 1], g1[:], ALU.mult, ALU.mult, obufs + [r_a.b, g1.b], [o_.b])
                        tt("gpsimd", o_[:], o_[:], x_[:], ALU.add, [o_.b, x_.b], [o_.b])
                        dma("sync", xmid[tsl, :], o_[:], r=[o_.b], w=[b_xmid])
                        norm_transpose_tile(ph2, t, o_[:], o_.b, g2, s_b, r_b, hb[p], jk, 4 + p)

                    for t in range(0, NT, 2):
                        S.replay([S.record(lambda: tileP(t)), S.record(lambda: tileP(t + 1))])
                    S.barrier()
            with ExitStack() as ph:
                hid = T(ph, "f_hid", [128, 22, SEQ], BF16)
                hid_b = [Buf(f"hid{j}") for j in range(22)]
                wfo = T(ph, "f_wfo", [128, 22, D], BF16)
                wfo_b = [Buf(f"wfo{j}") for j in range(11)]
                wfi = [T(ph, f"f_wfi{i}", [128, 8, 2, 128], BF16) for i in range(2)]
                sgt = [T(ph, f"f_sg{i}", [128, 512], F32) for i in range(2)]
                g3 = load_gain(ph, l, 3)
                xt = [T(ph, f"f_xt{i}", [128, D], F32) for i in range(2)]
                on = [T(ph, f"f_on{i}", [128, D], F32) for i in range(2)]
                junk = T(ph, "f_junk", [128, D], BF16)
                ss = T(ph, "f_ss", [128, NT], F32)
                rs = T(ph, "f_rs", [128, NT], F32)
                n = 0
                bn = 0
                for j in range(22):
                    wf = wfi[j % 2]
                    dma("gpsimd", wf[:, :, 0, :], w_ffn_in[l][:, j * 128:(j + 1) * 128].rearrange("(kc p) n -> p kc n", p=128), w=[wf.b])
                    dma("gpsimd", wf[:, :, 1, :], w_ffn_in[l][:, DFF + j * 128:DFF + (j + 1) * 128].rearrange("(kc p) n -> p kc n", p=128), w=[wf.b])
                    if j % 2 == 0:
                        jj = j // 2
                        dma("gpsimd", wfo[:, 2 * jj:2 * jj + 2, :], w_ffn_out[l][jj * 256:(jj + 1) * 256, :].rearrange("(j p) n -> p j n", p=128), w=[wfo_b[jj]])
                    for sc in range(4):
                        ssl = slice(sc * 512, (sc + 1) * 512)
                        bG, bU = (bn % 4) * 2, (bn % 4) * 2 + 1
                        bn += 1
                        for kc in range(8):
                            mm(bk(bG), wf[:, kc, 0, :], hT[:, kc, ssl], kc == 0, kc == 7, [wf.b] + hT_b[sc * 4:(sc + 1) * 4], [bb(bG)])
                        for kc in range(8):
                            mm(bk(bU), wf[:, kc, 1, :], hT[:, kc, ssl], kc == 0, kc == 7, [wf.b] + hT_b[sc * 4:(sc + 1) * 4], [bb(bU)])
                        sg_ = sgt[n % 2]
                        n += 1
                        act(sg_[:], bk(bG), AF.Silu, [bb(bG)], [sg_.b])
                        tt("vector", hid[:, j, ssl], sg_[:], bk(bU), ALU.mult, [sg_.b, bb(bU)], [hid_b[j]])
                for t in range(NT):
                    tsl = slice(t * 128, (t + 1) * 128)
                    b0 = (t % 2) * 2
                    ov = PA[:, b0 * 512:(b0 + 2) * 512]
                    obufs = [bb(b0), bb(b0 + 1)]
                    for half in range(2):
                        for j in range(22):
                            mm(bk(b0 + half), hid[:, j, tsl], wfo[:, j, half * 512:(half + 1) * 512], j == 0, j == 21, [hid_b[j], wfo_b[j // 2]], [bb(b0 + half)])
                    x_, o_ = xt[t % 2], on[t % 2]
                    dma("sync", x_[:], xmid[tsl, :], r=[b_xmid], w=[x_.b])
                    act(junk[:], ov, AF.Square, obufs, [junk.b, ss.b], accum_out=ss[:, t:t + 1])
                    act(rs[:, t:t + 1], ss[:, t:t + 1], AF.Sqrt, [ss.b], [rs.b], scale=1.0 / D, bias=EPS)
                    recip(rs[:, t:t + 1], rs[:, t:t + 1], [rs.b], [rs.b])
                    stt(o_[:], ov, rs[:, t:t + 1], g3[:], ALU.mult, ALU.mult, obufs + [rs.b, g3.b], [o_.b])
                    tt("gpsimd", o_[:], o_[:], x_[:], ALU.add, [o_.b, x_.b], [o_.b])
                    dma("sync", x_dst[tsl, :], o_[:], r=[o_.b], w=[b_xdst])
                S.barrier()

        setup_tables()
        for l in range(n_layers):
            phase_A(l, x_in if l == 0 else xres)
            import os
            if not os.environ.get("SKIP_LG"):
                phase_lru(l)
                if stop == "lru":
                    break
                phase_gla(l)
                if stop == "gla":
                    break
            if not os.environ.get("SKIP_NSA"):
                phase_nsa(l)
            if stop is not None and stop.startswith("nsa"):
                break
            last = (l == n_layers - 1)
            phase_tail(l, x_in if l == 0 else xres, y_out if last else xres, Buf() if l == 0 else b_xres, Buf() if last else b_xres)
        S.barrier()
        S.emit()
    return nc, consts


_CACHE = {}


def kernel(**inputs):
    if "nc" not in _CACHE:
        _CACHE["nc"] = build()
    nc, consts = _CACHE["nc"]
    x = np.ascontiguousarray(np.asarray(inputs["x"], dtype=np.float32))
    shared = {k: np.ascontiguousarray(np.asarray(v, dtype=np.float32)) for k, v in inputs.items() if k != "x"}
    for k, v in consts.items():
        shared["c_" + k] = v
    in_maps = [dict(shared, x=x[i]) for i in range(8)]
    res = run_bass_kernel_spmd(nc, in_maps, core_ids=list(range(8)))
    return np.stack([np.asarray(r["out"], dtype=np.float32) for r in res.results], axis=0)
```
